# Optimizing a Trainium2 kernel written in Bass

```python
import jax, jax.numpy as jnp
from jax import lax
import numpy as np

D_MODEL = 1024
BATCH = 16
SEQ = 2048
DEPTH = 2

N_A = DEPTH // 2
N_B = DEPTH - N_A
N_META = 16
A_EXPAND = 128
A_HEADS = D_MODEL // A_EXPAND
A_DV = D_MODEL // A_HEADS
A_CHUNK = 64
B_HEADS = 16
B_HDIM = D_MODEL // B_HEADS
Q_BLOCK = 128
FG_BIAS_INIT = 2.0
D_FF = ((8 * D_MODEL // 3 + 63) // 64) * 64
CONV_W = 3
EPS = 1e-6

kernel_name = "yoco_hgrn2_fox_hybrid"


def rmsnorm(x, g):
    xf = x.astype(jnp.float32)
    y = xf * lax.rsqrt(jnp.mean(xf * xf, axis=-1, keepdims=True) + EPS)
    return (y * g.astype(jnp.float32)).astype(x.dtype)


def causal_dwconv(u, w):
    C = u.shape[-1]
    return lax.conv_general_dilated(
        u, w[:, None, :].astype(u.dtype), window_strides=(1,),
        padding=[(CONV_W - 1, 0)], dimension_numbers=("NWC", "WIO", "NWC"),
        feature_group_count=C)


def conv_ffn(x, w_up, conv_w, w_down):
    u = causal_dwconv(x @ w_up, conv_w)
    gate, val = jnp.split(u, 2, axis=-1)
    return (jax.nn.silu(gate) * val) @ w_down


def gla_chunks(q, k, v, logf, s0):
    b = jnp.cumsum(logf, axis=3)
    b_last = b[..., -1:, :]
    q_in = q * jnp.exp(b)
    k_in = k * jnp.exp(-b)
    k_out = k * jnp.exp(b_last - b)
    C = q.shape[3]
    causal = jnp.tril(jnp.ones((C, C), dtype=bool))
    attn = jnp.where(causal, jnp.einsum("bhnck,bhnsk->bhncs", q_in, k_in), 0.0)
    o_intra = jnp.einsum("bhncs,bhnsv->bhncv", attn, v)
    dS = jnp.einsum("bhnck,bhncv->bhnkv", k_out, v)
    decay = jnp.exp(b_last[..., 0, :])

    def step(S, inp):
        d, ds = inp
        return d[..., :, None] * S + ds, S

    S_fin, S_in = lax.scan(step, s0, (jnp.moveaxis(decay, 2, 0), jnp.moveaxis(dS, 2, 0)))
    S_in = jnp.moveaxis(S_in, 0, 2)
    o_inter = jnp.einsum("bhnck,bhnkv->bhncv", q_in, S_in)
    return o_intra + o_inter, S_fin


def hgrn2_mixer(x, w_in, lb, head_gain, w_out):
    Bsz, T, _ = x.shape
    q, f_pre, i, g = jnp.split(x @ w_in, 4, axis=-1)
    f = lb + (1.0 - lb) * jax.nn.sigmoid(f_pre.astype(jnp.float32))
    logf = jnp.log(f)
    k = 1.0 - f

    def heads(t):
        return t.astype(jnp.float32).reshape(Bsz, T, A_HEADS, -1).transpose(0, 2, 1, 3)

    qh, kh, vh, lh = heads(q), heads(k), heads(i), heads(logf)

    def meta_part(t):
        return t[:, :, :N_META][:, :, None]

    def real_part(t):
        return t[:, :, N_META:].reshape(Bsz, A_HEADS, -1, A_CHUNK, t.shape[-1])

    s0 = jnp.zeros((Bsz, A_HEADS, A_EXPAND, A_DV), jnp.float32)
    o_meta, s_meta = gla_chunks(meta_part(qh), meta_part(kh), meta_part(vh), meta_part(lh), s0)
    o_real, _ = gla_chunks(real_part(qh), real_part(kh), real_part(vh), real_part(lh), s_meta)
    o = jnp.concatenate([o_meta.reshape(Bsz, A_HEADS, N_META, A_DV),
                         o_real.reshape(Bsz, A_HEADS, T - N_META, A_DV)], axis=2)
    o = o.transpose(0, 2, 1, 3)
    o = o * lax.rsqrt(jnp.mean(o * o, axis=-1, keepdims=True) + EPS)
    o = o * head_gain.astype(jnp.float32).reshape(A_HEADS, A_DV)
    o = o.reshape(Bsz, T, D_MODEL).astype(x.dtype) * jax.nn.silu(g)
    return o @ w_out


def shared_kv(h, kv_norm, kv_w, fg_b):
    Bsz, T, _ = h.shape
    proj = rmsnorm(h, kv_norm) @ kv_w
    k = proj[..., :D_MODEL].reshape(Bsz, T, B_HEADS, B_HDIM).transpose(0, 2, 1, 3)
    v = proj[..., D_MODEL:2 * D_MODEL].reshape(Bsz, T, B_HEADS, B_HDIM).transpose(0, 2, 1, 3)
    zf = proj[..., 2 * D_MODEL:].astype(jnp.float32) + fg_b.astype(jnp.float32)
    c = jnp.cumsum(jax.nn.log_sigmoid(zf), axis=1).transpose(0, 2, 1)
    return k, v, c


def fox_mixer(x, w_q, w_out, k, v, c):
    Bsz, T, _ = x.shape
    q = (x @ w_q).reshape(Bsz, T, B_HEADS, B_HDIM).transpose(0, 2, 1, 3)
    scale = 1.0 / np.sqrt(B_HDIM).astype(np.float32)
    bounds = [(0, N_META)] + [(N_META + n * Q_BLOCK, N_META + (n + 1) * Q_BLOCK)
                              for n in range((T - N_META) // Q_BLOCK)]
    outs = []
    for s, e in bounds:
        logits = jnp.einsum("bhqd,bhkd->bhqk", q[:, :, s:e], k[:, :, :e]).astype(jnp.float32) * scale
        logits = logits + (c[:, :, s:e, None] - c[:, :, None, :e])
        mask = jnp.arange(s, e)[:, None] >= jnp.arange(e)[None, :]
        p = jax.nn.softmax(jnp.where(mask, logits, -1e30), axis=-1)
        outs.append(jnp.einsum("bhqk,bhkd->bhqd", p.astype(v.dtype), v[:, :, :e]))
    o = jnp.concatenate(outs, axis=2).transpose(0, 2, 1, 3).reshape(Bsz, T, D_MODEL)
    return o @ w_out


def setup_inputs(seed: int = 0) -> dict:
    key = jax.random.key(seed)
    ks = jax.random.split(key, 16)
    D = D_MODEL

    def nrm(k, shape, scale):
        return jax.random.normal(k, shape, jnp.float32) * scale

    return {
        "x": nrm(ks[0], (BATCH, SEQ, D), 1.0),
        "meta_tokens": nrm(ks[1], (N_META, D), 1.0),
        "norm_gains": 1.0 + nrm(ks[2], (DEPTH, 4, D), 0.05),
        "a_w_in": nrm(ks[3], (N_A, D, 4 * D), D ** -0.5),
        "a_lb_logits": nrm(ks[4], (N_A + 1, D), 0.1),
        "a_head_norm": 1.0 + nrm(ks[5], (N_A, D), 0.05),
        "a_w_out": nrm(ks[6], (N_A, D, D), D ** -0.5),
        "kv_norm": 1.0 + nrm(ks[7], (D,), 0.05),
        "kv_w": nrm(ks[8], (D, 2 * D + B_HEADS), D ** -0.5),
        "fg_b": FG_BIAS_INIT + nrm(ks[9], (B_HEADS,), 0.1),
        "b_w_q": nrm(ks[10], (N_B, D, D), D ** -0.5),
        "b_w_out": nrm(ks[11], (N_B, D, D), D ** -0.5),
        "ffn_w_up": nrm(ks[12], (DEPTH, D, 2 * D_FF), D ** -0.5),
        "ffn_conv": nrm(ks[13], (DEPTH, CONV_W, 2 * D_FF), CONV_W ** -0.5),
        "ffn_w_down": nrm(ks[14], (DEPTH, D_FF, D), D_FF ** -0.5),
    }


def reference(x, meta_tokens, norm_gains, a_w_in, a_lb_logits, a_head_norm, a_w_out,
              kv_norm, kv_w, fg_b, b_w_q, b_w_out, ffn_w_up, ffn_conv, ffn_w_down):
    Bsz = x.shape[0]
    meta = jnp.broadcast_to(meta_tokens[None].astype(x.dtype), (Bsz, N_META, D_MODEL))
    h = jnp.concatenate([meta, x], axis=1)
    lb_all = jnp.cumsum(jax.nn.softmax(a_lb_logits.astype(jnp.float32), axis=0), axis=0)
    k_sh = v_sh = c_sh = None
    for l in range(DEPTH):
        g = norm_gains[l]
        hn = rmsnorm(h, g[0])
        if l < N_A:
            mix = hgrn2_mixer(hn, a_w_in[l], lb_all[l], a_head_norm[l], a_w_out[l])
        else:
            if l == N_A:
                k_sh, v_sh, c_sh = shared_kv(h, kv_norm, kv_w, fg_b)
            j = l - N_A
            mix = fox_mixer(hn, b_w_q[j], b_w_out[j], k_sh, v_sh, c_sh)
        h = h + rmsnorm(mix, g[1])
        ff = conv_ffn(rmsnorm(h, g[2]), ffn_w_up[l], ffn_conv[l], ffn_w_down[l])
        h = h + rmsnorm(ff, g[3])
    return h[:, N_META:]
```

```python
import numpy as np
import concourse.bass as bass
import concourse.mybir as mybir
from concourse.bass_utils import run_bass_kernel_spmd
from contextlib import ExitStack

F32 = mybir.dt.float32
BF16 = mybir.dt.bfloat16
AF = mybir.ActivationFunctionType
ALU = mybir.AluOpType
AX = mybir.AxisListType

SAME_ENG_SYNC = True


class _Op:
    __slots__ = ("eng", "fn", "reads", "writes", "dma_sem", "ndma", "deps",
                 "needs_inc", "token", "waits", "idx", "is_bar")

    def __init__(self, eng, fn, reads, writes, dma_sem=None, ndma=0):
        self.eng = eng
        self.fn = fn
        self.reads = reads
        self.writes = writes
        self.dma_sem = dma_sem
        self.ndma = ndma
        self.deps = []
        self.needs_inc = False
        self.token = None
        self.waits = []
        self.is_bar = False


class Sched:
    CENG = ("pe", "act", "dve", "pool")
    ALLENG = ("pe", "act", "dve", "pool", "sp")

    def __init__(self, nc, stack):
        self.nc = nc
        self.stack = stack
        self.ops = []
        self.esem = {e: stack.enter_context(nc.semaphore("s_" + e)) for e in self.CENG}
        self.dma_cum = {}
        self.dma_sems = {}
        self.dma_exempt = set()
        self.last_w = {}
        self.readers = {}
        self.last_op = {e: None for e in self.ALLENG}
        self.dma_last = {}

    def dma_sem(self, name, exempt=False):
        if name not in self.dma_sems:
            self.dma_sems[name] = self.stack.enter_context(self.nc.semaphore("d_" + name))
            self.dma_cum[name] = 0
            if exempt:
                self.dma_exempt.add(name)
        return name

    def _add(self, op):
        op.idx = len(self.ops)
        deps = set()
        for k in op.reads:
            w = self.last_w.get(k)
            if w is not None:
                deps.add(w)
        for k in op.writes:
            w = self.last_w.get(k)
            if w is not None:
                deps.add(w)
            for r in self.readers.get(k, ()):
                deps.add(r)
        deps.discard(op)
        op.deps = sorted(deps, key=lambda o: o.idx)
        for k in op.reads:
            self.readers.setdefault(k, []).append(op)
        for k in op.writes:
            self.last_w[k] = op
            self.readers[k] = []
        self.ops.append(op)
        self.last_op[op.eng] = op
        if op.dma_sem is not None:
            self.dma_cum[op.dma_sem] += 16 * op.ndma
            op.token = (op.dma_sem, self.dma_cum[op.dma_sem])
            self.dma_last[op.dma_sem] = op
        return op

    def op(self, eng, fn, reads=(), writes=()):
        return self._add(_Op(eng, fn, tuple(reads), tuple(writes)))

    def pe(self, fn, reads=(), writes=()):
        return self.op("pe", fn, reads, writes)

    def act(self, fn, reads=(), writes=()):
        return self.op("act", fn, reads, writes)

    def dve(self, fn, reads=(), writes=()):
        return self.op("dve", fn, reads, writes)

    def pool(self, fn, reads=(), writes=()):
        return self.op("pool", fn, reads, writes)

    def dma(self, eng, fn, sem, reads=(), writes=(), n=1):
        return self._add(_Op(eng, fn, tuple(reads), tuple(writes), dma_sem=sem, ndma=n))

    def barrier(self):
        prev = dict(self.last_op)
        dl = {k: v for k, v in self.dma_last.items() if k not in self.dma_exempt}
        for e in self.ALLENG:
            b = _Op(e, None, (), ())
            b.is_bar = True
            b.idx = len(self.ops)
            b.deps = [o for ee, o in prev.items() if o is not None and (ee != e or (SAME_ENG_SYNC and e != 'pe'))] + list(dl.values())
            self.ops.append(b)
            self.last_op[e] = b

    def finalize(self):
        for op in self.ops:
            for d in op.deps:
                if d.dma_sem is not None or d.is_bar:
                    continue
                if d.eng == op.eng and (d.eng == "pe" or not SAME_ENG_SYNC):
                    continue
                d.needs_inc = True
        cnt = {e: 0 for e in self.CENG}
        for op in self.ops:
            if op.dma_sem is None and op.needs_inc:
                cnt[op.eng] += 1
                op.token = (op.eng, cnt[op.eng])
        known = {e: {} for e in self.ALLENG}
        for op in self.ops:
            kn = known[op.eng]
            need = {}
            for d in op.deps:
                if d.token is None:
                    continue
                if d.dma_sem is None and d.eng == op.eng and (d.eng == "pe" or not SAME_ENG_SYNC):
                    continue
                s, v = d.token
                if need.get(s, 0) < v:
                    need[s] = v
            for s, v in need.items():
                if kn.get(s, 0) >= v:
                    continue
                kn[s] = v
                op.waits.append((s, v))
        self.counts = cnt

    def _sem(self, s):
        return self.esem[s] if s in self.esem else self.dma_sems[s]

    def emit(self):
        self.finalize()
        by_eng = {e: [o for o in self.ops if o.eng == e] for e in self.ALLENG}
        with self.nc.Block() as block:
            def run(e):
                def body(eng):
                    for op in by_eng[e]:
                        for (s, v) in op.waits:
                            eng.wait_ge(self._sem(s), v)
                        if op.fn is None:
                            continue
                        if op.dma_sem is not None:
                            op.fn(eng, self.dma_sems[op.dma_sem])
                        else:
                            ins = op.fn(eng)
                            if op.needs_inc:
                                ins.then_inc(self.esem[e], 1)
                return body
            block.tensor(run("pe"))
            block.scalar(run("act"))
            block.vector(run("dve"))
            block.gpsimd(run("pool"))
            block.sync(run("sp"))


import os
CUT = int(os.environ.get('KCUT', '99'))
D = 1024
T = 2064
NMETA = 16
DFF = 2752
NJ = 22
EPS = 1e-6
TT = [(0, 16)] + [(16 + 128 * i, 128) for i in range(16)]
GG = [(0, 16)] + [(16 + 512 * j, 512) for j in range(4)]


def tiles_of_group(gi):
    return [0] if gi == 0 else list(range(1 + 4 * (gi - 1), 1 + 4 * gi))


class Builder:
    def __init__(self, nseq=2, stop=None):
        self.nseq = nseq
        self.stop = stop
        nc = self.nc = bass.Bass("TRN2", target_bir_lowering=False)
        dt = lambda name, shape: nc.dram_tensor(name, shape, F32, kind="ExternalInput").ap()
        self.x = dt("x", [nseq, 2048, D])
        self.meta = dt("meta_tokens", [NMETA, D])
        self.norm_gains = dt("norm_gains", [2, 4, D])
        self.a_w_in = dt("a_w_in", [1, D, 4 * D])
        self.a_lb = dt("a_lb_logits", [2, D])
        self.a_hn = dt("a_head_norm", [1, D])
        self.a_w_out = dt("a_w_out", [1, D, D])
        self.kv_norm = dt("kv_norm", [D])
        self.kv_w = dt("kv_w", [D, 2 * D + 16])
        self.fg_b = dt("fg_b", [16])
        self.b_w_q = dt("b_w_q", [1, D, D])
        self.b_w_out = dt("b_w_out", [1, D, D])
        self.w_up = dt("ffn_w_up", [2, D, 2 * DFF])
        self.conv = dt("ffn_conv", [2, 3, 2 * DFF])
        self.w_down = dt("ffn_w_down", [2, DFF, D])
        self.out = nc.dram_tensor("out", [nseq, 2048, D], F32, kind="ExternalOutput").ap()
        self.uid = 0

    def sb(self, st, name, shape, dtype):
        self.uid += 1
        return st.enter_context(self.nc.sbuf_tensor("%s_%d" % (name, self.uid), shape, dtype))

    def build(self):
        nc = self.nc
        with ExitStack() as st:
            S = self.S = Sched(nc, st)
            self.ps = [st.enter_context(nc.psum_tensor("ps%d" % i, [128, 512], F32)) for i in range(7)]
            self.psb = st.enter_context(nc.psum_tensor("psb", [128, 1024], BF16))
            self.hT = self.sb(st, "hT", [128, 8, T], F32)
            self.consts(st)
            S.barrier()
            for s in range(self.nseq):
                self.seq(s)
            S.barrier()
            S.emit()
        return nc

    def consts(self, st):
        S = self.S
        self.ident = self.sb(st, "ident", [128, 128], F32)
        self.identb = self.sb(st, "identb", [128, 128], BF16)
        self.onesb = self.sb(st, "onesb", [128, 128], BF16)
        self.triu = self.sb(st, "triu", [128, 128], BF16)
        self.mask2 = self.sb(st, "mask2", [128, 128], F32)
        self.maskseg = self.sb(st, "maskseg", [128, 512], F32)
        self.epsc = self.sb(st, "epsc", [128, 1], F32)
        self.colv = self.sb(st, "colv", [128, 96], F32)
        self.convT = self.sb(st, "convT", [128, 3, 128], F32)
        self.lbc = self.sb(st, "lbc", [128, 24], F32)
        self.nfgb = self.sb(st, "nfgb", [16, 1], F32)
        self.i16 = self.sb(st, "i16", [16, 16], F32)
        self.ones16 = self.sb(st, "ones16", [16, 128], F32)
        self.rinvs = self.sb(st, "rinvs", [128, 512], F32)
        self.onesf = self.sb(st, "onesf", [16, 512], F32)
        self.onec = self.sb(st, "onec", [128, 1], F32)
        P = lambda fn, r=(), w=(): S.pool(fn, r, w)
        P(lambda e: e.memset(self.ident[:], 1.0), w=["ident"])
        P(lambda e: e.affine_select(out=self.ident[:], in_=self.ident[:], pattern=[[-1, 128]],
                                    compare_op=ALU.is_equal, fill=0.0, base=0, channel_multiplier=1),
          r=["ident"], w=["ident"])
        S.dve(lambda e: e.tensor_copy(out=self.identb[:], in_=self.ident[:]), ["ident"], ["identb"])
        S.dve(lambda e: e.tensor_copy(out=self.i16[:], in_=self.ident[0:16, 0:16]), ["ident"], ["i16"])
        P(lambda e: e.memset(self.onesb[:], 1.0), w=["onesb"])
        P(lambda e: e.memset(self.ones16[:], 1.0), w=["ones16"])
        P(lambda e: e.memset(self.triu[:], 1.0), w=["triu"])
        P(lambda e: e.affine_select(out=self.triu[:], in_=self.triu[:], pattern=[[1, 128]],
                                    compare_op=ALU.is_ge, fill=0.0, base=0, channel_multiplier=-1),
          r=["triu"], w=["triu"])
        P(lambda e: e.memset(self.mask2[:], 1.0), w=["mask2"])
        P(lambda e: e.affine_select(out=self.mask2[:], in_=self.mask2[:], pattern=[[1, 128]],
                                    compare_op=ALU.is_ge, fill=0.0, base=0, channel_multiplier=-1),
          r=["mask2"], w=["mask2"])
        P(lambda e: e.memset(self.mask2[0:64, 64:128], 0.0), r=["mask2"], w=["mask2"])
        P(lambda e: e.memset(self.maskseg[:], 1.0), w=["maskseg"])
        P(lambda e: e.memset(self.maskseg[:].rearrange("p (c k) -> p c k", k=64)[:, :, 0:1], 0.0),
          r=["maskseg"], w=["maskseg"])
        P(lambda e: e.memset(self.epsc[:], EPS), w=["epsc"])
        P(lambda e: e.memset(self.onesf[:], 1.0), w=["onesf"])
        P(lambda e: e.memset(self.onec[:], 1.0), w=["onec"])
        rowsA = self.sb(st, "rowsA", [96, 128], F32)
        rowsC = self.sb(st, "rowsC", [128, 3, 128], F32)
        P(lambda e: e.memset(rowsC[:], 0.0), w=["rowsC"])
        cs = S.dma_sem("const")
        nd = [0]

        def ld(dst, src, rk):
            S.dma("sp", lambda e, s, dst=dst, src=src: e.dma_start(out=dst, in_=src).then_inc(s, 16), cs,
                  reads=[rk], writes=[("rowsd", nd[0])])
            nd[0] += 1
        ld(rowsA[0:64, :], self.norm_gains.rearrange("l j (c p) -> (l j c) p", p=128), "rowsA")
        ld(rowsA[64:80, :], self.a_lb.rearrange("l (c p) -> (l c) p", p=128), "rowsA")
        ld(rowsA[80:88, :], self.a_hn.rearrange("l (c p) -> (l c) p", p=128), "rowsA")
        ld(rowsA[88:96, :], self.kv_norm.rearrange("(c p) -> c p", p=128), "rowsA")
        for l in range(2):
            for tap in range(3):
                for part in range(2):
                    r0 = ((l * 3 + tap) * 2 + part) * 22
                    src = self.conv[l, tap, part * DFF: part * DFF + 2688].rearrange("(j k) -> j k", k=128)
                    done = 0
                    while done < 21:
                        ti, ri = divmod(r0 + done, 128)
                        cnt = min(21 - done, 128 - ri)
                        ld(rowsC[ri:ri + cnt, ti, :], src[done:done + cnt, :], "rowsC")
                        done += cnt
                    ti, ri = divmod(r0 + 21, 128)
                    ld(rowsC[ri:ri + 1, ti, 0:64],
                       self.conv[l, tap, part * DFF + 2688: part * DFF + 2752].rearrange("(a k) -> a k", a=1), "rowsC")
        ld(self.nfgb[:, :], self.fg_b.rearrange("(h a) -> h a", a=1), "nfgb")
        allrows = [("rowsd", i) for i in range(nd[0])]
        ps = self.ps
        S.pe(lambda e: e.transpose(ps[0][:, 0:96], rowsA[:, :], self.ident[0:96, 0:96]), allrows + ["ident"], ["ps0"])
        S.dve(lambda e: e.tensor_copy(out=self.colv[:], in_=ps[0][:, 0:96]), ["ps0"], ["colv"])
        for ti in range(3):
            S.pe(lambda e, ti=ti: e.transpose(ps[1][:, ti * 128:(ti + 1) * 128], rowsC[:, ti, :], self.ident[:]),
                 allrows + ["ident", "rowsC"], ["ps1"])
        S.dve(lambda e: e.tensor_copy(out=self.convT[:], in_=ps[1][:, 0:384].rearrange("p (a b) -> p a b", b=128)),
              ["ps1"], ["convT"])
        dl = self.sb(st, "dl", [128, 8], F32)
        S.dve(lambda e: e.tensor_tensor(out=dl[:], in0=self.colv[:, 64:72], in1=self.colv[:, 72:80], op=ALU.subtract),
              ["colv"], ["dl"])
        S.act(lambda e: e.activation(out=self.lbc[:, 0:8], in_=dl[:], func=AF.Sigmoid), ["dl"], ["lbc0"])
        S.act(lambda e: e.activation(out=self.lbc[:, 8:16], in_=dl[:], func=AF.Sigmoid, scale=-1.0), ["dl"], ["lbc1"])
        S.dve(lambda e: e.tensor_scalar(out=self.lbc[:, 16:24], in0=self.lbc[:, 8:16], scalar1=-1.0, scalar2=None,
                                        op0=ALU.mult), ["lbc1"], ["lbc2"])
        S.dve(lambda e: e.tensor_scalar(out=self.nfgb[:], in0=self.nfgb[:], scalar1=-1.0, scalar2=None, op0=ALU.mult),
              allrows, ["nfgb2"])

    def gcol(self, l, j, c):
        k = (l * 4 + j) * 8 + c
        return self.colv[:, k:k + 1]

    def ccol(self, l, tap, part, j):
        r = ((l * 3 + tap) * 2 + part) * 22 + j
        ti, ri = divmod(r, 128)
        return self.convT[:, ti, ri:ri + 1]

    def wload(self, dst, src, slot, key, reads=()):
        S = self.S
        sem = S.dma_sem("w_" + "_".join(str(k) for k in (key if isinstance(key, tuple) else (key,))), exempt=True)
        S.dma("pool", lambda e, s: e.dma_start(out=dst, in_=src).then_inc(s, 16), sem,
              reads=list(reads), writes=[key])

    def rstd_from(self, srcs, n, sq, rtmp, rstd, pst, pkey, dscale=1.0 / D):
        S = self.S
        nsrc = len(srcs)
        for c, (ap, rk) in enumerate(srcs):
            S.act(lambda e, ap=ap, c=c: e.activation(out=sq[:, c, :n], in_=ap, func=AF.Square), rk, [("sq", c)])
            S.pe(lambda e, c=c: e.matmul(pst[:, :n], lhsT=self.onesb[:], rhs=sq[:, c, :n], start=(c == 0),
                                         stop=(c == nsrc - 1)), [("sq", c)], [pkey])
        S.act(lambda e: e.activation(out=rtmp[:, :n], in_=pst[:, :n], func=AF.Sqrt, scale=dscale, bias=self.epsc[:, 0:1]),
              [pkey], ["rtmp"])
        S.dve(lambda e: e.reciprocal(out=rstd[:, :n], in_=rtmp[:, :n]), ["rtmp"], ["rstd"])

    def seq(self, s):
        S = self.S
        self.load_x(s)
        S.barrier()
        if self.stop != "load":
            self.hgrn2(s)
            S.barrier()
            if self.stop not in ("mix0", "mix0a", "mix0b", "mix0c"):
                self.ffn(s, 0)
                S.barrier()
                if self.stop != "ffn0":
                    self.fox(s)
                    S.barrier()
                    if self.stop != "mix1":
                        self.ffn(s, 1)
                        S.barrier()
        self.store(s)
        S.barrier()

    def load_x(self, s):
        S, ps, hT = self.S, self.ps, self.hT
        with ExitStack() as st:
            xin = [self.sb(st, "xin%d" % i, [128, D], F32) for i in range(2)]
            xs = [S.dma_sem("xin%d" % i) for i in range(2)]
            for ti, (t0, n) in enumerate(TT):
                sl = ti % 2
                src = self.meta if ti == 0 else self.x[s, t0 - 16:t0 - 16 + 128, :]
                S.dma("sp", lambda e, sm, sl=sl, src=src, n=n: e.dma_start(out=xin[sl][:n, :], in_=src).then_inc(sm, 16),
                      xs[sl], writes=[("xin", sl)])
                for half in range(2):
                    bank = ps[half + 2 * sl]
                    bk = "ps%d" % (half + 2 * sl)
                    for j in range(4):
                        c = half * 4 + j
                        S.pe(lambda e, bank=bank, j=j, c=c, n=n, sl=sl: e.transpose(
                            bank[:, j * 128:j * 128 + n], xin[sl][:n, c * 128:(c + 1) * 128], self.ident[:n, :n]),
                            [("xin", sl)], [bk])
                    fn = lambda e, bank=bank, half=half, t0=t0, n=n: e.tensor_copy(
                        out=hT[:, half * 4:(half + 1) * 4, t0:t0 + n],
                        in_=bank[:, :].rearrange("p (j k) -> p j k", k=128)[:, :, 0:n])
                    if half == 0:
                        S.dve(fn, [bk], [("hT", ti, half)])
                    else:
                        S.act(lambda e, bank=bank, half=half, t0=t0, n=n: e.activation(
                            out=hT[:, half * 4:(half + 1) * 4, t0:t0 + n],
                            in_=bank[:, :].rearrange("p (j k) -> p j k", k=128)[:, :, 0:n], func=AF.Copy),
                            [bk], [("hT", ti, half)])

    def store(self, s):
        S, ps, hT = self.S, self.ps, self.hT
        with ExitStack() as st:
            xo = [self.sb(st, "xo%d" % i, [128, D], F32) for i in range(2)]
            os_ = [S.dma_sem("xo%d" % i) for i in range(2)]
            for ti, (t0, n) in enumerate(TT):
                if ti == 0:
                    continue
                sl = ti % 2
                for half in range(2):
                    bank = ps[half + 2 * sl]
                    bk = "ps%d" % (half + 2 * sl)
                    for j in range(4):
                        c = half * 4 + j
                        S.pe(lambda e, bank=bank, j=j, c=c, t0=t0: e.transpose(
                            bank[:, j * 128:(j + 1) * 128], hT[:, c, t0:t0 + 128], self.ident[:]), [], [bk])
                    if half == 0:
                        S.dve(lambda e, bank=bank, sl=sl: e.tensor_copy(out=xo[sl][:, 0:512], in_=bank[:, :]),
                              [bk], [("xo", sl)])
                    else:
                        S.act(lambda e, bank=bank, sl=sl: e.activation(out=xo[sl][:, 512:1024], in_=bank[:, :], func=AF.Copy),
                              [bk], [("xo", sl)])
                S.dma("sp", lambda e, sm, sl=sl, t0=t0: e.dma_start(out=self.out[s, t0 - 16:t0 - 16 + 128, :],
                                                                     in_=xo[sl][:, :]).then_inc(sm, 16),
                      os_[sl], reads=[("xo", sl)], writes=[("xo", sl)])

    def out_proj_residual(self, st, wsrc, src_act, l, jn, tag):
        S, ps, hT = self.S, self.ps, self.hT
        wo = self.sb(st, "wo", [128, 8, D], BF16)
        mix32 = self.sb(st, "mix32", [128, 8, 512], F32)
        sq = self.sb(st, "sqo", [128, 8, 512], BF16)
        rtmp = self.sb(st, "rtmpo", [128, 512], F32)
        rstd = self.sb(st, "rstdo", [128, 512], F32)
        tmp = self.sb(st, "tmpo", [128, 512], F32)
        wv = wsrc.rearrange("(kc p) n -> p kc n", p=128)
        for kc in range(8):
            self.wload(wo[:, kc, :], wv[:, kc, :], kc % 2, ("wo", kc))
        for gi, (g0, n) in enumerate(GG):
            for dc in range(8):
                bank = ps[dc % 2]
                bk = "ps%d" % (dc % 2)
                for kc in range(8):
                    S.pe(lambda e, bank=bank, dc=dc, kc=kc, g0=g0, n=n: e.matmul(
                        bank[:, :n], lhsT=wo[:, kc, dc * 128:(dc + 1) * 128], rhs=src_act[:, kc, g0:g0 + n],
                        start=(kc == 0), stop=(kc == 7)), [("wo", kc), (tag, gi)], [bk])
                S.act(lambda e, bank=bank, dc=dc, n=n: e.activation(out=mix32[:, dc, :n], in_=bank[:, :n], func=AF.Copy),
                      [bk], [("mix32", dc)])
            self.rstd_from([(mix32[:, dc, :n], [("mix32", dc)]) for dc in range(8)], n, sq, rtmp, rstd, ps[2], "ps2")
            for dc in range(8):
                S.dve(lambda e, dc=dc, n=n: e.scalar_tensor_tensor(
                    out=tmp[:, :n], in0=mix32[:, dc, :n], scalar=self.gcol(l, jn, dc), in1=rstd[:, :n],
                    op0=ALU.mult, op1=ALU.mult), [("mix32", dc), "rstd"], ["tmpo"])
                S.pool(lambda e, dc=dc, g0=g0, n=n: e.tensor_tensor(
                    out=hT[:, dc, g0:g0 + n], in0=hT[:, dc, g0:g0 + n], in1=tmp[:, :n], op=ALU.add),
                    ["tmpo"], [("hT", dc, gi)])

    def hgrn2(self, s):
        S, ps, psb, hT = self.S, self.ps, self.psb, self.hT
        with ExitStack() as st0:
            og = self.sb(st0, "og", [128, 8, T], BF16)
            with ExitStack() as st:
                xn = self.sb(st, "xn", [128, 8, T], BF16)
                sq = self.sb(st, "sq", [128, 8, 512], BF16)
                rtmp = self.sb(st, "rtmp", [128, 512], F32)
                rstd = self.sb(st, "rstd", [128, 512], F32)
                for gi, (g0, n) in enumerate(GG):
                    self.rstd_from([(hT[:, c, g0:g0 + n], []) for c in range(8)], n, sq, rtmp, rstd, ps[2], "ps2")
                    for c in range(8):
                        S.dve(lambda e, c=c, g0=g0, n=n: e.scalar_tensor_tensor(
                            out=xn[:, c, g0:g0 + n], in0=hT[:, c, g0:g0 + n], scalar=self.gcol(0, 0, c), in1=rstd[:, :n],
                            op0=ALU.mult, op1=ALU.mult), ["rstd"], [("xn", gi)])
                wh = [self.sb(st, "wh%d" % i, [128, 8, 4, 128], BF16) for i in range(2)]
                A = self.sb(st, "A", [128, 512], F32)
                B = self.sb(st, "B", [128, 512], F32)
                C = self.sb(st, "C", [128, 512], F32)
                Dn = self.sb(st, "Dn", [128, 512], F32)
                SG = self.sb(st, "SG", [128, 512], F32)
                O32 = self.sb(st, "O32", [128, 512], F32)
                qin = self.sb(st, "qin", [128, 512], BF16)
                kin = self.sb(st, "kin", [128, 512], BF16)
                kout = self.sb(st, "kout", [128, 512], BF16)
                sqh = self.sb(st, "sqh", [128, 1, 512], BF16)
                vtok = self.sb(st, "vtok", [128, 4, 128], BF16)
                ktok = self.sb(st, "ktok", [128, 4, 128], BF16)
                attT = self.sb(st, "attT", [128, 4, 128], BF16)
                S32 = self.sb(st, "S32", [128, 9, 128], F32)
                Sb = self.sb(st, "Sb", [128, 8, 128], BF16)
                win = self.a_w_in[0].rearrange("(kc p) n -> p kc n", p=128)
                nheads = {"mix0a": 0, "mix0b": 1, "mix0c": 1}.get(self.stop, 8)
                for hd in range(nheads):
                    sl = hd % 2
                    w = wh[sl]
                    for j in range(4):
                        self.wload(w[:, :, j, :], win[:, :, j * D + hd * 128: j * D + (hd + 1) * 128], sl, ("wh", sl, j))
                    wk = [("wh", sl, j) for j in range(4)]
                    S.dve(lambda e: e.memset(S32[:, 0, :], 0.0), [], [("S32", 0)])
                    for gi, (g0, n) in enumerate(GG):
                        if self.stop == "mix0b" and gi > 0:
                            break
                        tl_list = tiles_of_group(gi)
                        nch = 1 if gi == 0 else 8
                        for (j, bi) in ((0, 0), (1, 1), (3, 2)):
                            for kc in range(8):
                                S.pe(lambda e, j=j, bi=bi, kc=kc, g0=g0, n=n, w=w: e.matmul(
                                    ps[bi][:, :n], lhsT=w[:, kc, j, :], rhs=xn[:, kc, g0:g0 + n],
                                    start=(kc == 0), stop=(kc == 7)), [wk[j], ("xn", gi)], ["ps%d" % bi])
                        for li, ti in enumerate(tl_list):
                            t0, nt = TT[ti]
                            for kc in range(8):
                                S.pe(lambda e, li=li, kc=kc, t0=t0, nt=nt, w=w: e.matmul(
                                    ps[3][:nt, li * 128:(li + 1) * 128], lhsT=xn[:, kc, t0:t0 + nt], rhs=w[:, kc, 2, :],
                                    start=(kc == 0), stop=(kc == 7)), [wk[2], ("xn", gi)], ["ps3"])
                        if gi == 0:
                            S.act(lambda e: e.activation(out=vtok[:16, 0, :], in_=ps[3][:16, 0:128], func=AF.Copy),
                                  ["ps3"], ["vtok"])
                        else:
                            S.act(lambda e: e.activation(out=vtok[:, :, :], in_=ps[3][:, :].rearrange("p (a b) -> p a b", b=128),
                                                         func=AF.Copy), ["ps3"], ["vtok"])
                        if self.stop == 'mix0c' and gi > 0 and CUT <= 1:
                            continue
                        S.act(lambda e, n=n: e.activation(out=A[:, :n], in_=ps[1][:, :n], func=AF.Sigmoid), ["ps1"], ["A"])
                        S.act(lambda e, n=n: e.activation(out=SG[:, :n], in_=ps[2][:, :n], func=AF.Silu), ["ps2"], ["SG"])
                        S.act(lambda e, n=n, hd=hd: e.activation(out=B[:, :n], in_=A[:, :n], func=AF.Ln,
                                                                 scale=self.lbc[:, 8 + hd:9 + hd], bias=self.lbc[:, hd:hd + 1]),
                              ["A"], ["B"])
                        S.dve(lambda e, n=n, hd=hd: e.tensor_scalar(out=C[:, :n], in0=A[:, :n],
                                                                    scalar1=self.lbc[:, 16 + hd:17 + hd],
                                                                    scalar2=self.lbc[:, 8 + hd:9 + hd],
                                                                    op0=ALU.mult, op1=ALU.add), ["A"], ["C"])
                        S.dve(lambda e, n=n: e.tensor_tensor_scan(out=A[:, :n], data0=self.maskseg[:, :n], data1=B[:, :n],
                                                                  initial=0.0, op0=ALU.mult, op1=ALU.add), ["B", "A"], ["A"])
                        S.act(lambda e, n=n: e.activation(out=B[:, :n], in_=A[:, :n], func=AF.Exp), ["A"], ["B"])
                        S.act(lambda e, n=n: e.activation(out=Dn[:, :n], in_=A[:, :n], func=AF.Exp, scale=-1.0), ["A"], ["Dn"])
                        S.dve(lambda e, n=n: e.tensor_tensor(out=qin[:, :n], in0=ps[0][:, :n], in1=B[:, :n], op=ALU.mult),
                              ["ps0", "B"], ["qin"])
                        S.dve(lambda e, n=n: e.tensor_tensor(out=C[:, :n], in0=C[:, :n], in1=Dn[:, :n], op=ALU.mult),
                              ["C", "Dn"], ["C"])
                        S.act(lambda e, n=n: e.activation(out=kin[:, :n], in_=C[:, :n], func=AF.Copy), ["C"], ["kin"])
                        if gi == 0:
                            S.dve(lambda e: e.tensor_scalar(out=kout[:, :16], in0=C[:, :16], scalar1=B[:, 15:16], scalar2=None,
                                                            op0=ALU.mult), ["C", "B"], ["kout"])
                        else:
                            S.dve(lambda e: e.tensor_tensor(
                                out=kout[:, :].rearrange("p (c k) -> p c k", k=64),
                                in0=C[:, :].rearrange("p (c k) -> p c k", k=64),
                                in1=B[:, :].rearrange("p (c k) -> p c k", k=64)[:, :, 63:64].to_broadcast([128, 8, 64]),
                                op=ALU.mult), ["C", "B"], ["kout"])
                        if self.stop == 'mix0c' and gi > 0 and CUT <= 2:
                            continue
                        for li, ti in enumerate(tl_list):
                            t0, nt = TT[ti]
                            S.pe(lambda e, li=li, nt=nt: e.transpose(psb[:nt, li * 128:(li + 1) * 128],
                                                                      kout[:, li * 128:li * 128 + nt], self.identb[:]),
                                 ["kout"], ["psb"])
                        if gi == 0:
                            S.act(lambda e: e.activation(out=ktok[:16, 0, :], in_=psb[:16, 0:128], func=AF.Copy), ["psb"], ["ktok"])
                        else:
                            S.act(lambda e: e.activation(out=ktok[:, :, :], in_=psb[:, 0:512].rearrange("p (a b) -> p a b", b=128),
                                                         func=AF.Copy), ["psb"], ["ktok"])
                        if self.stop == 'mix0c' and gi > 0 and CUT <= 3:
                            continue
                        for cl in range(nch):
                            li, r0 = cl // 2, (cl % 2) * 64
                            nr = 16 if gi == 0 else 64
                            bi = 1 + cl % 2
                            S.pe(lambda e, cl=cl, li=li, r0=r0, nr=nr, bi=bi: e.matmul(
                                ps[bi][:, (cl // 2) * 128:(cl // 2 + 1) * 128], lhsT=ktok[r0:r0 + nr, li, :],
                                rhs=vtok[r0:r0 + nr, li, :], start=True, stop=True),
                                ["ktok", "vtok", "A", "SG"], ["ps%d" % bi])
                        for cl in range(nch):
                            bi = 1 + cl % 2
                            dcol = B[:, 15:16] if gi == 0 else B[:, cl * 64 + 63:cl * 64 + 64]
                            S.dve(lambda e, cl=cl, bi=bi, dcol=dcol: e.scalar_tensor_tensor(
                                out=S32[:, cl + 1, :], in0=S32[:, cl, :], scalar=dcol,
                                in1=ps[bi][:, (cl // 2) * 128:(cl // 2 + 1) * 128], op0=ALU.mult, op1=ALU.add),
                                [("S32", cl), "B", "ps%d" % bi], [("S32", cl + 1)])
                        S.act(lambda e, nch=nch: e.activation(out=Sb[:, 0:nch, :], in_=S32[:, 0:nch, :], func=AF.Copy),
                              [("S32", c) for c in range(nch)], ["Sb"])
                        if self.stop == 'mix0c' and gi > 0 and CUT <= 4:
                            continue
                        for li, ti in enumerate(tl_list):
                            t0, nt = TT[ti]
                            S.pe(lambda e, li=li, nt=nt: e.matmul(ps[0][:nt, li * 128:li * 128 + nt],
                                                                   lhsT=kin[:, li * 128:li * 128 + nt],
                                                                   rhs=qin[:, li * 128:li * 128 + nt], start=True, stop=True),
                                 ["kin", "qin"], ["ps0"])
                        if gi == 0:
                            S.dve(lambda e: e.tensor_tensor(out=attT[:16, 0, :16], in0=ps[0][:16, 0:16], in1=self.mask2[:16, :16],
                                                            op=ALU.mult), ["ps0"], ["attT"])
                        else:
                            S.dve(lambda e: e.tensor_tensor(
                                out=attT[:, :, :], in0=ps[0][:, :].rearrange("p (a b) -> p a b", b=128),
                                in1=self.mask2[:, :].unsqueeze(1).to_broadcast([128, 4, 128]), op=ALU.mult), ["ps0"], ["attT"])
                        if self.stop == 'mix0c' and gi > 0 and CUT <= 5:
                            continue
                        for li, ti in enumerate(tl_list):
                            t0, nt = TT[ti]
                            S.pe(lambda e, li=li, nt=nt, gi=gi: e.matmul(
                                ps[4][:, li * 128:li * 128 + nt], lhsT=vtok[:nt, li, :], rhs=attT[:nt, li, :nt],
                                start=True, stop=(gi == 0)), ["vtok", "attT"], ["ps4"])
                            if gi > 0:
                                for hh in range(2):
                                    cl = 2 * li + hh
                                    S.pe(lambda e, li=li, hh=hh, cl=cl: e.matmul(
                                        ps[4][:, cl * 64:(cl + 1) * 64], lhsT=Sb[:, cl, :], rhs=qin[:, cl * 64:(cl + 1) * 64],
                                        start=False, stop=(hh == 1)), ["Sb", "qin"], ["ps4"])
                        S.dve(lambda e, nch=nch: e.tensor_copy(out=S32[:, 0, :], in_=S32[:, nch, :]),
                              [("S32", nch), "Sb"], [("S32", 0)])
                        if self.stop == 'mix0c' and gi > 0 and CUT <= 6:
                            continue
                        self.rstd_from([(ps[4][:, :n], ["ps4"])], n, sqh, rtmp, rstd, ps[5], "ps5", dscale=1.0 / 128)
                        S.dve(lambda e, n=n, hd=hd: e.scalar_tensor_tensor(
                            out=O32[:, :n], in0=ps[4][:, :n], scalar=self.colv[:, 80 + hd:81 + hd], in1=rstd[:, :n],
                            op0=ALU.mult, op1=ALU.mult), ["ps4", "rstd"], ["O32"])
                        S.pool(lambda e, n=n, hd=hd, g0=g0: e.tensor_tensor(out=og[:, hd, g0:g0 + n], in0=O32[:, :n], in1=SG[:, :n],
                                                                              op=ALU.mult), ["O32", "SG"], [("og", gi)])
            self.S.barrier()
            if self.stop in ("mix0a", "mix0b", "mix0c"):
                return
            with ExitStack() as st:
                self.out_proj_residual(st, self.a_w_out[0], og, 0, 1, "og")

    def ffn(self, s, l):
        S, ps, hT = self.S, self.ps, self.hT
        halves = [(0, 1040), (1040, 1024)]
        with ExitStack() as st0:
            halo = self.sb(st0, "halo", [128, 8, 2], BF16)
            S.dve(lambda e: e.memset(halo[:], 0.0), [], ["halo"])
            def half(hf, h0, nh):
                with ExitStack() as st1:
                    act = self.sb(st1, "act", [128, NJ, 1040], BF16)
                    blocks = []
                    o = 0
                    while o < nh:
                        nb = min(510, nh - o)
                        blocks.append((o, nb))
                        o += nb
                    with ExitStack() as st:
                        xn = self.sb(st, "xn2", [128, 8, 1042], BF16)
                        sq = self.sb(st, "sq2", [128, 8, 512], BF16)
                        rtmp = self.sb(st, "rtmp2", [128, 512], F32)
                        rstd = self.sb(st, "rstd2", [128, 512], F32)
                        S.dve(lambda e: e.tensor_copy(out=xn[:, :, 0:2], in_=halo[:]), ["halo"], [("xn2", -1)])
                        subs = []
                        o = 0
                        while o < nh:
                            nn = min(512, nh - o)
                            subs.append((o, nn))
                            o += nn
                        for si, (o, nn) in enumerate(subs):
                            g0 = h0 + o
                            self.rstd_from([(hT[:, c, g0:g0 + nn], []) for c in range(8)], nn, sq, rtmp, rstd, ps[2], "ps2")
                            for c in range(8):
                                S.dve(lambda e, c=c, g0=g0, nn=nn, o=o: e.scalar_tensor_tensor(
                                    out=xn[:, c, 2 + o:2 + o + nn], in0=hT[:, c, g0:g0 + nn], scalar=self.gcol(l, 2, c),
                                    in1=rstd[:, :nn], op0=ALU.mult, op1=ALU.mult), ["rstd"], [("xn2", si)])
                        xkeys = [("xn2", -1)] + [("xn2", si) for si in range(len(subs))]
                        S.dve(lambda e, nh=nh: e.tensor_copy(out=halo[:], in_=xn[:, :, nh:nh + 2]), xkeys, ["halo"])
                        wu = [self.sb(st, "wu%d" % i, [128, 8, 2, 128], BF16) for i in range(2)]
                        G32 = self.sb(st, "G32", [128, 512], F32)
                        V32 = self.sb(st, "V32", [128, 512], F32)
                        SGf = self.sb(st, "SGf", [128, 512], F32)
                        wup = self.w_up[l].rearrange("(kc p) n -> p kc n", p=128)
                        for j in range(NJ):
                            mj = 128 if j < 21 else 64
                            sl = j % 2
                            w = wu[sl]
                            for part in range(2):
                                self.wload(w[:, :, part, :mj], wup[:, :, part * DFF + j * 128: part * DFF + j * 128 + mj],
                                           sl, ("wu", sl, part))
                            for bi_, (o, nb) in enumerate(blocks):
                                for part in range(2):
                                    bank = ps[part]
                                    bk = "ps%d" % part
                                    for kc in range(8):
                                        S.pe(lambda e, bank=bank, part=part, kc=kc, o=o, nb=nb, mj=mj, w=w: e.matmul(
                                            bank[:mj, :nb + 2], lhsT=w[:, kc, part, :mj], rhs=xn[:, kc, o:o + nb + 2],
                                            start=(kc == 0), stop=(kc == 7)), [("wu", sl, part)] + xkeys, [bk])
                                    dst = G32 if part == 0 else V32
                                    dk = "G32" if part == 0 else "V32"
                                    S.act(lambda e, bank=bank, dst=dst, part=part, nb=nb, mj=mj, j=j: e.activation(
                                        out=dst[:mj, :nb], in_=bank[:mj, 0:nb], func=AF.Identity, scale=self.ccol(l, 0, part, j)[:mj, :]),
                                        [bk], [dk])
                                    for tap in (1, 2):
                                        S.dve(lambda e, bank=bank, dst=dst, part=part, nb=nb, mj=mj, j=j, tap=tap: e.scalar_tensor_tensor(
                                            out=dst[:mj, :nb], in0=bank[:mj, tap:tap + nb], scalar=self.ccol(l, tap, part, j)[:mj, :],
                                            in1=dst[:mj, :nb], op0=ALU.mult, op1=ALU.add), [bk, dk], [dk])
                                S.act(lambda e, nb=nb, mj=mj: e.activation(out=SGf[:mj, :nb], in_=G32[:mj, :nb], func=AF.Silu),
                                      ["G32"], ["SGf"])
                                S.pool(lambda e, nb=nb, mj=mj, j=j, o=o: e.tensor_tensor(
                                    out=act[:mj, j, o:o + nb], in0=SGf[:mj, :nb], in1=V32[:mj, :nb], op=ALU.mult),
                                    ["SGf", "V32"], [("act", j, bi_)])
                    S.barrier()
                    with ExitStack() as st:
                        wd = [self.sb(st, "wd%d" % i, [128, NJ, 128], BF16) for i in range(2)]
                        mix = self.sb(st, "mixf", [128, 8, 1040], F32)
                        sq3 = self.sb(st, "sq3", [128, 8, 512], BF16)
                        rtmp3 = self.sb(st, "rtmp3", [128, 512], F32)
                        rstd3 = self.sb(st, "rstd3", [128, 512], F32)
                        tmp3 = self.sb(st, "tmp3", [128, 512], F32)
                        for dc in range(8):
                            sl = dc % 2
                            w = wd[sl]
                            self.wload(w[:, 0:21, :],
                                       self.w_down[l, 0:2688, dc * 128:(dc + 1) * 128].rearrange("(j p) n -> p j n", p=128),
                                       sl, ("wd", sl, 0))
                            self.wload(w[0:64, 21, :], self.w_down[l, 2688:2752, dc * 128:(dc + 1) * 128], sl, ("wd", sl, 1))
                            for bi_, (o, nb) in enumerate(blocks):
                                bank = ps[bi_ % 2]
                                bk = "ps%d" % (bi_ % 2)
                                for j in range(NJ):
                                    mj = 128 if j < 21 else 64
                                    S.pe(lambda e, bank=bank, j=j, mj=mj, o=o, nb=nb, w=w: e.matmul(
                                        bank[:, :nb], lhsT=w[:mj, j, :], rhs=act[:mj, j, o:o + nb],
                                        start=(j == 0), stop=(j == NJ - 1)),
                                        [("wd", sl, 0), ("wd", sl, 1), ("act", j, bi_)], [bk])
                                S.act(lambda e, bank=bank, dc=dc, o=o, nb=nb: e.activation(
                                    out=mix[:, dc, o:o + nb], in_=bank[:, :nb], func=AF.Copy), [bk], [("mix", dc, bi_)])
                        for bi_, (o, nb) in enumerate(blocks):
                            g0 = h0 + o
                            self.rstd_from([(mix[:, dc, o:o + nb], [("mix", dc, bi_)]) for dc in range(8)], nb, sq3, rtmp3, rstd3,
                                           ps[2], "ps2")
                            for dc in range(8):
                                S.dve(lambda e, dc=dc, o=o, nb=nb: e.scalar_tensor_tensor(
                                    out=tmp3[:, :nb], in0=mix[:, dc, o:o + nb], scalar=self.gcol(l, 3, dc), in1=rstd3[:, :nb],
                                    op0=ALU.mult, op1=ALU.mult), [("mix", dc, bi_), "rstd"], ["tmp3"])
                                S.pool(lambda e, dc=dc, g0=g0, nb=nb: e.tensor_tensor(
                                    out=hT[:, dc, g0:g0 + nb], in0=hT[:, dc, g0:g0 + nb], in1=tmp3[:, :nb], op=ALU.add),
                                    ["tmp3"], [("hT", dc, hf, bi_)])
                    S.barrier()

            for hf, (h0, nh) in enumerate(halves):
                half(hf, h0, nh)

    def fox(self, s):
        S, ps, psb, hT = self.S, self.ps, self.psb, self.hT
        scale = 1.0 / 8.0
        with ExitStack() as st0:
            O = self.sb(st0, "O", [128, 8, T], BF16)
            with ExitStack() as st:
                xr = self.sb(st, "xr", [128, 8, T], BF16)
                bias = self.sb(st, "bias", [128, 17, 16, 17], F32)
                with ExitStack() as stn:
                    sq = self.sb(stn, "sq4", [128, 8, 512], BF16)
                    rtmp = self.sb(stn, "rtmp4", [128, 512], F32)
                    rstd = self.sb(stn, "rstd4", [128, 512], F32)
                    for gi, (g0, n) in enumerate(GG):
                        self.rstd_from([(hT[:, c, g0:g0 + n], []) for c in range(8)], n, sq, rtmp, rstd, ps[2], "ps2")
                        for c in range(8):
                            S.dve(lambda e, c=c, g0=g0, n=n: e.tensor_tensor(
                                out=xr[:, c, g0:g0 + n], in0=hT[:, c, g0:g0 + n], in1=rstd[:, :n], op=ALU.mult),
                                ["rstd"], [("xr", gi)])
                    S.barrier()
                stf = ExitStack()
                xrk = [("xr", gi) for gi in range(5)]
                kvw = self.kv_w.rearrange("(kc p) n -> p kc n", p=128)
                wqv = self.b_w_q[0].rearrange("(kc p) n -> p kc n", p=128)
                wfg = self.sb(stf, "wfg", [128, 8, 16], BF16)
                Cp = self.sb(stf, "Cp", [16, T], F32)
                sp = self.sb(stf, "sp", [16, 512], F32)
                Ctok = self.sb(stf, "Ctok", [128, 17, 16], F32)
                Cend = self.sb(stf, "Cend", [16, 17], F32)
                Cdiag = self.sb(stf, "Cdiag", [16, 16, 17], F32)
                Cb = self.sb(stf, "Cb", [128, 16, 17], F32)
                self.wload(wfg[:, :, :], kvw[:, :, 2048:2064], 0, "wfg")
                gkv = self.colv[:, 88:96]
                S.dve(lambda e: e.tensor_tensor(out=wfg[:, :, :], in0=wfg[:, :, :],
                                                in1=gkv.unsqueeze(2).to_broadcast([128, 8, 16]), op=ALU.mult),
                      ["wfg"], ["wfg"])
                for gi, (g0, n) in enumerate(GG):
                    for kc in range(8):
                        S.pe(lambda e, kc=kc, g0=g0, n=n: e.matmul(ps[0][:16, :n], lhsT=wfg[:, kc, :], rhs=xr[:, kc, g0:g0 + n],
                                                                    start=(kc == 0), stop=(kc == 7)), ["wfg", ("xr", gi)], ["ps0"])
                    S.act(lambda e, n=n: e.activation(out=sp[:, :n], in_=ps[0][:16, :n], func=AF.Exp, scale=-1.0,
                                                      bias=self.nfgb[:, 0:1]), ["ps0", "nfgb2"], ["sp"])
                    S.act(lambda e, n=n: e.activation(out=sp[:, :n], in_=sp[:, :n], func=AF.Ln, scale=1.0, bias=self.onec[0:16, 0:1]),
                          ["sp"], ["sp"])
                    init = 0.0 if gi == 0 else Cp[:, g0 - 1:g0]
                    S.dve(lambda e, g0=g0, n=n, init=init: e.tensor_tensor_scan(
                        out=Cp[:, g0:g0 + n], data0=self.onesf[:16, :n],
                        data1=sp[:, :n], initial=init, op0=ALU.mult, op1=ALU.add), ["sp", ("Cp", gi - 1)], [("Cp", gi)])
                cpk = [("Cp", gi) for gi in range(5)]
                for ti, (t0, nt) in enumerate(TT):
                    S.pe(lambda e, t0=t0, nt=nt: e.transpose(ps[1][:nt, 0:16], Cp[:, t0:t0 + nt], self.i16[:]), cpk, ["ps1"])
                    S.dve(lambda e, ti=ti, nt=nt: e.tensor_copy(out=Ctok[:nt, ti, :], in_=ps[1][:nt, 0:16]), ["ps1"], ["Ctok"])
                S.dve(lambda e: e.tensor_copy(out=Cend[:, 0:1], in_=Cp[:, 15:16]), cpk, ["Cend"])
                S.dve(lambda e: e.tensor_copy(out=Cend[:, 1:17], in_=Cp[:, 16:T].rearrange("p (b k) -> p b k", k=128)[:, :, 127]),
                      cpk + ["Cend"], ["Cend"])
                S.dve(lambda e: e.tensor_tensor(out=Cdiag[:, :, :], in0=Cend[:, :].unsqueeze(1).to_broadcast([16, 16, 17]),
                                                in1=self.i16[:, :].unsqueeze(2).to_broadcast([16, 16, 17]), op=ALU.mult),
                      ["Cend"], ["Cdiag"])
                S.pe(lambda e: e.matmul(ps[2][:, 0:272], lhsT=self.ones16[:, :], rhs=Cdiag[:, :, :].rearrange("p a b -> p (a b)"),
                                        start=True, stop=True), ["Cdiag"], ["ps2"])
                S.act(lambda e: e.activation(out=Cb[:, :, :].rearrange("p a b -> p (a b)"), in_=ps[2][:, 0:272], func=AF.Copy),
                      ["ps2"], ["Cb"])
                for kt in range(17):
                    nt = TT[kt][1]
                    S.dve(lambda e, kt=kt, nt=nt: e.tensor_tensor(
                        out=bias[:nt, kt, :, :], in0=Ctok[:nt, kt, :].unsqueeze(2).to_broadcast([nt, 16, 17]),
                        in1=Cb[:nt, :, :], op=ALU.subtract), ["Ctok", "Cb"], ["bias"])
                S.barrier()
                stf.close()
                wp = [self.sb(st, "wp%d" % i, [128, 8, 3, 128], BF16) for i in range(2)]
                KT = self.sb(st, "KT", [128, T], BF16)
                QT = self.sb(st, "QT", [128, T], BF16)
                Va = self.sb(st, "Va", [128, 17, 192], BF16)
                PT = [self.sb(st, "PT%d" % i, [128, 512], BF16) for i in range(2)]
                S.pool(lambda e: e.memset(Va[:, :, 64:128], 1.0), [], ["Vones"])
                g0col = self.colv[:, 32:40]
                for p in range(8):
                    sl = p % 2
                    w = wp[sl]
                    self.wload(w[:, :, 0, :], kvw[:, :, p * 128:(p + 1) * 128], sl, ("wp", sl, 0))
                    self.wload(w[:, :, 1, :], kvw[:, :, D + p * 128:D + (p + 1) * 128], sl, ("wp", sl, 1))
                    self.wload(w[:, :, 2, :], wqv[:, :, p * 128:(p + 1) * 128], sl, ("wp", sl, 2))
                    for m in range(3):
                        gc = gkv if m < 2 else g0col
                        S.dve(lambda e, w=w, m=m, gc=gc: e.tensor_tensor(
                            out=w[:, :, m, :], in0=w[:, :, m, :], in1=gc.unsqueeze(2).to_broadcast([128, 8, 128]), op=ALU.mult),
                            [("wp", sl, m)], [("wp", sl, m)])
                    for gi, (g0, n) in enumerate(GG):
                        for (m, dst, dk) in ((0, KT, "KT"), (2, QT, "QT")):
                            bank = ps[m // 2]
                            bk = "ps%d" % (m // 2)
                            for kc in range(8):
                                S.pe(lambda e, bank=bank, m=m, kc=kc, g0=g0, n=n, w=w: e.matmul(
                                    bank[:, :n], lhsT=w[:, kc, m, :], rhs=xr[:, kc, g0:g0 + n], start=(kc == 0), stop=(kc == 7)),
                                    [("wp", sl, m), ("xr", gi)], [bk])
                            S.act(lambda e, bank=bank, dst=dst, g0=g0, n=n: e.activation(out=dst[:, g0:g0 + n], in_=bank[:, :n],
                                                                                          func=AF.Copy), [bk], [(dk, gi)])
                    for kt, (t0, nt) in enumerate(TT):
                        for kc in range(8):
                            S.pe(lambda e, kc=kc, t0=t0, nt=nt, w=w: e.matmul(
                                ps[2][:nt, 0:128], lhsT=xr[:, kc, t0:t0 + nt], rhs=w[:, kc, 1, :], start=(kc == 0), stop=(kc == 7)),
                                [("wp", sl, 1)] + xrk, ["ps2"])
                        S.act(lambda e, kt=kt, nt=nt: e.activation(
                            out=Va[:nt, kt, :].rearrange("p (a b) -> p a b", b=64)[:, 0:3:2, :],
                            in_=ps[2][:nt, 0:128].rearrange("p (a b) -> p a b", b=64), func=AF.Copy), ["ps2"], [("Va", kt)])
                    for hh in range(2):
                        h = 2 * p + hh
                        r0 = hh * 64
                        vlo = 0 if hh == 0 else 64
                        orow = r0
                        lrow = 64 - r0
                        for gi, (g0, n) in enumerate(GG):
                            tl = tiles_of_group(gi)
                            last = tl[-1]
                            ob = ps[3 + (gi + hh) % 2]
                            obk = "ps%d" % (3 + (gi + hh) % 2)
                            for kt in range(last + 1):
                                k0, nk = TT[kt]
                                if kt in tl:
                                    c0 = (kt - tl[0]) * 128 if gi > 0 else 0
                                else:
                                    c0 = 0
                                nq = n - c0
                                sbi = (5 + kt % 2) if hh == 0 else (kt % 2)
                                sb_ = ps[sbi]
                                sbk = "ps%d" % sbi
                                pt = PT[kt % 2]
                                ptk = ("PT", kt % 2)
                                S.pe(lambda e, sb_=sb_, k0=k0, nk=nk, g0=g0, c0=c0, nq=nq, r0=r0: e.matmul(
                                    sb_[:nk, :nq], lhsT=KT[r0:r0 + 64, k0:k0 + nk], rhs=QT[r0:r0 + 64, g0 + c0:g0 + c0 + nq],
                                    start=True, stop=True), [("KT", g_) for g_ in range(5)] + [("QT", gi)], [sbk])
                                nblk = (nq + 127) // 128
                                for b_ in range(nblk):
                                    q0 = b_ * 128
                                    qn = min(128, nq - q0)
                                    Bidx = 0 if gi == 0 else tl[0] + (c0 + q0) // 128
                                    S.act(lambda e, sb_=sb_, pt=pt, nk=nk, q0=q0, qn=qn, kt=kt, h=h, Bidx=Bidx: e.activation(
                                        out=pt[:nk, q0:q0 + qn], in_=sb_[:nk, q0:q0 + qn], func=AF.Exp, scale=scale,
                                        bias=bias[:nk, kt, h, Bidx:Bidx + 1]), [sbk, "bias"], [ptk])
                                if kt in tl:
                                    qn = min(128, nq)
                                    S.pool(lambda e, pt=pt, nk=nk, qn=qn: e.tensor_tensor(
                                        out=pt[:nk, 0:qn], in0=pt[:nk, 0:qn], in1=self.triu[:nk, :qn], op=ALU.mult), [ptk], [ptk])
                                S.pe(lambda e, ob=ob, pt=pt, nk=nk, kt=kt, c0=c0, nq=nq, vlo=vlo, last=last: e.matmul(
                                    ob[:, c0:c0 + nq], lhsT=Va[:nk, kt, vlo:vlo + 128], rhs=pt[:nk, 0:nq],
                                    start=(kt == 0), stop=(kt == last)), [ptk, ("Va", kt), "Vones"], [obk])
                            S.dve(lambda e, ob=ob, n=n, orow=orow, lrow=lrow: e.reciprocal(
                                out=self.rinvs[orow:orow + 64, :n], in_=ob[lrow:lrow + 64, :n]), [obk], ["rinvs"])
                            S.dve(lambda e, ob=ob, n=n, orow=orow, p=p, g0=g0: e.tensor_tensor(
                                out=O[orow:orow + 64, p, g0:g0 + n], in0=ob[orow:orow + 64, :n], in1=self.rinvs[orow:orow + 64, :n],
                                op=ALU.mult), [obk, "rinvs"], [("O", gi)])
            self.S.barrier()
            with ExitStack() as st:
                self.out_proj_residual(st, self.b_w_out[0], O, 1, 1, "O")


_CACHE = {}


def _get_nc(nseq=2, stop=None):
    key = (nseq, stop)
    if key not in _CACHE:
        _CACHE[key] = Builder(nseq, stop).build()
    return _CACHE[key]


def kernel(**inputs):
    ncores = 8
    nc = _get_nc(2, None)
    shared = {k: np.ascontiguousarray(np.asarray(v, dtype=np.float32)) for k, v in inputs.items() if k != "x"}
    x = np.ascontiguousarray(np.asarray(inputs["x"], dtype=np.float32))
    in_maps = []
    for c in range(ncores):
        m = dict(shared)
        m["x"] = x[2 * c:2 * c + 2]
        in_maps.append(m)
    res = run_bass_kernel_spmd(nc, in_maps, core_ids=list(range(ncores)))
    return np.concatenate([np.asarray(r["out"]) for r in res.results], axis=0).astype(np.float32)
```

```python
import numpy as np
import concourse.bass as bass
import concourse.mybir as mybir
from concourse.bass_utils import run_bass_kernel_spmd
from contextlib import ExitStack

F32 = mybir.dt.float32
BF16 = mybir.dt.bfloat16
AF = mybir.ActivationFunctionType
ALU = mybir.AluOpType
AX = mybir.AxisListType

SAME_ENG_SYNC = True


class _Op:
    __slots__ = ("eng", "fn", "reads", "writes", "dma_sem", "ndma", "deps",
                 "needs_inc", "token", "waits", "idx", "is_bar")

    def __init__(self, eng, fn, reads, writes, dma_sem=None, ndma=0):
        self.eng = eng
        self.fn = fn
        self.reads = reads
        self.writes = writes
        self.dma_sem = dma_sem
        self.ndma = ndma
        self.deps = []
        self.needs_inc = False
        self.token = None
        self.waits = []
        self.is_bar = False


class Sched:
    CENG = ("pe", "act", "dve", "pool")
    ALLENG = ("pe", "act", "dve", "pool", "sp")

    def __init__(self, nc, stack):
        self.nc = nc
        self.stack = stack
        self.ops = []
        self.esem = {e: stack.enter_context(nc.semaphore("s_" + e)) for e in self.CENG}
        self.dma_cum = {}
        self.dma_sems = {}
        self.dma_exempt = set()
        self.last_w = {}
        self.readers = {}
        self.last_op = {e: None for e in self.ALLENG}
        self.dma_last = {}

    def dma_sem(self, name, exempt=False):
        if name not in self.dma_sems:
            self.dma_sems[name] = self.stack.enter_context(self.nc.semaphore("d_" + name))
            self.dma_cum[name] = 0
            if exempt:
                self.dma_exempt.add(name)
        return name

    def _add(self, op):
        op.idx = len(self.ops)
        deps = set()
        for k in op.reads:
            w = self.last_w.get(k)
            if w is not None:
                deps.add(w)
        for k in op.writes:
            w = self.last_w.get(k)
            if w is not None:
                deps.add(w)
            for r in self.readers.get(k, ()):
                deps.add(r)
        deps.discard(op)
        op.deps = sorted(deps, key=lambda o: o.idx)
        for k in op.reads:
            self.readers.setdefault(k, []).append(op)
        for k in op.writes:
            self.last_w[k] = op
            self.readers[k] = []
        self.ops.append(op)
        self.last_op[op.eng] = op
        if op.dma_sem is not None:
            self.dma_cum[op.dma_sem] += 16 * op.ndma
            op.token = (op.dma_sem, self.dma_cum[op.dma_sem])
            self.dma_last[op.dma_sem] = op
        return op

    def op(self, eng, fn, reads=(), writes=()):
        return self._add(_Op(eng, fn, tuple(reads), tuple(writes)))

    def pe(self, fn, reads=(), writes=()):
        return self.op("pe", fn, reads, writes)

    def act(self, fn, reads=(), writes=()):
        return self.op("act", fn, reads, writes)

    def dve(self, fn, reads=(), writes=()):
        return self.op("dve", fn, reads, writes)

    def pool(self, fn, reads=(), writes=()):
        return self.op("pool", fn, reads, writes)

    def dma(self, eng, fn, sem, reads=(), writes=(), n=1):
        return self._add(_Op(eng, fn, tuple(reads), tuple(writes), dma_sem=sem, ndma=n))

    def barrier(self):
        prev = dict(self.last_op)
        dl = {k: v for k, v in self.dma_last.items() if k not in self.dma_exempt}
        for e in self.ALLENG:
            b = _Op(e, None, (), ())
            b.is_bar = True
            b.idx = len(self.ops)
            b.deps = [o for ee, o in prev.items() if o is not None and (ee != e or (SAME_ENG_SYNC and e != 'pe'))] + list(dl.values())
            self.ops.append(b)
            self.last_op[e] = b

    def finalize(self):
        for op in self.ops:
            for d in op.deps:
                if d.dma_sem is not None or d.is_bar:
                    continue
                if d.eng == op.eng and (d.eng == "pe" or not SAME_ENG_SYNC):
                    continue
                d.needs_inc = True
        cnt = {e: 0 for e in self.CENG}
        for op in self.ops:
            if op.dma_sem is None and op.needs_inc:
                cnt[op.eng] += 1
                op.token = (op.eng, cnt[op.eng])
        known = {e: {} for e in self.ALLENG}
        for op in self.ops:
            kn = known[op.eng]
            need = {}
            for d in op.deps:
                if d.token is None:
                    continue
                if d.dma_sem is None and d.eng == op.eng and (d.eng == "pe" or not SAME_ENG_SYNC):
                    continue
                s, v = d.token
                if need.get(s, 0) < v:
                    need[s] = v
            for s, v in need.items():
                if kn.get(s, 0) >= v:
                    continue
                kn[s] = v
                op.waits.append((s, v))
        self.counts = cnt

    def _sem(self, s):
        return self.esem[s] if s in self.esem else self.dma_sems[s]

    def emit(self):
        self.finalize()
        by_eng = {e: [o for o in self.ops if o.eng == e] for e in self.ALLENG}
        with self.nc.Block() as block:
            def run(e):
                def body(eng):
                    for op in by_eng[e]:
                        for (s, v) in op.waits:
                            eng.wait_ge(self._sem(s), v)
                        if op.fn is None:
                            continue
                        if op.dma_sem is not None:
                            op.fn(eng, self.dma_sems[op.dma_sem])
                        else:
                            ins = op.fn(eng)
                            if op.needs_inc:
                                ins.then_inc(self.esem[e], 1)
                return body
            block.tensor(run("pe"))
            block.scalar(run("act"))
            block.vector(run("dve"))
            block.gpsimd(run("pool"))
            block.sync(run("sp"))


import os
CUT = int(os.environ.get('KCUT', '99'))
D = 1024
T = 2064
NMETA = 16
DFF = 2752
NJ = 22
EPS = 1e-6
TT = [(0, 16)] + [(16 + 128 * i, 128) for i in range(16)]
GG = [(0, 16)] + [(16 + 512 * j, 512) for j in range(4)]


def tiles_of_group(gi):
    return [0] if gi == 0 else list(range(1 + 4 * (gi - 1), 1 + 4 * gi))


class Builder:
    def __init__(self, nseq=2, stop=None):
        self.nseq = nseq
        self.stop = stop
        nc = self.nc = bass.Bass("TRN2", target_bir_lowering=False)
        dt = lambda name, shape: nc.dram_tensor(name, shape, F32, kind="ExternalInput").ap()
        self.x = dt("x", [nseq, 2048, D])
        self.meta = dt("meta_tokens", [NMETA, D])
        self.norm_gains = dt("norm_gains", [2, 4, D])
        self.a_w_in = dt("a_w_in", [1, D, 4 * D])
        self.a_lb = dt("a_lb_logits", [2, D])
        self.a_hn = dt("a_head_norm", [1, D])
        self.a_w_out = dt("a_w_out", [1, D, D])
        self.kv_norm = dt("kv_norm", [D])
        self.kv_w = dt("kv_w", [D, 2 * D + 16])
        self.fg_b = dt("fg_b", [16])
        self.b_w_q = dt("b_w_q", [1, D, D])
        self.b_w_out = dt("b_w_out", [1, D, D])
        self.w_up = dt("ffn_w_up", [2, D, 2 * DFF])
        self.conv = dt("ffn_conv", [2, 3, 2 * DFF])
        self.w_down = dt("ffn_w_down", [2, DFF, D])
        self.out = nc.dram_tensor("out", [nseq, 2048, D], F32, kind="ExternalOutput").ap()
        self.uid = 0

    def sb(self, st, name, shape, dtype):
        self.uid += 1
        return st.enter_context(self.nc.sbuf_tensor("%s_%d" % (name, self.uid), shape, dtype))

    def build(self):
        nc = self.nc
        with ExitStack() as st:
            S = self.S = Sched(nc, st)
            self.ps = [st.enter_context(nc.psum_tensor("ps%d" % i, [128, 512], F32)) for i in range(7)]
            self.psb = st.enter_context(nc.psum_tensor("psb", [128, 1024], BF16))
            self.hT = self.sb(st, "hT", [128, 8, T], F32)
            self.consts(st)
            S.barrier()
            for s in range(self.nseq):
                self.seq(s)
            S.barrier()
            S.emit()
        return nc

    def consts(self, st):
        S = self.S
        self.ident = self.sb(st, "ident", [128, 128], F32)
        self.identb = self.sb(st, "identb", [128, 128], BF16)
        self.onesb = self.sb(st, "onesb", [128, 128], BF16)
        self.triu = self.sb(st, "triu", [128, 128], BF16)
        self.mask2 = self.sb(st, "mask2", [128, 128], F32)
        self.maskseg = self.sb(st, "maskseg", [128, 512], F32)
        self.epsc = self.sb(st, "epsc", [128, 1], F32)
        self.colv = self.sb(st, "colv", [128, 96], F32)
        self.convT = self.sb(st, "convT", [128, 3, 128], F32)
        self.lbc = self.sb(st, "lbc", [128, 24], F32)
        self.nfgb = self.sb(st, "nfgb", [16, 1], F32)
        self.i16 = self.sb(st, "i16", [16, 16], F32)
        self.ones16 = self.sb(st, "ones16", [16, 128], F32)
        self.rinvs = self.sb(st, "rinvs", [128, 512], F32)
        self.onesf = self.sb(st, "onesf", [16, 512], F32)
        self.onec = self.sb(st, "onec", [128, 1], F32)
        self.selq = self.sb(st, "selq", [16, 16, 65], BF16)
        P = lambda fn, r=(), w=(): S.pool(fn, r, w)
        P(lambda e: e.memset(self.ident[:], 1.0), w=["ident"])
        P(lambda e: e.affine_select(out=self.ident[:], in_=self.ident[:], pattern=[[-1, 128]],
                                    compare_op=ALU.is_equal, fill=0.0, base=0, channel_multiplier=1),
          r=["ident"], w=["ident"])
        S.dve(lambda e: e.tensor_copy(out=self.identb[:], in_=self.ident[:]), ["ident"], ["identb"])
        S.dve(lambda e: e.tensor_copy(out=self.i16[:], in_=self.ident[0:16, 0:16]), ["ident"], ["i16"])
        P(lambda e: e.memset(self.onesb[:], 1.0), w=["onesb"])
        P(lambda e: e.memset(self.ones16[:], 1.0), w=["ones16"])
        P(lambda e: e.memset(self.triu[:], 1.0), w=["triu"])
        P(lambda e: e.affine_select(out=self.triu[:], in_=self.triu[:], pattern=[[1, 128]],
                                    compare_op=ALU.is_ge, fill=0.0, base=0, channel_multiplier=-1),
          r=["triu"], w=["triu"])
        P(lambda e: e.memset(self.mask2[:], 1.0), w=["mask2"])
        P(lambda e: e.affine_select(out=self.mask2[:], in_=self.mask2[:], pattern=[[1, 128]],
                                    compare_op=ALU.is_ge, fill=0.0, base=0, channel_multiplier=-1),
          r=["mask2"], w=["mask2"])
        P(lambda e: e.memset(self.mask2[0:64, 64:128], 0.0), r=["mask2"], w=["mask2"])
        P(lambda e: e.memset(self.maskseg[:], 1.0), w=["maskseg"])
        P(lambda e: e.memset(self.maskseg[:].rearrange("p (c k) -> p c k", k=64)[:, :, 0:1], 0.0),
          r=["maskseg"], w=["maskseg"])
        P(lambda e: e.memset(self.epsc[:], EPS), w=["epsc"])
        P(lambda e: e.memset(self.onesf[:], 1.0), w=["onesf"])
        P(lambda e: e.memset(self.onec[:], 1.0), w=["onec"])
        P(lambda e: e.memset(self.selq[:], 0.0), w=["selq0"])
        S.dve(lambda e: e.tensor_scalar(out=self.selq[:, :, 64], in0=self.i16[:, :], scalar1=-8.0, scalar2=None, op0=ALU.mult),
              ["selq0", "i16"], ["selq"])
        rowsA = self.sb(st, "rowsA", [96, 128], F32)
        rowsC = self.sb(st, "rowsC", [128, 3, 128], F32)
        P(lambda e: e.memset(rowsC[:], 0.0), w=["rowsC"])
        cs = S.dma_sem("const")
        nd = [0]

        def ld(dst, src, rk):
            S.dma("sp", lambda e, s, dst=dst, src=src: e.dma_start(out=dst, in_=src).then_inc(s, 16), cs,
                  reads=[rk], writes=[("rowsd", nd[0])])
            nd[0] += 1
        ld(rowsA[0:64, :], self.norm_gains.rearrange("l j (c p) -> (l j c) p", p=128), "rowsA")
        ld(rowsA[64:80, :], self.a_lb.rearrange("l (c p) -> (l c) p", p=128), "rowsA")
        ld(rowsA[80:88, :], self.a_hn.rearrange("l (c p) -> (l c) p", p=128), "rowsA")
        ld(rowsA[88:96, :], self.kv_norm.rearrange("(c p) -> c p", p=128), "rowsA")
        for l in range(2):
            for tap in range(3):
                for part in range(2):
                    r0 = ((l * 3 + tap) * 2 + part) * 22
                    src = self.conv[l, tap, part * DFF: part * DFF + 2688].rearrange("(j k) -> j k", k=128)
                    done = 0
                    while done < 21:
                        ti, ri = divmod(r0 + done, 128)
                        cnt = min(21 - done, 128 - ri)
                        ld(rowsC[ri:ri + cnt, ti, :], src[done:done + cnt, :], "rowsC")
                        done += cnt
                    ti, ri = divmod(r0 + 21, 128)
                    ld(rowsC[ri:ri + 1, ti, 0:64],
                       self.conv[l, tap, part * DFF + 2688: part * DFF + 2752].rearrange("(a k) -> a k", a=1), "rowsC")
        ld(self.nfgb[:, :], self.fg_b.rearrange("(h a) -> h a", a=1), "nfgb")
        allrows = [("rowsd", i) for i in range(nd[0])]
        ps = self.ps
        S.pe(lambda e: e.transpose(ps[0][:, 0:96], rowsA[:, :], self.ident[0:96, 0:96]), allrows + ["ident"], ["ps0"])
        S.dve(lambda e: e.tensor_copy(out=self.colv[:], in_=ps[0][:, 0:96]), ["ps0"], ["colv"])
        for ti in range(3):
            S.pe(lambda e, ti=ti: e.transpose(ps[1][:, ti * 128:(ti + 1) * 128], rowsC[:, ti, :], self.ident[:]),
                 allrows + ["ident", "rowsC"], ["ps1"])
        S.dve(lambda e: e.tensor_copy(out=self.convT[:], in_=ps[1][:, 0:384].rearrange("p (a b) -> p a b", b=128)),
              ["ps1"], ["convT"])
        dl = self.sb(st, "dl", [128, 8], F32)
        S.dve(lambda e: e.tensor_tensor(out=dl[:], in0=self.colv[:, 64:72], in1=self.colv[:, 72:80], op=ALU.subtract),
              ["colv"], ["dl"])
        S.act(lambda e: e.activation(out=self.lbc[:, 0:8], in_=dl[:], func=AF.Sigmoid), ["dl"], ["lbc0"])
        S.act(lambda e: e.activation(out=self.lbc[:, 8:16], in_=dl[:], func=AF.Sigmoid, scale=-1.0), ["dl"], ["lbc1"])
        S.dve(lambda e: e.tensor_scalar(out=self.lbc[:, 16:24], in0=self.lbc[:, 8:16], scalar1=-1.0, scalar2=None,
                                        op0=ALU.mult), ["lbc1"], ["lbc2"])
        S.dve(lambda e: e.tensor_scalar(out=self.nfgb[:], in0=self.nfgb[:], scalar1=-1.0, scalar2=None, op0=ALU.mult),
              allrows, ["nfgb2"])

    def gcol(self, l, j, c):
        k = (l * 4 + j) * 8 + c
        return self.colv[:, k:k + 1]

    def ccol(self, l, tap, part, j):
        r = ((l * 3 + tap) * 2 + part) * 22 + j
        ti, ri = divmod(r, 128)
        return self.convT[:, ti, ri:ri + 1]

    def wload(self, dst, src, slot, key, reads=()):
        S = self.S
        sem = S.dma_sem("w_" + "_".join(str(k) for k in (key if isinstance(key, tuple) else (key,))), exempt=True)
        S.dma("pool", lambda e, s: e.dma_start(out=dst, in_=src).then_inc(s, 16), sem,
              reads=list(reads), writes=[key])

    def rstd_from(self, srcs, n, sq, rtmp, rstd, pst, pkey, dscale=1.0 / D):
        S = self.S
        nsrc = len(srcs)
        for c, (ap, rk) in enumerate(srcs):
            S.act(lambda e, ap=ap, c=c: e.activation(out=sq[:, c, :n], in_=ap, func=AF.Square), rk, [("sq", c)])
            S.pe(lambda e, c=c: e.matmul(pst[:, :n], lhsT=self.onesb[:], rhs=sq[:, c, :n], start=(c == 0),
                                         stop=(c == nsrc - 1)), [("sq", c)], [pkey])
        S.act(lambda e: e.activation(out=rtmp[:, :n], in_=pst[:, :n], func=AF.Sqrt, scale=dscale, bias=self.epsc[:, 0:1]),
              [pkey], ["rtmp"])
        S.dve(lambda e: e.reciprocal(out=rstd[:, :n], in_=rtmp[:, :n]), ["rtmp"], ["rstd"])

    def seq(self, s):
        S = self.S
        self.load_x(s)
        S.barrier()
        if self.stop != "load":
            self.hgrn2(s)
            S.barrier()
            if self.stop not in ("mix0", "mix0a", "mix0b", "mix0c"):
                self.ffn(s, 0)
                S.barrier()
                if self.stop != "ffn0":
                    self.fox(s)
                    S.barrier()
                    if self.stop != "mix1":
                        self.ffn(s, 1)
                        S.barrier()
        self.store(s)
        S.barrier()

    def load_x(self, s):
        S, ps, hT = self.S, self.ps, self.hT
        with ExitStack() as st:
            xin = [self.sb(st, "xin%d" % i, [128, D], F32) for i in range(2)]
            xs = [S.dma_sem("xin%d" % i) for i in range(2)]
            for ti, (t0, n) in enumerate(TT):
                sl = ti % 2
                src = self.meta if ti == 0 else self.x[s, t0 - 16:t0 - 16 + 128, :]
                S.dma("sp", lambda e, sm, sl=sl, src=src, n=n: e.dma_start(out=xin[sl][:n, :], in_=src).then_inc(sm, 16),
                      xs[sl], writes=[("xin", sl)])
                for half in range(2):
                    bank = ps[half + 2 * sl]
                    bk = "ps%d" % (half + 2 * sl)
                    for j in range(4):
                        c = half * 4 + j
                        S.pe(lambda e, bank=bank, j=j, c=c, n=n, sl=sl: e.transpose(
                            bank[:, j * 128:j * 128 + n], xin[sl][:n, c * 128:(c + 1) * 128], self.ident[:n, :n]),
                            [("xin", sl)], [bk])
                    fn = lambda e, bank=bank, half=half, t0=t0, n=n: e.tensor_copy(
                        out=hT[:, half * 4:(half + 1) * 4, t0:t0 + n],
                        in_=bank[:, :].rearrange("p (j k) -> p j k", k=128)[:, :, 0:n])
                    if half == 0:
                        S.dve(fn, [bk], [("hT", ti, half)])
                    else:
                        S.act(lambda e, bank=bank, half=half, t0=t0, n=n: e.activation(
                            out=hT[:, half * 4:(half + 1) * 4, t0:t0 + n],
                            in_=bank[:, :].rearrange("p (j k) -> p j k", k=128)[:, :, 0:n], func=AF.Copy),
                            [bk], [("hT", ti, half)])

    def store(self, s):
        S, ps, hT = self.S, self.ps, self.hT
        with ExitStack() as st:
            xo = [self.sb(st, "xo%d" % i, [128, D], F32) for i in range(2)]
            os_ = [S.dma_sem("xo%d" % i) for i in range(2)]
            for ti, (t0, n) in enumerate(TT):
                if ti == 0:
                    continue
                sl = ti % 2
                for half in range(2):
                    bank = ps[half + 2 * sl]
                    bk = "ps%d" % (half + 2 * sl)
                    for j in range(4):
                        c = half * 4 + j
                        S.pe(lambda e, bank=bank, j=j, c=c, t0=t0: e.transpose(
                            bank[:, j * 128:(j + 1) * 128], hT[:, c, t0:t0 + 128], self.ident[:]), [], [bk])
                    if half == 0:
                        S.dve(lambda e, bank=bank, sl=sl: e.tensor_copy(out=xo[sl][:, 0:512], in_=bank[:, :]),
                              [bk], [("xo", sl)])
                    else:
                        S.act(lambda e, bank=bank, sl=sl: e.activation(out=xo[sl][:, 512:1024], in_=bank[:, :], func=AF.Copy),
                              [bk], [("xo", sl)])
                S.dma("sp", lambda e, sm, sl=sl, t0=t0: e.dma_start(out=self.out[s, t0 - 16:t0 - 16 + 128, :],
                                                                     in_=xo[sl][:, :]).then_inc(sm, 16),
                      os_[sl], reads=[("xo", sl)], writes=[("xo", sl)])

    def out_proj_residual(self, st, wsrc, src_act, l, jn, tag):
        S, ps, hT = self.S, self.ps, self.hT
        wo = self.sb(st, "wo", [128, 8, D], BF16)
        mix32 = self.sb(st, "mix32", [128, 8, 512], F32)
        sq = self.sb(st, "sqo", [128, 8, 512], BF16)
        rtmp = self.sb(st, "rtmpo", [128, 512], F32)
        rstd = self.sb(st, "rstdo", [128, 512], F32)
        tmp = self.sb(st, "tmpo", [128, 512], F32)
        wv = wsrc.rearrange("(kc p) n -> p kc n", p=128)
        for kc in range(8):
            self.wload(wo[:, kc, :], wv[:, kc, :], kc % 2, ("wo", kc))
        for gi, (g0, n) in enumerate(GG):
            for dc in range(8):
                bank = ps[dc % 2]
                bk = "ps%d" % (dc % 2)
                for kc in range(8):
                    S.pe(lambda e, bank=bank, dc=dc, kc=kc, g0=g0, n=n: e.matmul(
                        bank[:, :n], lhsT=wo[:, kc, dc * 128:(dc + 1) * 128], rhs=src_act[:, kc, g0:g0 + n],
                        start=(kc == 0), stop=(kc == 7)), [("wo", kc), (tag, gi)], [bk])
                S.act(lambda e, bank=bank, dc=dc, n=n: e.activation(out=mix32[:, dc, :n], in_=bank[:, :n], func=AF.Copy),
                      [bk], [("mix32", dc)])
            self.rstd_from([(mix32[:, dc, :n], [("mix32", dc)]) for dc in range(8)], n, sq, rtmp, rstd, ps[2], "ps2")
            for dc in range(8):
                S.dve(lambda e, dc=dc, n=n: e.scalar_tensor_tensor(
                    out=tmp[:, :n], in0=mix32[:, dc, :n], scalar=self.gcol(l, jn, dc), in1=rstd[:, :n],
                    op0=ALU.mult, op1=ALU.mult), [("mix32", dc), "rstd"], ["tmpo"])
                S.pool(lambda e, dc=dc, g0=g0, n=n: e.tensor_tensor(
                    out=hT[:, dc, g0:g0 + n], in0=hT[:, dc, g0:g0 + n], in1=tmp[:, :n], op=ALU.add),
                    ["tmpo"], [("hT", dc, gi)])

    def hgrn2(self, s):
        S, ps, psb, hT = self.S, self.ps, self.psb, self.hT
        with ExitStack() as st0:
            og = self.sb(st0, "og", [128, 8, T], BF16)
            with ExitStack() as st:
                xn = self.sb(st, "xn", [128, 8, T], BF16)
                sq = self.sb(st, "sq", [128, 8, 512], BF16)
                rtmp = self.sb(st, "rtmp", [128, 512], F32)
                rstd = self.sb(st, "rstd", [128, 512], F32)
                for gi, (g0, n) in enumerate(GG):
                    self.rstd_from([(hT[:, c, g0:g0 + n], []) for c in range(8)], n, sq, rtmp, rstd, ps[2], "ps2")
                    for c in range(8):
                        S.dve(lambda e, c=c, g0=g0, n=n: e.scalar_tensor_tensor(
                            out=xn[:, c, g0:g0 + n], in0=hT[:, c, g0:g0 + n], scalar=self.gcol(0, 0, c), in1=rstd[:, :n],
                            op0=ALU.mult, op1=ALU.mult), ["rstd"], [("xn", gi)])
                wh = [self.sb(st, "wh%d" % i, [128, 8, 4, 128], BF16) for i in range(2)]
                A = self.sb(st, "A", [128, 512], F32)
                B = self.sb(st, "B", [128, 512], F32)
                C = self.sb(st, "C", [128, 512], F32)
                Dn = self.sb(st, "Dn", [128, 512], F32)
                SG = self.sb(st, "SG", [128, 512], F32)
                O32 = self.sb(st, "O32", [128, 512], F32)
                qin = self.sb(st, "qin", [128, 512], BF16)
                kin = self.sb(st, "kin", [128, 512], BF16)
                kout = self.sb(st, "kout", [128, 512], BF16)
                sqh = self.sb(st, "sqh", [128, 1, 512], BF16)
                vtok = self.sb(st, "vtok", [128, 4, 128], BF16)
                ktok = self.sb(st, "ktok", [128, 4, 128], BF16)
                attT = self.sb(st, "attT", [128, 4, 128], BF16)
                S32 = self.sb(st, "S32", [128, 9, 128], F32)
                Sb = self.sb(st, "Sb", [128, 8, 128], BF16)
                win = self.a_w_in[0].rearrange("(kc p) n -> p kc n", p=128)
                nheads = {"mix0a": 0, "mix0b": 1, "mix0c": 1}.get(self.stop, 8)
                for hd in range(nheads):
                    sl = hd % 2
                    w = wh[sl]
                    for j in range(4):
                        self.wload(w[:, :, j, :], win[:, :, j * D + hd * 128: j * D + (hd + 1) * 128], sl, ("wh", sl, j))
                    wk = [("wh", sl, j) for j in range(4)]
                    S.dve(lambda e: e.memset(S32[:, 0, :], 0.0), [], [("S32", 0)])
                    for gi, (g0, n) in enumerate(GG):
                        if self.stop == "mix0b" and gi > 0:
                            break
                        tl_list = tiles_of_group(gi)
                        nch = 1 if gi == 0 else 8
                        for (j, bi) in ((0, 0), (1, 1), (3, 2)):
                            for kc in range(8):
                                S.pe(lambda e, j=j, bi=bi, kc=kc, g0=g0, n=n, w=w: e.matmul(
                                    ps[bi][:, :n], lhsT=w[:, kc, j, :], rhs=xn[:, kc, g0:g0 + n],
                                    start=(kc == 0), stop=(kc == 7)), [wk[j], ("xn", gi)], ["ps%d" % bi])
                        for li, ti in enumerate(tl_list):
                            t0, nt = TT[ti]
                            for kc in range(8):
                                S.pe(lambda e, li=li, kc=kc, t0=t0, nt=nt, w=w: e.matmul(
                                    ps[3][:nt, li * 128:(li + 1) * 128], lhsT=xn[:, kc, t0:t0 + nt], rhs=w[:, kc, 2, :],
                                    start=(kc == 0), stop=(kc == 7)), [wk[2], ("xn", gi)], ["ps3"])
                        if gi == 0:
                            S.act(lambda e: e.activation(out=vtok[:16, 0, :], in_=ps[3][:16, 0:128], func=AF.Copy),
                                  ["ps3"], ["vtok"])
                        else:
                            S.act(lambda e: e.activation(out=vtok[:, :, :], in_=ps[3][:, :].rearrange("p (a b) -> p a b", b=128),
                                                         func=AF.Copy), ["ps3"], ["vtok"])
                        if self.stop == 'mix0c' and gi > 0 and CUT <= 1:
                            continue
                        S.act(lambda e, n=n: e.activation(out=A[:, :n], in_=ps[1][:, :n], func=AF.Sigmoid), ["ps1"], ["A"])
                        S.act(lambda e, n=n: e.activation(out=SG[:, :n], in_=ps[2][:, :n], func=AF.Silu), ["ps2"], ["SG"])
                        S.act(lambda e, n=n, hd=hd: e.activation(out=B[:, :n], in_=A[:, :n], func=AF.Ln,
                                                                 scale=self.lbc[:, 8 + hd:9 + hd], bias=self.lbc[:, hd:hd + 1]),
                              ["A"], ["B"])
                        S.dve(lambda e, n=n, hd=hd: e.tensor_scalar(out=C[:, :n], in0=A[:, :n],
                                                                    scalar1=self.lbc[:, 16 + hd:17 + hd],
                                                                    scalar2=self.lbc[:, 8 + hd:9 + hd],
                                                                    op0=ALU.mult, op1=ALU.add), ["A"], ["C"])
                        S.dve(lambda e, n=n: e.tensor_tensor_scan(out=A[:, :n], data0=self.maskseg[:, :n], data1=B[:, :n],
                                                                  initial=0.0, op0=ALU.mult, op1=ALU.add), ["B", "A"], ["A"])
                        S.act(lambda e, n=n: e.activation(out=B[:, :n], in_=A[:, :n], func=AF.Exp), ["A"], ["B"])
                        S.act(lambda e, n=n: e.activation(out=Dn[:, :n], in_=A[:, :n], func=AF.Exp, scale=-1.0), ["A"], ["Dn"])
                        S.dve(lambda e, n=n: e.tensor_tensor(out=qin[:, :n], in0=ps[0][:, :n], in1=B[:, :n], op=ALU.mult),
                              ["ps0", "B"], ["qin"])
                        S.dve(lambda e, n=n: e.tensor_tensor(out=C[:, :n], in0=C[:, :n], in1=Dn[:, :n], op=ALU.mult),
                              ["C", "Dn"], ["C"])
                        S.act(lambda e, n=n: e.activation(out=kin[:, :n], in_=C[:, :n], func=AF.Copy), ["C"], ["kin"])
                        if gi == 0:
                            S.dve(lambda e: e.tensor_scalar(out=kout[:, :16], in0=C[:, :16], scalar1=B[:, 15:16], scalar2=None,
                                                            op0=ALU.mult), ["C", "B"], ["kout"])
                        else:
                            S.dve(lambda e: e.tensor_tensor(
                                out=kout[:, :].rearrange("p (c k) -> p c k", k=64),
                                in0=C[:, :].rearrange("p (c k) -> p c k", k=64),
                                in1=B[:, :].rearrange("p (c k) -> p c k", k=64)[:, :, 63:64].to_broadcast([128, 8, 64]),
                                op=ALU.mult), ["C", "B"], ["kout"])
                        if self.stop == 'mix0c' and gi > 0 and CUT <= 2:
                            continue
                        for li, ti in enumerate(tl_list):
                            t0, nt = TT[ti]
                            S.pe(lambda e, li=li, nt=nt: e.transpose(psb[:nt, li * 128:(li + 1) * 128],
                                                                      kout[:, li * 128:li * 128 + nt], self.identb[:]),
                                 ["kout"], ["psb"])
                        if gi == 0:
                            S.act(lambda e: e.activation(out=ktok[:16, 0, :], in_=psb[:16, 0:128], func=AF.Copy), ["psb"], ["ktok"])
                        else:
                            S.act(lambda e: e.activation(out=ktok[:, :, :], in_=psb[:, 0:512].rearrange("p (a b) -> p a b", b=128),
                                                         func=AF.Copy), ["psb"], ["ktok"])
                        if self.stop == 'mix0c' and gi > 0 and CUT <= 3:
                            continue
                        for cl in range(nch):
                            li, r0 = cl // 2, (cl % 2) * 64
                            nr = 16 if gi == 0 else 64
                            bi = 1 + cl % 2
                            S.pe(lambda e, cl=cl, li=li, r0=r0, nr=nr, bi=bi: e.matmul(
                                ps[bi][:, (cl // 2) * 128:(cl // 2 + 1) * 128], lhsT=ktok[r0:r0 + nr, li, :],
                                rhs=vtok[r0:r0 + nr, li, :], start=True, stop=True),
                                ["ktok", "vtok", "A", "SG"], ["ps%d" % bi])
                        for cl in range(nch):
                            bi = 1 + cl % 2
                            dcol = B[:, 15:16] if gi == 0 else B[:, cl * 64 + 63:cl * 64 + 64]
                            S.dve(lambda e, cl=cl, bi=bi, dcol=dcol: e.scalar_tensor_tensor(
                                out=S32[:, cl + 1, :], in0=S32[:, cl, :], scalar=dcol,
                                in1=ps[bi][:, (cl // 2) * 128:(cl // 2 + 1) * 128], op0=ALU.mult, op1=ALU.add),
                                [("S32", cl), "B", "ps%d" % bi], [("S32", cl + 1)])
                        S.act(lambda e, nch=nch: e.activation(out=Sb[:, 0:nch, :], in_=S32[:, 0:nch, :], func=AF.Copy),
                              [("S32", c) for c in range(nch)], ["Sb"])
                        if self.stop == 'mix0c' and gi > 0 and CUT <= 4:
                            continue
                        for li, ti in enumerate(tl_list):
                            t0, nt = TT[ti]
                            S.pe(lambda e, li=li, nt=nt: e.matmul(ps[0][:nt, li * 128:li * 128 + nt],
                                                                   lhsT=kin[:, li * 128:li * 128 + nt],
                                                                   rhs=qin[:, li * 128:li * 128 + nt], start=True, stop=True),
                                 ["kin", "qin"], ["ps0"])
                        if gi == 0:
                            S.dve(lambda e: e.tensor_tensor(out=attT[:16, 0, :16], in0=ps[0][:16, 0:16], in1=self.mask2[:16, :16],
                                                            op=ALU.mult), ["ps0"], ["attT"])
                        else:
                            S.dve(lambda e: e.tensor_tensor(
                                out=attT[:, :, :], in0=ps[0][:, :].rearrange("p (a b) -> p a b", b=128),
                                in1=self.mask2[:, :].unsqueeze(1).to_broadcast([128, 4, 128]), op=ALU.mult), ["ps0"], ["attT"])
                        if self.stop == 'mix0c' and gi > 0 and CUT <= 5:
                            continue
                        for li, ti in enumerate(tl_list):
                            t0, nt = TT[ti]
                            S.pe(lambda e, li=li, nt=nt, gi=gi: e.matmul(
                                ps[4][:, li * 128:li * 128 + nt], lhsT=vtok[:nt, li, :], rhs=attT[:nt, li, :nt],
                                start=True, stop=(gi == 0)), ["vtok", "attT"], ["ps4"])
                            if gi > 0:
                                for hh in range(2):
                                    cl = 2 * li + hh
                                    S.pe(lambda e, li=li, hh=hh, cl=cl: e.matmul(
                                        ps[4][:, cl * 64:(cl + 1) * 64], lhsT=Sb[:, cl, :], rhs=qin[:, cl * 64:(cl + 1) * 64],
                                        start=False, stop=(hh == 1)), ["Sb", "qin"], ["ps4"])
                        S.dve(lambda e, nch=nch: e.tensor_copy(out=S32[:, 0, :], in_=S32[:, nch, :]),
                              [("S32", nch), "Sb"], [("S32", 0)])
                        if self.stop == 'mix0c' and gi > 0 and CUT <= 6:
                            continue
                        self.rstd_from([(ps[4][:, :n], ["ps4"])], n, sqh, rtmp, rstd, ps[5], "ps5", dscale=1.0 / 128)
                        S.dve(lambda e, n=n, hd=hd: e.scalar_tensor_tensor(
                            out=O32[:, :n], in0=ps[4][:, :n], scalar=self.colv[:, 80 + hd:81 + hd], in1=rstd[:, :n],
                            op0=ALU.mult, op1=ALU.mult), ["ps4", "rstd"], ["O32"])
                        S.pool(lambda e, n=n, hd=hd, g0=g0: e.tensor_tensor(out=og[:, hd, g0:g0 + n], in0=O32[:, :n], in1=SG[:, :n],
                                                                              op=ALU.mult), ["O32", "SG"], [("og", gi)])
            self.S.barrier()
            if self.stop in ("mix0a", "mix0b", "mix0c"):
                return
            with ExitStack() as st:
                self.out_proj_residual(st, self.a_w_out[0], og, 0, 1, "og")

    def ffn(self, s, l):
        S, ps, hT = self.S, self.ps, self.hT
        halves = [(0, 1040), (1040, 1024)]
        with ExitStack() as st0:
            halo = self.sb(st0, "halo", [128, 8, 2], BF16)
            S.dve(lambda e: e.memset(halo[:], 0.0), [], ["halo"])
            def half(hf, h0, nh):
                with ExitStack() as st1:
                    act = self.sb(st1, "act", [128, NJ, 1040], BF16)
                    blocks = []
                    o = 0
                    while o < nh:
                        nb = min(510, nh - o)
                        blocks.append((o, nb))
                        o += nb
                    with ExitStack() as st:
                        xn = self.sb(st, "xn2", [128, 8, 1042], BF16)
                        sq = self.sb(st, "sq2", [128, 8, 512], BF16)
                        rtmp = self.sb(st, "rtmp2", [128, 512], F32)
                        rstd = self.sb(st, "rstd2", [128, 512], F32)
                        S.dve(lambda e: e.tensor_copy(out=xn[:, :, 0:2], in_=halo[:]), ["halo"], [("xn2", -1)])
                        subs = []
                        o = 0
                        while o < nh:
                            nn = min(512, nh - o)
                            subs.append((o, nn))
                            o += nn
                        for si, (o, nn) in enumerate(subs):
                            g0 = h0 + o
                            self.rstd_from([(hT[:, c, g0:g0 + nn], []) for c in range(8)], nn, sq, rtmp, rstd, ps[2], "ps2")
                            for c in range(8):
                                S.dve(lambda e, c=c, g0=g0, nn=nn, o=o: e.scalar_tensor_tensor(
                                    out=xn[:, c, 2 + o:2 + o + nn], in0=hT[:, c, g0:g0 + nn], scalar=self.gcol(l, 2, c),
                                    in1=rstd[:, :nn], op0=ALU.mult, op1=ALU.mult), ["rstd"], [("xn2", si)])
                        xkeys = [("xn2", -1)] + [("xn2", si) for si in range(len(subs))]
                        S.dve(lambda e, nh=nh: e.tensor_copy(out=halo[:], in_=xn[:, :, nh:nh + 2]), xkeys, ["halo"])
                        wu = [self.sb(st, "wu%d" % i, [128, 8, 2, 128], BF16) for i in range(2)]
                        G32 = [self.sb(st, "G32_%d" % i, [128, 512], F32) for i in range(2)]
                        V32 = [self.sb(st, "V32_%d" % i, [128, 512], F32) for i in range(2)]
                        SGf = [self.sb(st, "SGf_%d" % i, [128, 512], F32) for i in range(2)]
                        wup = self.w_up[l].rearrange("(kc p) n -> p kc n", p=128)
                        units = [(j, bi_, o, nb) for j in range(NJ) for bi_, (o, nb) in enumerate(blocks)]

                        def front(k):
                            j, bi_, o, nb = units[k]
                            mj = 128 if j < 21 else 64
                            sl = j % 2
                            w = wu[sl]
                            ub = k % 2
                            if bi_ == 0:
                                for part in range(2):
                                    self.wload(w[:, :, part, :mj], wup[:, :, part * DFF + j * 128: part * DFF + j * 128 + mj],
                                               sl, ("wu", sl, part))
                            for part in range(2):
                                bi = part + 2 * ub
                                bank = ps[bi]
                                bk = "ps%d" % bi
                                for kc in range(8):
                                    S.pe(lambda e, bank=bank, part=part, kc=kc: e.matmul(
                                        bank[:mj, :nb + 2], lhsT=w[:, kc, part, :mj], rhs=xn[:, kc, o:o + nb + 2],
                                        start=(kc == 0), stop=(kc == 7)), [("wu", sl, part)] + xkeys, [bk])
                                dst = (G32 if part == 0 else V32)[ub]
                                dk = ("G32" if part == 0 else "V32", ub)
                                S.act(lambda e, bank=bank, dst=dst, part=part: e.activation(
                                    out=dst[:mj, :nb], in_=bank[:mj, 0:nb], func=AF.Identity, scale=self.ccol(l, 0, part, j)[:mj, :]),
                                    [bk], [dk])

                        def taps(k):
                            j, bi_, o, nb = units[k]
                            mj = 128 if j < 21 else 64
                            ub = k % 2
                            for tap in (1, 2):
                                for part in range(2):
                                    bi = part + 2 * ub
                                    bank = ps[bi]
                                    bk = "ps%d" % bi
                                    dst = (G32 if part == 0 else V32)[ub]
                                    dk = ("G32" if part == 0 else "V32", ub)
                                    S.dve(lambda e, bank=bank, dst=dst, part=part, tap=tap: e.scalar_tensor_tensor(
                                        out=dst[:mj, :nb], in0=bank[:mj, tap:tap + nb], scalar=self.ccol(l, tap, part, j)[:mj, :],
                                        in1=dst[:mj, :nb], op0=ALU.mult, op1=ALU.add), [bk, dk], [dk])

                        def back(k):
                            j, bi_, o, nb = units[k]
                            mj = 128 if j < 21 else 64
                            ub = k % 2
                            S.act(lambda e: e.activation(out=SGf[ub][:mj, :nb], in_=G32[ub][:mj, :nb], func=AF.Silu),
                                  [("G32", ub)], [("SGf", ub)])
                            S.pool(lambda e: e.tensor_tensor(out=act[:mj, j, o:o + nb], in0=SGf[ub][:mj, :nb], in1=V32[ub][:mj, :nb],
                                                             op=ALU.mult), [("SGf", ub), ("V32", ub)], [("act", j, bi_)])

                        for k in range(len(units)):
                            front(k)
                            if k > 0:
                                back(k - 1)
                            taps(k)
                        back(len(units) - 1)
                    S.barrier()
                    with ExitStack() as st:
                        wd = [self.sb(st, "wd%d" % i, [128, NJ, 128], BF16) for i in range(2)]
                        mix = self.sb(st, "mixf", [128, 8, 1040], F32)
                        sq3 = self.sb(st, "sq3", [128, 8, 512], BF16)
                        rtmp3 = self.sb(st, "rtmp3", [128, 512], F32)
                        rstd3 = self.sb(st, "rstd3", [128, 512], F32)
                        tmp3 = self.sb(st, "tmp3", [128, 512], F32)
                        for dc in range(8):
                            sl = dc % 2
                            w = wd[sl]
                            self.wload(w[:, 0:21, :],
                                       self.w_down[l, 0:2688, dc * 128:(dc + 1) * 128].rearrange("(j p) n -> p j n", p=128),
                                       sl, ("wd", sl, 0))
                            self.wload(w[0:64, 21, :], self.w_down[l, 2688:2752, dc * 128:(dc + 1) * 128], sl, ("wd", sl, 1))
                            for bi_, (o, nb) in enumerate(blocks):
                                bq = (dc * len(blocks) + bi_) % 2
                                bank = ps[bq]
                                bk = "ps%d" % bq
                                for j in range(NJ):
                                    mj = 128 if j < 21 else 64
                                    S.pe(lambda e, bank=bank, j=j, mj=mj, o=o, nb=nb, w=w: e.matmul(
                                        bank[:, :nb], lhsT=w[:mj, j, :], rhs=act[:mj, j, o:o + nb],
                                        start=(j == 0), stop=(j == NJ - 1)),
                                        [("wd", sl, 0), ("wd", sl, 1), ("act", j, bi_)], [bk])
                                S.act(lambda e, bank=bank, dc=dc, o=o, nb=nb: e.activation(
                                    out=mix[:, dc, o:o + nb], in_=bank[:, :nb], func=AF.Copy), [bk], [("mix", dc, bi_)])
                        for bi_, (o, nb) in enumerate(blocks):
                            g0 = h0 + o
                            self.rstd_from([(mix[:, dc, o:o + nb], [("mix", dc, bi_)]) for dc in range(8)], nb, sq3, rtmp3, rstd3,
                                           ps[2], "ps2")
                            for dc in range(8):
                                S.dve(lambda e, dc=dc, o=o, nb=nb: e.scalar_tensor_tensor(
                                    out=tmp3[:, :nb], in0=mix[:, dc, o:o + nb], scalar=self.gcol(l, 3, dc), in1=rstd3[:, :nb],
                                    op0=ALU.mult, op1=ALU.mult), [("mix", dc, bi_), "rstd"], ["tmp3"])
                                S.pool(lambda e, dc=dc, g0=g0, nb=nb: e.tensor_tensor(
                                    out=hT[:, dc, g0:g0 + nb], in0=hT[:, dc, g0:g0 + nb], in1=tmp3[:, :nb], op=ALU.add),
                                    ["tmp3"], [("hT", dc, hf, bi_)])
                    S.barrier()

            for hf, (h0, nh) in enumerate(halves):
                half(hf, h0, nh)

    def fox(self, s):
        S, ps, psb, hT = self.S, self.ps, self.psb, self.hT
        scale = 1.0 / 8.0
        with ExitStack() as st0:
            O = self.sb(st0, "O", [128, 8, T], BF16)
            with ExitStack() as st:
                xr = self.sb(st, "xr", [128, 8, T], BF16)
                Ctok = self.sb(st, "Ctok", [128, 17, 16], F32)
                Cpb = self.sb(st, "Cpb", [16, T], BF16)
                with ExitStack() as stn:
                    sq = self.sb(stn, "sq4", [128, 8, 512], BF16)
                    rtmp = self.sb(stn, "rtmp4", [128, 512], F32)
                    rstd = self.sb(stn, "rstd4", [128, 512], F32)
                    for gi, (g0, n) in enumerate(GG):
                        self.rstd_from([(hT[:, c, g0:g0 + n], []) for c in range(8)], n, sq, rtmp, rstd, ps[2], "ps2")
                        for c in range(8):
                            S.dve(lambda e, c=c, g0=g0, n=n: e.tensor_tensor(
                                out=xr[:, c, g0:g0 + n], in0=hT[:, c, g0:g0 + n], in1=rstd[:, :n], op=ALU.mult),
                                ["rstd"], [("xr", gi)])
                    S.barrier()
                xrk = [("xr", gi) for gi in range(5)]
                kvw = self.kv_w.rearrange("(kc p) n -> p kc n", p=128)
                wqv = self.b_w_q[0].rearrange("(kc p) n -> p kc n", p=128)
                gkv = self.colv[:, 88:96]
                with ExitStack() as stf:
                    wfg = self.sb(stf, "wfg", [128, 8, 16], BF16)
                    Cp = self.sb(stf, "Cp", [16, T], F32)
                    sp = self.sb(stf, "sp", [16, 512], F32)
                    self.wload(wfg[:, :, :], kvw[:, :, 2048:2064], 0, "wfg")
                    S.dve(lambda e: e.tensor_tensor(out=wfg[:, :, :], in0=wfg[:, :, :],
                                                    in1=gkv.unsqueeze(2).to_broadcast([128, 8, 16]), op=ALU.mult),
                          ["wfg"], ["wfg"])
                    for gi, (g0, n) in enumerate(GG):
                        for kc in range(8):
                            S.pe(lambda e, kc=kc, g0=g0, n=n: e.matmul(ps[0][:16, :n], lhsT=wfg[:, kc, :], rhs=xr[:, kc, g0:g0 + n],
                                                                        start=(kc == 0), stop=(kc == 7)), ["wfg", ("xr", gi)], ["ps0"])
                        S.act(lambda e, n=n: e.activation(out=sp[:, :n], in_=ps[0][:16, :n], func=AF.Exp, scale=-1.0,
                                                          bias=self.nfgb[:, 0:1]), ["ps0", "nfgb2"], ["sp"])
                        S.act(lambda e, n=n: e.activation(out=sp[:, :n], in_=sp[:, :n], func=AF.Ln, scale=1.0,
                                                          bias=self.onec[0:16, 0:1]), ["sp"], ["sp"])
                        init = 0.0 if gi == 0 else Cp[:, g0 - 1:g0]
                        S.dve(lambda e, g0=g0, n=n, init=init: e.tensor_tensor_scan(
                            out=Cp[:, g0:g0 + n], data0=self.onesf[:16, :n], data1=sp[:, :n], initial=init,
                            op0=ALU.mult, op1=ALU.add), ["sp", ("Cp", gi - 1)], [("Cp", gi)])
                    cpk = [("Cp", gi) for gi in range(5)]
                    S.act(lambda e: e.activation(out=Cpb[:, :], in_=Cp[:, :], func=AF.Copy), cpk, ["Cpb"])
                    for ti, (t0, nt) in enumerate(TT):
                        S.pe(lambda e, t0=t0, nt=nt: e.transpose(ps[1][:nt, 0:16], Cp[:, t0:t0 + nt], self.i16[:]), cpk, ["ps1"])
                        S.dve(lambda e, ti=ti, nt=nt: e.tensor_copy(out=Ctok[:nt, ti, :], in_=ps[1][:nt, 0:16]), ["ps1"], ["Ctok"])
                    S.barrier()
                wp = [self.sb(st, "wp%d" % i, [128, 8, 3, 128], BF16) for i in range(2)]
                KTh = [self.sb(st, "KT%d" % i, [65, T], BF16) for i in range(2)]
                QTh = [self.sb(st, "QT%d" % i, [65, T], BF16) for i in range(2)]
                Va = self.sb(st, "Va", [128, 17, 192], BF16)
                NPT = 4
                PT = [self.sb(st, "PT%d" % i, [128, 512], BF16) for i in range(NPT)]
                rinv = [self.sb(st, "rinv%d" % i, [128, 512], F32) for i in range(2)]
                S.pool(lambda e: e.memset(Va[:, :, 64:128], 1.0), [], ["Vones"])
                for hh in range(2):
                    S.pool(lambda e, hh=hh: e.memset(KTh[hh][64:65, :], 1.0), [], [("Kone", hh)])
                g0col = self.colv[:, 32:40]
                itc = 0
                for p in range(8):
                    sl = p % 2
                    w = wp[sl]
                    self.wload(w[:, :, 0, :], kvw[:, :, p * 128:(p + 1) * 128], sl, ("wp", sl, 0))
                    self.wload(w[:, :, 1, :], kvw[:, :, D + p * 128:D + (p + 1) * 128], sl, ("wp", sl, 1))
                    self.wload(w[:, :, 2, :], wqv[:, :, p * 128:(p + 1) * 128], sl, ("wp", sl, 2))
                    for m in range(3):
                        gc = gkv if m < 2 else g0col
                        S.dve(lambda e, w=w, m=m, gc=gc: e.tensor_tensor(
                            out=w[:, :, m, :], in0=w[:, :, m, :], in1=gc.unsqueeze(2).to_broadcast([128, 8, 128]), op=ALU.mult),
                            [("wp", sl, m)], [("wp", sl, m)])
                    for gi, (g0, n) in enumerate(GG):
                        for (m, dst, dk) in ((0, KTh, "KT"), (2, QTh, "QT")):
                            bank = ps[m // 2]
                            bk = "ps%d" % (m // 2)
                            for kc in range(8):
                                S.pe(lambda e, bank=bank, m=m, kc=kc, g0=g0, n=n, w=w: e.matmul(
                                    bank[:, :n], lhsT=w[:, kc, m, :], rhs=xr[:, kc, g0:g0 + n], start=(kc == 0), stop=(kc == 7)),
                                    [("wp", sl, m), ("xr", gi)], [bk])
                            for hh in range(2):
                                S.dve(lambda e, bank=bank, dst=dst, hh=hh, g0=g0, n=n: e.tensor_copy(
                                    out=dst[hh][0:64, g0:g0 + n], in_=bank[hh * 64:(hh + 1) * 64, :n]), [bk], [(dk, hh, gi)])
                        for hh in range(2):
                            h = 2 * p + hh
                            S.pe(lambda e, h=h, g0=g0, n=n: e.matmul(ps[2][0:65, :n], lhsT=self.selq[:, h, :], rhs=Cpb[:, g0:g0 + n],
                                                                      start=True, stop=True), ["Cpb", "selq"], ["ps2"])
                            S.dve(lambda e, hh=hh, g0=g0, n=n: e.tensor_copy(out=QTh[hh][64:65, g0:g0 + n], in_=ps[2][64:65, :n]),
                                  ["ps2"], [("QT", hh, gi)])
                    for kt, (t0, nt) in enumerate(TT):
                        for kc in range(8):
                            S.pe(lambda e, kc=kc, t0=t0, nt=nt, w=w: e.matmul(
                                ps[2][:nt, 0:128], lhsT=xr[:, kc, t0:t0 + nt], rhs=w[:, kc, 1, :], start=(kc == 0), stop=(kc == 7)),
                                [("wp", sl, 1)] + xrk, ["ps2"])
                        S.dve(lambda e, kt=kt, nt=nt: e.tensor_copy(
                            out=Va[:nt, kt, :].rearrange("p (a b) -> p a b", b=64)[:, 0:3:2, :],
                            in_=ps[2][:nt, 0:128].rearrange("p (a b) -> p a b", b=64)), ["ps2"], [("Va", kt)])
                    its = []
                    for hh in range(2):
                        for gi, (g0, n) in enumerate(GG):
                            tl = tiles_of_group(gi)
                            for kt in range(tl[-1] + 1):
                                its.append((hh, gi, kt))
                    LOOK = 2
                    SB = [5, 6, 2]

                    def emit_qk(ix):
                        hh, gi, kt = its[ix]
                        g0, n = GG[gi]
                        tl = tiles_of_group(gi)
                        k0, nk = TT[kt]
                        c0 = (kt - tl[0]) * 128 if (kt in tl and gi > 0) else 0
                        nq = n - c0
                        sbi = SB[(itc0 + ix) % 3]
                        sb_ = ps[sbi]
                        KT, QT = KTh[hh], QTh[hh]
                        S.pe(lambda e: e.matmul(sb_[:nk, :nq], lhsT=KT[0:65, k0:k0 + nk], rhs=QT[0:65, g0 + c0:g0 + c0 + nq],
                                                start=True, stop=True),
                             [("KT", hh, g_) for g_ in range(5)] + [("QT", hh, gi), ("Kone", hh)], ["ps%d" % sbi])

                    def emit_rest(ix, p=p):
                        hh, gi, kt = its[ix]
                        h = 2 * p + hh
                        g0, n = GG[gi]
                        tl = tiles_of_group(gi)
                        last = tl[-1]
                        k0, nk = TT[kt]
                        c0 = (kt - tl[0]) * 128 if (kt in tl and gi > 0) else 0
                        nq = n - c0
                        sbi = SB[(itc0 + ix) % 3]
                        sb_ = ps[sbi]
                        sbk = "ps%d" % sbi
                        pt = PT[(itc0 + ix) % NPT]
                        ptk = ("PT", (itc0 + ix) % NPT)
                        vlo = 0 if hh == 0 else 64
                        orow = hh * 64
                        lrow = 64 - orow
                        obi = 3 + (gi + hh) % 2
                        ob = ps[obi]
                        obk = "ps%d" % obi
                        S.act(lambda e: e.activation(out=pt[:nk, :nq], in_=sb_[:nk, :nq], func=AF.Exp, scale=scale,
                                                     bias=Ctok[:nk, kt, h:h + 1]), [sbk, "Ctok"], [ptk])
                        if kt in tl:
                            qn = min(128, nq)
                            S.pool(lambda e: e.tensor_tensor(out=pt[:nk, 0:qn], in0=pt[:nk, 0:qn], in1=self.triu[:nk, :qn],
                                                             op=ALU.mult), [ptk], [ptk])
                        S.pe(lambda e: e.matmul(ob[:, c0:c0 + nq], lhsT=Va[:nk, kt, vlo:vlo + 128], rhs=pt[:nk, 0:nq],
                                                start=(kt == 0), stop=(kt == last)), [ptk, ("Va", kt), "Vones"], [obk])
                        if kt == last:
                            rv = rinv[(gi + hh) % 2]
                            rk = ("rinv", (gi + hh) % 2)
                            S.dve(lambda e: e.reciprocal(out=rv[orow:orow + 64, :n], in_=ob[lrow:lrow + 64, :n]), [obk], [rk])
                            S.dve(lambda e: e.tensor_tensor(out=O[orow:orow + 64, p, g0:g0 + n], in0=ob[orow:orow + 64, :n],
                                                            in1=rv[orow:orow + 64, :n], op=ALU.mult), [obk, rk], [("O", gi)])

                    itc0 = itc
                    for ix in range(min(LOOK, len(its))):
                        emit_qk(ix)
                    for ix in range(len(its)):
                        if ix + LOOK < len(its):
                            emit_qk(ix + LOOK)
                        emit_rest(ix)
                    itc += len(its)
            self.S.barrier()
            with ExitStack() as st:
                self.out_proj_residual(st, self.b_w_out[0], O, 1, 1, "O")


_CACHE = {}


def _get_nc(nseq=2, stop=None):
    key = (nseq, stop)
    if key not in _CACHE:
        _CACHE[key] = Builder(nseq, stop).build()
    return _CACHE[key]


def kernel(**inputs):
    ncores = 8
    nc = _get_nc(2, None)
    shared = {k: np.ascontiguousarray(np.asarray(v, dtype=np.float32)) for k, v in inputs.items() if k != "x"}
    x = np.ascontiguousarray(np.asarray(inputs["x"], dtype=np.float32))
    in_maps = []
    for c in range(ncores):
        m = dict(shared)
        m["x"] = x[2 * c:2 * c + 2]
        in_maps.append(m)
    res = run_bass_kernel_spmd(nc, in_maps, core_ids=list(range(ncores)))
    return np.concatenate([np.asarray(r["out"]) for r in res.results], axis=0).astype(np.float32)
```

```python
import numpy as np
import concourse.bass as bass
import concourse.mybir as mybir
from concourse.bass_utils import run_bass_kernel_spmd
from contextlib import ExitStack

F32 = mybir.dt.float32
BF16 = mybir.dt.bfloat16
AF = mybir.ActivationFunctionType
ALU = mybir.AluOpType
AX = mybir.AxisListType

SAME_ENG_SYNC = True


class _Op:
    __slots__ = ("eng", "fn", "reads", "writes", "dma_sem", "ndma", "deps",
                 "needs_inc", "token", "waits", "idx", "is_bar")

    def __init__(self, eng, fn, reads, writes, dma_sem=None, ndma=0):
        self.eng = eng
        self.fn = fn
        self.reads = reads
        self.writes = writes
        self.dma_sem = dma_sem
        self.ndma = ndma
        self.deps = []
        self.needs_inc = False
        self.token = None
        self.waits = []
        self.is_bar = False


class Sched:
    CENG = ("pe", "act", "dve", "pool")
    ALLENG = ("pe", "act", "dve", "pool", "sp")

    def __init__(self, nc, stack):
        self.nc = nc
        self.stack = stack
        self.ops = []
        self.esem = {e: stack.enter_context(nc.semaphore("s_" + e)) for e in self.CENG}
        self.dma_cum = {}
        self.dma_sems = {}
        self.dma_exempt = set()
        self.last_w = {}
        self.readers = {}
        self.last_op = {e: None for e in self.ALLENG}
        self.dma_last = {}

    def dma_sem(self, name, exempt=False):
        if name not in self.dma_sems:
            self.dma_sems[name] = self.stack.enter_context(self.nc.semaphore("d_" + name))
            self.dma_cum[name] = 0
            if exempt:
                self.dma_exempt.add(name)
        return name

    def _add(self, op):
        op.idx = len(self.ops)
        deps = set()
        for k in op.reads:
            w = self.last_w.get(k)
            if w is not None:
                deps.add(w)
        for k in op.writes:
            w = self.last_w.get(k)
            if w is not None:
                deps.add(w)
            for r in self.readers.get(k, ()):
                deps.add(r)
        deps.discard(op)
        op.deps = sorted(deps, key=lambda o: o.idx)
        for k in op.reads:
            self.readers.setdefault(k, []).append(op)
        for k in op.writes:
            self.last_w[k] = op
            self.readers[k] = []
        self.ops.append(op)
        self.last_op[op.eng] = op
        if op.dma_sem is not None:
            self.dma_cum[op.dma_sem] += 16 * op.ndma
            op.token = (op.dma_sem, self.dma_cum[op.dma_sem])
            self.dma_last[op.dma_sem] = op
        return op

    def op(self, eng, fn, reads=(), writes=()):
        return self._add(_Op(eng, fn, tuple(reads), tuple(writes)))

    def pe(self, fn, reads=(), writes=()):
        return self.op("pe", fn, reads, writes)

    def act(self, fn, reads=(), writes=()):
        return self.op("act", fn, reads, writes)

    def dve(self, fn, reads=(), writes=()):
        return self.op("dve", fn, reads, writes)

    def pool(self, fn, reads=(), writes=()):
        return self.op("pool", fn, reads, writes)

    def dma(self, eng, fn, sem, reads=(), writes=(), n=1):
        return self._add(_Op(eng, fn, tuple(reads), tuple(writes), dma_sem=sem, ndma=n))

    def barrier(self):
        prev = dict(self.last_op)
        dl = {k: v for k, v in self.dma_last.items() if k not in self.dma_exempt}
        for e in self.ALLENG:
            b = _Op(e, None, (), ())
            b.is_bar = True
            b.idx = len(self.ops)
            b.deps = [o for ee, o in prev.items() if o is not None and (ee != e or (SAME_ENG_SYNC and e != 'pe'))] + list(dl.values())
            self.ops.append(b)
            self.last_op[e] = b

    def finalize(self):
        for op in self.ops:
            for d in op.deps:
                if d.dma_sem is not None or d.is_bar:
                    continue
                if d.eng == op.eng and (d.eng == "pe" or not SAME_ENG_SYNC):
                    continue
                d.needs_inc = True
        cnt = {e: 0 for e in self.CENG}
        for op in self.ops:
            if op.dma_sem is None and op.needs_inc:
                cnt[op.eng] += 1
                op.token = (op.eng, cnt[op.eng])
        known = {e: {} for e in self.ALLENG}
        for op in self.ops:
            kn = known[op.eng]
            need = {}
            for d in op.deps:
                if d.token is None:
                    continue
                if d.dma_sem is None and d.eng == op.eng and (d.eng == "pe" or not SAME_ENG_SYNC):
                    continue
                s, v = d.token
                if need.get(s, 0) < v:
                    need[s] = v
            for s, v in need.items():
                if kn.get(s, 0) >= v:
                    continue
                kn[s] = v
                op.waits.append((s, v))
        self.counts = cnt

    def _sem(self, s):
        return self.esem[s] if s in self.esem else self.dma_sems[s]

    def emit(self):
        self.finalize()
        by_eng = {e: [o for o in self.ops if o.eng == e] for e in self.ALLENG}
        with self.nc.Block() as block:
            def run(e):
                def body(eng):
                    for op in by_eng[e]:
                        for (s, v) in op.waits:
                            eng.wait_ge(self._sem(s), v)
                        if op.fn is None:
                            continue
                        if op.dma_sem is not None:
                            op.fn(eng, self.dma_sems[op.dma_sem])
                        else:
                            ins = op.fn(eng)
                            if op.needs_inc:
                                ins.then_inc(self.esem[e], 1)
                return body
            block.tensor(run("pe"))
            block.scalar(run("act"))
            block.vector(run("dve"))
            block.gpsimd(run("pool"))
            block.sync(run("sp"))


import os
CUT = int(os.environ.get('KCUT', '99'))
D = 1024
T = 2064
NMETA = 16
DFF = 2752
NJ = 22
EPS = 1e-6
TT = [(0, 16)] + [(16 + 128 * i, 128) for i in range(16)]
GG = [(0, 16)] + [(16 + 512 * j, 512) for j in range(4)]


def tiles_of_group(gi):
    return [0] if gi == 0 else list(range(1 + 4 * (gi - 1), 1 + 4 * gi))


class Builder:
    def __init__(self, nseq=2, stop=None):
        self.nseq = nseq
        self.stop = stop
        nc = self.nc = bass.Bass("TRN2", target_bir_lowering=False)
        dt = lambda name, shape: nc.dram_tensor(name, shape, F32, kind="ExternalInput").ap()
        self.x = dt("x", [nseq, 2048, D])
        self.meta = dt("meta_tokens", [NMETA, D])
        self.norm_gains = dt("norm_gains", [2, 4, D])
        self.a_w_in = dt("a_w_in", [1, D, 4 * D])
        self.a_lb = dt("a_lb_logits", [2, D])
        self.a_hn = dt("a_head_norm", [1, D])
        self.a_w_out = dt("a_w_out", [1, D, D])
        self.kv_norm = dt("kv_norm", [D])
        self.kv_w = dt("kv_w", [D, 2 * D + 16])
        self.fg_b = dt("fg_b", [16])
        self.b_w_q = dt("b_w_q", [1, D, D])
        self.b_w_out = dt("b_w_out", [1, D, D])
        self.w_up = dt("ffn_w_up", [2, D, 2 * DFF])
        self.conv = dt("ffn_conv", [2, 3, 2 * DFF])
        self.w_down = dt("ffn_w_down", [2, DFF, D])
        self.out = nc.dram_tensor("out", [nseq, 2048, D], F32, kind="ExternalOutput").ap()
        self.uid = 0

    def sb(self, st, name, shape, dtype):
        self.uid += 1
        return st.enter_context(self.nc.sbuf_tensor("%s_%d" % (name, self.uid), shape, dtype))

    def build(self):
        nc = self.nc
        with ExitStack() as st:
            S = self.S = Sched(nc, st)
            self.ps = [st.enter_context(nc.psum_tensor("ps%d" % i, [128, 512], F32)) for i in range(7)]
            self.psb = st.enter_context(nc.psum_tensor("psb", [128, 1024], BF16))
            self.hT = self.sb(st, "hT", [128, 8, T], F32)
            self.consts(st)
            S.barrier()
            for s in range(self.nseq):
                self.seq(s)
            S.barrier()
            S.emit()
        return nc

    def consts(self, st):
        S = self.S
        self.ident = self.sb(st, "ident", [128, 128], F32)
        self.identb = self.sb(st, "identb", [128, 128], BF16)
        self.onesb = self.sb(st, "onesb", [128, 128], BF16)
        self.triu = self.sb(st, "triu", [128, 128], BF16)
        self.mask2 = self.sb(st, "mask2", [128, 128], F32)
        self.maskseg = self.sb(st, "maskseg", [128, 512], F32)
        self.epsc = self.sb(st, "epsc", [128, 1], F32)
        self.colv = self.sb(st, "colv", [128, 96], F32)
        self.convT = self.sb(st, "convT", [128, 3, 128], F32)
        self.lbc = self.sb(st, "lbc", [128, 24], F32)
        self.nfgb = self.sb(st, "nfgb", [16, 1], F32)
        self.i16 = self.sb(st, "i16", [16, 16], F32)
        self.ones16 = self.sb(st, "ones16", [16, 128], F32)
        self.rinvs = self.sb(st, "rinvs", [128, 512], F32)
        self.onesf = self.sb(st, "onesf", [16, 512], F32)
        self.onec = self.sb(st, "onec", [128, 1], F32)
        self.selq = self.sb(st, "selq", [16, 16, 65], BF16)
        P = lambda fn, r=(), w=(): S.pool(fn, r, w)
        P(lambda e: e.memset(self.ident[:], 1.0), w=["ident"])
        P(lambda e: e.affine_select(out=self.ident[:], in_=self.ident[:], pattern=[[-1, 128]],
                                    compare_op=ALU.is_equal, fill=0.0, base=0, channel_multiplier=1),
          r=["ident"], w=["ident"])
        S.dve(lambda e: e.tensor_copy(out=self.identb[:], in_=self.ident[:]), ["ident"], ["identb"])
        S.dve(lambda e: e.tensor_copy(out=self.i16[:], in_=self.ident[0:16, 0:16]), ["ident"], ["i16"])
        P(lambda e: e.memset(self.onesb[:], 1.0), w=["onesb"])
        P(lambda e: e.memset(self.ones16[:], 1.0), w=["ones16"])
        P(lambda e: e.memset(self.triu[:], 1.0), w=["triu"])
        P(lambda e: e.affine_select(out=self.triu[:], in_=self.triu[:], pattern=[[1, 128]],
                                    compare_op=ALU.is_ge, fill=0.0, base=0, channel_multiplier=-1),
          r=["triu"], w=["triu"])
        P(lambda e: e.memset(self.mask2[:], 1.0), w=["mask2"])
        P(lambda e: e.affine_select(out=self.mask2[:], in_=self.mask2[:], pattern=[[1, 128]],
                                    compare_op=ALU.is_ge, fill=0.0, base=0, channel_multiplier=-1),
          r=["mask2"], w=["mask2"])
        P(lambda e: e.memset(self.mask2[0:64, 64:128], 0.0), r=["mask2"], w=["mask2"])
        P(lambda e: e.memset(self.maskseg[:], 1.0), w=["maskseg"])
        P(lambda e: e.memset(self.maskseg[:].rearrange("p (c k) -> p c k", k=64)[:, :, 0:1], 0.0),
          r=["maskseg"], w=["maskseg"])
        P(lambda e: e.memset(self.epsc[:], EPS), w=["epsc"])
        P(lambda e: e.memset(self.onesf[:], 1.0), w=["onesf"])
        P(lambda e: e.memset(self.onec[:], 1.0), w=["onec"])
        P(lambda e: e.memset(self.selq[:], 0.0), w=["selq0"])
        S.dve(lambda e: e.tensor_scalar(out=self.selq[:, :, 64], in0=self.i16[:, :], scalar1=-8.0, scalar2=None, op0=ALU.mult),
              ["selq0", "i16"], ["selq"])
        rowsA = self.sb(st, "rowsA", [96, 128], F32)
        rowsC = self.sb(st, "rowsC", [128, 3, 128], F32)
        P(lambda e: e.memset(rowsC[:], 0.0), w=["rowsC"])
        cs = S.dma_sem("const")
        nd = [0]

        def ld(dst, src, rk):
            S.dma("sp", lambda e, s, dst=dst, src=src: e.dma_start(out=dst, in_=src).then_inc(s, 16), cs,
                  reads=[rk], writes=[("rowsd", nd[0])])
            nd[0] += 1
        ld(rowsA[0:64, :], self.norm_gains.rearrange("l j (c p) -> (l j c) p", p=128), "rowsA")
        ld(rowsA[64:80, :], self.a_lb.rearrange("l (c p) -> (l c) p", p=128), "rowsA")
        ld(rowsA[80:88, :], self.a_hn.rearrange("l (c p) -> (l c) p", p=128), "rowsA")
        ld(rowsA[88:96, :], self.kv_norm.rearrange("(c p) -> c p", p=128), "rowsA")
        for l in range(2):
            for tap in range(3):
                for part in range(2):
                    r0 = ((l * 3 + tap) * 2 + part) * 22
                    src = self.conv[l, tap, part * DFF: part * DFF + 2688].rearrange("(j k) -> j k", k=128)
                    done = 0
                    while done < 21:
                        ti, ri = divmod(r0 + done, 128)
                        cnt = min(21 - done, 128 - ri)
                        ld(rowsC[ri:ri + cnt, ti, :], src[done:done + cnt, :], "rowsC")
                        done += cnt
                    ti, ri = divmod(r0 + 21, 128)
                    ld(rowsC[ri:ri + 1, ti, 0:64],
                       self.conv[l, tap, part * DFF + 2688: part * DFF + 2752].rearrange("(a k) -> a k", a=1), "rowsC")
        ld(self.nfgb[:, :], self.fg_b.rearrange("(h a) -> h a", a=1), "nfgb")
        allrows = [("rowsd", i) for i in range(nd[0])]
        ps = self.ps
        S.pe(lambda e: e.transpose(ps[0][:, 0:96], rowsA[:, :], self.ident[0:96, 0:96]), allrows + ["ident"], ["ps0"])
        S.dve(lambda e: e.tensor_copy(out=self.colv[:], in_=ps[0][:, 0:96]), ["ps0"], ["colv"])
        for ti in range(3):
            S.pe(lambda e, ti=ti: e.transpose(ps[1][:, ti * 128:(ti + 1) * 128], rowsC[:, ti, :], self.ident[:]),
                 allrows + ["ident", "rowsC"], ["ps1"])
        S.dve(lambda e: e.tensor_copy(out=self.convT[:], in_=ps[1][:, 0:384].rearrange("p (a b) -> p a b", b=128)),
              ["ps1"], ["convT"])
        dl = self.sb(st, "dl", [128, 8], F32)
        S.dve(lambda e: e.tensor_tensor(out=dl[:], in0=self.colv[:, 64:72], in1=self.colv[:, 72:80], op=ALU.subtract),
              ["colv"], ["dl"])
        S.act(lambda e: e.activation(out=self.lbc[:, 0:8], in_=dl[:], func=AF.Sigmoid), ["dl"], ["lbc0"])
        S.act(lambda e: e.activation(out=self.lbc[:, 8:16], in_=dl[:], func=AF.Sigmoid, scale=-1.0), ["dl"], ["lbc1"])
        S.dve(lambda e: e.tensor_scalar(out=self.lbc[:, 16:24], in0=self.lbc[:, 8:16], scalar1=-1.0, scalar2=None,
                                        op0=ALU.mult), ["lbc1"], ["lbc2"])
        S.dve(lambda e: e.tensor_scalar(out=self.nfgb[:], in0=self.nfgb[:], scalar1=-1.0, scalar2=None, op0=ALU.mult),
              allrows, ["nfgb2"])

    def gcol(self, l, j, c):
        k = (l * 4 + j) * 8 + c
        return self.colv[:, k:k + 1]

    def ccol(self, l, tap, part, j):
        r = ((l * 3 + tap) * 2 + part) * 22 + j
        ti, ri = divmod(r, 128)
        return self.convT[:, ti, ri:ri + 1]

    def wload(self, dst, src, slot, key, reads=()):
        S = self.S
        sem = S.dma_sem("w_" + "_".join(str(k) for k in (key if isinstance(key, tuple) else (key,))), exempt=True)
        S.dma("pool", lambda e, s: e.dma_start(out=dst, in_=src).then_inc(s, 16), sem,
              reads=list(reads), writes=[key])

    def rstd_from(self, srcs, n, sq, rtmp, rstd, pst, pkey, dscale=1.0 / D):
        S = self.S
        nsrc = len(srcs)
        for c, (ap, rk) in enumerate(srcs):
            S.act(lambda e, ap=ap, c=c: e.activation(out=sq[:, c, :n], in_=ap, func=AF.Square), rk, [("sq", c)])
            S.pe(lambda e, c=c: e.matmul(pst[:, :n], lhsT=self.onesb[:], rhs=sq[:, c, :n], start=(c == 0),
                                         stop=(c == nsrc - 1)), [("sq", c)], [pkey])
        S.act(lambda e: e.activation(out=rtmp[:, :n], in_=pst[:, :n], func=AF.Sqrt, scale=dscale, bias=self.epsc[:, 0:1]),
              [pkey], ["rtmp"])
        S.dve(lambda e: e.reciprocal(out=rstd[:, :n], in_=rtmp[:, :n]), ["rtmp"], ["rstd"])

    def seq(self, s):
        S = self.S
        self.load_x(s)
        S.barrier()
        if self.stop != "load":
            self.hgrn2(s)
            S.barrier()
            if self.stop not in ("mix0", "mix0a", "mix0b", "mix0c"):
                self.ffn(s, 0)
                S.barrier()
                if self.stop != "ffn0":
                    self.fox(s)
                    S.barrier()
                    if self.stop != "mix1":
                        self.ffn(s, 1)
                        S.barrier()
        self.store(s)
        S.barrier()

    def load_x(self, s):
        S, ps, hT = self.S, self.ps, self.hT
        with ExitStack() as st:
            xin = [self.sb(st, "xin%d" % i, [128, D], F32) for i in range(2)]
            xs = [S.dma_sem("xin%d" % i) for i in range(2)]
            for ti, (t0, n) in enumerate(TT):
                sl = ti % 2
                src = self.meta if ti == 0 else self.x[s, t0 - 16:t0 - 16 + 128, :]
                S.dma("sp", lambda e, sm, sl=sl, src=src, n=n: e.dma_start(out=xin[sl][:n, :], in_=src).then_inc(sm, 16),
                      xs[sl], writes=[("xin", sl)])
                for half in range(2):
                    bank = ps[half + 2 * sl]
                    bk = "ps%d" % (half + 2 * sl)
                    for j in range(4):
                        c = half * 4 + j
                        S.pe(lambda e, bank=bank, j=j, c=c, n=n, sl=sl: e.transpose(
                            bank[:, j * 128:j * 128 + n], xin[sl][:n, c * 128:(c + 1) * 128], self.ident[:n, :n]),
                            [("xin", sl)], [bk])
                    fn = lambda e, bank=bank, half=half, t0=t0, n=n: e.tensor_copy(
                        out=hT[:, half * 4:(half + 1) * 4, t0:t0 + n],
                        in_=bank[:, :].rearrange("p (j k) -> p j k", k=128)[:, :, 0:n])
                    if half == 0:
                        S.dve(fn, [bk], [("hT", ti, half)])
                    else:
                        S.act(lambda e, bank=bank, half=half, t0=t0, n=n: e.activation(
                            out=hT[:, half * 4:(half + 1) * 4, t0:t0 + n],
                            in_=bank[:, :].rearrange("p (j k) -> p j k", k=128)[:, :, 0:n], func=AF.Copy),
                            [bk], [("hT", ti, half)])

    def store(self, s):
        S, ps, hT = self.S, self.ps, self.hT
        with ExitStack() as st:
            xo = [self.sb(st, "xo%d" % i, [128, D], F32) for i in range(2)]
            os_ = [S.dma_sem("xo%d" % i) for i in range(2)]
            for ti, (t0, n) in enumerate(TT):
                if ti == 0:
                    continue
                sl = ti % 2
                for half in range(2):
                    bank = ps[half + 2 * sl]
                    bk = "ps%d" % (half + 2 * sl)
                    for j in range(4):
                        c = half * 4 + j
                        S.pe(lambda e, bank=bank, j=j, c=c, t0=t0: e.transpose(
                            bank[:, j * 128:(j + 1) * 128], hT[:, c, t0:t0 + 128], self.ident[:]), [], [bk])
                    if half == 0:
                        S.dve(lambda e, bank=bank, sl=sl: e.tensor_copy(out=xo[sl][:, 0:512], in_=bank[:, :]),
                              [bk], [("xo", sl)])
                    else:
                        S.act(lambda e, bank=bank, sl=sl: e.activation(out=xo[sl][:, 512:1024], in_=bank[:, :], func=AF.Copy),
                              [bk], [("xo", sl)])
                S.dma("sp", lambda e, sm, sl=sl, t0=t0: e.dma_start(out=self.out[s, t0 - 16:t0 - 16 + 128, :],
                                                                     in_=xo[sl][:, :]).then_inc(sm, 16),
                      os_[sl], reads=[("xo", sl)], writes=[("xo", sl)])

    def out_proj_residual(self, st, wsrc, src_act, l, jn, tag):
        S, ps, hT = self.S, self.ps, self.hT
        wo = self.sb(st, "wo", [128, 8, D], BF16)
        mix32 = self.sb(st, "mix32", [128, 8, 512], F32)
        sq = self.sb(st, "sqo", [128, 8, 512], BF16)
        rtmp = self.sb(st, "rtmpo", [128, 512], F32)
        rstd = self.sb(st, "rstdo", [128, 512], F32)
        tmp = self.sb(st, "tmpo", [128, 512], F32)
        wv = wsrc.rearrange("(kc p) n -> p kc n", p=128)
        for kc in range(8):
            self.wload(wo[:, kc, :], wv[:, kc, :], kc % 2, ("wo", kc))
        for gi, (g0, n) in enumerate(GG):
            for dc in range(8):
                bank = ps[dc % 2]
                bk = "ps%d" % (dc % 2)
                for kc in range(8):
                    S.pe(lambda e, bank=bank, dc=dc, kc=kc, g0=g0, n=n: e.matmul(
                        bank[:, :n], lhsT=wo[:, kc, dc * 128:(dc + 1) * 128], rhs=src_act[:, kc, g0:g0 + n],
                        start=(kc == 0), stop=(kc == 7)), [("wo", kc), (tag, gi)], [bk])
                S.act(lambda e, bank=bank, dc=dc, n=n: e.activation(out=mix32[:, dc, :n], in_=bank[:, :n], func=AF.Copy),
                      [bk], [("mix32", dc)])
            self.rstd_from([(mix32[:, dc, :n], [("mix32", dc)]) for dc in range(8)], n, sq, rtmp, rstd, ps[2], "ps2")
            for dc in range(8):
                S.dve(lambda e, dc=dc, n=n: e.scalar_tensor_tensor(
                    out=tmp[:, :n], in0=mix32[:, dc, :n], scalar=self.gcol(l, jn, dc), in1=rstd[:, :n],
                    op0=ALU.mult, op1=ALU.mult), [("mix32", dc), "rstd"], ["tmpo"])
                S.pool(lambda e, dc=dc, g0=g0, n=n: e.tensor_tensor(
                    out=hT[:, dc, g0:g0 + n], in0=hT[:, dc, g0:g0 + n], in1=tmp[:, :n], op=ALU.add),
                    ["tmpo"], [("hT", dc, gi)])

    def hgrn2(self, s):
        S, ps, psb, hT = self.S, self.ps, self.psb, self.hT
        with ExitStack() as st0:
            og = self.sb(st0, "og", [128, 8, T], BF16)
            with ExitStack() as st:
                xn = self.sb(st, "xn", [128, 8, T], BF16)
                sq = self.sb(st, "sq", [128, 8, 512], BF16)
                rtmp = self.sb(st, "rtmp", [128, 512], F32)
                rstd = self.sb(st, "rstd", [128, 512], F32)
                for gi, (g0, n) in enumerate(GG):
                    self.rstd_from([(hT[:, c, g0:g0 + n], []) for c in range(8)], n, sq, rtmp, rstd, ps[2], "ps2")
                    for c in range(8):
                        S.dve(lambda e, c=c, g0=g0, n=n: e.scalar_tensor_tensor(
                            out=xn[:, c, g0:g0 + n], in0=hT[:, c, g0:g0 + n], scalar=self.gcol(0, 0, c), in1=rstd[:, :n],
                            op0=ALU.mult, op1=ALU.mult), ["rstd"], [("xn", gi)])
                wh = [self.sb(st, "wh%d" % i, [128, 8, 4, 128], BF16) for i in range(2)]
                A = self.sb(st, "A", [128, 512], F32)
                B = self.sb(st, "B", [128, 512], F32)
                C = self.sb(st, "C", [128, 512], F32)
                Dn = self.sb(st, "Dn", [128, 512], F32)
                SG = self.sb(st, "SG", [128, 512], F32)
                O32 = self.sb(st, "O32", [128, 512], F32)
                qin = self.sb(st, "qin", [128, 512], BF16)
                kin = self.sb(st, "kin", [128, 512], BF16)
                kout = self.sb(st, "kout", [128, 512], BF16)
                sqh = self.sb(st, "sqh", [128, 1, 512], BF16)
                vtok = self.sb(st, "vtok", [128, 4, 128], BF16)
                ktok = self.sb(st, "ktok", [128, 4, 128], BF16)
                attT = self.sb(st, "attT", [128, 4, 128], BF16)
                S32 = self.sb(st, "S32", [128, 9, 128], F32)
                Sb = self.sb(st, "Sb", [128, 8, 128], BF16)
                win = self.a_w_in[0].rearrange("(kc p) n -> p kc n", p=128)
                nheads = {"mix0a": 0, "mix0b": 1, "mix0c": 1}.get(self.stop, 8)
                def load_head(hd):
                    sl = hd % 2
                    for j in range(4):
                        self.wload(wh[sl][:, :, j, :], win[:, :, j * D + hd * 128: j * D + (hd + 1) * 128], sl, ("wh", sl, j))

                if nheads > 0:
                    load_head(0)
                for hd in range(nheads):
                    sl = hd % 2
                    w = wh[sl]
                    if hd + 1 < nheads:
                        load_head(hd + 1)
                    wk = [("wh", sl, j) for j in range(4)]
                    S.dve(lambda e: e.memset(S32[:, 0, :], 0.0), [], [("S32", 0)])
                    for gi, (g0, n) in enumerate(GG):
                        if self.stop == "mix0b" and gi > 0:
                            break
                        tl_list = tiles_of_group(gi)
                        nch = 1 if gi == 0 else 8
                        for (j, bi) in ((0, 0), (1, 1), (3, 2)):
                            for kc in range(8):
                                S.pe(lambda e, j=j, bi=bi, kc=kc, g0=g0, n=n, w=w: e.matmul(
                                    ps[bi][:, :n], lhsT=w[:, kc, j, :], rhs=xn[:, kc, g0:g0 + n],
                                    start=(kc == 0), stop=(kc == 7)), [wk[j], ("xn", gi)], ["ps%d" % bi])
                        for li, ti in enumerate(tl_list):
                            t0, nt = TT[ti]
                            for kc in range(8):
                                S.pe(lambda e, li=li, kc=kc, t0=t0, nt=nt, w=w: e.matmul(
                                    ps[3][:nt, li * 128:(li + 1) * 128], lhsT=xn[:, kc, t0:t0 + nt], rhs=w[:, kc, 2, :],
                                    start=(kc == 0), stop=(kc == 7)), [wk[2], ("xn", gi)], ["ps3"])
                        if gi == 0:
                            S.act(lambda e: e.activation(out=vtok[:16, 0, :], in_=ps[3][:16, 0:128], func=AF.Copy),
                                  ["ps3"], ["vtok"])
                        else:
                            S.act(lambda e: e.activation(out=vtok[:, :, :], in_=ps[3][:, :].rearrange("p (a b) -> p a b", b=128),
                                                         func=AF.Copy), ["ps3"], ["vtok"])
                        if self.stop == 'mix0c' and gi > 0 and CUT <= 1:
                            continue
                        S.act(lambda e, n=n: e.activation(out=A[:, :n], in_=ps[1][:, :n], func=AF.Sigmoid), ["ps1"], ["A"])
                        S.act(lambda e, n=n: e.activation(out=SG[:, :n], in_=ps[2][:, :n], func=AF.Silu), ["ps2"], ["SG"])
                        S.act(lambda e, n=n, hd=hd: e.activation(out=B[:, :n], in_=A[:, :n], func=AF.Ln,
                                                                 scale=self.lbc[:, 8 + hd:9 + hd], bias=self.lbc[:, hd:hd + 1]),
                              ["A"], ["B"])
                        S.dve(lambda e, n=n, hd=hd: e.tensor_scalar(out=C[:, :n], in0=A[:, :n],
                                                                    scalar1=self.lbc[:, 16 + hd:17 + hd],
                                                                    scalar2=self.lbc[:, 8 + hd:9 + hd],
                                                                    op0=ALU.mult, op1=ALU.add), ["A"], ["C"])
                        S.dve(lambda e, n=n: e.tensor_tensor_scan(out=A[:, :n], data0=self.maskseg[:, :n], data1=B[:, :n],
                                                                  initial=0.0, op0=ALU.mult, op1=ALU.add), ["B", "A"], ["A"])
                        S.act(lambda e, n=n: e.activation(out=B[:, :n], in_=A[:, :n], func=AF.Exp), ["A"], ["B"])
                        S.act(lambda e, n=n: e.activation(out=Dn[:, :n], in_=A[:, :n], func=AF.Exp, scale=-1.0), ["A"], ["Dn"])
                        S.dve(lambda e, n=n: e.tensor_tensor(out=qin[:, :n], in0=ps[0][:, :n], in1=B[:, :n], op=ALU.mult),
                              ["ps0", "B"], ["qin"])
                        S.dve(lambda e, n=n: e.tensor_tensor(out=C[:, :n], in0=C[:, :n], in1=Dn[:, :n], op=ALU.mult),
                              ["C", "Dn"], ["C"])
                        S.act(lambda e, n=n: e.activation(out=kin[:, :n], in_=C[:, :n], func=AF.Copy), ["C"], ["kin"])
                        if gi == 0:
                            S.dve(lambda e: e.tensor_scalar(out=kout[:, :16], in0=C[:, :16], scalar1=B[:, 15:16], scalar2=None,
                                                            op0=ALU.mult), ["C", "B"], ["kout"])
                        else:
                            S.dve(lambda e: e.tensor_tensor(
                                out=kout[:, :].rearrange("p (c k) -> p c k", k=64),
                                in0=C[:, :].rearrange("p (c k) -> p c k", k=64),
                                in1=B[:, :].rearrange("p (c k) -> p c k", k=64)[:, :, 63:64].to_broadcast([128, 8, 64]),
                                op=ALU.mult), ["C", "B"], ["kout"])
                        if self.stop == 'mix0c' and gi > 0 and CUT <= 2:
                            continue
                        for li, ti in enumerate(tl_list):
                            t0, nt = TT[ti]
                            S.pe(lambda e, li=li, nt=nt: e.transpose(psb[:nt, li * 128:(li + 1) * 128],
                                                                      kout[:, li * 128:li * 128 + nt], self.identb[:]),
                                 ["kout"], ["psb"])
                        if gi == 0:
                            S.act(lambda e: e.activation(out=ktok[:16, 0, :], in_=psb[:16, 0:128], func=AF.Copy), ["psb"], ["ktok"])
                        else:
                            S.act(lambda e: e.activation(out=ktok[:, :, :], in_=psb[:, 0:512].rearrange("p (a b) -> p a b", b=128),
                                                         func=AF.Copy), ["psb"], ["ktok"])
                        if self.stop == 'mix0c' and gi > 0 and CUT <= 3:
                            continue
                        for cl in range(nch):
                            li, r0 = cl // 2, (cl % 2) * 64
                            nr = 16 if gi == 0 else 64
                            bi = 1 + cl % 2
                            S.pe(lambda e, cl=cl, li=li, r0=r0, nr=nr, bi=bi: e.matmul(
                                ps[bi][:, (cl // 2) * 128:(cl // 2 + 1) * 128], lhsT=ktok[r0:r0 + nr, li, :],
                                rhs=vtok[r0:r0 + nr, li, :], start=True, stop=True),
                                ["ktok", "vtok", "A", "SG"], ["ps%d" % bi])
                        for cl in range(nch):
                            bi = 1 + cl % 2
                            dcol = B[:, 15:16] if gi == 0 else B[:, cl * 64 + 63:cl * 64 + 64]
                            S.dve(lambda e, cl=cl, bi=bi, dcol=dcol: e.scalar_tensor_tensor(
                                out=S32[:, cl + 1, :], in0=S32[:, cl, :], scalar=dcol,
                                in1=ps[bi][:, (cl // 2) * 128:(cl // 2 + 1) * 128], op0=ALU.mult, op1=ALU.add),
                                [("S32", cl), "B", "ps%d" % bi], [("S32", cl + 1)])
                        S.act(lambda e, nch=nch: e.activation(out=Sb[:, 0:nch, :], in_=S32[:, 0:nch, :], func=AF.Copy),
                              [("S32", c) for c in range(nch)], ["Sb"])
                        if self.stop == 'mix0c' and gi > 0 and CUT <= 4:
                            continue
                        for li, ti in enumerate(tl_list):
                            t0, nt = TT[ti]
                            S.pe(lambda e, li=li, nt=nt: e.matmul(ps[0][:nt, li * 128:li * 128 + nt],
                                                                   lhsT=kin[:, li * 128:li * 128 + nt],
                                                                   rhs=qin[:, li * 128:li * 128 + nt], start=True, stop=True),
                                 ["kin", "qin"], ["ps0"])
                        if gi == 0:
                            S.dve(lambda e: e.tensor_tensor(out=attT[:16, 0, :16], in0=ps[0][:16, 0:16], in1=self.mask2[:16, :16],
                                                            op=ALU.mult), ["ps0"], ["attT"])
                        else:
                            S.dve(lambda e: e.tensor_tensor(
                                out=attT[:, :, :], in0=ps[0][:, :].rearrange("p (a b) -> p a b", b=128),
                                in1=self.mask2[:, :].unsqueeze(1).to_broadcast([128, 4, 128]), op=ALU.mult), ["ps0"], ["attT"])
                        if self.stop == 'mix0c' and gi > 0 and CUT <= 5:
                            continue
                        for li, ti in enumerate(tl_list):
                            t0, nt = TT[ti]
                            S.pe(lambda e, li=li, nt=nt, gi=gi: e.matmul(
                                ps[4][:, li * 128:li * 128 + nt], lhsT=vtok[:nt, li, :], rhs=attT[:nt, li, :nt],
                                start=True, stop=(gi == 0)), ["vtok", "attT"], ["ps4"])
                            if gi > 0:
                                for hh in range(2):
                                    cl = 2 * li + hh
                                    S.pe(lambda e, li=li, hh=hh, cl=cl: e.matmul(
                                        ps[4][:, cl * 64:(cl + 1) * 64], lhsT=Sb[:, cl, :], rhs=qin[:, cl * 64:(cl + 1) * 64],
                                        start=False, stop=(hh == 1)), ["Sb", "qin"], ["ps4"])
                        S.dve(lambda e, nch=nch: e.tensor_copy(out=S32[:, 0, :], in_=S32[:, nch, :]),
                              [("S32", nch), "Sb"], [("S32", 0)])
                        if self.stop == 'mix0c' and gi > 0 and CUT <= 6:
                            continue
                        self.rstd_from([(ps[4][:, :n], ["ps4"])], n, sqh, rtmp, rstd, ps[5], "ps5", dscale=1.0 / 128)
                        S.dve(lambda e, n=n, hd=hd: e.scalar_tensor_tensor(
                            out=O32[:, :n], in0=ps[4][:, :n], scalar=self.colv[:, 80 + hd:81 + hd], in1=rstd[:, :n],
                            op0=ALU.mult, op1=ALU.mult), ["ps4", "rstd"], ["O32"])
                        S.pool(lambda e, n=n, hd=hd, g0=g0: e.tensor_tensor(out=og[:, hd, g0:g0 + n], in0=O32[:, :n], in1=SG[:, :n],
                                                                              op=ALU.mult), ["O32", "SG"], [("og", gi)])
            self.S.barrier()
            if self.stop in ("mix0a", "mix0b", "mix0c"):
                return
            with ExitStack() as st:
                self.out_proj_residual(st, self.a_w_out[0], og, 0, 1, "og")

    def ffn(self, s, l):
        S, ps, hT = self.S, self.ps, self.hT
        halves = [(0, 1040), (1040, 1024)]
        with ExitStack() as st0:
            halo = self.sb(st0, "halo", [128, 8, 2], BF16)
            S.dve(lambda e: e.memset(halo[:], 0.0), [], ["halo"])
            def half(hf, h0, nh):
                with ExitStack() as st1:
                    act = self.sb(st1, "act", [128, NJ, 1040], BF16)
                    blocks = []
                    o = 0
                    while o < nh:
                        nb = min(510, nh - o)
                        blocks.append((o, nb))
                        o += nb
                    with ExitStack() as st:
                        xn = self.sb(st, "xn2", [128, 8, 1042], BF16)
                        sq = self.sb(st, "sq2", [128, 8, 512], BF16)
                        rtmp = self.sb(st, "rtmp2", [128, 512], F32)
                        rstd = self.sb(st, "rstd2", [128, 512], F32)
                        S.dve(lambda e: e.tensor_copy(out=xn[:, :, 0:2], in_=halo[:]), ["halo"], [("xn2", -1)])
                        subs = []
                        o = 0
                        while o < nh:
                            nn = min(512, nh - o)
                            subs.append((o, nn))
                            o += nn
                        for si, (o, nn) in enumerate(subs):
                            g0 = h0 + o
                            self.rstd_from([(hT[:, c, g0:g0 + nn], []) for c in range(8)], nn, sq, rtmp, rstd, ps[2], "ps2")
                            for c in range(8):
                                S.dve(lambda e, c=c, g0=g0, nn=nn, o=o: e.scalar_tensor_tensor(
                                    out=xn[:, c, 2 + o:2 + o + nn], in0=hT[:, c, g0:g0 + nn], scalar=self.gcol(l, 2, c),
                                    in1=rstd[:, :nn], op0=ALU.mult, op1=ALU.mult), ["rstd"], [("xn2", si)])
                        xkeys = [("xn2", -1)] + [("xn2", si) for si in range(len(subs))]
                        S.dve(lambda e, nh=nh: e.tensor_copy(out=halo[:], in_=xn[:, :, nh:nh + 2]), xkeys, ["halo"])
                        wu = [self.sb(st, "wu%d" % i, [128, 8, 2, 128], BF16) for i in range(2)]
                        G32 = [self.sb(st, "G32_%d" % i, [128, 512], F32) for i in range(2)]
                        V32 = [self.sb(st, "V32_%d" % i, [128, 512], F32) for i in range(2)]
                        SGf = [self.sb(st, "SGf_%d" % i, [128, 512], F32) for i in range(2)]
                        wup = self.w_up[l].rearrange("(kc p) n -> p kc n", p=128)
                        units = [(j, bi_, o, nb) for j in range(NJ) for bi_, (o, nb) in enumerate(blocks)]

                        def load_pair(j):
                            mj = 128 if j < 21 else 64
                            sl = j % 2
                            for part in range(2):
                                self.wload(wu[sl][:, :, part, :mj], wup[:, :, part * DFF + j * 128: part * DFF + j * 128 + mj],
                                           sl, ("wu", sl, part))

                        load_pair(0)

                        def front(k):
                            j, bi_, o, nb = units[k]
                            mj = 128 if j < 21 else 64
                            sl = j % 2
                            w = wu[sl]
                            ub = k % 2
                            if bi_ == 0 and j + 1 < NJ:
                                load_pair(j + 1)
                            for part in range(2):
                                bi = part + 2 * ub
                                bank = ps[bi]
                                bk = "ps%d" % bi
                                for kc in range(8):
                                    S.pe(lambda e, bank=bank, part=part, kc=kc: e.matmul(
                                        bank[:mj, :nb + 2], lhsT=w[:, kc, part, :mj], rhs=xn[:, kc, o:o + nb + 2],
                                        start=(kc == 0), stop=(kc == 7)), [("wu", sl, part)] + xkeys, [bk])
                                dst = (G32 if part == 0 else V32)[ub]
                                dk = ("G32" if part == 0 else "V32", ub)
                                S.act(lambda e, bank=bank, dst=dst, part=part: e.activation(
                                    out=dst[:mj, :nb], in_=bank[:mj, 0:nb], func=AF.Identity, scale=self.ccol(l, 0, part, j)[:mj, :]),
                                    [bk], [dk])

                        def taps(k):
                            j, bi_, o, nb = units[k]
                            mj = 128 if j < 21 else 64
                            ub = k % 2
                            for tap in (1, 2):
                                for part in range(2):
                                    bi = part + 2 * ub
                                    bank = ps[bi]
                                    bk = "ps%d" % bi
                                    dst = (G32 if part == 0 else V32)[ub]
                                    dk = ("G32" if part == 0 else "V32", ub)
                                    S.dve(lambda e, bank=bank, dst=dst, part=part, tap=tap: e.scalar_tensor_tensor(
                                        out=dst[:mj, :nb], in0=bank[:mj, tap:tap + nb], scalar=self.ccol(l, tap, part, j)[:mj, :],
                                        in1=dst[:mj, :nb], op0=ALU.mult, op1=ALU.add), [bk, dk], [dk])

                        def back(k):
                            j, bi_, o, nb = units[k]
                            mj = 128 if j < 21 else 64
                            ub = k % 2
                            S.act(lambda e: e.activation(out=SGf[ub][:mj, :nb], in_=G32[ub][:mj, :nb], func=AF.Silu),
                                  [("G32", ub)], [("SGf", ub)])
                            S.dve(lambda e: e.tensor_tensor(out=act[:mj, j, o:o + nb], in0=SGf[ub][:mj, :nb], in1=V32[ub][:mj, :nb],
                                                            op=ALU.mult), [("SGf", ub), ("V32", ub)], [("act", j, bi_)])

                        for k in range(len(units)):
                            front(k)
                            if k > 0:
                                back(k - 1)
                            taps(k)
                        back(len(units) - 1)
                    S.barrier()
                    with ExitStack() as st:
                        wd = [self.sb(st, "wd%d" % i, [128, NJ, 128], BF16) for i in range(2)]
                        mix = self.sb(st, "mixf", [128, 8, 1040], F32)
                        sq3 = self.sb(st, "sq3", [128, 8, 512], BF16)
                        rtmp3 = self.sb(st, "rtmp3", [128, 512], F32)
                        rstd3 = self.sb(st, "rstd3", [128, 512], F32)
                        tmp3 = self.sb(st, "tmp3", [128, 512], F32)
                        def load_dc(dc):
                            sl = dc % 2
                            self.wload(wd[sl][:, 0:21, :],
                                       self.w_down[l, 0:2688, dc * 128:(dc + 1) * 128].rearrange("(j p) n -> p j n", p=128),
                                       sl, ("wd", sl, 0))
                            self.wload(wd[sl][0:64, 21, :], self.w_down[l, 2688:2752, dc * 128:(dc + 1) * 128], sl, ("wd", sl, 1))

                        load_dc(0)
                        for dc in range(8):
                            sl = dc % 2
                            w = wd[sl]
                            if dc + 1 < 8:
                                load_dc(dc + 1)
                            for bi_, (o, nb) in enumerate(blocks):
                                bq = (dc * len(blocks) + bi_) % 2
                                bank = ps[bq]
                                bk = "ps%d" % bq
                                for j in range(NJ):
                                    mj = 128 if j < 21 else 64
                                    S.pe(lambda e, bank=bank, j=j, mj=mj, o=o, nb=nb, w=w: e.matmul(
                                        bank[:, :nb], lhsT=w[:mj, j, :], rhs=act[:mj, j, o:o + nb],
                                        start=(j == 0), stop=(j == NJ - 1)),
                                        [("wd", sl, 0), ("wd", sl, 1), ("act", j, bi_)], [bk])
                                S.act(lambda e, bank=bank, dc=dc, o=o, nb=nb: e.activation(
                                    out=mix[:, dc, o:o + nb], in_=bank[:, :nb], func=AF.Copy), [bk], [("mix", dc, bi_)])
                        for bi_, (o, nb) in enumerate(blocks):
                            g0 = h0 + o
                            self.rstd_from([(mix[:, dc, o:o + nb], [("mix", dc, bi_)]) for dc in range(8)], nb, sq3, rtmp3, rstd3,
                                           ps[2], "ps2")
                            for dc in range(8):
                                S.dve(lambda e, dc=dc, o=o, nb=nb: e.scalar_tensor_tensor(
                                    out=tmp3[:, :nb], in0=mix[:, dc, o:o + nb], scalar=self.gcol(l, 3, dc), in1=rstd3[:, :nb],
                                    op0=ALU.mult, op1=ALU.mult), [("mix", dc, bi_), "rstd"], ["tmp3"])
                                S.pool(lambda e, dc=dc, g0=g0, nb=nb: e.tensor_tensor(
                                    out=hT[:, dc, g0:g0 + nb], in0=hT[:, dc, g0:g0 + nb], in1=tmp3[:, :nb], op=ALU.add),
                                    ["tmp3"], [("hT", dc, hf, bi_)])
                    S.barrier()

            for hf, (h0, nh) in enumerate(halves):
                half(hf, h0, nh)

    def fox(self, s):
        S, ps, psb, hT = self.S, self.ps, self.psb, self.hT
        scale = 1.0 / 8.0
        with ExitStack() as st0:
            O = self.sb(st0, "O", [128, 8, T], BF16)
            with ExitStack() as st:
                xr = self.sb(st, "xr", [128, 8, T], BF16)
                Ctok = self.sb(st, "Ctok", [128, 17, 16], F32)
                Cpb = self.sb(st, "Cpb", [16, T], BF16)
                with ExitStack() as stn:
                    sq = self.sb(stn, "sq4", [128, 8, 512], BF16)
                    rtmp = self.sb(stn, "rtmp4", [128, 512], F32)
                    rstd = self.sb(stn, "rstd4", [128, 512], F32)
                    for gi, (g0, n) in enumerate(GG):
                        self.rstd_from([(hT[:, c, g0:g0 + n], []) for c in range(8)], n, sq, rtmp, rstd, ps[2], "ps2")
                        for c in range(8):
                            S.dve(lambda e, c=c, g0=g0, n=n: e.tensor_tensor(
                                out=xr[:, c, g0:g0 + n], in0=hT[:, c, g0:g0 + n], in1=rstd[:, :n], op=ALU.mult),
                                ["rstd"], [("xr", gi)])
                    S.barrier()
                xrk = [("xr", gi) for gi in range(5)]
                kvw = self.kv_w.rearrange("(kc p) n -> p kc n", p=128)
                wqv = self.b_w_q[0].rearrange("(kc p) n -> p kc n", p=128)
                gkv = self.colv[:, 88:96]
                with ExitStack() as stf:
                    wfg = self.sb(stf, "wfg", [128, 8, 16], BF16)
                    Cp = self.sb(stf, "Cp", [16, T], F32)
                    sp = self.sb(stf, "sp", [16, 512], F32)
                    self.wload(wfg[:, :, :], kvw[:, :, 2048:2064], 0, "wfg")
                    S.dve(lambda e: e.tensor_tensor(out=wfg[:, :, :], in0=wfg[:, :, :],
                                                    in1=gkv.unsqueeze(2).to_broadcast([128, 8, 16]), op=ALU.mult),
                          ["wfg"], ["wfg"])
                    for gi, (g0, n) in enumerate(GG):
                        for kc in range(8):
                            S.pe(lambda e, kc=kc, g0=g0, n=n: e.matmul(ps[0][:16, :n], lhsT=wfg[:, kc, :], rhs=xr[:, kc, g0:g0 + n],
                                                                        start=(kc == 0), stop=(kc == 7)), ["wfg", ("xr", gi)], ["ps0"])
                        S.act(lambda e, n=n: e.activation(out=sp[:, :n], in_=ps[0][:16, :n], func=AF.Exp, scale=-1.0,
                                                          bias=self.nfgb[:, 0:1]), ["ps0", "nfgb2"], ["sp"])
                        S.act(lambda e, n=n: e.activation(out=sp[:, :n], in_=sp[:, :n], func=AF.Ln, scale=1.0,
                                                          bias=self.onec[0:16, 0:1]), ["sp"], ["sp"])
                        init = 0.0 if gi == 0 else Cp[:, g0 - 1:g0]
                        S.dve(lambda e, g0=g0, n=n, init=init: e.tensor_tensor_scan(
                            out=Cp[:, g0:g0 + n], data0=self.onesf[:16, :n], data1=sp[:, :n], initial=init,
                            op0=ALU.mult, op1=ALU.add), ["sp", ("Cp", gi - 1)], [("Cp", gi)])
                    cpk = [("Cp", gi) for gi in range(5)]
                    S.act(lambda e: e.activation(out=Cpb[:, :], in_=Cp[:, :], func=AF.Copy), cpk, ["Cpb"])
                    for ti, (t0, nt) in enumerate(TT):
                        S.pe(lambda e, t0=t0, nt=nt: e.transpose(ps[1][:nt, 0:16], Cp[:, t0:t0 + nt], self.i16[:]), cpk, ["ps1"])
                        S.dve(lambda e, ti=ti, nt=nt: e.tensor_copy(out=Ctok[:nt, ti, :], in_=ps[1][:nt, 0:16]), ["ps1"], ["Ctok"])
                    S.barrier()
                wp = [self.sb(st, "wp%d" % i, [128, 8, 3, 128], BF16) for i in range(2)]
                KTh = [self.sb(st, "KT%d" % i, [65, T], BF16) for i in range(2)]
                QTh = [self.sb(st, "QT%d" % i, [65, T], BF16) for i in range(2)]
                Va = self.sb(st, "Va", [128, 17, 192], BF16)
                NPT = 4
                PT = [self.sb(st, "PT%d" % i, [128, 512], BF16) for i in range(NPT)]
                rinv = [self.sb(st, "rinv%d" % i, [128, 512], F32) for i in range(2)]
                S.pool(lambda e: e.memset(Va[:, :, 64:128], 1.0), [], ["Vones"])
                for hh in range(2):
                    S.pool(lambda e, hh=hh: e.memset(KTh[hh][64:65, :], 1.0), [], [("Kone", hh)])
                g0col = self.colv[:, 32:40]
                itc = 0
                def load_wp(p):
                    sl = p % 2
                    w = wp[sl]
                    self.wload(w[:, :, 0, :], kvw[:, :, p * 128:(p + 1) * 128], sl, ("wp", sl, 0))
                    self.wload(w[:, :, 1, :], kvw[:, :, D + p * 128:D + (p + 1) * 128], sl, ("wp", sl, 1))
                    self.wload(w[:, :, 2, :], wqv[:, :, p * 128:(p + 1) * 128], sl, ("wp", sl, 2))
                    for m in range(3):
                        gc = gkv if m < 2 else g0col
                        S.dve(lambda e, w=w, m=m, gc=gc: e.tensor_tensor(
                            out=w[:, :, m, :], in0=w[:, :, m, :], in1=gc.unsqueeze(2).to_broadcast([128, 8, 128]), op=ALU.mult),
                            [("wp", sl, m)], [("wp", sl, m)])

                load_wp(0)
                for p in range(8):
                    sl = p % 2
                    w = wp[sl]
                    if p + 1 < 8:
                        load_wp(p + 1)
                    for gi, (g0, n) in enumerate(GG):
                        for (m, dst, dk) in ((0, KTh, "KT"), (2, QTh, "QT")):
                            bank = ps[m // 2]
                            bk = "ps%d" % (m // 2)
                            for kc in range(8):
                                S.pe(lambda e, bank=bank, m=m, kc=kc, g0=g0, n=n, w=w: e.matmul(
                                    bank[:, :n], lhsT=w[:, kc, m, :], rhs=xr[:, kc, g0:g0 + n], start=(kc == 0), stop=(kc == 7)),
                                    [("wp", sl, m), ("xr", gi)], [bk])
                            for hh in range(2):
                                S.dve(lambda e, bank=bank, dst=dst, hh=hh, g0=g0, n=n: e.tensor_copy(
                                    out=dst[hh][0:64, g0:g0 + n], in_=bank[hh * 64:(hh + 1) * 64, :n]), [bk], [(dk, hh, gi)])
                        for hh in range(2):
                            h = 2 * p + hh
                            S.pe(lambda e, h=h, g0=g0, n=n: e.matmul(ps[2][0:65, :n], lhsT=self.selq[:, h, :], rhs=Cpb[:, g0:g0 + n],
                                                                      start=True, stop=True), ["Cpb", "selq"], ["ps2"])
                            S.dve(lambda e, hh=hh, g0=g0, n=n: e.tensor_copy(out=QTh[hh][64:65, g0:g0 + n], in_=ps[2][64:65, :n]),
                                  ["ps2"], [("QT", hh, gi)])
                    for kt, (t0, nt) in enumerate(TT):
                        for kc in range(8):
                            S.pe(lambda e, kc=kc, t0=t0, nt=nt, w=w: e.matmul(
                                ps[2][:nt, 0:128], lhsT=xr[:, kc, t0:t0 + nt], rhs=w[:, kc, 1, :], start=(kc == 0), stop=(kc == 7)),
                                [("wp", sl, 1)] + xrk, ["ps2"])
                        S.dve(lambda e, kt=kt, nt=nt: e.tensor_copy(
                            out=Va[:nt, kt, :].rearrange("p (a b) -> p a b", b=64)[:, 0:3:2, :],
                            in_=ps[2][:nt, 0:128].rearrange("p (a b) -> p a b", b=64)), ["ps2"], [("Va", kt)])
                    its = []
                    for hh in range(2):
                        for gi, (g0, n) in enumerate(GG):
                            tl = tiles_of_group(gi)
                            for kt in range(tl[-1] + 1):
                                its.append((hh, gi, kt))
                    LOOK = 2
                    SB = [5, 6, 2]

                    def emit_qk(ix):
                        hh, gi, kt = its[ix]
                        g0, n = GG[gi]
                        tl = tiles_of_group(gi)
                        k0, nk = TT[kt]
                        c0 = (kt - tl[0]) * 128 if (kt in tl and gi > 0) else 0
                        nq = n - c0
                        sbi = SB[(itc0 + ix) % 3]
                        sb_ = ps[sbi]
                        KT, QT = KTh[hh], QTh[hh]
                        S.pe(lambda e: e.matmul(sb_[:nk, :nq], lhsT=KT[0:65, k0:k0 + nk], rhs=QT[0:65, g0 + c0:g0 + c0 + nq],
                                                start=True, stop=True),
                             [("KT", hh, g_) for g_ in range(5)] + [("QT", hh, gi), ("Kone", hh)], ["ps%d" % sbi])

                    def emit_rest(ix, p=p):
                        hh, gi, kt = its[ix]
                        h = 2 * p + hh
                        g0, n = GG[gi]
                        tl = tiles_of_group(gi)
                        last = tl[-1]
                        k0, nk = TT[kt]
                        c0 = (kt - tl[0]) * 128 if (kt in tl and gi > 0) else 0
                        nq = n - c0
                        sbi = SB[(itc0 + ix) % 3]
                        sb_ = ps[sbi]
                        sbk = "ps%d" % sbi
                        pt = PT[(itc0 + ix) % NPT]
                        ptk = ("PT", (itc0 + ix) % NPT)
                        vlo = 0 if hh == 0 else 64
                        orow = hh * 64
                        lrow = 64 - orow
                        obi = 3 + (gi + hh) % 2
                        ob = ps[obi]
                        obk = "ps%d" % obi
                        S.act(lambda e: e.activation(out=pt[:nk, :nq], in_=sb_[:nk, :nq], func=AF.Exp, scale=scale,
                                                     bias=Ctok[:nk, kt, h:h + 1]), [sbk, "Ctok"], [ptk])
                        if kt in tl:
                            qn = min(128, nq)
                            S.pool(lambda e: e.tensor_tensor(out=pt[:nk, 0:qn], in0=pt[:nk, 0:qn], in1=self.triu[:nk, :qn],
                                                             op=ALU.mult), [ptk], [ptk])
                        S.pe(lambda e: e.matmul(ob[:, c0:c0 + nq], lhsT=Va[:nk, kt, vlo:vlo + 128], rhs=pt[:nk, 0:nq],
                                                start=(kt == 0), stop=(kt == last)), [ptk, ("Va", kt), "Vones"], [obk])
                        if kt == last:
                            rv = rinv[(gi + hh) % 2]
                            rk = ("rinv", (gi + hh) % 2)
                            S.dve(lambda e: e.reciprocal(out=rv[orow:orow + 64, :n], in_=ob[lrow:lrow + 64, :n]), [obk], [rk])
                            S.dve(lambda e: e.tensor_tensor(out=O[orow:orow + 64, p, g0:g0 + n], in0=ob[orow:orow + 64, :n],
                                                            in1=rv[orow:orow + 64, :n], op=ALU.mult), [obk, rk], [("O", gi)])

                    itc0 = itc
                    for ix in range(min(LOOK, len(its))):
                        emit_qk(ix)
                    for ix in range(len(its)):
                        if ix + LOOK < len(its):
                            emit_qk(ix + LOOK)
                        emit_rest(ix)
                    itc += len(its)
            self.S.barrier()
            with ExitStack() as st:
                self.out_proj_residual(st, self.b_w_out[0], O, 1, 1, "O")


_CACHE = {}


def _get_nc(nseq=2, stop=None):
    key = (nseq, stop)
    if key not in _CACHE:
        _CACHE[key] = Builder(nseq, stop).build()
    return _CACHE[key]


def kernel(**inputs):
    ncores = 8
    nc = _get_nc(2, None)
    shared = {k: np.ascontiguousarray(np.asarray(v, dtype=np.float32)) for k, v in inputs.items() if k != "x"}
    x = np.ascontiguousarray(np.asarray(inputs["x"], dtype=np.float32))
    in_maps = []
    for c in range(ncores):
        m = dict(shared)
        m["x"] = x[2 * c:2 * c + 2]
        in_maps.append(m)
    res = run_bass_kernel_spmd(nc, in_maps, core_ids=list(range(ncores)))
    return np.concatenate([np.asarray(r["out"]) for r in res.results], axis=0).astype(np.float32)
```

```python
import numpy as np
import concourse.bass as bass
import concourse.mybir as mybir
from concourse.bass_utils import run_bass_kernel_spmd
from contextlib import ExitStack

F32 = mybir.dt.float32
BF16 = mybir.dt.bfloat16
AF = mybir.ActivationFunctionType
ALU = mybir.AluOpType
AX = mybir.AxisListType

SAME_ENG_SYNC = True


class _Op:
    __slots__ = ("eng", "fn", "reads", "writes", "dma_sem", "ndma", "deps",
                 "needs_inc", "token", "waits", "idx", "is_bar")

    def __init__(self, eng, fn, reads, writes, dma_sem=None, ndma=0):
        self.eng = eng
        self.fn = fn
        self.reads = reads
        self.writes = writes
        self.dma_sem = dma_sem
        self.ndma = ndma
        self.deps = []
        self.needs_inc = False
        self.token = None
        self.waits = []
        self.is_bar = False


class Sched:
    CENG = ("pe", "act", "dve", "pool")
    ALLENG = ("pe", "act", "dve", "pool", "sp")

    def __init__(self, nc, stack):
        self.nc = nc
        self.stack = stack
        self.ops = []
        self.esem = {e: stack.enter_context(nc.semaphore("s_" + e)) for e in self.CENG}
        self.dma_cum = {}
        self.dma_sems = {}
        self.dma_exempt = set()
        self.last_w = {}
        self.readers = {}
        self.last_op = {e: None for e in self.ALLENG}
        self.dma_last = {}

    def dma_sem(self, name, exempt=False):
        if name not in self.dma_sems:
            self.dma_sems[name] = self.stack.enter_context(self.nc.semaphore("d_" + name))
            self.dma_cum[name] = 0
            if exempt:
                self.dma_exempt.add(name)
        return name

    def _add(self, op):
        op.idx = len(self.ops)
        deps = set()
        for k in op.reads:
            w = self.last_w.get(k)
            if w is not None:
                deps.add(w)
        for k in op.writes:
            w = self.last_w.get(k)
            if w is not None:
                deps.add(w)
            for r in self.readers.get(k, ()):
                deps.add(r)
        deps.discard(op)
        op.deps = sorted(deps, key=lambda o: o.idx)
        for k in op.reads:
            self.readers.setdefault(k, []).append(op)
        for k in op.writes:
            self.last_w[k] = op
            self.readers[k] = []
        self.ops.append(op)
        self.last_op[op.eng] = op
        if op.dma_sem is not None:
            self.dma_cum[op.dma_sem] += 16 * op.ndma
            op.token = (op.dma_sem, self.dma_cum[op.dma_sem])
            self.dma_last[op.dma_sem] = op
        return op

    def op(self, eng, fn, reads=(), writes=()):
        return self._add(_Op(eng, fn, tuple(reads), tuple(writes)))

    def pe(self, fn, reads=(), writes=()):
        return self.op("pe", fn, reads, writes)

    def act(self, fn, reads=(), writes=()):
        return self.op("act", fn, reads, writes)

    def dve(self, fn, reads=(), writes=()):
        return self.op("dve", fn, reads, writes)

    def pool(self, fn, reads=(), writes=()):
        return self.op("pool", fn, reads, writes)

    def dma(self, eng, fn, sem, reads=(), writes=(), n=1):
        return self._add(_Op(eng, fn, tuple(reads), tuple(writes), dma_sem=sem, ndma=n))

    def barrier(self):
        prev = dict(self.last_op)
        dl = {k: v for k, v in self.dma_last.items() if k not in self.dma_exempt}
        for e in self.ALLENG:
            b = _Op(e, None, (), ())
            b.is_bar = True
            b.idx = len(self.ops)
            b.deps = [o for ee, o in prev.items() if o is not None and (ee != e or (SAME_ENG_SYNC and e != 'pe'))] + list(dl.values())
            self.ops.append(b)
            self.last_op[e] = b

    def finalize(self):
        for op in self.ops:
            for d in op.deps:
                if d.dma_sem is not None or d.is_bar:
                    continue
                if d.eng == op.eng and (d.eng == "pe" or not SAME_ENG_SYNC):
                    continue
                d.needs_inc = True
        cnt = {e: 0 for e in self.CENG}
        for op in self.ops:
            if op.dma_sem is None and op.needs_inc:
                cnt[op.eng] += 1
                op.token = (op.eng, cnt[op.eng])
        known = {e: {} for e in self.ALLENG}
        for op in self.ops:
            kn = known[op.eng]
            need = {}
            for d in op.deps:
                if d.token is None:
                    continue
                if d.dma_sem is None and d.eng == op.eng and (d.eng == "pe" or not SAME_ENG_SYNC):
                    continue
                s, v = d.token
                if need.get(s, 0) < v:
                    need[s] = v
            for s, v in need.items():
                if kn.get(s, 0) >= v:
                    continue
                kn[s] = v
                op.waits.append((s, v))
        self.counts = cnt

    def _sem(self, s):
        return self.esem[s] if s in self.esem else self.dma_sems[s]

    def emit(self):
        self.finalize()
        by_eng = {e: [o for o in self.ops if o.eng == e] for e in self.ALLENG}
        with self.nc.Block() as block:
            def run(e):
                def body(eng):
                    for op in by_eng[e]:
                        for (s, v) in op.waits:
                            eng.wait_ge(self._sem(s), v)
                        if op.fn is None:
                            continue
                        if op.dma_sem is not None:
                            op.fn(eng, self.dma_sems[op.dma_sem])
                        else:
                            ins = op.fn(eng)
                            if op.needs_inc:
                                ins.then_inc(self.esem[e], 1)
                return body
            block.tensor(run("pe"))
            block.scalar(run("act"))
            block.vector(run("dve"))
            block.gpsimd(run("pool"))
            block.sync(run("sp"))


import os
CUT = int(os.environ.get('KCUT', '99'))
D = 1024
T = 2064
NMETA = 16
DFF = 2752
NJ = 22
EPS = 1e-6
TT = [(0, 16)] + [(16 + 128 * i, 128) for i in range(16)]
GG = [(0, 16)] + [(16 + 512 * j, 512) for j in range(4)]


def tiles_of_group(gi):
    return [0] if gi == 0 else list(range(1 + 4 * (gi - 1), 1 + 4 * gi))


class Builder:
    def __init__(self, nseq=2, stop=None):
        self.nseq = nseq
        self.stop = stop
        nc = self.nc = bass.Bass("TRN2", target_bir_lowering=False)
        dt = lambda name, shape: nc.dram_tensor(name, shape, F32, kind="ExternalInput").ap()
        self.x = dt("x", [nseq, 2048, D])
        self.meta = dt("meta_tokens", [NMETA, D])
        self.norm_gains = dt("norm_gains", [2, 4, D])
        self.a_w_in = dt("a_w_in", [1, D, 4 * D])
        self.a_lb = dt("a_lb_logits", [2, D])
        self.a_hn = dt("a_head_norm", [1, D])
        self.a_w_out = dt("a_w_out", [1, D, D])
        self.kv_norm = dt("kv_norm", [D])
        self.kv_w = dt("kv_w", [D, 2 * D + 16])
        self.fg_b = dt("fg_b", [16])
        self.b_w_q = dt("b_w_q", [1, D, D])
        self.b_w_out = dt("b_w_out", [1, D, D])
        self.w_up = dt("ffn_w_up", [2, D, 2 * DFF])
        self.conv = dt("ffn_conv", [2, 3, 2 * DFF])
        self.w_down = dt("ffn_w_down", [2, DFF, D])
        self.out = nc.dram_tensor("out", [nseq, 2048, D], F32, kind="ExternalOutput").ap()
        self.uid = 0

    def sb(self, st, name, shape, dtype):
        self.uid += 1
        return st.enter_context(self.nc.sbuf_tensor("%s_%d" % (name, self.uid), shape, dtype))

    def build(self):
        nc = self.nc
        with ExitStack() as st:
            S = self.S = Sched(nc, st)
            self.ps = [st.enter_context(nc.psum_tensor("ps%d" % i, [128, 512], F32)) for i in range(7)]
            self.psb = st.enter_context(nc.psum_tensor("psb", [128, 1024], BF16))
            self.hT = self.sb(st, "hT", [128, 8, T], F32)
            self.consts(st)
            S.barrier()
            for s in range(self.nseq):
                self.seq(s)
            S.barrier()
            S.emit()
        return nc

    def consts(self, st):
        S = self.S
        self.ident = self.sb(st, "ident", [128, 128], F32)
        self.identb = self.sb(st, "identb", [128, 128], BF16)
        self.onesb = self.sb(st, "onesb", [128, 128], BF16)
        self.triu = self.sb(st, "triu", [128, 128], BF16)
        self.mask2 = self.sb(st, "mask2", [128, 128], F32)
        self.maskseg = self.sb(st, "maskseg", [128, 512], F32)
        self.epsc = self.sb(st, "epsc", [128, 1], F32)
        self.colv = self.sb(st, "colv", [128, 96], F32)
        self.convT = self.sb(st, "convT", [128, 3, 128], F32)
        self.lbc = self.sb(st, "lbc", [128, 24], F32)
        self.nfgb = self.sb(st, "nfgb", [16, 1], F32)
        self.i16 = self.sb(st, "i16", [16, 16], F32)
        self.ones16 = self.sb(st, "ones16", [16, 128], F32)
        self.rinvs = self.sb(st, "rinvs", [128, 512], F32)
        self.onesf = self.sb(st, "onesf", [16, 512], F32)
        self.onec = self.sb(st, "onec", [128, 1], F32)
        self.selq = self.sb(st, "selq", [16, 16, 65], BF16)
        P = lambda fn, r=(), w=(): S.pool(fn, r, w)
        P(lambda e: e.memset(self.ident[:], 1.0), w=["ident"])
        P(lambda e: e.affine_select(out=self.ident[:], in_=self.ident[:], pattern=[[-1, 128]],
                                    compare_op=ALU.is_equal, fill=0.0, base=0, channel_multiplier=1),
          r=["ident"], w=["ident"])
        S.dve(lambda e: e.tensor_copy(out=self.identb[:], in_=self.ident[:]), ["ident"], ["identb"])
        S.dve(lambda e: e.tensor_copy(out=self.i16[:], in_=self.ident[0:16, 0:16]), ["ident"], ["i16"])
        P(lambda e: e.memset(self.onesb[:], 1.0), w=["onesb"])
        P(lambda e: e.memset(self.ones16[:], 1.0), w=["ones16"])
        P(lambda e: e.memset(self.triu[:], 1.0), w=["triu"])
        P(lambda e: e.affine_select(out=self.triu[:], in_=self.triu[:], pattern=[[1, 128]],
                                    compare_op=ALU.is_ge, fill=0.0, base=0, channel_multiplier=-1),
          r=["triu"], w=["triu"])
        P(lambda e: e.memset(self.mask2[:], 1.0), w=["mask2"])
        P(lambda e: e.affine_select(out=self.mask2[:], in_=self.mask2[:], pattern=[[1, 128]],
                                    compare_op=ALU.is_ge, fill=0.0, base=0, channel_multiplier=-1),
          r=["mask2"], w=["mask2"])
        P(lambda e: e.memset(self.mask2[0:64, 64:128], 0.0), r=["mask2"], w=["mask2"])
        P(lambda e: e.memset(self.maskseg[:], 1.0), w=["maskseg"])
        P(lambda e: e.memset(self.maskseg[:].rearrange("p (c k) -> p c k", k=64)[:, :, 0:1], 0.0),
          r=["maskseg"], w=["maskseg"])
        P(lambda e: e.memset(self.epsc[:], EPS), w=["epsc"])
        P(lambda e: e.memset(self.onesf[:], 1.0), w=["onesf"])
        P(lambda e: e.memset(self.onec[:], 1.0), w=["onec"])
        P(lambda e: e.memset(self.selq[:], 0.0), w=["selq0"])
        S.dve(lambda e: e.tensor_scalar(out=self.selq[:, :, 64], in0=self.i16[:, :], scalar1=-8.0, scalar2=None, op0=ALU.mult),
              ["selq0", "i16"], ["selq"])
        rowsA = self.sb(st, "rowsA", [96, 128], F32)
        rowsC = self.sb(st, "rowsC", [128, 3, 128], F32)
        P(lambda e: e.memset(rowsC[:], 0.0), w=["rowsC"])
        cs = S.dma_sem("const")
        nd = [0]

        def ld(dst, src, rk):
            S.dma("sp", lambda e, s, dst=dst, src=src: e.dma_start(out=dst, in_=src).then_inc(s, 16), cs,
                  reads=[rk], writes=[("rowsd", nd[0])])
            nd[0] += 1
        ld(rowsA[0:64, :], self.norm_gains.rearrange("l j (c p) -> (l j c) p", p=128), "rowsA")
        ld(rowsA[64:80, :], self.a_lb.rearrange("l (c p) -> (l c) p", p=128), "rowsA")
        ld(rowsA[80:88, :], self.a_hn.rearrange("l (c p) -> (l c) p", p=128), "rowsA")
        ld(rowsA[88:96, :], self.kv_norm.rearrange("(c p) -> c p", p=128), "rowsA")
        for l in range(2):
            for tap in range(3):
                for part in range(2):
                    r0 = ((l * 3 + tap) * 2 + part) * 22
                    src = self.conv[l, tap, part * DFF: part * DFF + 2688].rearrange("(j k) -> j k", k=128)
                    done = 0
                    while done < 21:
                        ti, ri = divmod(r0 + done, 128)
                        cnt = min(21 - done, 128 - ri)
                        ld(rowsC[ri:ri + cnt, ti, :], src[done:done + cnt, :], "rowsC")
                        done += cnt
                    ti, ri = divmod(r0 + 21, 128)
                    ld(rowsC[ri:ri + 1, ti, 0:64],
                       self.conv[l, tap, part * DFF + 2688: part * DFF + 2752].rearrange("(a k) -> a k", a=1), "rowsC")
        ld(self.nfgb[:, :], self.fg_b.rearrange("(h a) -> h a", a=1), "nfgb")
        allrows = [("rowsd", i) for i in range(nd[0])]
        ps = self.ps
        S.pe(lambda e: e.transpose(ps[0][:, 0:96], rowsA[:, :], self.ident[0:96, 0:96]), allrows + ["ident"], ["ps0"])
        S.dve(lambda e: e.tensor_copy(out=self.colv[:], in_=ps[0][:, 0:96]), ["ps0"], ["colv"])
        for ti in range(3):
            S.pe(lambda e, ti=ti: e.transpose(ps[1][:, ti * 128:(ti + 1) * 128], rowsC[:, ti, :], self.ident[:]),
                 allrows + ["ident", "rowsC"], ["ps1"])
        S.dve(lambda e: e.tensor_copy(out=self.convT[:], in_=ps[1][:, 0:384].rearrange("p (a b) -> p a b", b=128)),
              ["ps1"], ["convT"])
        dl = self.sb(st, "dl", [128, 8], F32)
        S.dve(lambda e: e.tensor_tensor(out=dl[:], in0=self.colv[:, 64:72], in1=self.colv[:, 72:80], op=ALU.subtract),
              ["colv"], ["dl"])
        S.act(lambda e: e.activation(out=self.lbc[:, 0:8], in_=dl[:], func=AF.Sigmoid), ["dl"], ["lbc0"])
        S.act(lambda e: e.activation(out=self.lbc[:, 8:16], in_=dl[:], func=AF.Sigmoid, scale=-1.0), ["dl"], ["lbc1"])
        S.dve(lambda e: e.tensor_scalar(out=self.lbc[:, 16:24], in0=self.lbc[:, 8:16], scalar1=-1.0, scalar2=None,
                                        op0=ALU.mult), ["lbc1"], ["lbc2"])
        S.dve(lambda e: e.tensor_scalar(out=self.nfgb[:], in0=self.nfgb[:], scalar1=-1.0, scalar2=None, op0=ALU.mult),
              allrows, ["nfgb2"])

    def gcol(self, l, j, c):
        k = (l * 4 + j) * 8 + c
        return self.colv[:, k:k + 1]

    def ccol(self, l, tap, part, j):
        r = ((l * 3 + tap) * 2 + part) * 22 + j
        ti, ri = divmod(r, 128)
        return self.convT[:, ti, ri:ri + 1]

    def wload(self, dst, src, slot, key, reads=()):
        S = self.S
        sem = S.dma_sem("w_" + "_".join(str(k) for k in (key if isinstance(key, tuple) else (key,))), exempt=True)
        S.dma("pool", lambda e, s: e.dma_start(out=dst, in_=src).then_inc(s, 16), sem,
              reads=list(reads), writes=[key])

    def rstd_from(self, srcs, n, sq, rtmp, rstd, pst, pkey, dscale=1.0 / D):
        S = self.S
        nsrc = len(srcs)
        for c, (ap, rk) in enumerate(srcs):
            S.act(lambda e, ap=ap, c=c: e.activation(out=sq[:, c, :n], in_=ap, func=AF.Square), rk, [("sq", c)])
            S.pe(lambda e, c=c: e.matmul(pst[:, :n], lhsT=self.onesb[:], rhs=sq[:, c, :n], start=(c == 0),
                                         stop=(c == nsrc - 1)), [("sq", c)], [pkey])
        S.act(lambda e: e.activation(out=rtmp[:, :n], in_=pst[:, :n], func=AF.Sqrt, scale=dscale, bias=self.epsc[:, 0:1]),
              [pkey], ["rtmp"])
        S.dve(lambda e: e.reciprocal(out=rstd[:, :n], in_=rtmp[:, :n]), ["rtmp"], ["rstd"])

    def seq(self, s):
        S = self.S
        self.load_x(s)
        S.barrier()
        if self.stop != "load":
            self.hgrn2(s)
            S.barrier()
            if self.stop not in ("mix0", "mix0a", "mix0b", "mix0c"):
                self.ffn(s, 0)
                S.barrier()
                if self.stop != "ffn0":
                    self.fox(s)
                    S.barrier()
                    if self.stop != "mix1":
                        self.ffn(s, 1)
                        S.barrier()
        self.store(s)
        S.barrier()

    def load_x(self, s):
        S, ps, hT = self.S, self.ps, self.hT
        with ExitStack() as st:
            xin = [self.sb(st, "xin%d" % i, [128, D], F32) for i in range(2)]
            xs = [S.dma_sem("xin%d" % i) for i in range(2)]
            for ti, (t0, n) in enumerate(TT):
                sl = ti % 2
                src = self.meta if ti == 0 else self.x[s, t0 - 16:t0 - 16 + 128, :]
                S.dma("sp", lambda e, sm, sl=sl, src=src, n=n: e.dma_start(out=xin[sl][:n, :], in_=src).then_inc(sm, 16),
                      xs[sl], writes=[("xin", sl)])
                for half in range(2):
                    bank = ps[half + 2 * sl]
                    bk = "ps%d" % (half + 2 * sl)
                    for j in range(4):
                        c = half * 4 + j
                        S.pe(lambda e, bank=bank, j=j, c=c, n=n, sl=sl: e.transpose(
                            bank[:, j * 128:j * 128 + n], xin[sl][:n, c * 128:(c + 1) * 128], self.ident[:n, :n]),
                            [("xin", sl)], [bk])
                    fn = lambda e, bank=bank, half=half, t0=t0, n=n: e.tensor_copy(
                        out=hT[:, half * 4:(half + 1) * 4, t0:t0 + n],
                        in_=bank[:, :].rearrange("p (j k) -> p j k", k=128)[:, :, 0:n])
                    if half == 0:
                        S.dve(fn, [bk], [("hT", ti, half)])
                    else:
                        S.act(lambda e, bank=bank, half=half, t0=t0, n=n: e.activation(
                            out=hT[:, half * 4:(half + 1) * 4, t0:t0 + n],
                            in_=bank[:, :].rearrange("p (j k) -> p j k", k=128)[:, :, 0:n], func=AF.Copy),
                            [bk], [("hT", ti, half)])

    def store(self, s):
        S, ps, hT = self.S, self.ps, self.hT
        with ExitStack() as st:
            xo = [self.sb(st, "xo%d" % i, [128, D], F32) for i in range(2)]
            os_ = [S.dma_sem("xo%d" % i) for i in range(2)]
            for ti, (t0, n) in enumerate(TT):
                if ti == 0:
                    continue
                sl = ti % 2
                for half in range(2):
                    bank = ps[half + 2 * sl]
                    bk = "ps%d" % (half + 2 * sl)
                    for j in range(4):
                        c = half * 4 + j
                        S.pe(lambda e, bank=bank, j=j, c=c, t0=t0: e.transpose(
                            bank[:, j * 128:(j + 1) * 128], hT[:, c, t0:t0 + 128], self.ident[:]), [], [bk])
                    if half == 0:
                        S.dve(lambda e, bank=bank, sl=sl: e.tensor_copy(out=xo[sl][:, 0:512], in_=bank[:, :]),
                              [bk], [("xo", sl)])
                    else:
                        S.act(lambda e, bank=bank, sl=sl: e.activation(out=xo[sl][:, 512:1024], in_=bank[:, :], func=AF.Copy),
                              [bk], [("xo", sl)])
                S.dma("sp", lambda e, sm, sl=sl, t0=t0: e.dma_start(out=self.out[s, t0 - 16:t0 - 16 + 128, :],
                                                                     in_=xo[sl][:, :]).then_inc(sm, 16),
                      os_[sl], reads=[("xo", sl)], writes=[("xo", sl)])

    def out_proj_residual(self, st, wsrc, src_act, l, jn, tag):
        S, ps, hT = self.S, self.ps, self.hT
        wo = self.sb(st, "wo", [128, 8, D], BF16)
        mix32 = self.sb(st, "mix32", [128, 8, 512], F32)
        sq = self.sb(st, "sqo", [128, 8, 512], BF16)
        rtmp = self.sb(st, "rtmpo", [128, 512], F32)
        rstd = self.sb(st, "rstdo", [128, 512], F32)
        tmp = self.sb(st, "tmpo", [128, 512], F32)
        wv = wsrc.rearrange("(kc p) n -> p kc n", p=128)
        for kc in range(8):
            self.wload(wo[:, kc, :], wv[:, kc, :], kc % 2, ("wo", kc))
        for gi, (g0, n) in enumerate(GG):
            for dc in range(8):
                bank = ps[dc % 2]
                bk = "ps%d" % (dc % 2)
                for kc in range(8):
                    S.pe(lambda e, bank=bank, dc=dc, kc=kc, g0=g0, n=n: e.matmul(
                        bank[:, :n], lhsT=wo[:, kc, dc * 128:(dc + 1) * 128], rhs=src_act[:, kc, g0:g0 + n],
                        start=(kc == 0), stop=(kc == 7)), [("wo", kc), (tag, gi)], [bk])
                S.act(lambda e, bank=bank, dc=dc, n=n: e.activation(out=mix32[:, dc, :n], in_=bank[:, :n], func=AF.Copy),
                      [bk], [("mix32", dc)])
            self.rstd_from([(mix32[:, dc, :n], [("mix32", dc)]) for dc in range(8)], n, sq, rtmp, rstd, ps[2], "ps2")
            for dc in range(8):
                S.dve(lambda e, dc=dc, n=n: e.scalar_tensor_tensor(
                    out=tmp[:, :n], in0=mix32[:, dc, :n], scalar=self.gcol(l, jn, dc), in1=rstd[:, :n],
                    op0=ALU.mult, op1=ALU.mult), [("mix32", dc), "rstd"], ["tmpo"])
                S.pool(lambda e, dc=dc, g0=g0, n=n: e.tensor_tensor(
                    out=hT[:, dc, g0:g0 + n], in0=hT[:, dc, g0:g0 + n], in1=tmp[:, :n], op=ALU.add),
                    ["tmpo"], [("hT", dc, gi)])

    def hgrn2(self, s):
        S, ps, psb, hT = self.S, self.ps, self.psb, self.hT
        with ExitStack() as st0:
            og = self.sb(st0, "og", [128, 8, T], BF16)
            with ExitStack() as st:
                xn = self.sb(st, "xn", [128, 8, T], BF16)
                sq = self.sb(st, "sq", [128, 8, 512], BF16)
                rtmp = self.sb(st, "rtmp", [128, 512], F32)
                rstd = self.sb(st, "rstd", [128, 512], F32)
                for gi, (g0, n) in enumerate(GG):
                    self.rstd_from([(hT[:, c, g0:g0 + n], []) for c in range(8)], n, sq, rtmp, rstd, ps[2], "ps2")
                    for c in range(8):
                        S.dve(lambda e, c=c, g0=g0, n=n: e.scalar_tensor_tensor(
                            out=xn[:, c, g0:g0 + n], in0=hT[:, c, g0:g0 + n], scalar=self.gcol(0, 0, c), in1=rstd[:, :n],
                            op0=ALU.mult, op1=ALU.mult), ["rstd"], [("xn", gi)])
                wh = [self.sb(st, "wh%d" % i, [128, 8, 4, 128], BF16) for i in range(2)]
                A = self.sb(st, "A", [128, 512], F32)
                C = self.sb(st, "C", [128, 512], F32)
                Dn = self.sb(st, "Dn", [128, 512], F32)
                Bs = [self.sb(st, "B%d" % i, [128, 512], F32) for i in range(2)]
                SGs = [self.sb(st, "SG%d" % i, [128, 512], F32) for i in range(2)]
                qins = [self.sb(st, "qin%d" % i, [128, 512], BF16) for i in range(2)]
                kins = [self.sb(st, "kin%d" % i, [128, 512], BF16) for i in range(2)]
                kouts = [self.sb(st, "kout%d" % i, [128, 512], BF16) for i in range(2)]
                vtoks = [self.sb(st, "vtok%d" % i, [128, 4, 128], BF16) for i in range(2)]
                O32 = self.sb(st, "O32", [128, 512], F32)
                sqh = self.sb(st, "sqh", [128, 1, 512], BF16)
                ktok = self.sb(st, "ktok", [128, 4, 128], BF16)
                attT = self.sb(st, "attT", [128, 4, 128], BF16)
                S32 = self.sb(st, "S32", [128, 9, 128], F32)
                Sb = self.sb(st, "Sb", [128, 8, 128], BF16)
                win = self.a_w_in[0].rearrange("(kc p) n -> p kc n", p=128)

                def load_head(hd):
                    sl = hd % 2
                    for j in range(4):
                        self.wload(wh[sl][:, :, j, :], win[:, :, j * D + hd * 128: j * D + (hd + 1) * 128], sl, ("wh", sl, j))

                nheads = {"mix0a": 0, "mix0b": 1, "mix0c": 1}.get(self.stop, 8)
                iters = [(hd, gi) for hd in range(nheads) for gi in range(5)]

                def stage1(it):
                    hd, gi = iters[it]
                    g0, n = GG[gi]
                    z = it % 2
                    sl = hd % 2
                    w = wh[sl]
                    wk = [("wh", sl, j) for j in range(4)]
                    B, SG, qin, kin, kout, vtok = Bs[z], SGs[z], qins[z], kins[z], kouts[z], vtoks[z]
                    kB, kSG, kq, kk_, ko, kv = ("B", z), ("SG", z), ("qin", z), ("kin", z), ("kout", z), ("vtok", z)
                    tl_list = tiles_of_group(gi)
                    if gi == 0 and hd + 1 < nheads:
                        load_head(hd + 1)

                    def proj(j, bi):
                        for kc in range(8):
                            S.pe(lambda e, kc=kc: e.matmul(ps[bi][:, :n], lhsT=w[:, kc, j, :], rhs=xn[:, kc, g0:g0 + n],
                                                           start=(kc == 0), stop=(kc == 7)), [wk[j], ("xn", gi)], ["ps%d" % bi])
                    proj(1, 1)
                    S.act(lambda e: e.activation(out=A[:, :n], in_=ps[1][:, :n], func=AF.Sigmoid), ["ps1"], ["A"])
                    proj(0, 0)
                    proj(3, 1)
                    S.act(lambda e: e.activation(out=SG[:, :n], in_=ps[1][:, :n], func=AF.Silu), ["ps1"], [kSG])
                    for li, ti in enumerate(tl_list):
                        t0, nt = TT[ti]
                        for kc in range(8):
                            S.pe(lambda e, li=li, kc=kc, t0=t0, nt=nt: e.matmul(
                                ps[2][:nt, li * 128:(li + 1) * 128], lhsT=xn[:, kc, t0:t0 + nt], rhs=w[:, kc, 2, :],
                                start=(kc == 0), stop=(kc == 7)), [wk[2], ("xn", gi)], ["ps2"])
                    if gi == 0:
                        S.act(lambda e: e.activation(out=vtok[:16, 0, :], in_=ps[2][:16, 0:128], func=AF.Copy), ["ps2"], [kv])
                    else:
                        S.act(lambda e: e.activation(out=vtok[:, :, :], in_=ps[2][:, :].rearrange("p (a b) -> p a b", b=128),
                                                     func=AF.Copy), ["ps2"], [kv])
                    S.act(lambda e: e.activation(out=B[:, :n], in_=A[:, :n], func=AF.Ln,
                                                 scale=self.lbc[:, 8 + hd:9 + hd], bias=self.lbc[:, hd:hd + 1]), ["A"], [kB])
                    S.dve(lambda e: e.tensor_scalar(out=C[:, :n], in0=A[:, :n], scalar1=self.lbc[:, 16 + hd:17 + hd],
                                                    scalar2=self.lbc[:, 8 + hd:9 + hd], op0=ALU.mult, op1=ALU.add), ["A"], ["C"])
                    S.dve(lambda e: e.tensor_tensor_scan(out=A[:, :n], data0=self.maskseg[:, :n], data1=B[:, :n],
                                                         initial=0.0, op0=ALU.mult, op1=ALU.add), [kB, "A"], ["A"])
                    S.act(lambda e: e.activation(out=B[:, :n], in_=A[:, :n], func=AF.Exp), ["A"], [kB])
                    S.act(lambda e: e.activation(out=Dn[:, :n], in_=A[:, :n], func=AF.Exp, scale=-1.0), ["A"], ["Dn"])
                    S.dve(lambda e: e.tensor_tensor(out=qin[:, :n], in0=ps[0][:, :n], in1=B[:, :n], op=ALU.mult),
                          ["ps0", kB], [kq])
                    S.dve(lambda e: e.tensor_tensor(out=C[:, :n], in0=C[:, :n], in1=Dn[:, :n], op=ALU.mult), ["C", "Dn"], ["C"])
                    S.act(lambda e: e.activation(out=kin[:, :n], in_=C[:, :n], func=AF.Copy), ["C"], [kk_])
                    if gi == 0:
                        S.dve(lambda e: e.tensor_scalar(out=kout[:, :16], in0=C[:, :16], scalar1=B[:, 15:16], scalar2=None,
                                                        op0=ALU.mult), ["C", kB], [ko])
                    else:
                        S.dve(lambda e: e.tensor_tensor(
                            out=kout[:, :].rearrange("p (c k) -> p c k", k=64),
                            in0=C[:, :].rearrange("p (c k) -> p c k", k=64),
                            in1=B[:, :].rearrange("p (c k) -> p c k", k=64)[:, :, 63:64].to_broadcast([128, 8, 64]),
                            op=ALU.mult), ["C", kB], [ko])

                def stage2(it):
                    hd, gi = iters[it]
                    g0, n = GG[gi]
                    z = it % 2
                    B, SG, qin, kin, kout, vtok = Bs[z], SGs[z], qins[z], kins[z], kouts[z], vtoks[z]
                    kB, kSG, kq, kk_, ko, kv = ("B", z), ("SG", z), ("qin", z), ("kin", z), ("kout", z), ("vtok", z)
                    tl_list = tiles_of_group(gi)
                    nch = 1 if gi == 0 else 8
                    if gi == 0:
                        S.dve(lambda e: e.memset(S32[:, 0, :], 0.0), [], [("S32", 0)])
                    for li, ti in enumerate(tl_list):
                        t0, nt = TT[ti]
                        S.pe(lambda e, li=li, nt=nt: e.transpose(psb[:nt, li * 128:(li + 1) * 128],
                                                                  kout[:, li * 128:li * 128 + nt], self.identb[:]), [ko], ["psb"])
                    if gi == 0:
                        S.act(lambda e: e.activation(out=ktok[:16, 0, :], in_=psb[:16, 0:128], func=AF.Copy), ["psb"], ["ktok"])
                    else:
                        S.act(lambda e: e.activation(out=ktok[:, :, :], in_=psb[:, 0:512].rearrange("p (a b) -> p a b", b=128),
                                                     func=AF.Copy), ["psb"], ["ktok"])
                    for li, ti in enumerate(tl_list):
                        t0, nt = TT[ti]
                        S.pe(lambda e, li=li, nt=nt: e.matmul(ps[5][:nt, li * 128:li * 128 + nt], lhsT=kin[:, li * 128:li * 128 + nt],
                                                               rhs=qin[:, li * 128:li * 128 + nt], start=True, stop=True),
                             [kk_, kq], ["ps5"])
                    if gi == 0:
                        S.dve(lambda e: e.tensor_tensor(out=attT[:16, 0, :16], in0=ps[5][:16, 0:16], in1=self.mask2[:16, :16],
                                                        op=ALU.mult), ["ps5"], ["attT"])
                    else:
                        S.dve(lambda e: e.tensor_tensor(
                            out=attT[:, :, :], in0=ps[5][:, :].rearrange("p (a b) -> p a b", b=128),
                            in1=self.mask2[:, :].unsqueeze(1).to_broadcast([128, 4, 128]), op=ALU.mult), ["ps5"], ["attT"])
                    for cl in range(nch):
                        li, r0 = cl // 2, (cl % 2) * 64
                        nr = 16 if gi == 0 else 64
                        bi = 3 + cl % 2
                        S.pe(lambda e, cl=cl, li=li, r0=r0, nr=nr, bi=bi: e.matmul(
                            ps[bi][:, (cl // 2) * 128:(cl // 2 + 1) * 128], lhsT=ktok[r0:r0 + nr, li, :],
                            rhs=vtok[r0:r0 + nr, li, :], start=True, stop=True), ["ktok", kv], ["ps%d" % bi])
                    for cl in range(nch):
                        bi = 3 + cl % 2
                        dcol = B[:, 15:16] if gi == 0 else B[:, cl * 64 + 63:cl * 64 + 64]
                        S.dve(lambda e, cl=cl, bi=bi, dcol=dcol: e.scalar_tensor_tensor(
                            out=S32[:, cl + 1, :], in0=S32[:, cl, :], scalar=dcol,
                            in1=ps[bi][:, (cl // 2) * 128:(cl // 2 + 1) * 128], op0=ALU.mult, op1=ALU.add),
                            [("S32", cl), kB, "ps%d" % bi], [("S32", cl + 1)])
                    S.act(lambda e: e.activation(out=Sb[:, 0:nch, :], in_=S32[:, 0:nch, :], func=AF.Copy),
                          [("S32", c) for c in range(nch)], ["Sb"])
                    for li, ti in enumerate(tl_list):
                        t0, nt = TT[ti]
                        S.pe(lambda e, li=li, nt=nt: e.matmul(
                            ps[6][:, li * 128:li * 128 + nt], lhsT=vtok[:nt, li, :], rhs=attT[:nt, li, :nt],
                            start=True, stop=(gi == 0)), [kv, "attT"], ["ps6"])
                        if gi > 0:
                            for hh in range(2):
                                cl = 2 * li + hh
                                S.pe(lambda e, hh=hh, cl=cl: e.matmul(
                                    ps[6][:, cl * 64:(cl + 1) * 64], lhsT=Sb[:, cl, :], rhs=qin[:, cl * 64:(cl + 1) * 64],
                                    start=False, stop=(hh == 1)), ["Sb", kq], ["ps6"])
                    S.dve(lambda e: e.tensor_copy(out=S32[:, 0, :], in_=S32[:, nch, :]), [("S32", nch), "Sb"], [("S32", 0)])
                    self.rstd_from([(ps[6][:, :n], ["ps6"])], n, sqh, rtmp, rstd, ps[5], "ps5", dscale=1.0 / 128)
                    S.dve(lambda e: e.scalar_tensor_tensor(
                        out=O32[:, :n], in0=ps[6][:, :n], scalar=self.colv[:, 80 + hd:81 + hd], in1=rstd[:, :n],
                        op0=ALU.mult, op1=ALU.mult), ["ps6", "rstd"], ["O32"])
                    S.pool(lambda e: e.tensor_tensor(out=og[:, hd, g0:g0 + n], in0=O32[:, :n], in1=SG[:, :n], op=ALU.mult),
                           ["O32", kSG], [("og", gi)])

                if nheads > 0:
                    load_head(0)
                    stage1(0)
                for it in range(len(iters)):
                    if it + 1 < len(iters):
                        stage1(it + 1)
                    stage2(it)
            self.S.barrier()
            if self.stop in ("mix0a", "mix0b", "mix0c"):
                return
            with ExitStack() as st:
                self.out_proj_residual(st, self.a_w_out[0], og, 0, 1, "og")

    def ffn(self, s, l):
        S, ps, hT = self.S, self.ps, self.hT
        halves = [(0, 1032), (1032, 1032)]
        BLK = 344
        with ExitStack() as st0:
            halo = self.sb(st0, "halo", [128, 8, 2], BF16)
            S.dve(lambda e: e.memset(halo[:], 0.0), [], ["halo"])
            def half(hf, h0, nh):
                with ExitStack() as st1:
                    act = self.sb(st1, "act", [128, NJ, 1032], BF16)
                    blocks = [(o, BLK) for o in range(0, nh, BLK)]
                    with ExitStack() as st:
                        xn = self.sb(st, "xn2", [128, 8, 1034], BF16)
                        sq = self.sb(st, "sq2", [128, 8, 512], BF16)
                        rtmp = self.sb(st, "rtmp2", [128, 512], F32)
                        rstd = self.sb(st, "rstd2", [128, 512], F32)
                        S.dve(lambda e: e.tensor_copy(out=xn[:, :, 0:2], in_=halo[:]), ["halo"], [("xn2", -1)])
                        subs = list(blocks)
                        for si, (o, nn) in enumerate(subs):
                            g0 = h0 + o
                            self.rstd_from([(hT[:, c, g0:g0 + nn], []) for c in range(8)], nn, sq, rtmp, rstd, ps[2], "ps2")
                            for c in range(8):
                                S.dve(lambda e, c=c, g0=g0, nn=nn, o=o: e.scalar_tensor_tensor(
                                    out=xn[:, c, 2 + o:2 + o + nn], in0=hT[:, c, g0:g0 + nn], scalar=self.gcol(l, 2, c),
                                    in1=rstd[:, :nn], op0=ALU.mult, op1=ALU.mult), ["rstd"], [("xn2", si)])
                        xkeys = [("xn2", -1)] + [("xn2", si) for si in range(len(subs))]
                        S.dve(lambda e, nh=nh: e.tensor_copy(out=halo[:], in_=xn[:, :, nh:nh + 2]), xkeys, ["halo"])
                        wu = [self.sb(st, "wu%d" % i, [128, 8, 2, 128], BF16) for i in range(2)]
                        G32 = [self.sb(st, "G32_%d" % i, [128, 352], F32) for i in range(3)]
                        V32 = [self.sb(st, "V32_%d" % i, [128, 352], F32) for i in range(3)]
                        SGf = [self.sb(st, "SGf_%d" % i, [128, 352], F32) for i in range(3)]
                        wup = self.w_up[l].rearrange("(kc p) n -> p kc n", p=128)
                        units = [(j, bi_, o, nb) for j in range(NJ) for bi_, (o, nb) in enumerate(blocks)]

                        def load_pair(j):
                            mj = 128 if j < 21 else 64
                            sl = j % 2
                            for part in range(2):
                                self.wload(wu[sl][:, :, part, :mj], wup[:, :, part * DFF + j * 128: part * DFF + j * 128 + mj],
                                           sl, ("wu", sl, part))

                        load_pair(0)

                        def front(k):
                            j, bi_, o, nb = units[k]
                            mj = 128 if j < 21 else 64
                            sl = j % 2
                            w = wu[sl]
                            ub = k % 3
                            if bi_ == 0 and j + 1 < NJ:
                                load_pair(j + 1)
                            for part in range(2):
                                bi = part + 2 * ub
                                bank = ps[bi]
                                bk = "ps%d" % bi
                                for kc in range(8):
                                    S.pe(lambda e, bank=bank, part=part, kc=kc: e.matmul(
                                        bank[:mj, :nb + 2], lhsT=w[:, kc, part, :mj], rhs=xn[:, kc, o:o + nb + 2],
                                        start=(kc == 0), stop=(kc == 7)), [("wu", sl, part)] + xkeys, [bk])
                                dst = (G32 if part == 0 else V32)[ub]
                                dk = ("G32" if part == 0 else "V32", ub)
                                S.act(lambda e, bank=bank, dst=dst, part=part: e.activation(
                                    out=dst[:mj, :nb], in_=bank[:mj, 0:nb], func=AF.Identity, scale=self.ccol(l, 0, part, j)[:mj, :]),
                                    [bk], [dk])

                        def taps(k):
                            j, bi_, o, nb = units[k]
                            mj = 128 if j < 21 else 64
                            ub = k % 3
                            for tap in (1, 2):
                                for part in range(2):
                                    bi = part + 2 * ub
                                    bank = ps[bi]
                                    bk = "ps%d" % bi
                                    dst = (G32 if part == 0 else V32)[ub]
                                    dk = ("G32" if part == 0 else "V32", ub)
                                    S.dve(lambda e, bank=bank, dst=dst, part=part, tap=tap: e.scalar_tensor_tensor(
                                        out=dst[:mj, :nb], in0=bank[:mj, tap:tap + nb], scalar=self.ccol(l, tap, part, j)[:mj, :],
                                        in1=dst[:mj, :nb], op0=ALU.mult, op1=ALU.add), [bk, dk], [dk])

                        def back(k):
                            j, bi_, o, nb = units[k]
                            mj = 128 if j < 21 else 64
                            ub = k % 3
                            S.act(lambda e: e.activation(out=SGf[ub][:mj, :nb], in_=G32[ub][:mj, :nb], func=AF.Silu),
                                  [("G32", ub)], [("SGf", ub)])
                            S.dve(lambda e: e.tensor_tensor(out=act[:mj, j, o:o + nb], in0=SGf[ub][:mj, :nb], in1=V32[ub][:mj, :nb],
                                                            op=ALU.mult), [("SGf", ub), ("V32", ub)], [("act", j, bi_)])

                        for k in range(len(units)):
                            front(k)
                            if k > 0:
                                back(k - 1)
                            taps(k)
                        back(len(units) - 1)
                    S.barrier()
                    with ExitStack() as st:
                        wd = [self.sb(st, "wd%d" % i, [128, NJ, 128], BF16) for i in range(2)]
                        mix = self.sb(st, "mixf", [128, 8, 1032], F32)
                        sq3 = self.sb(st, "sq3", [128, 8, 512], BF16)
                        rtmp3 = self.sb(st, "rtmp3", [128, 512], F32)
                        rstd3 = self.sb(st, "rstd3", [128, 512], F32)
                        tmp3 = self.sb(st, "tmp3", [128, 512], F32)
                        def load_dc(dc):
                            sl = dc % 2
                            self.wload(wd[sl][:, 0:21, :],
                                       self.w_down[l, 0:2688, dc * 128:(dc + 1) * 128].rearrange("(j p) n -> p j n", p=128),
                                       sl, ("wd", sl, 0))
                            self.wload(wd[sl][0:64, 21, :], self.w_down[l, 2688:2752, dc * 128:(dc + 1) * 128], sl, ("wd", sl, 1))

                        load_dc(0)
                        for dc in range(8):
                            sl = dc % 2
                            w = wd[sl]
                            if dc + 1 < 8:
                                load_dc(dc + 1)
                            for bi_, (o, nb) in enumerate(blocks):
                                bq = (dc * len(blocks) + bi_) % 2
                                bank = ps[bq]
                                bk = "ps%d" % bq
                                for j in range(NJ):
                                    mj = 128 if j < 21 else 64
                                    S.pe(lambda e, bank=bank, j=j, mj=mj, o=o, nb=nb, w=w: e.matmul(
                                        bank[:, :nb], lhsT=w[:mj, j, :], rhs=act[:mj, j, o:o + nb],
                                        start=(j == 0), stop=(j == NJ - 1)),
                                        [("wd", sl, 0), ("wd", sl, 1), ("act", j, bi_)], [bk])
                                S.act(lambda e, bank=bank, dc=dc, o=o, nb=nb: e.activation(
                                    out=mix[:, dc, o:o + nb], in_=bank[:, :nb], func=AF.Copy), [bk], [("mix", dc, bi_)])
                        for bi_, (o, nb) in enumerate(blocks):
                            g0 = h0 + o
                            self.rstd_from([(mix[:, dc, o:o + nb], [("mix", dc, bi_)]) for dc in range(8)], nb, sq3, rtmp3, rstd3,
                                           ps[2], "ps2")
                            for dc in range(8):
                                S.dve(lambda e, dc=dc, o=o, nb=nb: e.scalar_tensor_tensor(
                                    out=tmp3[:, :nb], in0=mix[:, dc, o:o + nb], scalar=self.gcol(l, 3, dc), in1=rstd3[:, :nb],
                                    op0=ALU.mult, op1=ALU.mult), [("mix", dc, bi_), "rstd"], ["tmp3"])
                                S.pool(lambda e, dc=dc, g0=g0, nb=nb: e.tensor_tensor(
                                    out=hT[:, dc, g0:g0 + nb], in0=hT[:, dc, g0:g0 + nb], in1=tmp3[:, :nb], op=ALU.add),
                                    ["tmp3"], [("hT", dc, hf, bi_)])
                    S.barrier()

            for hf, (h0, nh) in enumerate(halves):
                half(hf, h0, nh)

    def fox(self, s):
        S, ps, psb, hT = self.S, self.ps, self.psb, self.hT
        scale = 1.0 / 8.0
        with ExitStack() as st0:
            O = self.sb(st0, "O", [128, 8, T], BF16)
            with ExitStack() as st:
                xr = self.sb(st, "xr", [128, 8, T], BF16)
                Ctok = self.sb(st, "Ctok", [128, 17, 16], F32)
                Cpb = self.sb(st, "Cpb", [16, T], BF16)
                with ExitStack() as stn:
                    sq = self.sb(stn, "sq4", [128, 8, 512], BF16)
                    rtmp = self.sb(stn, "rtmp4", [128, 512], F32)
                    rstd = self.sb(stn, "rstd4", [128, 512], F32)
                    for gi, (g0, n) in enumerate(GG):
                        self.rstd_from([(hT[:, c, g0:g0 + n], []) for c in range(8)], n, sq, rtmp, rstd, ps[2], "ps2")
                        for c in range(8):
                            S.dve(lambda e, c=c, g0=g0, n=n: e.tensor_tensor(
                                out=xr[:, c, g0:g0 + n], in0=hT[:, c, g0:g0 + n], in1=rstd[:, :n], op=ALU.mult),
                                ["rstd"], [("xr", gi)])
                    S.barrier()
                xrk = [("xr", gi) for gi in range(5)]
                kvw = self.kv_w.rearrange("(kc p) n -> p kc n", p=128)
                wqv = self.b_w_q[0].rearrange("(kc p) n -> p kc n", p=128)
                gkv = self.colv[:, 88:96]
                with ExitStack() as stf:
                    wfg = self.sb(stf, "wfg", [128, 8, 16], BF16)
                    Cp = self.sb(stf, "Cp", [16, T], F32)
                    sp = self.sb(stf, "sp", [16, 512], F32)
                    self.wload(wfg[:, :, :], kvw[:, :, 2048:2064], 0, "wfg")
                    S.dve(lambda e: e.tensor_tensor(out=wfg[:, :, :], in0=wfg[:, :, :],
                                                    in1=gkv.unsqueeze(2).to_broadcast([128, 8, 16]), op=ALU.mult),
                          ["wfg"], ["wfg"])
                    for gi, (g0, n) in enumerate(GG):
                        for kc in range(8):
                            S.pe(lambda e, kc=kc, g0=g0, n=n: e.matmul(ps[0][:16, :n], lhsT=wfg[:, kc, :], rhs=xr[:, kc, g0:g0 + n],
                                                                        start=(kc == 0), stop=(kc == 7)), ["wfg", ("xr", gi)], ["ps0"])
                        S.act(lambda e, n=n: e.activation(out=sp[:, :n], in_=ps[0][:16, :n], func=AF.Exp, scale=-1.0,
                                                          bias=self.nfgb[:, 0:1]), ["ps0", "nfgb2"], ["sp"])
                        S.act(lambda e, n=n: e.activation(out=sp[:, :n], in_=sp[:, :n], func=AF.Ln, scale=1.0,
                                                          bias=self.onec[0:16, 0:1]), ["sp"], ["sp"])
                        init = 0.0 if gi == 0 else Cp[:, g0 - 1:g0]
                        S.dve(lambda e, g0=g0, n=n, init=init: e.tensor_tensor_scan(
                            out=Cp[:, g0:g0 + n], data0=self.onesf[:16, :n], data1=sp[:, :n], initial=init,
                            op0=ALU.mult, op1=ALU.add), ["sp", ("Cp", gi - 1)], [("Cp", gi)])
                    cpk = [("Cp", gi) for gi in range(5)]
                    S.act(lambda e: e.activation(out=Cpb[:, :], in_=Cp[:, :], func=AF.Copy), cpk, ["Cpb"])
                    for ti, (t0, nt) in enumerate(TT):
                        S.pe(lambda e, t0=t0, nt=nt: e.transpose(ps[1][:nt, 0:16], Cp[:, t0:t0 + nt], self.i16[:]), cpk, ["ps1"])
                        S.dve(lambda e, ti=ti, nt=nt: e.tensor_copy(out=Ctok[:nt, ti, :], in_=ps[1][:nt, 0:16]), ["ps1"], ["Ctok"])
                    S.barrier()
                wp = [self.sb(st, "wp%d" % i, [128, 8, 3, 128], BF16) for i in range(2)]
                KTh = [self.sb(st, "KT%d" % i, [65, T], BF16) for i in range(2)]
                QTh = [self.sb(st, "QT%d" % i, [65, T], BF16) for i in range(2)]
                Va = self.sb(st, "Va", [128, 17, 192], BF16)
                NPT = 4
                PT = [self.sb(st, "PT%d" % i, [128, 512], BF16) for i in range(NPT)]
                rinv = [self.sb(st, "rinv%d" % i, [128, 512], F32) for i in range(2)]
                S.pool(lambda e: e.memset(Va[:, :, 64:128], 1.0), [], ["Vones"])
                for hh in range(2):
                    S.pool(lambda e, hh=hh: e.memset(KTh[hh][64:65, :], 1.0), [], [("Kone", hh)])
                g0col = self.colv[:, 32:40]
                itc = 0
                def load_wp(p):
                    sl = p % 2
                    w = wp[sl]
                    self.wload(w[:, :, 0, :], kvw[:, :, p * 128:(p + 1) * 128], sl, ("wp", sl, 0))
                    self.wload(w[:, :, 1, :], kvw[:, :, D + p * 128:D + (p + 1) * 128], sl, ("wp", sl, 1))
                    self.wload(w[:, :, 2, :], wqv[:, :, p * 128:(p + 1) * 128], sl, ("wp", sl, 2))
                    for m in range(3):
                        gc = gkv if m < 2 else g0col
                        S.dve(lambda e, w=w, m=m, gc=gc: e.tensor_tensor(
                            out=w[:, :, m, :], in0=w[:, :, m, :], in1=gc.unsqueeze(2).to_broadcast([128, 8, 128]), op=ALU.mult),
                            [("wp", sl, m)], [("wp", sl, m)])

                load_wp(0)
                for p in range(8):
                    sl = p % 2
                    w = wp[sl]
                    if p + 1 < 8:
                        load_wp(p + 1)
                    for gi, (g0, n) in enumerate(GG):
                        for (m, dst, dk) in ((0, KTh, "KT"), (2, QTh, "QT")):
                            bank = ps[m // 2]
                            bk = "ps%d" % (m // 2)
                            for kc in range(8):
                                S.pe(lambda e, bank=bank, m=m, kc=kc, g0=g0, n=n, w=w: e.matmul(
                                    bank[:, :n], lhsT=w[:, kc, m, :], rhs=xr[:, kc, g0:g0 + n], start=(kc == 0), stop=(kc == 7)),
                                    [("wp", sl, m), ("xr", gi)], [bk])
                            for hh in range(2):
                                S.dve(lambda e, bank=bank, dst=dst, hh=hh, g0=g0, n=n: e.tensor_copy(
                                    out=dst[hh][0:64, g0:g0 + n], in_=bank[hh * 64:(hh + 1) * 64, :n]), [bk], [(dk, hh, gi)])
                        for hh in range(2):
                            h = 2 * p + hh
                            S.pe(lambda e, h=h, g0=g0, n=n: e.matmul(ps[2][0:65, :n], lhsT=self.selq[:, h, :], rhs=Cpb[:, g0:g0 + n],
                                                                      start=True, stop=True), ["Cpb", "selq"], ["ps2"])
                            S.dve(lambda e, hh=hh, g0=g0, n=n: e.tensor_copy(out=QTh[hh][64:65, g0:g0 + n], in_=ps[2][64:65, :n]),
                                  ["ps2"], [("QT", hh, gi)])
                    for kt, (t0, nt) in enumerate(TT):
                        for kc in range(8):
                            S.pe(lambda e, kc=kc, t0=t0, nt=nt, w=w: e.matmul(
                                ps[2][:nt, 0:128], lhsT=xr[:, kc, t0:t0 + nt], rhs=w[:, kc, 1, :], start=(kc == 0), stop=(kc == 7)),
                                [("wp", sl, 1)] + xrk, ["ps2"])
                        S.dve(lambda e, kt=kt, nt=nt: e.tensor_copy(
                            out=Va[:nt, kt, :].rearrange("p (a b) -> p a b", b=64)[:, 0:3:2, :],
                            in_=ps[2][:nt, 0:128].rearrange("p (a b) -> p a b", b=64)), ["ps2"], [("Va", kt)])
                    its = []
                    for hh in range(2):
                        for gi, (g0, n) in enumerate(GG):
                            tl = tiles_of_group(gi)
                            for kt in range(tl[-1] + 1):
                                its.append((hh, gi, kt))
                    LOOK = 2
                    SB = [5, 6, 2]

                    def emit_qk(ix):
                        hh, gi, kt = its[ix]
                        g0, n = GG[gi]
                        tl = tiles_of_group(gi)
                        k0, nk = TT[kt]
                        c0 = (kt - tl[0]) * 128 if (kt in tl and gi > 0) else 0
                        nq = n - c0
                        sbi = SB[(itc0 + ix) % 3]
                        sb_ = ps[sbi]
                        KT, QT = KTh[hh], QTh[hh]
                        S.pe(lambda e: e.matmul(sb_[:nk, :nq], lhsT=KT[0:65, k0:k0 + nk], rhs=QT[0:65, g0 + c0:g0 + c0 + nq],
                                                start=True, stop=True),
                             [("KT", hh, g_) for g_ in range(5)] + [("QT", hh, gi), ("Kone", hh)], ["ps%d" % sbi])

                    def emit_rest(ix, p=p):
                        hh, gi, kt = its[ix]
                        h = 2 * p + hh
                        g0, n = GG[gi]
                        tl = tiles_of_group(gi)
                        last = tl[-1]
                        k0, nk = TT[kt]
                        c0 = (kt - tl[0]) * 128 if (kt in tl and gi > 0) else 0
                        nq = n - c0
                        sbi = SB[(itc0 + ix) % 3]
                        sb_ = ps[sbi]
                        sbk = "ps%d" % sbi
                        pt = PT[(itc0 + ix) % NPT]
                        ptk = ("PT", (itc0 + ix) % NPT)
                        vlo = 0 if hh == 0 else 64
                        orow = hh * 64
                        lrow = 64 - orow
                        obi = 3 + (gi + hh) % 2
                        ob = ps[obi]
                        obk = "ps%d" % obi
                        S.act(lambda e: e.activation(out=pt[:nk, :nq], in_=sb_[:nk, :nq], func=AF.Exp, scale=scale,
                                                     bias=Ctok[:nk, kt, h:h + 1]), [sbk, "Ctok"], [ptk])
                        if kt in tl:
                            qn = min(128, nq)
                            S.pool(lambda e: e.tensor_tensor(out=pt[:nk, 0:qn], in0=pt[:nk, 0:qn], in1=self.triu[:nk, :qn],
                                                             op=ALU.mult), [ptk], [ptk])
                        S.pe(lambda e: e.matmul(ob[:, c0:c0 + nq], lhsT=Va[:nk, kt, vlo:vlo + 128], rhs=pt[:nk, 0:nq],
                                                start=(kt == 0), stop=(kt == last)), [ptk, ("Va", kt), "Vones"], [obk])
                        if kt == last:
                            rv = rinv[(gi + hh) % 2]
                            rk = ("rinv", (gi + hh) % 2)
                            S.dve(lambda e: e.reciprocal(out=rv[orow:orow + 64, :n], in_=ob[lrow:lrow + 64, :n]), [obk], [rk])
                            S.dve(lambda e: e.tensor_tensor(out=O[orow:orow + 64, p, g0:g0 + n], in0=ob[orow:orow + 64, :n],
                                                            in1=rv[orow:orow + 64, :n], op=ALU.mult), [obk, rk], [("O", gi)])

                    itc0 = itc
                    for ix in range(min(LOOK, len(its))):
                        emit_qk(ix)
                    for ix in range(len(its)):
                        if ix + LOOK < len(its):
                            emit_qk(ix + LOOK)
                        emit_rest(ix)
                    itc += len(its)
            self.S.barrier()
            with ExitStack() as st:
                self.out_proj_residual(st, self.b_w_out[0], O, 1, 1, "O")


_CACHE = {}


def _get_nc(nseq=2, stop=None):
    key = (nseq, stop)
    if key not in _CACHE:
        _CACHE[key] = Builder(nseq, stop).build()
    return _CACHE[key]


def kernel(**inputs):
    ncores = 8
    nc = _get_nc(2, None)
    shared = {k: np.ascontiguousarray(np.asarray(v, dtype=np.float32)) for k, v in inputs.items() if k != "x"}
    x = np.ascontiguousarray(np.asarray(inputs["x"], dtype=np.float32))
    in_maps = []
    for c in range(ncores):
        m = dict(shared)
        m["x"] = x[2 * c:2 * c + 2]
        in_maps.append(m)
    res = run_bass_kernel_spmd(nc, in_maps, core_ids=list(range(ncores)))
    return np.concatenate([np.asarray(r["out"]) for r in res.results], axis=0).astype(np.float32)
```

```python
import numpy as np
import concourse.bass as bass
import concourse.mybir as mybir
from concourse.bass_utils import run_bass_kernel_spmd
from contextlib import ExitStack

F32 = mybir.dt.float32
BF16 = mybir.dt.bfloat16
AF = mybir.ActivationFunctionType
ALU = mybir.AluOpType
AX = mybir.AxisListType

SAME_ENG_SYNC = True


class _Op:
    __slots__ = ("eng", "fn", "reads", "writes", "dma_sem", "ndma", "deps",
                 "needs_inc", "token", "waits", "idx", "is_bar")

    def __init__(self, eng, fn, reads, writes, dma_sem=None, ndma=0):
        self.eng = eng
        self.fn = fn
        self.reads = reads
        self.writes = writes
        self.dma_sem = dma_sem
        self.ndma = ndma
        self.deps = []
        self.needs_inc = False
        self.token = None
        self.waits = []
        self.is_bar = False


class Sched:
    CENG = ("pe", "act", "dve", "pool")
    ALLENG = ("pe", "act", "dve", "pool", "sp")

    def __init__(self, nc, stack):
        self.nc = nc
        self.stack = stack
        self.ops = []
        self.esem = {e: stack.enter_context(nc.semaphore("s_" + e)) for e in self.CENG}
        self.dma_cum = {}
        self.dma_sems = {}
        self.dma_exempt = set()
        self.last_w = {}
        self.readers = {}
        self.last_op = {e: None for e in self.ALLENG}
        self.dma_last = {}

    def dma_sem(self, name, exempt=False):
        if name not in self.dma_sems:
            self.dma_sems[name] = self.stack.enter_context(self.nc.semaphore("d_" + name))
            self.dma_cum[name] = 0
            if exempt:
                self.dma_exempt.add(name)
        return name

    def _add(self, op):
        op.idx = len(self.ops)
        deps = set()
        for k in op.reads:
            w = self.last_w.get(k)
            if w is not None:
                deps.add(w)
        for k in op.writes:
            w = self.last_w.get(k)
            if w is not None:
                deps.add(w)
            for r in self.readers.get(k, ()):
                deps.add(r)
        deps.discard(op)
        op.deps = sorted(deps, key=lambda o: o.idx)
        for k in op.reads:
            self.readers.setdefault(k, []).append(op)
        for k in op.writes:
            self.last_w[k] = op
            self.readers[k] = []
        self.ops.append(op)
        self.last_op[op.eng] = op
        if op.dma_sem is not None:
            self.dma_cum[op.dma_sem] += 16 * op.ndma
            op.token = (op.dma_sem, self.dma_cum[op.dma_sem])
            self.dma_last[op.dma_sem] = op
        return op

    def op(self, eng, fn, reads=(), writes=()):
        return self._add(_Op(eng, fn, tuple(reads), tuple(writes)))

    def pe(self, fn, reads=(), writes=()):
        return self.op("pe", fn, reads, writes)

    def act(self, fn, reads=(), writes=()):
        return self.op("act", fn, reads, writes)

    def dve(self, fn, reads=(), writes=()):
        return self.op("dve", fn, reads, writes)

    def pool(self, fn, reads=(), writes=()):
        return self.op("pool", fn, reads, writes)

    def dma(self, eng, fn, sem, reads=(), writes=(), n=1):
        return self._add(_Op(eng, fn, tuple(reads), tuple(writes), dma_sem=sem, ndma=n))

    def barrier(self):
        prev = dict(self.last_op)
        dl = {k: v for k, v in self.dma_last.items() if k not in self.dma_exempt}
        for e in self.ALLENG:
            b = _Op(e, None, (), ())
            b.is_bar = True
            b.idx = len(self.ops)
            b.deps = [o for ee, o in prev.items() if o is not None and (ee != e or (SAME_ENG_SYNC and e != 'pe'))] + list(dl.values())
            self.ops.append(b)
            self.last_op[e] = b

    def finalize(self):
        for op in self.ops:
            for d in op.deps:
                if d.dma_sem is not None or d.is_bar:
                    continue
                if d.eng == op.eng and (d.eng == "pe" or not SAME_ENG_SYNC):
                    continue
                d.needs_inc = True
        cnt = {e: 0 for e in self.CENG}
        for op in self.ops:
            if op.dma_sem is None and op.needs_inc:
                cnt[op.eng] += 1
                op.token = (op.eng, cnt[op.eng])
        known = {e: {} for e in self.ALLENG}
        for op in self.ops:
            kn = known[op.eng]
            need = {}
            for d in op.deps:
                if d.token is None:
                    continue
                if d.dma_sem is None and d.eng == op.eng and (d.eng == "pe" or not SAME_ENG_SYNC):
                    continue
                s, v = d.token
                if need.get(s, 0) < v:
                    need[s] = v
            for s, v in need.items():
                if kn.get(s, 0) >= v:
                    continue
                kn[s] = v
                op.waits.append((s, v))
        self.counts = cnt

    def _sem(self, s):
        return self.esem[s] if s in self.esem else self.dma_sems[s]

    def emit(self):
        self.finalize()
        by_eng = {e: [o for o in self.ops if o.eng == e] for e in self.ALLENG}
        with self.nc.Block() as block:
            def run(e):
                def body(eng):
                    for op in by_eng[e]:
                        for (s, v) in op.waits:
                            eng.wait_ge(self._sem(s), v)
                        if op.fn is None:
                            continue
                        if op.dma_sem is not None:
                            op.fn(eng, self.dma_sems[op.dma_sem])
                        else:
                            ins = op.fn(eng)
                            if op.needs_inc:
                                ins.then_inc(self.esem[e], 1)
                return body
            block.tensor(run("pe"))
            block.scalar(run("act"))
            block.vector(run("dve"))
            block.gpsimd(run("pool"))
            block.sync(run("sp"))


import os
CUT = int(os.environ.get('KCUT', '99'))
D = 1024
T = 2064
NMETA = 16
DFF = 2752
NJ = 22
EPS = 1e-6
TT = [(0, 16)] + [(16 + 128 * i, 128) for i in range(16)]
GG = [(0, 16)] + [(16 + 512 * j, 512) for j in range(4)]


def tiles_of_group(gi):
    return [0] if gi == 0 else list(range(1 + 4 * (gi - 1), 1 + 4 * gi))


class Builder:
    def __init__(self, nseq=2, stop=None):
        self.nseq = nseq
        self.stop = stop
        nc = self.nc = bass.Bass("TRN2", target_bir_lowering=False)
        dt = lambda name, shape: nc.dram_tensor(name, shape, F32, kind="ExternalInput").ap()
        self.x = dt("x", [nseq, 2048, D])
        self.meta = dt("meta_tokens", [NMETA, D])
        self.norm_gains = dt("norm_gains", [2, 4, D])
        self.a_w_in = dt("a_w_in", [1, D, 4 * D])
        self.a_lb = dt("a_lb_logits", [2, D])
        self.a_hn = dt("a_head_norm", [1, D])
        self.a_w_out = dt("a_w_out", [1, D, D])
        self.kv_norm = dt("kv_norm", [D])
        self.kv_w = dt("kv_w", [D, 2 * D + 16])
        self.fg_b = dt("fg_b", [16])
        self.b_w_q = dt("b_w_q", [1, D, D])
        self.b_w_out = dt("b_w_out", [1, D, D])
        self.w_up = dt("ffn_w_up", [2, D, 2 * DFF])
        self.conv = dt("ffn_conv", [2, 3, 2 * DFF])
        self.w_down = dt("ffn_w_down", [2, DFF, D])
        self.out = nc.dram_tensor("out", [nseq, 2048, D], F32, kind="ExternalOutput").ap()
        self.uid = 0

    def sb(self, st, name, shape, dtype):
        self.uid += 1
        return st.enter_context(self.nc.sbuf_tensor("%s_%d" % (name, self.uid), shape, dtype))

    def build(self):
        nc = self.nc
        with ExitStack() as st:
            S = self.S = Sched(nc, st)
            self.ps = [st.enter_context(nc.psum_tensor("ps%d" % i, [128, 512], F32)) for i in range(7)]
            self.psb = st.enter_context(nc.psum_tensor("psb", [128, 1024], BF16))
            self.hT = self.sb(st, "hT", [128, 8, T], F32)
            self.consts(st)
            S.barrier()
            for s in range(self.nseq):
                self.seq(s)
            S.barrier()
            S.emit()
        return nc

    def consts(self, st):
        S = self.S
        self.ident = self.sb(st, "ident", [128, 128], F32)
        self.identb = self.sb(st, "identb", [128, 128], BF16)
        self.onesb = self.sb(st, "onesb", [128, 128], BF16)
        self.triu = self.sb(st, "triu", [128, 128], BF16)
        self.mask2 = self.sb(st, "mask2", [128, 128], F32)
        self.maskseg = self.sb(st, "maskseg", [128, 512], F32)
        self.epsc = self.sb(st, "epsc", [128, 1], F32)
        self.colv = self.sb(st, "colv", [128, 96], F32)
        self.convT = self.sb(st, "convT", [128, 3, 128], F32)
        self.lbc = self.sb(st, "lbc", [128, 24], F32)
        self.nfgb = self.sb(st, "nfgb", [16, 1], F32)
        self.i16 = self.sb(st, "i16", [16, 16], F32)
        self.ones16 = self.sb(st, "ones16", [16, 128], F32)
        self.rinvs = self.sb(st, "rinvs", [128, 512], F32)
        self.onesf = self.sb(st, "onesf", [16, 512], F32)
        self.onec = self.sb(st, "onec", [128, 1], F32)
        self.selq = self.sb(st, "selq", [16, 16, 65], BF16)
        P = lambda fn, r=(), w=(): S.pool(fn, r, w)
        P(lambda e: e.memset(self.ident[:], 1.0), w=["ident"])
        P(lambda e: e.affine_select(out=self.ident[:], in_=self.ident[:], pattern=[[-1, 128]],
                                    compare_op=ALU.is_equal, fill=0.0, base=0, channel_multiplier=1),
          r=["ident"], w=["ident"])
        S.dve(lambda e: e.tensor_copy(out=self.identb[:], in_=self.ident[:]), ["ident"], ["identb"])
        S.dve(lambda e: e.tensor_copy(out=self.i16[:], in_=self.ident[0:16, 0:16]), ["ident"], ["i16"])
        P(lambda e: e.memset(self.onesb[:], 1.0), w=["onesb"])
        P(lambda e: e.memset(self.ones16[:], 1.0), w=["ones16"])
        P(lambda e: e.memset(self.triu[:], 1.0), w=["triu"])
        P(lambda e: e.affine_select(out=self.triu[:], in_=self.triu[:], pattern=[[1, 128]],
                                    compare_op=ALU.is_ge, fill=0.0, base=0, channel_multiplier=-1),
          r=["triu"], w=["triu"])
        P(lambda e: e.memset(self.mask2[:], 1.0), w=["mask2"])
        P(lambda e: e.affine_select(out=self.mask2[:], in_=self.mask2[:], pattern=[[1, 128]],
                                    compare_op=ALU.is_ge, fill=0.0, base=0, channel_multiplier=-1),
          r=["mask2"], w=["mask2"])
        P(lambda e: e.memset(self.mask2[0:64, 64:128], 0.0), r=["mask2"], w=["mask2"])
        P(lambda e: e.memset(self.maskseg[:], 1.0), w=["maskseg"])
        P(lambda e: e.memset(self.maskseg[:].rearrange("p (c k) -> p c k", k=64)[:, :, 0:1], 0.0),
          r=["maskseg"], w=["maskseg"])
        P(lambda e: e.memset(self.epsc[:], EPS), w=["epsc"])
        P(lambda e: e.memset(self.onesf[:], 1.0), w=["onesf"])
        P(lambda e: e.memset(self.onec[:], 1.0), w=["onec"])
        P(lambda e: e.memset(self.selq[:], 0.0), w=["selq0"])
        S.dve(lambda e: e.tensor_scalar(out=self.selq[:, :, 64], in0=self.i16[:, :], scalar1=-8.0, scalar2=None, op0=ALU.mult),
              ["selq0", "i16"], ["selq"])
        rowsA = self.sb(st, "rowsA", [96, 128], F32)
        rowsC = self.sb(st, "rowsC", [128, 3, 128], F32)
        P(lambda e: e.memset(rowsC[:], 0.0), w=["rowsC"])
        cs = S.dma_sem("const")
        nd = [0]

        def ld(dst, src, rk):
            S.dma("sp", lambda e, s, dst=dst, src=src: e.dma_start(out=dst, in_=src).then_inc(s, 16), cs,
                  reads=[rk], writes=[("rowsd", nd[0])])
            nd[0] += 1
        ld(rowsA[0:64, :], self.norm_gains.rearrange("l j (c p) -> (l j c) p", p=128), "rowsA")
        ld(rowsA[64:80, :], self.a_lb.rearrange("l (c p) -> (l c) p", p=128), "rowsA")
        ld(rowsA[80:88, :], self.a_hn.rearrange("l (c p) -> (l c) p", p=128), "rowsA")
        ld(rowsA[88:96, :], self.kv_norm.rearrange("(c p) -> c p", p=128), "rowsA")
        for l in range(2):
            for tap in range(3):
                for part in range(2):
                    r0 = ((l * 3 + tap) * 2 + part) * 22
                    src = self.conv[l, tap, part * DFF: part * DFF + 2688].rearrange("(j k) -> j k", k=128)
                    done = 0
                    while done < 21:
                        ti, ri = divmod(r0 + done, 128)
                        cnt = min(21 - done, 128 - ri)
                        ld(rowsC[ri:ri + cnt, ti, :], src[done:done + cnt, :], "rowsC")
                        done += cnt
                    ti, ri = divmod(r0 + 21, 128)
                    ld(rowsC[ri:ri + 1, ti, 0:64],
                       self.conv[l, tap, part * DFF + 2688: part * DFF + 2752].rearrange("(a k) -> a k", a=1), "rowsC")
        ld(self.nfgb[:, :], self.fg_b.rearrange("(h a) -> h a", a=1), "nfgb")
        allrows = [("rowsd", i) for i in range(nd[0])]
        ps = self.ps
        S.pe(lambda e: e.transpose(ps[0][:, 0:96], rowsA[:, :], self.ident[0:96, 0:96]), allrows + ["ident"], ["ps0"])
        S.dve(lambda e: e.tensor_copy(out=self.colv[:], in_=ps[0][:, 0:96]), ["ps0"], ["colv"])
        for ti in range(3):
            S.pe(lambda e, ti=ti: e.transpose(ps[1][:, ti * 128:(ti + 1) * 128], rowsC[:, ti, :], self.ident[:]),
                 allrows + ["ident", "rowsC"], ["ps1"])
        S.dve(lambda e: e.tensor_copy(out=self.convT[:], in_=ps[1][:, 0:384].rearrange("p (a b) -> p a b", b=128)),
              ["ps1"], ["convT"])
        dl = self.sb(st, "dl", [128, 8], F32)
        S.dve(lambda e: e.tensor_tensor(out=dl[:], in0=self.colv[:, 64:72], in1=self.colv[:, 72:80], op=ALU.subtract),
              ["colv"], ["dl"])
        S.act(lambda e: e.activation(out=self.lbc[:, 0:8], in_=dl[:], func=AF.Sigmoid), ["dl"], ["lbc0"])
        S.act(lambda e: e.activation(out=self.lbc[:, 8:16], in_=dl[:], func=AF.Sigmoid, scale=-1.0), ["dl"], ["lbc1"])
        S.dve(lambda e: e.tensor_scalar(out=self.lbc[:, 16:24], in0=self.lbc[:, 8:16], scalar1=-1.0, scalar2=None,
                                        op0=ALU.mult), ["lbc1"], ["lbc2"])
        S.dve(lambda e: e.tensor_scalar(out=self.nfgb[:], in0=self.nfgb[:], scalar1=-1.0, scalar2=None, op0=ALU.mult),
              allrows, ["nfgb2"])

    def gcol(self, l, j, c):
        k = (l * 4 + j) * 8 + c
        return self.colv[:, k:k + 1]

    def ccol(self, l, tap, part, j):
        r = ((l * 3 + tap) * 2 + part) * 22 + j
        ti, ri = divmod(r, 128)
        return self.convT[:, ti, ri:ri + 1]

    def wload(self, dst, src, slot, key, reads=()):
        S = self.S
        sem = S.dma_sem("w_" + "_".join(str(k) for k in (key if isinstance(key, tuple) else (key,))), exempt=True)
        S.dma("pool", lambda e, s: e.dma_start(out=dst, in_=src).then_inc(s, 16), sem,
              reads=list(reads), writes=[key])

    def rstd_from(self, srcs, n, sq, rtmp, rstd, pst, pkey, dscale=1.0 / D):
        S = self.S
        nsrc = len(srcs)
        for c, (ap, rk) in enumerate(srcs):
            S.act(lambda e, ap=ap, c=c: e.activation(out=sq[:, c, :n], in_=ap, func=AF.Square), rk, [("sq", c)])
            S.pe(lambda e, c=c: e.matmul(pst[:, :n], lhsT=self.onesb[:], rhs=sq[:, c, :n], start=(c == 0),
                                         stop=(c == nsrc - 1)), [("sq", c)], [pkey])
        S.act(lambda e: e.activation(out=rtmp[:, :n], in_=pst[:, :n], func=AF.Ln, scale=dscale, bias=self.epsc[:, 0:1]),
              [pkey], ["rtmp"])
        S.act(lambda e: e.activation(out=rstd[:, :n], in_=rtmp[:, :n], func=AF.Exp, scale=-0.5), ["rtmp"], ["rstd"])

    def seq(self, s):
        S = self.S
        self.load_x(s)
        S.barrier()
        if self.stop != "load":
            self.hgrn2(s)
            S.barrier()
            if self.stop not in ("mix0", "mix0a", "mix0b", "mix0c"):
                self.ffn(s, 0)
                S.barrier()
                if self.stop != "ffn0":
                    self.fox(s)
                    S.barrier()
                    if self.stop != "mix1":
                        self.ffn(s, 1)
                        S.barrier()
        self.store(s)
        S.barrier()

    def load_x(self, s):
        S, ps, hT = self.S, self.ps, self.hT
        with ExitStack() as st:
            xin = [self.sb(st, "xin%d" % i, [128, D], F32) for i in range(2)]
            xs = [S.dma_sem("xin%d" % i) for i in range(2)]
            for ti, (t0, n) in enumerate(TT):
                sl = ti % 2
                src = self.meta if ti == 0 else self.x[s, t0 - 16:t0 - 16 + 128, :]
                S.dma("sp", lambda e, sm, sl=sl, src=src, n=n: e.dma_start(out=xin[sl][:n, :], in_=src).then_inc(sm, 16),
                      xs[sl], writes=[("xin", sl)])
                for half in range(2):
                    bank = ps[half + 2 * sl]
                    bk = "ps%d" % (half + 2 * sl)
                    for j in range(4):
                        c = half * 4 + j
                        S.pe(lambda e, bank=bank, j=j, c=c, n=n, sl=sl: e.transpose(
                            bank[:, j * 128:j * 128 + n], xin[sl][:n, c * 128:(c + 1) * 128], self.ident[:n, :n]),
                            [("xin", sl)], [bk])
                    fn = lambda e, bank=bank, half=half, t0=t0, n=n: e.tensor_copy(
                        out=hT[:, half * 4:(half + 1) * 4, t0:t0 + n],
                        in_=bank[:, :].rearrange("p (j k) -> p j k", k=128)[:, :, 0:n])
                    if half == 0:
                        S.dve(fn, [bk], [("hT", ti, half)])
                    else:
                        S.act(lambda e, bank=bank, half=half, t0=t0, n=n: e.activation(
                            out=hT[:, half * 4:(half + 1) * 4, t0:t0 + n],
                            in_=bank[:, :].rearrange("p (j k) -> p j k", k=128)[:, :, 0:n], func=AF.Copy),
                            [bk], [("hT", ti, half)])

    def store(self, s):
        S, ps, hT = self.S, self.ps, self.hT
        with ExitStack() as st:
            xo = [self.sb(st, "xo%d" % i, [128, D], F32) for i in range(2)]
            os_ = [S.dma_sem("xo%d" % i) for i in range(2)]
            for ti, (t0, n) in enumerate(TT):
                if ti == 0:
                    continue
                sl = ti % 2
                for half in range(2):
                    bank = ps[half + 2 * sl]
                    bk = "ps%d" % (half + 2 * sl)
                    for j in range(4):
                        c = half * 4 + j
                        S.pe(lambda e, bank=bank, j=j, c=c, t0=t0: e.transpose(
                            bank[:, j * 128:(j + 1) * 128], hT[:, c, t0:t0 + 128], self.ident[:]), [], [bk])
                    if half == 0:
                        S.dve(lambda e, bank=bank, sl=sl: e.tensor_copy(out=xo[sl][:, 0:512], in_=bank[:, :]),
                              [bk], [("xo", sl)])
                    else:
                        S.act(lambda e, bank=bank, sl=sl: e.activation(out=xo[sl][:, 512:1024], in_=bank[:, :], func=AF.Copy),
                              [bk], [("xo", sl)])
                S.dma("sp", lambda e, sm, sl=sl, t0=t0: e.dma_start(out=self.out[s, t0 - 16:t0 - 16 + 128, :],
                                                                     in_=xo[sl][:, :]).then_inc(sm, 16),
                      os_[sl], reads=[("xo", sl)], writes=[("xo", sl)])

    def out_proj_residual(self, st, wsrc, src_act, l, jn, tag):
        S, ps, hT = self.S, self.ps, self.hT
        wo = self.sb(st, "wo", [128, 8, D], BF16)
        mix32 = self.sb(st, "mix32", [128, 8, 512], F32)
        sq = self.sb(st, "sqo", [128, 8, 512], BF16)
        rtmp = self.sb(st, "rtmpo", [128, 512], F32)
        rstd = self.sb(st, "rstdo", [128, 512], F32)
        tmp = self.sb(st, "tmpo", [128, 512], F32)
        wv = wsrc.rearrange("(kc p) n -> p kc n", p=128)
        for kc in range(8):
            self.wload(wo[:, kc, :], wv[:, kc, :], kc % 2, ("wo", kc))
        for gi, (g0, n) in enumerate(GG):
            for dc in range(8):
                bank = ps[dc % 2]
                bk = "ps%d" % (dc % 2)
                for kc in range(8):
                    S.pe(lambda e, bank=bank, dc=dc, kc=kc, g0=g0, n=n: e.matmul(
                        bank[:, :n], lhsT=wo[:, kc, dc * 128:(dc + 1) * 128], rhs=src_act[:, kc, g0:g0 + n],
                        start=(kc == 0), stop=(kc == 7)), [("wo", kc), (tag, gi)], [bk])
                S.act(lambda e, bank=bank, dc=dc, n=n: e.activation(out=mix32[:, dc, :n], in_=bank[:, :n], func=AF.Copy),
                      [bk], [("mix32", dc)])
            self.rstd_from([(mix32[:, dc, :n], [("mix32", dc)]) for dc in range(8)], n, sq, rtmp, rstd, ps[2], "ps2")
            for dc in range(8):
                S.dve(lambda e, dc=dc, n=n: e.scalar_tensor_tensor(
                    out=tmp[:, :n], in0=mix32[:, dc, :n], scalar=self.gcol(l, jn, dc), in1=rstd[:, :n],
                    op0=ALU.mult, op1=ALU.mult), [("mix32", dc), "rstd"], ["tmpo"])
                S.dve(lambda e, dc=dc, g0=g0, n=n: e.tensor_tensor(
                    out=hT[:, dc, g0:g0 + n], in0=hT[:, dc, g0:g0 + n], in1=tmp[:, :n], op=ALU.add),
                    ["tmpo"], [("hT", dc, gi)])

    def hgrn2(self, s):
        S, ps, psb, hT = self.S, self.ps, self.psb, self.hT
        with ExitStack() as st0:
            og = self.sb(st0, "og", [128, 8, T], BF16)
            with ExitStack() as st:
                xn = self.sb(st, "xn", [128, 8, T], BF16)
                sq = self.sb(st, "sq", [128, 8, 512], BF16)
                rtmp = self.sb(st, "rtmp", [128, 512], F32)
                rstd = self.sb(st, "rstd", [128, 512], F32)
                for gi, (g0, n) in enumerate(GG):
                    self.rstd_from([(hT[:, c, g0:g0 + n], []) for c in range(8)], n, sq, rtmp, rstd, ps[2], "ps2")
                    for c in range(8):
                        S.dve(lambda e, c=c, g0=g0, n=n: e.scalar_tensor_tensor(
                            out=xn[:, c, g0:g0 + n], in0=hT[:, c, g0:g0 + n], scalar=self.gcol(0, 0, c), in1=rstd[:, :n],
                            op0=ALU.mult, op1=ALU.mult), ["rstd"], [("xn", gi)])
                wh = [self.sb(st, "wh%d" % i, [128, 8, 4, 128], BF16) for i in range(2)]
                A = self.sb(st, "A", [128, 512], F32)
                C = self.sb(st, "C", [128, 512], F32)
                Dn = self.sb(st, "Dn", [128, 512], F32)
                Bs = [self.sb(st, "B%d" % i, [128, 512], F32) for i in range(2)]
                SGs = [self.sb(st, "SG%d" % i, [128, 512], F32) for i in range(2)]
                qins = [self.sb(st, "qin%d" % i, [128, 512], BF16) for i in range(2)]
                kins = [self.sb(st, "kin%d" % i, [128, 512], BF16) for i in range(2)]
                kouts = [self.sb(st, "kout%d" % i, [128, 512], BF16) for i in range(2)]
                vtoks = [self.sb(st, "vtok%d" % i, [128, 4, 128], BF16) for i in range(2)]
                O32 = self.sb(st, "O32", [128, 512], F32)
                sqh = self.sb(st, "sqh", [128, 1, 512], BF16)
                ktok = self.sb(st, "ktok", [128, 4, 128], BF16)
                attT = self.sb(st, "attT", [128, 4, 128], BF16)
                S32 = self.sb(st, "S32", [128, 9, 128], F32)
                Sb = self.sb(st, "Sb", [128, 8, 128], BF16)
                win = self.a_w_in[0].rearrange("(kc p) n -> p kc n", p=128)

                def load_head(hd):
                    sl = hd % 2
                    for j in range(4):
                        self.wload(wh[sl][:, :, j, :], win[:, :, j * D + hd * 128: j * D + (hd + 1) * 128], sl, ("wh", sl, j))

                nheads = {"mix0a": 0, "mix0b": 1, "mix0c": 1}.get(self.stop, 8)
                iters = [(hd, gi) for hd in range(nheads) for gi in range(5)]

                def stage1(it):
                    hd, gi = iters[it]
                    g0, n = GG[gi]
                    z = it % 2
                    sl = hd % 2
                    w = wh[sl]
                    wk = [("wh", sl, j) for j in range(4)]
                    B, SG, qin, kin, kout, vtok = Bs[z], SGs[z], qins[z], kins[z], kouts[z], vtoks[z]
                    kB, kSG, kq, kk_, ko, kv = ("B", z), ("SG", z), ("qin", z), ("kin", z), ("kout", z), ("vtok", z)
                    tl_list = tiles_of_group(gi)
                    if gi == 0 and hd + 1 < nheads:
                        load_head(hd + 1)

                    def proj(j, bi):
                        for kc in range(8):
                            S.pe(lambda e, kc=kc: e.matmul(ps[bi][:, :n], lhsT=w[:, kc, j, :], rhs=xn[:, kc, g0:g0 + n],
                                                           start=(kc == 0), stop=(kc == 7)), [wk[j], ("xn", gi)], ["ps%d" % bi])
                    proj(1, 1)
                    S.act(lambda e: e.activation(out=A[:, :n], in_=ps[1][:, :n], func=AF.Sigmoid), ["ps1"], ["A"])
                    proj(0, 0)
                    proj(3, 1)
                    S.act(lambda e: e.activation(out=SG[:, :n], in_=ps[1][:, :n], func=AF.Silu), ["ps1"], [kSG])
                    for li, ti in enumerate(tl_list):
                        t0, nt = TT[ti]
                        for kc in range(8):
                            S.pe(lambda e, li=li, kc=kc, t0=t0, nt=nt: e.matmul(
                                ps[2][:nt, li * 128:(li + 1) * 128], lhsT=xn[:, kc, t0:t0 + nt], rhs=w[:, kc, 2, :],
                                start=(kc == 0), stop=(kc == 7)), [wk[2], ("xn", gi)], ["ps2"])
                    if gi == 0:
                        S.act(lambda e: e.activation(out=vtok[:16, 0, :], in_=ps[2][:16, 0:128], func=AF.Copy), ["ps2"], [kv])
                    else:
                        S.act(lambda e: e.activation(out=vtok[:, :, :], in_=ps[2][:, :].rearrange("p (a b) -> p a b", b=128),
                                                     func=AF.Copy), ["ps2"], [kv])
                    S.act(lambda e: e.activation(out=B[:, :n], in_=A[:, :n], func=AF.Ln,
                                                 scale=self.lbc[:, 8 + hd:9 + hd], bias=self.lbc[:, hd:hd + 1]), ["A"], [kB])
                    S.dve(lambda e: e.tensor_scalar(out=C[:, :n], in0=A[:, :n], scalar1=self.lbc[:, 16 + hd:17 + hd],
                                                    scalar2=self.lbc[:, 8 + hd:9 + hd], op0=ALU.mult, op1=ALU.add), ["A"], ["C"])
                    S.dve(lambda e: e.tensor_tensor_scan(out=A[:, :n], data0=self.maskseg[:, :n], data1=B[:, :n],
                                                         initial=0.0, op0=ALU.mult, op1=ALU.add), [kB, "A"], ["A"])
                    S.act(lambda e: e.activation(out=B[:, :n], in_=A[:, :n], func=AF.Exp), ["A"], [kB])
                    S.act(lambda e: e.activation(out=Dn[:, :n], in_=A[:, :n], func=AF.Exp, scale=-1.0), ["A"], ["Dn"])
                    S.dve(lambda e: e.tensor_tensor(out=qin[:, :n], in0=ps[0][:, :n], in1=B[:, :n], op=ALU.mult),
                          ["ps0", kB], [kq])
                    S.dve(lambda e: e.tensor_tensor(out=C[:, :n], in0=C[:, :n], in1=Dn[:, :n], op=ALU.mult), ["C", "Dn"], ["C"])
                    S.act(lambda e: e.activation(out=kin[:, :n], in_=C[:, :n], func=AF.Copy), ["C"], [kk_])
                    if gi == 0:
                        S.dve(lambda e: e.tensor_scalar(out=kout[:, :16], in0=C[:, :16], scalar1=B[:, 15:16], scalar2=None,
                                                        op0=ALU.mult), ["C", kB], [ko])
                    else:
                        S.dve(lambda e: e.tensor_tensor(
                            out=kout[:, :].rearrange("p (c k) -> p c k", k=64),
                            in0=C[:, :].rearrange("p (c k) -> p c k", k=64),
                            in1=B[:, :].rearrange("p (c k) -> p c k", k=64)[:, :, 63:64].to_broadcast([128, 8, 64]),
                            op=ALU.mult), ["C", kB], [ko])

                def stage2(it):
                    hd, gi = iters[it]
                    g0, n = GG[gi]
                    z = it % 2
                    B, SG, qin, kin, kout, vtok = Bs[z], SGs[z], qins[z], kins[z], kouts[z], vtoks[z]
                    kB, kSG, kq, kk_, ko, kv = ("B", z), ("SG", z), ("qin", z), ("kin", z), ("kout", z), ("vtok", z)
                    tl_list = tiles_of_group(gi)
                    nch = 1 if gi == 0 else 8
                    if gi == 0:
                        S.dve(lambda e: e.memset(S32[:, 0, :], 0.0), [], [("S32", 0)])
                    for li, ti in enumerate(tl_list):
                        t0, nt = TT[ti]
                        S.pe(lambda e, li=li, nt=nt: e.transpose(psb[:nt, li * 128:(li + 1) * 128],
                                                                  kout[:, li * 128:li * 128 + nt], self.identb[:]), [ko], ["psb"])
                    if gi == 0:
                        S.act(lambda e: e.activation(out=ktok[:16, 0, :], in_=psb[:16, 0:128], func=AF.Copy), ["psb"], ["ktok"])
                    else:
                        S.act(lambda e: e.activation(out=ktok[:, :, :], in_=psb[:, 0:512].rearrange("p (a b) -> p a b", b=128),
                                                     func=AF.Copy), ["psb"], ["ktok"])
                    for li, ti in enumerate(tl_list):
                        t0, nt = TT[ti]
                        S.pe(lambda e, li=li, nt=nt: e.matmul(ps[5][:nt, li * 128:li * 128 + nt], lhsT=kin[:, li * 128:li * 128 + nt],
                                                               rhs=qin[:, li * 128:li * 128 + nt], start=True, stop=True),
                             [kk_, kq], ["ps5"])
                    if gi == 0:
                        S.dve(lambda e: e.tensor_tensor(out=attT[:16, 0, :16], in0=ps[5][:16, 0:16], in1=self.mask2[:16, :16],
                                                        op=ALU.mult), ["ps5"], ["attT"])
                    else:
                        S.dve(lambda e: e.tensor_tensor(
                            out=attT[:, :, :], in0=ps[5][:, :].rearrange("p (a b) -> p a b", b=128),
                            in1=self.mask2[:, :].unsqueeze(1).to_broadcast([128, 4, 128]), op=ALU.mult), ["ps5"], ["attT"])
                    for cl in range(nch):
                        li, r0 = cl // 2, (cl % 2) * 64
                        nr = 16 if gi == 0 else 64
                        bi = 3 + cl % 2
                        S.pe(lambda e, cl=cl, li=li, r0=r0, nr=nr, bi=bi: e.matmul(
                            ps[bi][:, (cl // 2) * 128:(cl // 2 + 1) * 128], lhsT=ktok[r0:r0 + nr, li, :],
                            rhs=vtok[r0:r0 + nr, li, :], start=True, stop=True), ["ktok", kv], ["ps%d" % bi])
                    for cl in range(nch):
                        bi = 3 + cl % 2
                        dcol = B[:, 15:16] if gi == 0 else B[:, cl * 64 + 63:cl * 64 + 64]
                        S.dve(lambda e, cl=cl, bi=bi, dcol=dcol: e.scalar_tensor_tensor(
                            out=S32[:, cl + 1, :], in0=S32[:, cl, :], scalar=dcol,
                            in1=ps[bi][:, (cl // 2) * 128:(cl // 2 + 1) * 128], op0=ALU.mult, op1=ALU.add),
                            [("S32", cl), kB, "ps%d" % bi], [("S32", cl + 1)])
                    S.act(lambda e: e.activation(out=Sb[:, 0:nch, :], in_=S32[:, 0:nch, :], func=AF.Copy),
                          [("S32", c) for c in range(nch)], ["Sb"])
                    for li, ti in enumerate(tl_list):
                        t0, nt = TT[ti]
                        S.pe(lambda e, li=li, nt=nt: e.matmul(
                            ps[6][:, li * 128:li * 128 + nt], lhsT=vtok[:nt, li, :], rhs=attT[:nt, li, :nt],
                            start=True, stop=(gi == 0)), [kv, "attT"], ["ps6"])
                        if gi > 0:
                            for hh in range(2):
                                cl = 2 * li + hh
                                S.pe(lambda e, hh=hh, cl=cl: e.matmul(
                                    ps[6][:, cl * 64:(cl + 1) * 64], lhsT=Sb[:, cl, :], rhs=qin[:, cl * 64:(cl + 1) * 64],
                                    start=False, stop=(hh == 1)), ["Sb", kq], ["ps6"])
                    S.dve(lambda e: e.tensor_copy(out=S32[:, 0, :], in_=S32[:, nch, :]), [("S32", nch), "Sb"], [("S32", 0)])
                    self.rstd_from([(ps[6][:, :n], ["ps6"])], n, sqh, rtmp, rstd, ps[5], "ps5", dscale=1.0 / 128)
                    S.dve(lambda e: e.scalar_tensor_tensor(
                        out=O32[:, :n], in0=ps[6][:, :n], scalar=self.colv[:, 80 + hd:81 + hd], in1=rstd[:, :n],
                        op0=ALU.mult, op1=ALU.mult), ["ps6", "rstd"], ["O32"])
                    S.pool(lambda e: e.tensor_tensor(out=og[:, hd, g0:g0 + n], in0=O32[:, :n], in1=SG[:, :n], op=ALU.mult),
                           ["O32", kSG], [("og", gi)])

                if nheads > 0:
                    load_head(0)
                    stage1(0)
                for it in range(len(iters)):
                    if it + 1 < len(iters):
                        stage1(it + 1)
                    stage2(it)
            self.S.barrier()
            if self.stop in ("mix0a", "mix0b", "mix0c"):
                return
            with ExitStack() as st:
                self.out_proj_residual(st, self.a_w_out[0], og, 0, 1, "og")

    def ffn(self, s, l):
        S, ps, hT = self.S, self.ps, self.hT
        halves = [(0, 1032), (1032, 1032)]
        BLK = 344
        with ExitStack() as st0:
            halo = self.sb(st0, "halo", [128, 8, 2], BF16)
            S.dve(lambda e: e.memset(halo[:], 0.0), [], ["halo"])
            def half(hf, h0, nh):
                with ExitStack() as st1:
                    act = self.sb(st1, "act", [128, NJ, 1032], BF16)
                    blocks = [(o, BLK) for o in range(0, nh, BLK)]
                    with ExitStack() as st:
                        xn = self.sb(st, "xn2", [128, 8, 1034], BF16)
                        sq = self.sb(st, "sq2", [128, 8, 512], BF16)
                        rtmp = self.sb(st, "rtmp2", [128, 512], F32)
                        rstd = self.sb(st, "rstd2", [128, 512], F32)
                        S.dve(lambda e: e.tensor_copy(out=xn[:, :, 0:2], in_=halo[:]), ["halo"], [("xn2", -1)])
                        subs = list(blocks)
                        for si, (o, nn) in enumerate(subs):
                            g0 = h0 + o
                            self.rstd_from([(hT[:, c, g0:g0 + nn], []) for c in range(8)], nn, sq, rtmp, rstd, ps[2], "ps2")
                            for c in range(8):
                                S.dve(lambda e, c=c, g0=g0, nn=nn, o=o: e.scalar_tensor_tensor(
                                    out=xn[:, c, 2 + o:2 + o + nn], in0=hT[:, c, g0:g0 + nn], scalar=self.gcol(l, 2, c),
                                    in1=rstd[:, :nn], op0=ALU.mult, op1=ALU.mult), ["rstd"], [("xn2", si)])
                        xkeys = [("xn2", -1)] + [("xn2", si) for si in range(len(subs))]
                        S.dve(lambda e, nh=nh: e.tensor_copy(out=halo[:], in_=xn[:, :, nh:nh + 2]), xkeys, ["halo"])
                        wu = [self.sb(st, "wu%d" % i, [128, 8, 2, 128], BF16) for i in range(2)]
                        G32 = [self.sb(st, "G32_%d" % i, [128, 352], F32) for i in range(3)]
                        V32 = [self.sb(st, "V32_%d" % i, [128, 352], F32) for i in range(3)]
                        SGf = [self.sb(st, "SGf_%d" % i, [128, 352], F32) for i in range(3)]
                        wup = self.w_up[l].rearrange("(kc p) n -> p kc n", p=128)
                        units = [(j, bi_, o, nb) for j in range(NJ) for bi_, (o, nb) in enumerate(blocks)]

                        def load_pair(j):
                            mj = 128 if j < 21 else 64
                            sl = j % 2
                            for part in range(2):
                                self.wload(wu[sl][:, :, part, :mj], wup[:, :, part * DFF + j * 128: part * DFF + j * 128 + mj],
                                           sl, ("wu", sl, part))

                        load_pair(0)

                        def front(k):
                            j, bi_, o, nb = units[k]
                            mj = 128 if j < 21 else 64
                            sl = j % 2
                            w = wu[sl]
                            ub = k % 3
                            if bi_ == 0 and j + 1 < NJ:
                                load_pair(j + 1)
                            for part in range(2):
                                bi = part + 2 * ub
                                bank = ps[bi]
                                bk = "ps%d" % bi
                                for kc in range(8):
                                    S.pe(lambda e, bank=bank, part=part, kc=kc: e.matmul(
                                        bank[:mj, :nb + 2], lhsT=w[:, kc, part, :mj], rhs=xn[:, kc, o:o + nb + 2],
                                        start=(kc == 0), stop=(kc == 7)), [("wu", sl, part)] + xkeys, [bk])
                                dst = (G32 if part == 0 else V32)[ub]
                                dk = ("G32" if part == 0 else "V32", ub)
                                S.act(lambda e, bank=bank, dst=dst, part=part: e.activation(
                                    out=dst[:mj, :nb], in_=bank[:mj, 0:nb], func=AF.Identity, scale=self.ccol(l, 0, part, j)[:mj, :]),
                                    [bk], [dk])

                        def taps(k):
                            j, bi_, o, nb = units[k]
                            mj = 128 if j < 21 else 64
                            ub = k % 3
                            for tap in (1, 2):
                                for part in range(2):
                                    bi = part + 2 * ub
                                    bank = ps[bi]
                                    bk = "ps%d" % bi
                                    dst = (G32 if part == 0 else V32)[ub]
                                    dk = ("G32" if part == 0 else "V32", ub)
                                    S.dve(lambda e, bank=bank, dst=dst, part=part, tap=tap: e.scalar_tensor_tensor(
                                        out=dst[:mj, :nb], in0=bank[:mj, tap:tap + nb], scalar=self.ccol(l, tap, part, j)[:mj, :],
                                        in1=dst[:mj, :nb], op0=ALU.mult, op1=ALU.add), [bk, dk], [dk])

                        def back(k):
                            j, bi_, o, nb = units[k]
                            mj = 128 if j < 21 else 64
                            ub = k % 3
                            S.act(lambda e: e.activation(out=SGf[ub][:mj, :nb], in_=G32[ub][:mj, :nb], func=AF.Silu),
                                  [("G32", ub)], [("SGf", ub)])
                            S.dve(lambda e: e.tensor_tensor(out=act[:mj, j, o:o + nb], in0=SGf[ub][:mj, :nb], in1=V32[ub][:mj, :nb],
                                                            op=ALU.mult), [("SGf", ub), ("V32", ub)], [("act", j, bi_)])

                        for k in range(len(units)):
                            front(k)
                            if k > 0:
                                back(k - 1)
                            taps(k)
                        back(len(units) - 1)
                    S.barrier()
                    with ExitStack() as st:
                        wd = [self.sb(st, "wd%d" % i, [128, NJ, 128], BF16) for i in range(2)]
                        mix = self.sb(st, "mixf", [128, 8, 1032], F32)
                        sq3 = self.sb(st, "sq3", [128, 8, 512], BF16)
                        rtmp3 = self.sb(st, "rtmp3", [128, 512], F32)
                        rstd3 = self.sb(st, "rstd3", [128, 512], F32)
                        tmp3 = self.sb(st, "tmp3", [128, 512], F32)
                        def load_dc(dc):
                            sl = dc % 2
                            self.wload(wd[sl][:, 0:21, :],
                                       self.w_down[l, 0:2688, dc * 128:(dc + 1) * 128].rearrange("(j p) n -> p j n", p=128),
                                       sl, ("wd", sl, 0))
                            self.wload(wd[sl][0:64, 21, :], self.w_down[l, 2688:2752, dc * 128:(dc + 1) * 128], sl, ("wd", sl, 1))

                        load_dc(0)
                        for dc in range(8):
                            sl = dc % 2
                            w = wd[sl]
                            if dc + 1 < 8:
                                load_dc(dc + 1)
                            for bi_, (o, nb) in enumerate(blocks):
                                bq = (dc * len(blocks) + bi_) % 2
                                bank = ps[bq]
                                bk = "ps%d" % bq
                                for j in range(NJ):
                                    mj = 128 if j < 21 else 64
                                    S.pe(lambda e, bank=bank, j=j, mj=mj, o=o, nb=nb, w=w: e.matmul(
                                        bank[:, :nb], lhsT=w[:mj, j, :], rhs=act[:mj, j, o:o + nb],
                                        start=(j == 0), stop=(j == NJ - 1)),
                                        [("wd", sl, 0), ("wd", sl, 1), ("act", j, bi_)], [bk])
                                S.act(lambda e, bank=bank, dc=dc, o=o, nb=nb: e.activation(
                                    out=mix[:, dc, o:o + nb], in_=bank[:, :nb], func=AF.Copy), [bk], [("mix", dc, bi_)])
                        for bi_, (o, nb) in enumerate(blocks):
                            g0 = h0 + o
                            self.rstd_from([(mix[:, dc, o:o + nb], [("mix", dc, bi_)]) for dc in range(8)], nb, sq3, rtmp3, rstd3,
                                           ps[2], "ps2")
                            for dc in range(8):
                                S.dve(lambda e, dc=dc, o=o, nb=nb: e.scalar_tensor_tensor(
                                    out=tmp3[:, :nb], in0=mix[:, dc, o:o + nb], scalar=self.gcol(l, 3, dc), in1=rstd3[:, :nb],
                                    op0=ALU.mult, op1=ALU.mult), [("mix", dc, bi_), "rstd"], ["tmp3"])
                                S.dve(lambda e, dc=dc, g0=g0, nb=nb: e.tensor_tensor(
                                    out=hT[:, dc, g0:g0 + nb], in0=hT[:, dc, g0:g0 + nb], in1=tmp3[:, :nb], op=ALU.add),
                                    ["tmp3"], [("hT", dc, hf, bi_)])
                    S.barrier()

            for hf, (h0, nh) in enumerate(halves):
                half(hf, h0, nh)

    def fox(self, s):
        S, ps, psb, hT = self.S, self.ps, self.psb, self.hT
        scale = 1.0 / 8.0
        with ExitStack() as st0:
            O = self.sb(st0, "O", [128, 8, T], BF16)
            with ExitStack() as st:
                xr = self.sb(st, "xr", [128, 8, T], BF16)
                Ctok = self.sb(st, "Ctok", [128, 17, 16], F32)
                Cpb = self.sb(st, "Cpb", [16, T], BF16)
                with ExitStack() as stn:
                    sq = self.sb(stn, "sq4", [128, 8, 512], BF16)
                    rtmp = self.sb(stn, "rtmp4", [128, 512], F32)
                    rstd = self.sb(stn, "rstd4", [128, 512], F32)
                    for gi, (g0, n) in enumerate(GG):
                        self.rstd_from([(hT[:, c, g0:g0 + n], []) for c in range(8)], n, sq, rtmp, rstd, ps[2], "ps2")
                        for c in range(8):
                            S.dve(lambda e, c=c, g0=g0, n=n: e.tensor_tensor(
                                out=xr[:, c, g0:g0 + n], in0=hT[:, c, g0:g0 + n], in1=rstd[:, :n], op=ALU.mult),
                                ["rstd"], [("xr", gi)])
                    S.barrier()
                xrk = [("xr", gi) for gi in range(5)]
                kvw = self.kv_w.rearrange("(kc p) n -> p kc n", p=128)
                wqv = self.b_w_q[0].rearrange("(kc p) n -> p kc n", p=128)
                gkv = self.colv[:, 88:96]
                with ExitStack() as stf:
                    wfg = self.sb(stf, "wfg", [128, 8, 16], BF16)
                    Cp = self.sb(stf, "Cp", [16, T], F32)
                    sp = self.sb(stf, "sp", [16, 512], F32)
                    self.wload(wfg[:, :, :], kvw[:, :, 2048:2064], 0, "wfg")
                    S.dve(lambda e: e.tensor_tensor(out=wfg[:, :, :], in0=wfg[:, :, :],
                                                    in1=gkv.unsqueeze(2).to_broadcast([128, 8, 16]), op=ALU.mult),
                          ["wfg"], ["wfg"])
                    for gi, (g0, n) in enumerate(GG):
                        for kc in range(8):
                            S.pe(lambda e, kc=kc, g0=g0, n=n: e.matmul(ps[0][:16, :n], lhsT=wfg[:, kc, :], rhs=xr[:, kc, g0:g0 + n],
                                                                        start=(kc == 0), stop=(kc == 7)), ["wfg", ("xr", gi)], ["ps0"])
                        S.act(lambda e, n=n: e.activation(out=sp[:, :n], in_=ps[0][:16, :n], func=AF.Exp, scale=-1.0,
                                                          bias=self.nfgb[:, 0:1]), ["ps0", "nfgb2"], ["sp"])
                        S.act(lambda e, n=n: e.activation(out=sp[:, :n], in_=sp[:, :n], func=AF.Ln, scale=1.0,
                                                          bias=self.onec[0:16, 0:1]), ["sp"], ["sp"])
                        init = 0.0 if gi == 0 else Cp[:, g0 - 1:g0]
                        S.dve(lambda e, g0=g0, n=n, init=init: e.tensor_tensor_scan(
                            out=Cp[:, g0:g0 + n], data0=self.onesf[:16, :n], data1=sp[:, :n], initial=init,
                            op0=ALU.mult, op1=ALU.add), ["sp", ("Cp", gi - 1)], [("Cp", gi)])
                    cpk = [("Cp", gi) for gi in range(5)]
                    S.act(lambda e: e.activation(out=Cpb[:, :], in_=Cp[:, :], func=AF.Copy), cpk, ["Cpb"])
                    for ti, (t0, nt) in enumerate(TT):
                        S.pe(lambda e, t0=t0, nt=nt: e.transpose(ps[1][:nt, 0:16], Cp[:, t0:t0 + nt], self.i16[:]), cpk, ["ps1"])
                        S.dve(lambda e, ti=ti, nt=nt: e.tensor_copy(out=Ctok[:nt, ti, :], in_=ps[1][:nt, 0:16]), ["ps1"], ["Ctok"])
                    S.barrier()
                wp = [self.sb(st, "wp%d" % i, [128, 8, 3, 128], BF16) for i in range(2)]
                KTh = [self.sb(st, "KT%d" % i, [65, T], BF16) for i in range(2)]
                QTh = [self.sb(st, "QT%d" % i, [65, T], BF16) for i in range(2)]
                Va = self.sb(st, "Va", [128, 17, 192], BF16)
                NPT = 4
                PT = [self.sb(st, "PT%d" % i, [128, 512], BF16) for i in range(NPT)]
                rinv = [self.sb(st, "rinv%d" % i, [128, 512], F32) for i in range(2)]
                S.pool(lambda e: e.memset(Va[:, :, 64:128], 1.0), [], ["Vones"])
                for hh in range(2):
                    S.pool(lambda e, hh=hh: e.memset(KTh[hh][64:65, :], 1.0), [], [("Kone", hh)])
                g0col = self.colv[:, 32:40]
                itc = 0
                def load_wp(p):
                    sl = p % 2
                    w = wp[sl]
                    self.wload(w[:, :, 0, :], kvw[:, :, p * 128:(p + 1) * 128], sl, ("wp", sl, 0))
                    self.wload(w[:, :, 1, :], kvw[:, :, D + p * 128:D + (p + 1) * 128], sl, ("wp", sl, 1))
                    self.wload(w[:, :, 2, :], wqv[:, :, p * 128:(p + 1) * 128], sl, ("wp", sl, 2))
                    for m in range(3):
                        gc = gkv if m < 2 else g0col
                        S.dve(lambda e, w=w, m=m, gc=gc: e.tensor_tensor(
                            out=w[:, :, m, :], in0=w[:, :, m, :], in1=gc.unsqueeze(2).to_broadcast([128, 8, 128]), op=ALU.mult),
                            [("wp", sl, m)], [("wp", sl, m)])

                load_wp(0)
                for p in range(8):
                    sl = p % 2
                    w = wp[sl]
                    if p + 1 < 8:
                        load_wp(p + 1)
                    for gi, (g0, n) in enumerate(GG):
                        for (m, dst, dk) in ((0, KTh, "KT"), (2, QTh, "QT")):
                            bank = ps[m // 2]
                            bk = "ps%d" % (m // 2)
                            for kc in range(8):
                                S.pe(lambda e, bank=bank, m=m, kc=kc, g0=g0, n=n, w=w: e.matmul(
                                    bank[:, :n], lhsT=w[:, kc, m, :], rhs=xr[:, kc, g0:g0 + n], start=(kc == 0), stop=(kc == 7)),
                                    [("wp", sl, m), ("xr", gi)], [bk])
                            for hh in range(2):
                                S.dve(lambda e, bank=bank, dst=dst, hh=hh, g0=g0, n=n: e.tensor_copy(
                                    out=dst[hh][0:64, g0:g0 + n], in_=bank[hh * 64:(hh + 1) * 64, :n]), [bk], [(dk, hh, gi)])
                        for hh in range(2):
                            h = 2 * p + hh
                            S.pe(lambda e, h=h, g0=g0, n=n: e.matmul(ps[2][0:65, :n], lhsT=self.selq[:, h, :], rhs=Cpb[:, g0:g0 + n],
                                                                      start=True, stop=True), ["Cpb", "selq"], ["ps2"])
                            S.dve(lambda e, hh=hh, g0=g0, n=n: e.tensor_copy(out=QTh[hh][64:65, g0:g0 + n], in_=ps[2][64:65, :n]),
                                  ["ps2"], [("QT", hh, gi)])
                    for kt, (t0, nt) in enumerate(TT):
                        for kc in range(8):
                            S.pe(lambda e, kc=kc, t0=t0, nt=nt, w=w: e.matmul(
                                ps[2][:nt, 0:128], lhsT=xr[:, kc, t0:t0 + nt], rhs=w[:, kc, 1, :], start=(kc == 0), stop=(kc == 7)),
                                [("wp", sl, 1)] + xrk, ["ps2"])
                        S.dve(lambda e, kt=kt, nt=nt: e.tensor_copy(
                            out=Va[:nt, kt, :].rearrange("p (a b) -> p a b", b=64)[:, 0:3:2, :],
                            in_=ps[2][:nt, 0:128].rearrange("p (a b) -> p a b", b=64)), ["ps2"], [("Va", kt)])
                    its = []
                    for hh in range(2):
                        for gi, (g0, n) in enumerate(GG):
                            tl = tiles_of_group(gi)
                            for kt in range(tl[-1] + 1):
                                its.append((hh, gi, kt))
                    LOOK = 2
                    SB = [5, 6, 2]

                    def emit_qk(ix):
                        hh, gi, kt = its[ix]
                        g0, n = GG[gi]
                        tl = tiles_of_group(gi)
                        k0, nk = TT[kt]
                        c0 = (kt - tl[0]) * 128 if (kt in tl and gi > 0) else 0
                        nq = n - c0
                        sbi = SB[(itc0 + ix) % 3]
                        sb_ = ps[sbi]
                        KT, QT = KTh[hh], QTh[hh]
                        S.pe(lambda e: e.matmul(sb_[:nk, :nq], lhsT=KT[0:65, k0:k0 + nk], rhs=QT[0:65, g0 + c0:g0 + c0 + nq],
                                                start=True, stop=True),
                             [("KT", hh, g_) for g_ in range(5)] + [("QT", hh, gi), ("Kone", hh)], ["ps%d" % sbi])

                    def emit_rest(ix, p=p):
                        hh, gi, kt = its[ix]
                        h = 2 * p + hh
                        g0, n = GG[gi]
                        tl = tiles_of_group(gi)
                        last = tl[-1]
                        k0, nk = TT[kt]
                        c0 = (kt - tl[0]) * 128 if (kt in tl and gi > 0) else 0
                        nq = n - c0
                        sbi = SB[(itc0 + ix) % 3]
                        sb_ = ps[sbi]
                        sbk = "ps%d" % sbi
                        pt = PT[(itc0 + ix) % NPT]
                        ptk = ("PT", (itc0 + ix) % NPT)
                        vlo = 0 if hh == 0 else 64
                        orow = hh * 64
                        lrow = 64 - orow
                        obi = 3 + (gi + hh) % 2
                        ob = ps[obi]
                        obk = "ps%d" % obi
                        S.act(lambda e: e.activation(out=pt[:nk, :nq], in_=sb_[:nk, :nq], func=AF.Exp, scale=scale,
                                                     bias=Ctok[:nk, kt, h:h + 1]), [sbk, "Ctok"], [ptk])
                        if kt in tl:
                            qn = min(128, nq)
                            S.pool(lambda e: e.tensor_tensor(out=pt[:nk, 0:qn], in0=pt[:nk, 0:qn], in1=self.triu[:nk, :qn],
                                                             op=ALU.mult), [ptk], [ptk])
                        S.pe(lambda e: e.matmul(ob[:, c0:c0 + nq], lhsT=Va[:nk, kt, vlo:vlo + 128], rhs=pt[:nk, 0:nq],
                                                start=(kt == 0), stop=(kt == last)), [ptk, ("Va", kt), "Vones"], [obk])
                        if kt == last:
                            rv = rinv[(gi + hh) % 2]
                            rk = ("rinv", (gi + hh) % 2)
                            S.dve(lambda e: e.reciprocal(out=rv[orow:orow + 64, :n], in_=ob[lrow:lrow + 64, :n]), [obk], [rk])
                            S.dve(lambda e: e.tensor_tensor(out=O[orow:orow + 64, p, g0:g0 + n], in0=ob[orow:orow + 64, :n],
                                                            in1=rv[orow:orow + 64, :n], op=ALU.mult), [obk, rk], [("O", gi)])

                    itc0 = itc
                    for ix in range(min(LOOK, len(its))):
                        emit_qk(ix)
                    for ix in range(len(its)):
                        if ix + LOOK < len(its):
                            emit_qk(ix + LOOK)
                        emit_rest(ix)
                    itc += len(its)
            self.S.barrier()
            with ExitStack() as st:
                self.out_proj_residual(st, self.b_w_out[0], O, 1, 1, "O")


_CACHE = {}


def _get_nc(nseq=2, stop=None):
    key = (nseq, stop)
    if key not in _CACHE:
        _CACHE[key] = Builder(nseq, stop).build()
    return _CACHE[key]


def kernel(**inputs):
    ncores = 8
    nc = _get_nc(2, None)
    shared = {k: np.ascontiguousarray(np.asarray(v, dtype=np.float32)) for k, v in inputs.items() if k != "x"}
    x = np.ascontiguousarray(np.asarray(inputs["x"], dtype=np.float32))
    in_maps = []
    for c in range(ncores):
        m = dict(shared)
        m["x"] = x[2 * c:2 * c + 2]
        in_maps.append(m)
    res = run_bass_kernel_spmd(nc, in_maps, core_ids=list(range(ncores)))
    return np.concatenate([np.asarray(r["out"]) for r in res.results], axis=0).astype(np.float32)
```

```python
import numpy as np
import concourse.bass as bass
import concourse.mybir as mybir
from concourse.bass_utils import run_bass_kernel_spmd
from contextlib import ExitStack

F32 = mybir.dt.float32
BF16 = mybir.dt.bfloat16
AF = mybir.ActivationFunctionType
ALU = mybir.AluOpType
AX = mybir.AxisListType

SAME_ENG_SYNC = True


class _Op:
    __slots__ = ("eng", "fn", "reads", "writes", "dma_sem", "ndma", "deps",
                 "needs_inc", "token", "waits", "idx", "is_bar")

    def __init__(self, eng, fn, reads, writes, dma_sem=None, ndma=0):
        self.eng = eng
        self.fn = fn
        self.reads = reads
        self.writes = writes
        self.dma_sem = dma_sem
        self.ndma = ndma
        self.deps = []
        self.needs_inc = False
        self.token = None
        self.waits = []
        self.is_bar = False


class Sched:
    CENG = ("pe", "act", "dve", "pool")
    ALLENG = ("pe", "act", "dve", "pool", "sp")

    def __init__(self, nc, stack):
        self.nc = nc
        self.stack = stack
        self.ops = []
        self.esem = {e: stack.enter_context(nc.semaphore("s_" + e)) for e in self.CENG}
        self.dma_cum = {}
        self.dma_sems = {}
        self.dma_exempt = set()
        self.last_w = {}
        self.readers = {}
        self.last_op = {e: None for e in self.ALLENG}
        self.dma_last = {}

    def dma_sem(self, name, exempt=False):
        if name not in self.dma_sems:
            self.dma_sems[name] = self.stack.enter_context(self.nc.semaphore("d_" + name))
            self.dma_cum[name] = 0
            if exempt:
                self.dma_exempt.add(name)
        return name

    def _add(self, op):
        op.idx = len(self.ops)
        deps = set()
        for k in op.reads:
            w = self.last_w.get(k)
            if w is not None:
                deps.add(w)
        for k in op.writes:
            w = self.last_w.get(k)
            if w is not None:
                deps.add(w)
            for r in self.readers.get(k, ()):
                deps.add(r)
        deps.discard(op)
        op.deps = sorted(deps, key=lambda o: o.idx)
        for k in op.reads:
            self.readers.setdefault(k, []).append(op)
        for k in op.writes:
            self.last_w[k] = op
            self.readers[k] = []
        self.ops.append(op)
        self.last_op[op.eng] = op
        if op.dma_sem is not None:
            self.dma_cum[op.dma_sem] += 16 * op.ndma
            op.token = (op.dma_sem, self.dma_cum[op.dma_sem])
            self.dma_last[op.dma_sem] = op
        return op

    def op(self, eng, fn, reads=(), writes=()):
        return self._add(_Op(eng, fn, tuple(reads), tuple(writes)))

    def pe(self, fn, reads=(), writes=()):
        return self.op("pe", fn, reads, writes)

    def act(self, fn, reads=(), writes=()):
        return self.op("act", fn, reads, writes)

    def dve(self, fn, reads=(), writes=()):
        return self.op("dve", fn, reads, writes)

    def pool(self, fn, reads=(), writes=()):
        return self.op("pool", fn, reads, writes)

    def dma(self, eng, fn, sem, reads=(), writes=(), n=1):
        return self._add(_Op(eng, fn, tuple(reads), tuple(writes), dma_sem=sem, ndma=n))

    def barrier(self):
        prev = dict(self.last_op)
        dl = {k: v for k, v in self.dma_last.items() if k not in self.dma_exempt}
        for e in self.ALLENG:
            b = _Op(e, None, (), ())
            b.is_bar = True
            b.idx = len(self.ops)
            b.deps = [o for ee, o in prev.items() if o is not None and (ee != e or (SAME_ENG_SYNC and e != 'pe'))] + list(dl.values())
            self.ops.append(b)
            self.last_op[e] = b

    def finalize(self):
        for op in self.ops:
            for d in op.deps:
                if d.dma_sem is not None or d.is_bar:
                    continue
                if d.eng == op.eng and (d.eng == "pe" or not SAME_ENG_SYNC):
                    continue
                d.needs_inc = True
        cnt = {e: 0 for e in self.CENG}
        for op in self.ops:
            if op.dma_sem is None and op.needs_inc:
                cnt[op.eng] += 1
                op.token = (op.eng, cnt[op.eng])
        known = {e: {} for e in self.ALLENG}
        for op in self.ops:
            kn = known[op.eng]
            need = {}
            for d in op.deps:
                if d.token is None:
                    continue
                if d.dma_sem is None and d.eng == op.eng and (d.eng == "pe" or not SAME_ENG_SYNC):
                    continue
                s, v = d.token
                if need.get(s, 0) < v:
                    need[s] = v
            for s, v in need.items():
                if kn.get(s, 0) >= v:
                    continue
                kn[s] = v
                op.waits.append((s, v))
        self.counts = cnt

    def _sem(self, s):
        return self.esem[s] if s in self.esem else self.dma_sems[s]

    def emit(self):
        self.finalize()
        by_eng = {e: [o for o in self.ops if o.eng == e] for e in self.ALLENG}
        with self.nc.Block() as block:
            def run(e):
                def body(eng):
                    for op in by_eng[e]:
                        for (s, v) in op.waits:
                            eng.wait_ge(self._sem(s), v)
                        if op.fn is None:
                            continue
                        if op.dma_sem is not None:
                            op.fn(eng, self.dma_sems[op.dma_sem])
                        else:
                            ins = op.fn(eng)
                            if op.needs_inc:
                                ins.then_inc(self.esem[e], 1)
                return body
            block.tensor(run("pe"))
            block.scalar(run("act"))
            block.vector(run("dve"))
            block.gpsimd(run("pool"))
            block.sync(run("sp"))


import os
CUT = int(os.environ.get('KCUT', '99'))
D = 1024
T = 2064
NMETA = 16
DFF = 2752
NJ = 22
EPS = 1e-6
TT = [(0, 16)] + [(16 + 128 * i, 128) for i in range(16)]
GG = [(0, 16)] + [(16 + 512 * j, 512) for j in range(4)]


def tiles_of_group(gi):
    return [0] if gi == 0 else list(range(1 + 4 * (gi - 1), 1 + 4 * gi))


class Builder:
    def __init__(self, nseq=2, stop=None):
        self.nseq = nseq
        self.stop = stop
        nc = self.nc = bass.Bass("TRN2", target_bir_lowering=False)
        dt = lambda name, shape: nc.dram_tensor(name, shape, F32, kind="ExternalInput").ap()
        self.x = dt("x", [nseq, 2048, D])
        self.meta = dt("meta_tokens", [NMETA, D])
        self.norm_gains = dt("norm_gains", [2, 4, D])
        self.a_w_in = dt("a_w_in", [1, D, 4 * D])
        self.a_lb = dt("a_lb_logits", [2, D])
        self.a_hn = dt("a_head_norm", [1, D])
        self.a_w_out = dt("a_w_out", [1, D, D])
        self.kv_norm = dt("kv_norm", [D])
        self.kv_w = dt("kv_w", [D, 2 * D + 16])
        self.fg_b = dt("fg_b", [16])
        self.b_w_q = dt("b_w_q", [1, D, D])
        self.b_w_out = dt("b_w_out", [1, D, D])
        self.w_up = dt("ffn_w_up", [2, D, 2 * DFF])
        self.conv = dt("ffn_conv", [2, 3, 2 * DFF])
        self.w_down = dt("ffn_w_down", [2, DFF, D])
        self.out = nc.dram_tensor("out", [nseq, 2048, D], F32, kind="ExternalOutput").ap()
        self.uid = 0

    def sb(self, st, name, shape, dtype):
        self.uid += 1
        return st.enter_context(self.nc.sbuf_tensor("%s_%d" % (name, self.uid), shape, dtype))

    def build(self):
        nc = self.nc
        with ExitStack() as st:
            S = self.S = Sched(nc, st)
            self.ps = [st.enter_context(nc.psum_tensor("ps%d" % i, [128, 512], F32)) for i in range(7)]
            self.psb = st.enter_context(nc.psum_tensor("psb", [128, 1024], BF16))
            self.hT = self.sb(st, "hT", [128, 8, T], F32)
            self.consts(st)
            S.barrier()
            for s in range(self.nseq):
                self.seq(s)
            S.barrier()
            S.emit()
        return nc

    def consts(self, st):
        S = self.S
        self.ident = self.sb(st, "ident", [128, 128], F32)
        self.identb = self.sb(st, "identb", [128, 128], BF16)
        self.onesb = self.sb(st, "onesb", [128, 128], BF16)
        self.triu = self.sb(st, "triu", [128, 128], BF16)
        self.mask2 = self.sb(st, "mask2", [128, 128], F32)
        self.maskseg = self.sb(st, "maskseg", [128, 512], F32)
        self.epsc = self.sb(st, "epsc", [128, 1], F32)
        self.colv = self.sb(st, "colv", [128, 96], F32)
        self.convT = self.sb(st, "convT", [128, 3, 128], F32)
        self.lbc = self.sb(st, "lbc", [128, 24], F32)
        self.nfgb = self.sb(st, "nfgb", [16, 1], F32)
        self.i16 = self.sb(st, "i16", [16, 16], F32)
        self.ones16 = self.sb(st, "ones16", [16, 128], F32)
        self.rinvs = self.sb(st, "rinvs", [128, 512], F32)
        self.onesf = self.sb(st, "onesf", [16, 512], F32)
        self.onec = self.sb(st, "onec", [128, 1], F32)
        self.selq = self.sb(st, "selq", [16, 16, 65], BF16)
        P = lambda fn, r=(), w=(): S.pool(fn, r, w)
        P(lambda e: e.memset(self.ident[:], 1.0), w=["ident"])
        P(lambda e: e.affine_select(out=self.ident[:], in_=self.ident[:], pattern=[[-1, 128]],
                                    compare_op=ALU.is_equal, fill=0.0, base=0, channel_multiplier=1),
          r=["ident"], w=["ident"])
        S.dve(lambda e: e.tensor_copy(out=self.identb[:], in_=self.ident[:]), ["ident"], ["identb"])
        S.dve(lambda e: e.tensor_copy(out=self.i16[:], in_=self.ident[0:16, 0:16]), ["ident"], ["i16"])
        P(lambda e: e.memset(self.onesb[:], 1.0), w=["onesb"])
        P(lambda e: e.memset(self.ones16[:], 1.0), w=["ones16"])
        P(lambda e: e.memset(self.triu[:], 1.0), w=["triu"])
        P(lambda e: e.affine_select(out=self.triu[:], in_=self.triu[:], pattern=[[1, 128]],
                                    compare_op=ALU.is_ge, fill=0.0, base=0, channel_multiplier=-1),
          r=["triu"], w=["triu"])
        P(lambda e: e.memset(self.mask2[:], 1.0), w=["mask2"])
        P(lambda e: e.affine_select(out=self.mask2[:], in_=self.mask2[:], pattern=[[1, 128]],
                                    compare_op=ALU.is_ge, fill=0.0, base=0, channel_multiplier=-1),
          r=["mask2"], w=["mask2"])
        P(lambda e: e.memset(self.mask2[0:64, 64:128], 0.0), r=["mask2"], w=["mask2"])
        P(lambda e: e.memset(self.maskseg[:], 1.0), w=["maskseg"])
        P(lambda e: e.memset(self.maskseg[:].rearrange("p (c k) -> p c k", k=64)[:, :, 0:1], 0.0),
          r=["maskseg"], w=["maskseg"])
        P(lambda e: e.memset(self.epsc[:], EPS), w=["epsc"])
        P(lambda e: e.memset(self.onesf[:], 1.0), w=["onesf"])
        P(lambda e: e.memset(self.onec[:], 1.0), w=["onec"])
        P(lambda e: e.memset(self.selq[:], 0.0), w=["selq0"])
        S.dve(lambda e: e.tensor_scalar(out=self.selq[:, :, 64], in0=self.i16[:, :], scalar1=-8.0, scalar2=None, op0=ALU.mult),
              ["selq0", "i16"], ["selq"])
        rowsA = self.sb(st, "rowsA", [96, 128], F32)
        rowsC = self.sb(st, "rowsC", [128, 3, 128], F32)
        P(lambda e: e.memset(rowsC[:], 0.0), w=["rowsC"])
        cs = S.dma_sem("const")
        nd = [0]

        def ld(dst, src, rk):
            S.dma("sp", lambda e, s, dst=dst, src=src: e.dma_start(out=dst, in_=src).then_inc(s, 16), cs,
                  reads=[rk], writes=[("rowsd", nd[0])])
            nd[0] += 1
        ld(rowsA[0:64, :], self.norm_gains.rearrange("l j (c p) -> (l j c) p", p=128), "rowsA")
        ld(rowsA[64:80, :], self.a_lb.rearrange("l (c p) -> (l c) p", p=128), "rowsA")
        ld(rowsA[80:88, :], self.a_hn.rearrange("l (c p) -> (l c) p", p=128), "rowsA")
        ld(rowsA[88:96, :], self.kv_norm.rearrange("(c p) -> c p", p=128), "rowsA")
        for l in range(2):
            for tap in range(3):
                for part in range(2):
                    r0 = ((l * 3 + tap) * 2 + part) * 22
                    src = self.conv[l, tap, part * DFF: part * DFF + 2688].rearrange("(j k) -> j k", k=128)
                    done = 0
                    while done < 21:
                        ti, ri = divmod(r0 + done, 128)
                        cnt = min(21 - done, 128 - ri)
                        ld(rowsC[ri:ri + cnt, ti, :], src[done:done + cnt, :], "rowsC")
                        done += cnt
                    ti, ri = divmod(r0 + 21, 128)
                    ld(rowsC[ri:ri + 1, ti, 0:64],
                       self.conv[l, tap, part * DFF + 2688: part * DFF + 2752].rearrange("(a k) -> a k", a=1), "rowsC")
        ld(self.nfgb[:, :], self.fg_b.rearrange("(h a) -> h a", a=1), "nfgb")
        allrows = [("rowsd", i) for i in range(nd[0])]
        ps = self.ps
        S.pe(lambda e: e.transpose(ps[0][:, 0:96], rowsA[:, :], self.ident[0:96, 0:96]), allrows + ["ident"], ["ps0"])
        S.dve(lambda e: e.tensor_copy(out=self.colv[:], in_=ps[0][:, 0:96]), ["ps0"], ["colv"])
        for ti in range(3):
            S.pe(lambda e, ti=ti: e.transpose(ps[1][:, ti * 128:(ti + 1) * 128], rowsC[:, ti, :], self.ident[:]),
                 allrows + ["ident", "rowsC"], ["ps1"])
        S.dve(lambda e: e.tensor_copy(out=self.convT[:], in_=ps[1][:, 0:384].rearrange("p (a b) -> p a b", b=128)),
              ["ps1"], ["convT"])
        dl = self.sb(st, "dl", [128, 8], F32)
        S.dve(lambda e: e.tensor_tensor(out=dl[:], in0=self.colv[:, 64:72], in1=self.colv[:, 72:80], op=ALU.subtract),
              ["colv"], ["dl"])
        S.act(lambda e: e.activation(out=self.lbc[:, 0:8], in_=dl[:], func=AF.Sigmoid), ["dl"], ["lbc0"])
        S.act(lambda e: e.activation(out=self.lbc[:, 8:16], in_=dl[:], func=AF.Sigmoid, scale=-1.0), ["dl"], ["lbc1"])
        S.dve(lambda e: e.tensor_scalar(out=self.lbc[:, 16:24], in0=self.lbc[:, 8:16], scalar1=-1.0, scalar2=None,
                                        op0=ALU.mult), ["lbc1"], ["lbc2"])
        S.dve(lambda e: e.tensor_scalar(out=self.nfgb[:], in0=self.nfgb[:], scalar1=-1.0, scalar2=None, op0=ALU.mult),
              allrows, ["nfgb2"])

    def gcol(self, l, j, c):
        k = (l * 4 + j) * 8 + c
        return self.colv[:, k:k + 1]

    def ccol(self, l, tap, part, j):
        r = ((l * 3 + tap) * 2 + part) * 22 + j
        ti, ri = divmod(r, 128)
        return self.convT[:, ti, ri:ri + 1]

    def wload(self, dst, src, slot, key, reads=()):
        S = self.S
        sem = S.dma_sem("w_" + "_".join(str(k) for k in (key if isinstance(key, tuple) else (key,))), exempt=True)
        S.dma("pool", lambda e, s: e.dma_start(out=dst, in_=src).then_inc(s, 16), sem,
              reads=list(reads), writes=[key])

    def rstd_from(self, srcs, n, sq, rtmp, rstd, pst, pkey, dscale=1.0 / D):
        S = self.S
        nsrc = len(srcs)
        for c, (ap, rk) in enumerate(srcs):
            S.act(lambda e, ap=ap, c=c: e.activation(out=sq[:, c, :n], in_=ap, func=AF.Square), rk, [("sq", c)])
            S.pe(lambda e, c=c: e.matmul(pst[:, :n], lhsT=self.onesb[:], rhs=sq[:, c, :n], start=(c == 0),
                                         stop=(c == nsrc - 1)), [("sq", c)], [pkey])
        S.act(lambda e: e.activation(out=rtmp[:, :n], in_=pst[:, :n], func=AF.Ln, scale=dscale, bias=self.epsc[:, 0:1]),
              [pkey], ["rtmp"])
        S.act(lambda e: e.activation(out=rstd[:, :n], in_=rtmp[:, :n], func=AF.Exp, scale=-0.5), ["rtmp"], ["rstd"])

    def seq(self, s):
        S = self.S
        self.load_x(s)
        S.barrier()
        if self.stop != "load":
            self.hgrn2(s)
            S.barrier()
            if self.stop not in ("mix0", "mix0a", "mix0b", "mix0c"):
                self.ffn(s, 0)
                S.barrier()
                if self.stop != "ffn0":
                    self.fox(s)
                    S.barrier()
                    if self.stop != "mix1":
                        self.ffn(s, 1)
                        S.barrier()
        self.store(s)
        S.barrier()

    def load_x(self, s):
        S, ps, hT = self.S, self.ps, self.hT
        with ExitStack() as st:
            xin = [self.sb(st, "xin%d" % i, [128, D], F32) for i in range(2)]
            xs = [S.dma_sem("xin%d" % i) for i in range(2)]
            for ti, (t0, n) in enumerate(TT):
                sl = ti % 2
                src = self.meta if ti == 0 else self.x[s, t0 - 16:t0 - 16 + 128, :]
                S.dma("sp", lambda e, sm, sl=sl, src=src, n=n: e.dma_start(out=xin[sl][:n, :], in_=src).then_inc(sm, 16),
                      xs[sl], writes=[("xin", sl)])
                for half in range(2):
                    bank = ps[half + 2 * sl]
                    bk = "ps%d" % (half + 2 * sl)
                    for j in range(4):
                        c = half * 4 + j
                        S.pe(lambda e, bank=bank, j=j, c=c, n=n, sl=sl: e.transpose(
                            bank[:, j * 128:j * 128 + n], xin[sl][:n, c * 128:(c + 1) * 128], self.ident[:n, :n]),
                            [("xin", sl)], [bk])
                    fn = lambda e, bank=bank, half=half, t0=t0, n=n: e.tensor_copy(
                        out=hT[:, half * 4:(half + 1) * 4, t0:t0 + n],
                        in_=bank[:, :].rearrange("p (j k) -> p j k", k=128)[:, :, 0:n])
                    if half == 0:
                        S.dve(fn, [bk], [("hT", ti, half)])
                    else:
                        S.act(lambda e, bank=bank, half=half, t0=t0, n=n: e.activation(
                            out=hT[:, half * 4:(half + 1) * 4, t0:t0 + n],
                            in_=bank[:, :].rearrange("p (j k) -> p j k", k=128)[:, :, 0:n], func=AF.Copy),
                            [bk], [("hT", ti, half)])

    def store(self, s):
        S, ps, hT = self.S, self.ps, self.hT
        with ExitStack() as st:
            xo = [self.sb(st, "xo%d" % i, [128, D], F32) for i in range(2)]
            os_ = [S.dma_sem("xo%d" % i) for i in range(2)]
            for ti, (t0, n) in enumerate(TT):
                if ti == 0:
                    continue
                sl = ti % 2
                for half in range(2):
                    bank = ps[half + 2 * sl]
                    bk = "ps%d" % (half + 2 * sl)
                    for j in range(4):
                        c = half * 4 + j
                        S.pe(lambda e, bank=bank, j=j, c=c, t0=t0: e.transpose(
                            bank[:, j * 128:(j + 1) * 128], hT[:, c, t0:t0 + 128], self.ident[:]), [], [bk])
                    if half == 0:
                        S.dve(lambda e, bank=bank, sl=sl: e.tensor_copy(out=xo[sl][:, 0:512], in_=bank[:, :]),
                              [bk], [("xo", sl)])
                    else:
                        S.act(lambda e, bank=bank, sl=sl: e.activation(out=xo[sl][:, 512:1024], in_=bank[:, :], func=AF.Copy),
                              [bk], [("xo", sl)])
                S.dma("sp", lambda e, sm, sl=sl, t0=t0: e.dma_start(out=self.out[s, t0 - 16:t0 - 16 + 128, :],
                                                                     in_=xo[sl][:, :]).then_inc(sm, 16),
                      os_[sl], reads=[("xo", sl)], writes=[("xo", sl)])

    def out_proj_residual(self, st, wsrc, src_act, l, jn, tag):
        S, ps, hT = self.S, self.ps, self.hT
        wo = self.sb(st, "wo", [128, 8, D], BF16)
        mix32 = self.sb(st, "mix32", [128, 8, 512], F32)
        sq = self.sb(st, "sqo", [128, 8, 512], BF16)
        rtmp = self.sb(st, "rtmpo", [128, 512], F32)
        rstd = self.sb(st, "rstdo", [128, 512], F32)
        tmp = self.sb(st, "tmpo", [128, 512], F32)
        wv = wsrc.rearrange("(kc p) n -> p kc n", p=128)
        for kc in range(8):
            self.wload(wo[:, kc, :], wv[:, kc, :], kc % 2, ("wo", kc))
        for gi, (g0, n) in enumerate(GG):
            for dc in range(8):
                bank = ps[dc % 2]
                bk = "ps%d" % (dc % 2)
                for kc in range(8):
                    S.pe(lambda e, bank=bank, dc=dc, kc=kc, g0=g0, n=n: e.matmul(
                        bank[:, :n], lhsT=wo[:, kc, dc * 128:(dc + 1) * 128], rhs=src_act[:, kc, g0:g0 + n],
                        start=(kc == 0), stop=(kc == 7)), [("wo", kc), (tag, gi)], [bk])
                S.act(lambda e, bank=bank, dc=dc, n=n: e.activation(out=mix32[:, dc, :n], in_=bank[:, :n], func=AF.Copy),
                      [bk], [("mix32", dc)])
            self.rstd_from([(mix32[:, dc, :n], [("mix32", dc)]) for dc in range(8)], n, sq, rtmp, rstd, ps[2], "ps2")
            for dc in range(8):
                S.dve(lambda e, dc=dc, n=n: e.scalar_tensor_tensor(
                    out=tmp[:, :n], in0=mix32[:, dc, :n], scalar=self.gcol(l, jn, dc), in1=rstd[:, :n],
                    op0=ALU.mult, op1=ALU.mult), [("mix32", dc), "rstd"], ["tmpo"])
                S.dve(lambda e, dc=dc, g0=g0, n=n: e.tensor_tensor(
                    out=hT[:, dc, g0:g0 + n], in0=hT[:, dc, g0:g0 + n], in1=tmp[:, :n], op=ALU.add),
                    ["tmpo"], [("hT", dc, gi)])

    def hgrn2(self, s):
        S, ps, psb, hT = self.S, self.ps, self.psb, self.hT
        with ExitStack() as st0:
            og = self.sb(st0, "og", [128, 8, T], BF16)
            with ExitStack() as st:
                xn = self.sb(st, "xn", [128, 8, T], BF16)
                sq = self.sb(st, "sq", [128, 8, 512], BF16)
                rtmp = self.sb(st, "rtmp", [128, 512], F32)
                rstd = self.sb(st, "rstd", [128, 512], F32)
                for gi, (g0, n) in enumerate(GG):
                    self.rstd_from([(hT[:, c, g0:g0 + n], []) for c in range(8)], n, sq, rtmp, rstd, ps[2], "ps2")
                    for c in range(8):
                        S.dve(lambda e, c=c, g0=g0, n=n: e.scalar_tensor_tensor(
                            out=xn[:, c, g0:g0 + n], in0=hT[:, c, g0:g0 + n], scalar=self.gcol(0, 0, c), in1=rstd[:, :n],
                            op0=ALU.mult, op1=ALU.mult), ["rstd"], [("xn", gi)])
                wh = [self.sb(st, "wh%d" % i, [128, 8, 4, 128], BF16) for i in range(2)]
                A = self.sb(st, "A", [128, 512], F32)
                C = self.sb(st, "C", [128, 512], F32)
                Dn = self.sb(st, "Dn", [128, 512], F32)
                Bs = [self.sb(st, "B%d" % i, [128, 512], F32) for i in range(2)]
                SGs = [self.sb(st, "SG%d" % i, [128, 512], F32) for i in range(2)]
                qins = [self.sb(st, "qin%d" % i, [128, 512], BF16) for i in range(2)]
                kins = [self.sb(st, "kin%d" % i, [128, 512], BF16) for i in range(2)]
                kouts = [self.sb(st, "kout%d" % i, [128, 512], BF16) for i in range(2)]
                vtoks = [self.sb(st, "vtok%d" % i, [128, 4, 128], BF16) for i in range(2)]
                O32 = self.sb(st, "O32", [128, 512], F32)
                sqh = self.sb(st, "sqh", [128, 1, 512], BF16)
                ktok = self.sb(st, "ktok", [128, 4, 128], BF16)
                attT = self.sb(st, "attT", [128, 4, 128], BF16)
                S32 = self.sb(st, "S32", [128, 9, 128], F32)
                Sb = self.sb(st, "Sb", [128, 8, 128], BF16)
                win = self.a_w_in[0].rearrange("(kc p) n -> p kc n", p=128)

                def load_head(hd):
                    sl = hd % 2
                    for j in range(4):
                        self.wload(wh[sl][:, :, j, :], win[:, :, j * D + hd * 128: j * D + (hd + 1) * 128], sl, ("wh", sl, j))

                nheads = {"mix0a": 0, "mix0b": 1, "mix0c": 1}.get(self.stop, 8)
                iters = [(hd, gi) for hd in range(nheads) for gi in range(5)]

                def stage1(it):
                    hd, gi = iters[it]
                    g0, n = GG[gi]
                    z = it % 2
                    sl = hd % 2
                    w = wh[sl]
                    wk = [("wh", sl, j) for j in range(4)]
                    B, SG, qin, kin, kout, vtok = Bs[z], SGs[z], qins[z], kins[z], kouts[z], vtoks[z]
                    kB, kSG, kq, kk_, ko, kv = ("B", z), ("SG", z), ("qin", z), ("kin", z), ("kout", z), ("vtok", z)
                    tl_list = tiles_of_group(gi)
                    if gi == 0 and hd + 1 < nheads:
                        load_head(hd + 1)
                    yield

                    def proj(j, bi):
                        for kc in range(8):
                            S.pe(lambda e, kc=kc: e.matmul(ps[bi][:, :n], lhsT=w[:, kc, j, :], rhs=xn[:, kc, g0:g0 + n],
                                                           start=(kc == 0), stop=(kc == 7)), [wk[j], ("xn", gi)], ["ps%d" % bi])
                    yield
                    proj(1, 1)
                    yield
                    S.act(lambda e: e.activation(out=A[:, :n], in_=ps[1][:, :n], func=AF.Sigmoid), ["ps1"], ["A"])
                    yield
                    proj(0, 0)
                    yield
                    proj(3, 1)
                    yield
                    S.act(lambda e: e.activation(out=SG[:, :n], in_=ps[1][:, :n], func=AF.Silu), ["ps1"], [kSG])
                    yield
                    for li, ti in enumerate(tl_list):
                        t0, nt = TT[ti]
                        for kc in range(8):
                            S.pe(lambda e, li=li, kc=kc, t0=t0, nt=nt: e.matmul(
                                ps[2][:nt, li * 128:(li + 1) * 128], lhsT=xn[:, kc, t0:t0 + nt], rhs=w[:, kc, 2, :],
                                start=(kc == 0), stop=(kc == 7)), [wk[2], ("xn", gi)], ["ps2"])
                    yield
                    if gi == 0:
                        S.act(lambda e: e.activation(out=vtok[:16, 0, :], in_=ps[2][:16, 0:128], func=AF.Copy), ["ps2"], [kv])
                    else:
                        S.act(lambda e: e.activation(out=vtok[:, :, :], in_=ps[2][:, :].rearrange("p (a b) -> p a b", b=128),
                                                     func=AF.Copy), ["ps2"], [kv])
                    yield
                    S.act(lambda e: e.activation(out=B[:, :n], in_=A[:, :n], func=AF.Ln,
                                                 scale=self.lbc[:, 8 + hd:9 + hd], bias=self.lbc[:, hd:hd + 1]), ["A"], [kB])
                    yield
                    S.dve(lambda e: e.tensor_scalar(out=C[:, :n], in0=A[:, :n], scalar1=self.lbc[:, 16 + hd:17 + hd],
                                                    scalar2=self.lbc[:, 8 + hd:9 + hd], op0=ALU.mult, op1=ALU.add), ["A"], ["C"])
                    yield
                    S.dve(lambda e: e.tensor_tensor_scan(out=A[:, :n], data0=self.maskseg[:, :n], data1=B[:, :n],
                                                         initial=0.0, op0=ALU.mult, op1=ALU.add), [kB, "A"], ["A"])
                    yield
                    S.act(lambda e: e.activation(out=B[:, :n], in_=A[:, :n], func=AF.Exp), ["A"], [kB])
                    yield
                    S.act(lambda e: e.activation(out=Dn[:, :n], in_=A[:, :n], func=AF.Exp, scale=-1.0), ["A"], ["Dn"])
                    yield
                    S.dve(lambda e: e.tensor_tensor(out=qin[:, :n], in0=ps[0][:, :n], in1=B[:, :n], op=ALU.mult),
                          ["ps0", kB], [kq])
                    yield
                    S.dve(lambda e: e.tensor_tensor(out=C[:, :n], in0=C[:, :n], in1=Dn[:, :n], op=ALU.mult), ["C", "Dn"], ["C"])
                    yield
                    S.act(lambda e: e.activation(out=kin[:, :n], in_=C[:, :n], func=AF.Copy), ["C"], [kk_])
                    yield
                    if gi == 0:
                        S.dve(lambda e: e.tensor_scalar(out=kout[:, :16], in0=C[:, :16], scalar1=B[:, 15:16], scalar2=None,
                                                        op0=ALU.mult), ["C", kB], [ko])
                    else:
                        S.dve(lambda e: e.tensor_tensor(
                            out=kout[:, :].rearrange("p (c k) -> p c k", k=64),
                            in0=C[:, :].rearrange("p (c k) -> p c k", k=64),
                            in1=B[:, :].rearrange("p (c k) -> p c k", k=64)[:, :, 63:64].to_broadcast([128, 8, 64]),
                            op=ALU.mult), ["C", kB], [ko])
                    yield

                def stage2(it):
                    hd, gi = iters[it]
                    g0, n = GG[gi]
                    z = it % 2
                    B, SG, qin, kin, kout, vtok = Bs[z], SGs[z], qins[z], kins[z], kouts[z], vtoks[z]
                    kB, kSG, kq, kk_, ko, kv = ("B", z), ("SG", z), ("qin", z), ("kin", z), ("kout", z), ("vtok", z)
                    tl_list = tiles_of_group(gi)
                    nch = 1 if gi == 0 else 8
                    if gi == 0:
                        S.dve(lambda e: e.memset(S32[:, 0, :], 0.0), [], [("S32", 0)])
                    yield
                    for li, ti in enumerate(tl_list):
                        t0, nt = TT[ti]
                        S.pe(lambda e, li=li, nt=nt: e.transpose(psb[:nt, li * 128:(li + 1) * 128],
                                                                  kout[:, li * 128:li * 128 + nt], self.identb[:]), [ko], ["psb"])
                    yield
                    if gi == 0:
                        S.act(lambda e: e.activation(out=ktok[:16, 0, :], in_=psb[:16, 0:128], func=AF.Copy), ["psb"], ["ktok"])
                    else:
                        S.act(lambda e: e.activation(out=ktok[:, :, :], in_=psb[:, 0:512].rearrange("p (a b) -> p a b", b=128),
                                                     func=AF.Copy), ["psb"], ["ktok"])
                    yield
                    for li, ti in enumerate(tl_list):
                        t0, nt = TT[ti]
                        S.pe(lambda e, li=li, nt=nt: e.matmul(ps[5][:nt, li * 128:li * 128 + nt], lhsT=kin[:, li * 128:li * 128 + nt],
                                                               rhs=qin[:, li * 128:li * 128 + nt], start=True, stop=True),
                             [kk_, kq], ["ps5"])
                    yield
                    if gi == 0:
                        S.dve(lambda e: e.tensor_tensor(out=attT[:16, 0, :16], in0=ps[5][:16, 0:16], in1=self.mask2[:16, :16],
                                                        op=ALU.mult), ["ps5"], ["attT"])
                    else:
                        S.dve(lambda e: e.tensor_tensor(
                            out=attT[:, :, :], in0=ps[5][:, :].rearrange("p (a b) -> p a b", b=128),
                            in1=self.mask2[:, :].unsqueeze(1).to_broadcast([128, 4, 128]), op=ALU.mult), ["ps5"], ["attT"])
                    yield
                    for cl in range(nch):
                        li, r0 = cl // 2, (cl % 2) * 64
                        nr = 16 if gi == 0 else 64
                        bi = 3 + cl % 2
                        S.pe(lambda e, cl=cl, li=li, r0=r0, nr=nr, bi=bi: e.matmul(
                            ps[bi][:, (cl // 2) * 128:(cl // 2 + 1) * 128], lhsT=ktok[r0:r0 + nr, li, :],
                            rhs=vtok[r0:r0 + nr, li, :], start=True, stop=True), ["ktok", kv], ["ps%d" % bi])
                    yield
                    for cl in range(nch):
                        bi = 3 + cl % 2
                        dcol = B[:, 15:16] if gi == 0 else B[:, cl * 64 + 63:cl * 64 + 64]
                        S.dve(lambda e, cl=cl, bi=bi, dcol=dcol: e.scalar_tensor_tensor(
                            out=S32[:, cl + 1, :], in0=S32[:, cl, :], scalar=dcol,
                            in1=ps[bi][:, (cl // 2) * 128:(cl // 2 + 1) * 128], op0=ALU.mult, op1=ALU.add),
                            [("S32", cl), kB, "ps%d" % bi], [("S32", cl + 1)])
                    yield
                    S.act(lambda e: e.activation(out=Sb[:, 0:nch, :], in_=S32[:, 0:nch, :], func=AF.Copy),
                          [("S32", c) for c in range(nch)], ["Sb"])
                    yield
                    for li, ti in enumerate(tl_list):
                        t0, nt = TT[ti]
                        S.pe(lambda e, li=li, nt=nt: e.matmul(
                            ps[6][:, li * 128:li * 128 + nt], lhsT=vtok[:nt, li, :], rhs=attT[:nt, li, :nt],
                            start=True, stop=(gi == 0)), [kv, "attT"], ["ps6"])
                        if gi > 0:
                            for hh in range(2):
                                cl = 2 * li + hh
                                S.pe(lambda e, hh=hh, cl=cl: e.matmul(
                                    ps[6][:, cl * 64:(cl + 1) * 64], lhsT=Sb[:, cl, :], rhs=qin[:, cl * 64:(cl + 1) * 64],
                                    start=False, stop=(hh == 1)), ["Sb", kq], ["ps6"])
                    yield
                    S.dve(lambda e: e.tensor_copy(out=S32[:, 0, :], in_=S32[:, nch, :]), [("S32", nch), "Sb"], [("S32", 0)])
                    yield
                    self.rstd_from([(ps[6][:, :n], ["ps6"])], n, sqh, rtmp, rstd, ps[5], "ps5", dscale=1.0 / 128)
                    yield
                    S.dve(lambda e: e.scalar_tensor_tensor(
                        out=O32[:, :n], in0=ps[6][:, :n], scalar=self.colv[:, 80 + hd:81 + hd], in1=rstd[:, :n],
                        op0=ALU.mult, op1=ALU.mult), ["ps6", "rstd"], ["O32"])
                    yield
                    S.pool(lambda e: e.tensor_tensor(out=og[:, hd, g0:g0 + n], in0=O32[:, :n], in1=SG[:, :n], op=ALU.mult),
                           ["O32", kSG], [("og", gi)])

                def drain(g):
                    for _ in g:
                        pass

                def zipper(ga, gb, ra=1, rb=2):
                    alive_a, alive_b = True, True
                    while alive_a or alive_b:
                        for _ in range(ra):
                            if alive_a:
                                try:
                                    next(ga)
                                except StopIteration:
                                    alive_a = False
                        for _ in range(rb):
                            if alive_b:
                                try:
                                    next(gb)
                                except StopIteration:
                                    alive_b = False

                if nheads > 0:
                    load_head(0)
                    drain(stage1(0))
                for it in range(len(iters)):
                    if it + 1 < len(iters):
                        zipper(stage1(it + 1), stage2(it), 3, 2)
                    else:
                        drain(stage2(it))
            self.S.barrier()
            if self.stop in ("mix0a", "mix0b", "mix0c"):
                return
            with ExitStack() as st:
                self.out_proj_residual(st, self.a_w_out[0], og, 0, 1, "og")

    def ffn(self, s, l):
        S, ps, hT = self.S, self.ps, self.hT
        halves = [(0, 1032), (1032, 1032)]
        BLK = 344
        with ExitStack() as st0:
            halo = self.sb(st0, "halo", [128, 8, 2], BF16)
            S.dve(lambda e: e.memset(halo[:], 0.0), [], ["halo"])
            def half(hf, h0, nh):
                with ExitStack() as st1:
                    act = self.sb(st1, "act", [128, NJ, 1032], BF16)
                    blocks = [(o, BLK) for o in range(0, nh, BLK)]
                    with ExitStack() as st:
                        xn = self.sb(st, "xn2", [128, 8, 1034], BF16)
                        sq = self.sb(st, "sq2", [128, 8, 512], BF16)
                        rtmp = self.sb(st, "rtmp2", [128, 512], F32)
                        rstd = self.sb(st, "rstd2", [128, 512], F32)
                        S.dve(lambda e: e.tensor_copy(out=xn[:, :, 0:2], in_=halo[:]), ["halo"], [("xn2", -1)])
                        subs = list(blocks)
                        for si, (o, nn) in enumerate(subs):
                            g0 = h0 + o
                            self.rstd_from([(hT[:, c, g0:g0 + nn], []) for c in range(8)], nn, sq, rtmp, rstd, ps[2], "ps2")
                            for c in range(8):
                                S.dve(lambda e, c=c, g0=g0, nn=nn, o=o: e.scalar_tensor_tensor(
                                    out=xn[:, c, 2 + o:2 + o + nn], in0=hT[:, c, g0:g0 + nn], scalar=self.gcol(l, 2, c),
                                    in1=rstd[:, :nn], op0=ALU.mult, op1=ALU.mult), ["rstd"], [("xn2", si)])
                        xkeys = [("xn2", -1)] + [("xn2", si) for si in range(len(subs))]
                        S.dve(lambda e, nh=nh: e.tensor_copy(out=halo[:], in_=xn[:, :, nh:nh + 2]), xkeys, ["halo"])
                        wu = [self.sb(st, "wu%d" % i, [128, 8, 2, 128], BF16) for i in range(2)]
                        G32 = [self.sb(st, "G32_%d" % i, [128, 352], F32) for i in range(3)]
                        V32 = [self.sb(st, "V32_%d" % i, [128, 352], F32) for i in range(3)]
                        SGf = [self.sb(st, "SGf_%d" % i, [128, 352], F32) for i in range(3)]
                        wup = self.w_up[l].rearrange("(kc p) n -> p kc n", p=128)
                        units = [(j, bi_, o, nb) for j in range(NJ) for bi_, (o, nb) in enumerate(blocks)]

                        def load_pair(j):
                            mj = 128 if j < 21 else 64
                            sl = j % 2
                            for part in range(2):
                                self.wload(wu[sl][:, :, part, :mj], wup[:, :, part * DFF + j * 128: part * DFF + j * 128 + mj],
                                           sl, ("wu", sl, part))

                        load_pair(0)

                        def front(k):
                            j, bi_, o, nb = units[k]
                            mj = 128 if j < 21 else 64
                            sl = j % 2
                            w = wu[sl]
                            ub = k % 3
                            if bi_ == 0 and j + 1 < NJ:
                                load_pair(j + 1)
                            for part in range(2):
                                bi = part + 2 * ub
                                bank = ps[bi]
                                bk = "ps%d" % bi
                                for kc in range(8):
                                    S.pe(lambda e, bank=bank, part=part, kc=kc: e.matmul(
                                        bank[:mj, :nb + 2], lhsT=w[:, kc, part, :mj], rhs=xn[:, kc, o:o + nb + 2],
                                        start=(kc == 0), stop=(kc == 7)), [("wu", sl, part)] + xkeys, [bk])
                                dst = (G32 if part == 0 else V32)[ub]
                                dk = ("G32" if part == 0 else "V32", ub)
                                S.act(lambda e, bank=bank, dst=dst, part=part: e.activation(
                                    out=dst[:mj, :nb], in_=bank[:mj, 0:nb], func=AF.Identity, scale=self.ccol(l, 0, part, j)[:mj, :]),
                                    [bk], [dk])

                        def taps(k):
                            j, bi_, o, nb = units[k]
                            mj = 128 if j < 21 else 64
                            ub = k % 3
                            for tap in (1, 2):
                                for part in range(2):
                                    bi = part + 2 * ub
                                    bank = ps[bi]
                                    bk = "ps%d" % bi
                                    dst = (G32 if part == 0 else V32)[ub]
                                    dk = ("G32" if part == 0 else "V32", ub)
                                    S.dve(lambda e, bank=bank, dst=dst, part=part, tap=tap: e.scalar_tensor_tensor(
                                        out=dst[:mj, :nb], in0=bank[:mj, tap:tap + nb], scalar=self.ccol(l, tap, part, j)[:mj, :],
                                        in1=dst[:mj, :nb], op0=ALU.mult, op1=ALU.add), [bk, dk], [dk])

                        def back(k):
                            j, bi_, o, nb = units[k]
                            mj = 128 if j < 21 else 64
                            ub = k % 3
                            S.act(lambda e: e.activation(out=SGf[ub][:mj, :nb], in_=G32[ub][:mj, :nb], func=AF.Silu),
                                  [("G32", ub)], [("SGf", ub)])
                            S.dve(lambda e: e.tensor_tensor(out=act[:mj, j, o:o + nb], in0=SGf[ub][:mj, :nb], in1=V32[ub][:mj, :nb],
                                                            op=ALU.mult), [("SGf", ub), ("V32", ub)], [("act", j, bi_)])

                        for k in range(len(units)):
                            front(k)
                            if k > 0:
                                back(k - 1)
                            taps(k)
                        back(len(units) - 1)
                    S.barrier()
                    with ExitStack() as st:
                        wd = [self.sb(st, "wd%d" % i, [128, NJ, 128], BF16) for i in range(2)]
                        mix = self.sb(st, "mixf", [128, 8, 1032], F32)
                        sq3 = self.sb(st, "sq3", [128, 8, 512], BF16)
                        rtmp3 = self.sb(st, "rtmp3", [128, 512], F32)
                        rstd3 = self.sb(st, "rstd3", [128, 512], F32)
                        tmp3 = self.sb(st, "tmp3", [128, 512], F32)
                        def load_dc(dc):
                            sl = dc % 2
                            self.wload(wd[sl][:, 0:21, :],
                                       self.w_down[l, 0:2688, dc * 128:(dc + 1) * 128].rearrange("(j p) n -> p j n", p=128),
                                       sl, ("wd", sl, 0))
                            self.wload(wd[sl][0:64, 21, :], self.w_down[l, 2688:2752, dc * 128:(dc + 1) * 128], sl, ("wd", sl, 1))

                        load_dc(0)
                        for dc in range(8):
                            sl = dc % 2
                            w = wd[sl]
                            if dc + 1 < 8:
                                load_dc(dc + 1)
                            for bi_, (o, nb) in enumerate(blocks):
                                bq = (dc * len(blocks) + bi_) % 2
                                bank = ps[bq]
                                bk = "ps%d" % bq
                                for j in range(NJ):
                                    mj = 128 if j < 21 else 64
                                    S.pe(lambda e, bank=bank, j=j, mj=mj, o=o, nb=nb, w=w: e.matmul(
                                        bank[:, :nb], lhsT=w[:mj, j, :], rhs=act[:mj, j, o:o + nb],
                                        start=(j == 0), stop=(j == NJ - 1)),
                                        [("wd", sl, 0), ("wd", sl, 1), ("act", j, bi_)], [bk])
                                S.act(lambda e, bank=bank, dc=dc, o=o, nb=nb: e.activation(
                                    out=mix[:, dc, o:o + nb], in_=bank[:, :nb], func=AF.Copy), [bk], [("mix", dc, bi_)])
                        for bi_, (o, nb) in enumerate(blocks):
                            g0 = h0 + o
                            self.rstd_from([(mix[:, dc, o:o + nb], [("mix", dc, bi_)]) for dc in range(8)], nb, sq3, rtmp3, rstd3,
                                           ps[2], "ps2")
                            for dc in range(8):
                                S.dve(lambda e, dc=dc, o=o, nb=nb: e.scalar_tensor_tensor(
                                    out=tmp3[:, :nb], in0=mix[:, dc, o:o + nb], scalar=self.gcol(l, 3, dc), in1=rstd3[:, :nb],
                                    op0=ALU.mult, op1=ALU.mult), [("mix", dc, bi_), "rstd"], ["tmp3"])
                                S.dve(lambda e, dc=dc, g0=g0, nb=nb: e.tensor_tensor(
                                    out=hT[:, dc, g0:g0 + nb], in0=hT[:, dc, g0:g0 + nb], in1=tmp3[:, :nb], op=ALU.add),
                                    ["tmp3"], [("hT", dc, hf, bi_)])
                    S.barrier()

            for hf, (h0, nh) in enumerate(halves):
                half(hf, h0, nh)

    def fox(self, s):
        S, ps, psb, hT = self.S, self.ps, self.psb, self.hT
        scale = 1.0 / 8.0
        with ExitStack() as st0:
            O = self.sb(st0, "O", [128, 8, T], BF16)
            with ExitStack() as st:
                xr = self.sb(st, "xr", [128, 8, T], BF16)
                Ctok = self.sb(st, "Ctok", [128, 17, 16], F32)
                Cpb = self.sb(st, "Cpb", [16, T], BF16)
                with ExitStack() as stn:
                    sq = self.sb(stn, "sq4", [128, 8, 512], BF16)
                    rtmp = self.sb(stn, "rtmp4", [128, 512], F32)
                    rstd = self.sb(stn, "rstd4", [128, 512], F32)
                    for gi, (g0, n) in enumerate(GG):
                        self.rstd_from([(hT[:, c, g0:g0 + n], []) for c in range(8)], n, sq, rtmp, rstd, ps[2], "ps2")
                        for c in range(8):
                            S.dve(lambda e, c=c, g0=g0, n=n: e.tensor_tensor(
                                out=xr[:, c, g0:g0 + n], in0=hT[:, c, g0:g0 + n], in1=rstd[:, :n], op=ALU.mult),
                                ["rstd"], [("xr", gi)])
                    S.barrier()
                xrk = [("xr", gi) for gi in range(5)]
                kvw = self.kv_w.rearrange("(kc p) n -> p kc n", p=128)
                wqv = self.b_w_q[0].rearrange("(kc p) n -> p kc n", p=128)
                gkv = self.colv[:, 88:96]
                with ExitStack() as stf:
                    wfg = self.sb(stf, "wfg", [128, 8, 16], BF16)
                    Cp = self.sb(stf, "Cp", [16, T], F32)
                    sp = self.sb(stf, "sp", [16, 512], F32)
                    self.wload(wfg[:, :, :], kvw[:, :, 2048:2064], 0, "wfg")
                    S.dve(lambda e: e.tensor_tensor(out=wfg[:, :, :], in0=wfg[:, :, :],
                                                    in1=gkv.unsqueeze(2).to_broadcast([128, 8, 16]), op=ALU.mult),
                          ["wfg"], ["wfg"])
                    for gi, (g0, n) in enumerate(GG):
                        for kc in range(8):
                            S.pe(lambda e, kc=kc, g0=g0, n=n: e.matmul(ps[0][:16, :n], lhsT=wfg[:, kc, :], rhs=xr[:, kc, g0:g0 + n],
                                                                        start=(kc == 0), stop=(kc == 7)), ["wfg", ("xr", gi)], ["ps0"])
                        S.act(lambda e, n=n: e.activation(out=sp[:, :n], in_=ps[0][:16, :n], func=AF.Exp, scale=-1.0,
                                                          bias=self.nfgb[:, 0:1]), ["ps0", "nfgb2"], ["sp"])
                        S.act(lambda e, n=n: e.activation(out=sp[:, :n], in_=sp[:, :n], func=AF.Ln, scale=1.0,
                                                          bias=self.onec[0:16, 0:1]), ["sp"], ["sp"])
                        init = 0.0 if gi == 0 else Cp[:, g0 - 1:g0]
                        S.dve(lambda e, g0=g0, n=n, init=init: e.tensor_tensor_scan(
                            out=Cp[:, g0:g0 + n], data0=self.onesf[:16, :n], data1=sp[:, :n], initial=init,
                            op0=ALU.mult, op1=ALU.add), ["sp", ("Cp", gi - 1)], [("Cp", gi)])
                    cpk = [("Cp", gi) for gi in range(5)]
                    S.act(lambda e: e.activation(out=Cpb[:, :], in_=Cp[:, :], func=AF.Copy), cpk, ["Cpb"])
                    for ti, (t0, nt) in enumerate(TT):
                        S.pe(lambda e, t0=t0, nt=nt: e.transpose(ps[1][:nt, 0:16], Cp[:, t0:t0 + nt], self.i16[:]), cpk, ["ps1"])
                        S.dve(lambda e, ti=ti, nt=nt: e.tensor_copy(out=Ctok[:nt, ti, :], in_=ps[1][:nt, 0:16]), ["ps1"], ["Ctok"])
                    S.barrier()
                wp = [self.sb(st, "wp%d" % i, [128, 8, 3, 128], BF16) for i in range(2)]
                KTh = [self.sb(st, "KT%d" % i, [65, T], BF16) for i in range(2)]
                QTh = [self.sb(st, "QT%d" % i, [65, T], BF16) for i in range(2)]
                Va = self.sb(st, "Va", [128, 17, 192], BF16)
                NPT = 4
                PT = [self.sb(st, "PT%d" % i, [128, 512], BF16) for i in range(NPT)]
                rinv = [self.sb(st, "rinv%d" % i, [128, 512], F32) for i in range(2)]
                S.pool(lambda e: e.memset(Va[:, :, 64:128], 1.0), [], ["Vones"])
                for hh in range(2):
                    S.pool(lambda e, hh=hh: e.memset(KTh[hh][64:65, :], 1.0), [], [("Kone", hh)])
                g0col = self.colv[:, 32:40]
                itc = 0
                def load_wp(p):
                    sl = p % 2
                    w = wp[sl]
                    self.wload(w[:, :, 0, :], kvw[:, :, p * 128:(p + 1) * 128], sl, ("wp", sl, 0))
                    self.wload(w[:, :, 1, :], kvw[:, :, D + p * 128:D + (p + 1) * 128], sl, ("wp", sl, 1))
                    self.wload(w[:, :, 2, :], wqv[:, :, p * 128:(p + 1) * 128], sl, ("wp", sl, 2))
                    for m in range(3):
                        gc = gkv if m < 2 else g0col
                        S.dve(lambda e, w=w, m=m, gc=gc: e.tensor_tensor(
                            out=w[:, :, m, :], in0=w[:, :, m, :], in1=gc.unsqueeze(2).to_broadcast([128, 8, 128]), op=ALU.mult),
                            [("wp", sl, m)], [("wp", sl, m)])

                load_wp(0)
                for p in range(8):
                    sl = p % 2
                    w = wp[sl]
                    if p + 1 < 8:
                        load_wp(p + 1)
                    for gi, (g0, n) in enumerate(GG):
                        for (m, dst, dk) in ((0, KTh, "KT"), (2, QTh, "QT")):
                            bank = ps[m // 2]
                            bk = "ps%d" % (m // 2)
                            for kc in range(8):
                                S.pe(lambda e, bank=bank, m=m, kc=kc, g0=g0, n=n, w=w: e.matmul(
                                    bank[:, :n], lhsT=w[:, kc, m, :], rhs=xr[:, kc, g0:g0 + n], start=(kc == 0), stop=(kc == 7)),
                                    [("wp", sl, m), ("xr", gi)], [bk])
                            for hh in range(2):
                                S.dve(lambda e, bank=bank, dst=dst, hh=hh, g0=g0, n=n: e.tensor_copy(
                                    out=dst[hh][0:64, g0:g0 + n], in_=bank[hh * 64:(hh + 1) * 64, :n]), [bk], [(dk, hh, gi)])
                        for hh in range(2):
                            h = 2 * p + hh
                            S.pe(lambda e, h=h, g0=g0, n=n: e.matmul(ps[2][0:65, :n], lhsT=self.selq[:, h, :], rhs=Cpb[:, g0:g0 + n],
                                                                      start=True, stop=True), ["Cpb", "selq"], ["ps2"])
                            S.dve(lambda e, hh=hh, g0=g0, n=n: e.tensor_copy(out=QTh[hh][64:65, g0:g0 + n], in_=ps[2][64:65, :n]),
                                  ["ps2"], [("QT", hh, gi)])
                    for kt, (t0, nt) in enumerate(TT):
                        for kc in range(8):
                            S.pe(lambda e, kc=kc, t0=t0, nt=nt, w=w: e.matmul(
                                ps[2][:nt, 0:128], lhsT=xr[:, kc, t0:t0 + nt], rhs=w[:, kc, 1, :], start=(kc == 0), stop=(kc == 7)),
                                [("wp", sl, 1)] + xrk, ["ps2"])
                        S.dve(lambda e, kt=kt, nt=nt: e.tensor_copy(
                            out=Va[:nt, kt, :].rearrange("p (a b) -> p a b", b=64)[:, 0:3:2, :],
                            in_=ps[2][:nt, 0:128].rearrange("p (a b) -> p a b", b=64)), ["ps2"], [("Va", kt)])
                    its = []
                    for hh in range(2):
                        for gi, (g0, n) in enumerate(GG):
                            tl = tiles_of_group(gi)
                            for kt in range(tl[-1] + 1):
                                its.append((hh, gi, kt))
                    LOOK = 2
                    SB = [5, 6, 2]

                    def emit_qk(ix):
                        hh, gi, kt = its[ix]
                        g0, n = GG[gi]
                        tl = tiles_of_group(gi)
                        k0, nk = TT[kt]
                        c0 = (kt - tl[0]) * 128 if (kt in tl and gi > 0) else 0
                        nq = n - c0
                        sbi = SB[(itc0 + ix) % 3]
                        sb_ = ps[sbi]
                        KT, QT = KTh[hh], QTh[hh]
                        S.pe(lambda e: e.matmul(sb_[:nk, :nq], lhsT=KT[0:65, k0:k0 + nk], rhs=QT[0:65, g0 + c0:g0 + c0 + nq],
                                                start=True, stop=True),
                             [("KT", hh, g_) for g_ in range(5)] + [("QT", hh, gi), ("Kone", hh)], ["ps%d" % sbi])

                    def emit_rest(ix, p=p):
                        hh, gi, kt = its[ix]
                        h = 2 * p + hh
                        g0, n = GG[gi]
                        tl = tiles_of_group(gi)
                        last = tl[-1]
                        k0, nk = TT[kt]
                        c0 = (kt - tl[0]) * 128 if (kt in tl and gi > 0) else 0
                        nq = n - c0
                        sbi = SB[(itc0 + ix) % 3]
                        sb_ = ps[sbi]
                        sbk = "ps%d" % sbi
                        pt = PT[(itc0 + ix) % NPT]
                        ptk = ("PT", (itc0 + ix) % NPT)
                        vlo = 0 if hh == 0 else 64
                        orow = hh * 64
                        lrow = 64 - orow
                        obi = 3 + (gi + hh) % 2
                        ob = ps[obi]
                        obk = "ps%d" % obi
                        S.act(lambda e: e.activation(out=pt[:nk, :nq], in_=sb_[:nk, :nq], func=AF.Exp, scale=scale,
                                                     bias=Ctok[:nk, kt, h:h + 1]), [sbk, "Ctok"], [ptk])
                        if kt in tl:
                            qn = min(128, nq)
                            S.pool(lambda e: e.tensor_tensor(out=pt[:nk, 0:qn], in0=pt[:nk, 0:qn], in1=self.triu[:nk, :qn],
                                                             op=ALU.mult), [ptk], [ptk])
                        S.pe(lambda e: e.matmul(ob[:, c0:c0 + nq], lhsT=Va[:nk, kt, vlo:vlo + 128], rhs=pt[:nk, 0:nq],
                                                start=(kt == 0), stop=(kt == last)), [ptk, ("Va", kt), "Vones"], [obk])
                        if kt == last:
                            rv = rinv[(gi + hh) % 2]
                            rk = ("rinv", (gi + hh) % 2)
                            S.dve(lambda e: e.reciprocal(out=rv[orow:orow + 64, :n], in_=ob[lrow:lrow + 64, :n]), [obk], [rk])
                            S.dve(lambda e: e.tensor_tensor(out=O[orow:orow + 64, p, g0:g0 + n], in0=ob[orow:orow + 64, :n],
                                                            in1=rv[orow:orow + 64, :n], op=ALU.mult), [obk, rk], [("O", gi)])

                    itc0 = itc
                    for ix in range(min(LOOK, len(its))):
                        emit_qk(ix)
                    for ix in range(len(its)):
                        if ix + LOOK < len(its):
                            emit_qk(ix + LOOK)
                        emit_rest(ix)
                    itc += len(its)
            self.S.barrier()
            with ExitStack() as st:
                self.out_proj_residual(st, self.b_w_out[0], O, 1, 1, "O")


_CACHE = {}


def _get_nc(nseq=2, stop=None):
    key = (nseq, stop)
    if key not in _CACHE:
        _CACHE[key] = Builder(nseq, stop).build()
    return _CACHE[key]


def kernel(**inputs):
    ncores = 8
    nc = _get_nc(2, None)
    shared = {k: np.ascontiguousarray(np.asarray(v, dtype=np.float32)) for k, v in inputs.items() if k != "x"}
    x = np.ascontiguousarray(np.asarray(inputs["x"], dtype=np.float32))
    in_maps = []
    for c in range(ncores):
        m = dict(shared)
        m["x"] = x[2 * c:2 * c + 2]
        in_maps.append(m)
    res = run_bass_kernel_spmd(nc, in_maps, core_ids=list(range(ncores)))
    return np.concatenate([np.asarray(r["out"]) for r in res.results], axis=0).astype(np.float32)
```

```python
import numpy as np
import concourse.bass as bass
import concourse.mybir as mybir
from concourse.bass_utils import run_bass_kernel_spmd
from contextlib import ExitStack

F32 = mybir.dt.float32
BF16 = mybir.dt.bfloat16
AF = mybir.ActivationFunctionType
ALU = mybir.AluOpType
AX = mybir.AxisListType

SAME_ENG_SYNC = True


class _Op:
    __slots__ = ("eng", "fn", "reads", "writes", "dma_sem", "ndma", "deps",
                 "needs_inc", "token", "waits", "idx", "is_bar")

    def __init__(self, eng, fn, reads, writes, dma_sem=None, ndma=0):
        self.eng = eng
        self.fn = fn
        self.reads = reads
        self.writes = writes
        self.dma_sem = dma_sem
        self.ndma = ndma
        self.deps = []
        self.needs_inc = False
        self.token = None
        self.waits = []
        self.is_bar = False


class Sched:
    CENG = ("pe", "act", "dve", "pool")
    ALLENG = ("pe", "act", "dve", "pool", "sp")

    def __init__(self, nc, stack):
        self.nc = nc
        self.stack = stack
        self.ops = []
        self.esem = {e: stack.enter_context(nc.semaphore("s_" + e)) for e in self.CENG}
        self.dma_cum = {}
        self.dma_sems = {}
        self.dma_exempt = set()
        self.last_w = {}
        self.readers = {}
        self.last_op = {e: None for e in self.ALLENG}
        self.dma_last = {}

    def dma_sem(self, name, exempt=False):
        if name not in self.dma_sems:
            self.dma_sems[name] = self.stack.enter_context(self.nc.semaphore("d_" + name))
            self.dma_cum[name] = 0
            if exempt:
                self.dma_exempt.add(name)
        return name

    def _add(self, op):
        op.idx = len(self.ops)
        deps = set()
        for k in op.reads:
            w = self.last_w.get(k)
            if w is not None:
                deps.add(w)
        for k in op.writes:
            w = self.last_w.get(k)
            if w is not None:
                deps.add(w)
            for r in self.readers.get(k, ()):
                deps.add(r)
        deps.discard(op)
        op.deps = sorted(deps, key=lambda o: o.idx)
        for k in op.reads:
            self.readers.setdefault(k, []).append(op)
        for k in op.writes:
            self.last_w[k] = op
            self.readers[k] = []
        self.ops.append(op)
        self.last_op[op.eng] = op
        if op.dma_sem is not None:
            self.dma_cum[op.dma_sem] += 16 * op.ndma
            op.token = (op.dma_sem, self.dma_cum[op.dma_sem])
            self.dma_last[op.dma_sem] = op
        return op

    def op(self, eng, fn, reads=(), writes=()):
        return self._add(_Op(eng, fn, tuple(reads), tuple(writes)))

    def pe(self, fn, reads=(), writes=()):
        return self.op("pe", fn, reads, writes)

    def act(self, fn, reads=(), writes=()):
        return self.op("act", fn, reads, writes)

    def dve(self, fn, reads=(), writes=()):
        return self.op("dve", fn, reads, writes)

    def pool(self, fn, reads=(), writes=()):
        return self.op("pool", fn, reads, writes)

    def dma(self, eng, fn, sem, reads=(), writes=(), n=1):
        return self._add(_Op(eng, fn, tuple(reads), tuple(writes), dma_sem=sem, ndma=n))

    def barrier(self):
        prev = dict(self.last_op)
        dl = {k: v for k, v in self.dma_last.items() if k not in self.dma_exempt}
        for e in self.ALLENG:
            b = _Op(e, None, (), ())
            b.is_bar = True
            b.idx = len(self.ops)
            b.deps = [o for ee, o in prev.items() if o is not None and (ee != e or (SAME_ENG_SYNC and e != 'pe'))] + list(dl.values())
            self.ops.append(b)
            self.last_op[e] = b

    def finalize(self):
        for op in self.ops:
            for d in op.deps:
                if d.dma_sem is not None or d.is_bar:
                    continue
                if d.eng == op.eng and (d.eng == "pe" or not SAME_ENG_SYNC):
                    continue
                d.needs_inc = True
        cnt = {e: 0 for e in self.CENG}
        for op in self.ops:
            if op.dma_sem is None and op.needs_inc:
                cnt[op.eng] += 1
                op.token = (op.eng, cnt[op.eng])
        known = {e: {} for e in self.ALLENG}
        for op in self.ops:
            kn = known[op.eng]
            need = {}
            for d in op.deps:
                if d.token is None:
                    continue
                if d.dma_sem is None and d.eng == op.eng and (d.eng == "pe" or not SAME_ENG_SYNC):
                    continue
                s, v = d.token
                if need.get(s, 0) < v:
                    need[s] = v
            for s, v in need.items():
                if kn.get(s, 0) >= v:
                    continue
                kn[s] = v
                op.waits.append((s, v))
        self.counts = cnt

    def _sem(self, s):
        return self.esem[s] if s in self.esem else self.dma_sems[s]

    def emit(self):
        self.finalize()
        by_eng = {e: [o for o in self.ops if o.eng == e] for e in self.ALLENG}
        with self.nc.Block() as block:
            def run(e):
                def body(eng):
                    for op in by_eng[e]:
                        for (s, v) in op.waits:
                            eng.wait_ge(self._sem(s), v)
                        if op.fn is None:
                            continue
                        if op.dma_sem is not None:
                            op.fn(eng, self.dma_sems[op.dma_sem])
                        else:
                            ins = op.fn(eng)
                            if op.needs_inc:
                                ins.then_inc(self.esem[e], 1)
                return body
            block.tensor(run("pe"))
            block.scalar(run("act"))
            block.vector(run("dve"))
            block.gpsimd(run("pool"))
            block.sync(run("sp"))


import os
CUT = int(os.environ.get('KCUT', '99'))
D = 1024
T = 2064
NMETA = 16
DFF = 2752
NJ = 22
EPS = 1e-6
TT = [(0, 16)] + [(16 + 128 * i, 128) for i in range(16)]
GG = [(0, 16)] + [(16 + 512 * j, 512) for j in range(4)]


def tiles_of_group(gi):
    return [0] if gi == 0 else list(range(1 + 4 * (gi - 1), 1 + 4 * gi))


class Builder:
    def __init__(self, nseq=2, stop=None):
        self.nseq = nseq
        self.stop = stop
        nc = self.nc = bass.Bass("TRN2", target_bir_lowering=False)
        dt = lambda name, shape: nc.dram_tensor(name, shape, F32, kind="ExternalInput").ap()
        self.x = dt("x", [nseq, 2048, D])
        self.meta = dt("meta_tokens", [NMETA, D])
        self.norm_gains = dt("norm_gains", [2, 4, D])
        self.a_w_in = dt("a_w_in", [1, D, 4 * D])
        self.a_lb = dt("a_lb_logits", [2, D])
        self.a_hn = dt("a_head_norm", [1, D])
        self.a_w_out = dt("a_w_out", [1, D, D])
        self.kv_norm = dt("kv_norm", [D])
        self.kv_w = dt("kv_w", [D, 2 * D + 16])
        self.fg_b = dt("fg_b", [16])
        self.b_w_q = dt("b_w_q", [1, D, D])
        self.b_w_out = dt("b_w_out", [1, D, D])
        self.w_up = dt("ffn_w_up", [2, D, 2 * DFF])
        self.conv = dt("ffn_conv", [2, 3, 2 * DFF])
        self.w_down = dt("ffn_w_down", [2, DFF, D])
        self.out = nc.dram_tensor("out", [nseq, 2048, D], F32, kind="ExternalOutput").ap()
        self.uid = 0

    def sb(self, st, name, shape, dtype):
        self.uid += 1
        return st.enter_context(self.nc.sbuf_tensor("%s_%d" % (name, self.uid), shape, dtype))

    def build(self):
        nc = self.nc
        with ExitStack() as st:
            S = self.S = Sched(nc, st)
            self.ps = [st.enter_context(nc.psum_tensor("ps%d" % i, [128, 512], F32)) for i in range(7)]
            self.psb = st.enter_context(nc.psum_tensor("psb", [128, 1024], BF16))
            self.hT = self.sb(st, "hT", [128, 8, T], F32)
            self.consts(st)
            S.barrier()
            for s in range(self.nseq):
                self.seq(s)
            S.barrier()
            S.emit()
        return nc

    def consts(self, st):
        S = self.S
        self.ident = self.sb(st, "ident", [128, 128], F32)
        self.identb = self.sb(st, "identb", [128, 128], BF16)
        self.onesb = self.sb(st, "onesb", [128, 128], BF16)
        self.triu = self.sb(st, "triu", [128, 128], BF16)
        self.mask2 = self.sb(st, "mask2", [128, 128], F32)
        self.maskseg = self.sb(st, "maskseg", [128, 512], F32)
        self.epsc = self.sb(st, "epsc", [128, 1], F32)
        self.colv = self.sb(st, "colv", [128, 96], F32)
        self.convT = self.sb(st, "convT", [128, 3, 128], F32)
        self.lbc = self.sb(st, "lbc", [128, 24], F32)
        self.nfgb = self.sb(st, "nfgb", [16, 1], F32)
        self.i16 = self.sb(st, "i16", [16, 16], F32)
        self.ones16 = self.sb(st, "ones16", [16, 128], F32)
        self.onesf = self.sb(st, "onesf", [16, 512], F32)
        self.onec = self.sb(st, "onec", [128, 1], F32)
        self.selq = self.sb(st, "selq", [16, 16, 65], BF16)
        P = lambda fn, r=(), w=(): S.pool(fn, r, w)
        P(lambda e: e.memset(self.ident[:], 1.0), w=["ident"])
        P(lambda e: e.affine_select(out=self.ident[:], in_=self.ident[:], pattern=[[-1, 128]],
                                    compare_op=ALU.is_equal, fill=0.0, base=0, channel_multiplier=1),
          r=["ident"], w=["ident"])
        S.dve(lambda e: e.tensor_copy(out=self.identb[:], in_=self.ident[:]), ["ident"], ["identb"])
        S.dve(lambda e: e.tensor_copy(out=self.i16[:], in_=self.ident[0:16, 0:16]), ["ident"], ["i16"])
        P(lambda e: e.memset(self.onesb[:], 1.0), w=["onesb"])
        P(lambda e: e.memset(self.ones16[:], 1.0), w=["ones16"])
        P(lambda e: e.memset(self.triu[:], 1.0), w=["triu"])
        P(lambda e: e.affine_select(out=self.triu[:], in_=self.triu[:], pattern=[[1, 128]],
                                    compare_op=ALU.is_ge, fill=0.0, base=0, channel_multiplier=-1),
          r=["triu"], w=["triu"])
        P(lambda e: e.memset(self.mask2[:], 1.0), w=["mask2"])
        P(lambda e: e.affine_select(out=self.mask2[:], in_=self.mask2[:], pattern=[[1, 128]],
                                    compare_op=ALU.is_ge, fill=0.0, base=0, channel_multiplier=-1),
          r=["mask2"], w=["mask2"])
        P(lambda e: e.memset(self.mask2[0:64, 64:128], 0.0), r=["mask2"], w=["mask2"])
        P(lambda e: e.memset(self.maskseg[:], 1.0), w=["maskseg"])
        P(lambda e: e.memset(self.maskseg[:].rearrange("p (c k) -> p c k", k=64)[:, :, 0:1], 0.0),
          r=["maskseg"], w=["maskseg"])
        P(lambda e: e.memset(self.epsc[:], EPS), w=["epsc"])
        P(lambda e: e.memset(self.onesf[:], 1.0), w=["onesf"])
        P(lambda e: e.memset(self.onec[:], 1.0), w=["onec"])
        P(lambda e: e.memset(self.selq[:], 0.0), w=["selq0"])
        S.dve(lambda e: e.tensor_scalar(out=self.selq[:, :, 64], in0=self.i16[:, :], scalar1=-8.0, scalar2=None, op0=ALU.mult),
              ["selq0", "i16"], ["selq"])
        rowsA = self.sb(st, "rowsA", [96, 128], F32)
        rowsC = self.sb(st, "rowsC", [128, 3, 128], F32)
        P(lambda e: e.memset(rowsC[:], 0.0), w=["rowsC"])
        cs = S.dma_sem("const")
        nd = [0]

        def ld(dst, src, rk):
            S.dma("sp", lambda e, s, dst=dst, src=src: e.dma_start(out=dst, in_=src).then_inc(s, 16), cs,
                  reads=[rk], writes=[("rowsd", nd[0])])
            nd[0] += 1
        ld(rowsA[0:64, :], self.norm_gains.rearrange("l j (c p) -> (l j c) p", p=128), "rowsA")
        ld(rowsA[64:80, :], self.a_lb.rearrange("l (c p) -> (l c) p", p=128), "rowsA")
        ld(rowsA[80:88, :], self.a_hn.rearrange("l (c p) -> (l c) p", p=128), "rowsA")
        ld(rowsA[88:96, :], self.kv_norm.rearrange("(c p) -> c p", p=128), "rowsA")
        for l in range(2):
            for tap in range(3):
                for part in range(2):
                    r0 = ((l * 3 + tap) * 2 + part) * 22
                    src = self.conv[l, tap, part * DFF: part * DFF + 2688].rearrange("(j k) -> j k", k=128)
                    done = 0
                    while done < 21:
                        ti, ri = divmod(r0 + done, 128)
                        cnt = min(21 - done, 128 - ri)
                        ld(rowsC[ri:ri + cnt, ti, :], src[done:done + cnt, :], "rowsC")
                        done += cnt
                    ti, ri = divmod(r0 + 21, 128)
                    ld(rowsC[ri:ri + 1, ti, 0:64],
                       self.conv[l, tap, part * DFF + 2688: part * DFF + 2752].rearrange("(a k) -> a k", a=1), "rowsC")
        ld(self.nfgb[:, :], self.fg_b.rearrange("(h a) -> h a", a=1), "nfgb")
        allrows = [("rowsd", i) for i in range(nd[0])]
        ps = self.ps
        S.pe(lambda e: e.transpose(ps[0][:, 0:96], rowsA[:, :], self.ident[0:96, 0:96]), allrows + ["ident"], ["ps0"])
        S.dve(lambda e: e.tensor_copy(out=self.colv[:], in_=ps[0][:, 0:96]), ["ps0"], ["colv"])
        for ti in range(3):
            S.pe(lambda e, ti=ti: e.transpose(ps[1][:, ti * 128:(ti + 1) * 128], rowsC[:, ti, :], self.ident[:]),
                 allrows + ["ident", "rowsC"], ["ps1"])
        S.dve(lambda e: e.tensor_copy(out=self.convT[:], in_=ps[1][:, 0:384].rearrange("p (a b) -> p a b", b=128)),
              ["ps1"], ["convT"])
        dl = self.sb(st, "dl", [128, 8], F32)
        S.dve(lambda e: e.tensor_tensor(out=dl[:], in0=self.colv[:, 64:72], in1=self.colv[:, 72:80], op=ALU.subtract),
              ["colv"], ["dl"])
        S.act(lambda e: e.activation(out=self.lbc[:, 0:8], in_=dl[:], func=AF.Sigmoid), ["dl"], ["lbc0"])
        S.act(lambda e: e.activation(out=self.lbc[:, 8:16], in_=dl[:], func=AF.Sigmoid, scale=-1.0), ["dl"], ["lbc1"])
        S.dve(lambda e: e.tensor_scalar(out=self.lbc[:, 16:24], in0=self.lbc[:, 8:16], scalar1=-1.0, scalar2=None,
                                        op0=ALU.mult), ["lbc1"], ["lbc2"])
        S.dve(lambda e: e.tensor_scalar(out=self.nfgb[:], in0=self.nfgb[:], scalar1=-1.0, scalar2=None, op0=ALU.mult),
              allrows, ["nfgb2"])

    def gcol(self, l, j, c):
        k = (l * 4 + j) * 8 + c
        return self.colv[:, k:k + 1]

    def ccol(self, l, tap, part, j):
        r = ((l * 3 + tap) * 2 + part) * 22 + j
        ti, ri = divmod(r, 128)
        return self.convT[:, ti, ri:ri + 1]

    def wload(self, dst, src, slot, key, reads=()):
        S = self.S
        sem = S.dma_sem("w_" + "_".join(str(k) for k in (key if isinstance(key, tuple) else (key,))), exempt=True)
        S.dma("pool", lambda e, s: e.dma_start(out=dst, in_=src).then_inc(s, 16), sem,
              reads=list(reads), writes=[key])

    def rstd_from(self, srcs, n, sq, rtmp, rstd, pst, pkey, dscale=1.0 / D):
        S = self.S
        nsrc = len(srcs)
        nsq = sq.shape[1]
        for c, (ap, rk) in enumerate(srcs):
            S.act(lambda e, ap=ap, c=c: e.activation(out=sq[:, c % nsq, :n], in_=ap, func=AF.Square), rk, [("sq", c % nsq)])
            S.pe(lambda e, c=c: e.matmul(pst[:, :n], lhsT=self.onesb[:], rhs=sq[:, c % nsq, :n], start=(c == 0),
                                         stop=(c == nsrc - 1)), [("sq", c % nsq)], [pkey])
        S.act(lambda e: e.activation(out=rtmp[:, :n], in_=pst[:, :n], func=AF.Ln, scale=dscale, bias=self.epsc[:, 0:1]),
              [pkey], ["rtmp"])
        S.act(lambda e: e.activation(out=rstd[:, :n], in_=rtmp[:, :n], func=AF.Exp, scale=-0.5), ["rtmp"], ["rstd"])

    def seq(self, s):
        S = self.S
        self.load_x(s)
        S.barrier()
        if self.stop != "load":
            self.hgrn2(s)
            S.barrier()
            if self.stop not in ("mix0", "mix0a", "mix0b", "mix0c"):
                self.ffn(s, 0)
                S.barrier()
                if self.stop != "ffn0":
                    self.fox(s)
                    S.barrier()
                    if self.stop != "mix1":
                        self.ffn(s, 1)
                        S.barrier()
        self.store(s)
        S.barrier()

    def load_x(self, s):
        S, ps, hT = self.S, self.ps, self.hT
        with ExitStack() as st:
            xin = [self.sb(st, "xin%d" % i, [128, D], F32) for i in range(2)]
            xs = [S.dma_sem("xin%d" % i) for i in range(2)]
            for ti, (t0, n) in enumerate(TT):
                sl = ti % 2
                src = self.meta if ti == 0 else self.x[s, t0 - 16:t0 - 16 + 128, :]
                S.dma("sp", lambda e, sm, sl=sl, src=src, n=n: e.dma_start(out=xin[sl][:n, :], in_=src).then_inc(sm, 16),
                      xs[sl], writes=[("xin", sl)])
                for half in range(2):
                    bank = ps[half + 2 * sl]
                    bk = "ps%d" % (half + 2 * sl)
                    for j in range(4):
                        c = half * 4 + j
                        S.pe(lambda e, bank=bank, j=j, c=c, n=n, sl=sl: e.transpose(
                            bank[:, j * 128:j * 128 + n], xin[sl][:n, c * 128:(c + 1) * 128], self.ident[:n, :n]),
                            [("xin", sl)], [bk])
                    fn = lambda e, bank=bank, half=half, t0=t0, n=n: e.tensor_copy(
                        out=hT[:, half * 4:(half + 1) * 4, t0:t0 + n],
                        in_=bank[:, :].rearrange("p (j k) -> p j k", k=128)[:, :, 0:n])
                    if half == 0:
                        S.dve(fn, [bk], [("hT", ti, half)])
                    else:
                        S.act(lambda e, bank=bank, half=half, t0=t0, n=n: e.activation(
                            out=hT[:, half * 4:(half + 1) * 4, t0:t0 + n],
                            in_=bank[:, :].rearrange("p (j k) -> p j k", k=128)[:, :, 0:n], func=AF.Copy),
                            [bk], [("hT", ti, half)])

    def store(self, s):
        S, ps, hT = self.S, self.ps, self.hT
        with ExitStack() as st:
            xo = [self.sb(st, "xo%d" % i, [128, D], F32) for i in range(2)]
            os_ = [S.dma_sem("xo%d" % i) for i in range(2)]
            for ti, (t0, n) in enumerate(TT):
                if ti == 0:
                    continue
                sl = ti % 2
                for half in range(2):
                    bank = ps[half + 2 * sl]
                    bk = "ps%d" % (half + 2 * sl)
                    for j in range(4):
                        c = half * 4 + j
                        S.pe(lambda e, bank=bank, j=j, c=c, t0=t0: e.transpose(
                            bank[:, j * 128:(j + 1) * 128], hT[:, c, t0:t0 + 128], self.ident[:]), [], [bk])
                    if half == 0:
                        S.dve(lambda e, bank=bank, sl=sl: e.tensor_copy(out=xo[sl][:, 0:512], in_=bank[:, :]),
                              [bk], [("xo", sl)])
                    else:
                        S.act(lambda e, bank=bank, sl=sl: e.activation(out=xo[sl][:, 512:1024], in_=bank[:, :], func=AF.Copy),
                              [bk], [("xo", sl)])
                S.dma("sp", lambda e, sm, sl=sl, t0=t0: e.dma_start(out=self.out[s, t0 - 16:t0 - 16 + 128, :],
                                                                     in_=xo[sl][:, :]).then_inc(sm, 16),
                      os_[sl], reads=[("xo", sl)], writes=[("xo", sl)])

    def out_proj_residual(self, st, wsrc, src_act, l, jn, tag):
        S, ps, hT = self.S, self.ps, self.hT
        wo = self.sb(st, "wo", [128, 8, D], BF16)
        mix32 = self.sb(st, "mix32", [128, 8, 512], F32)
        sq = self.sb(st, "sqo", [128, 8, 512], BF16)
        rtmp = self.sb(st, "rtmpo", [128, 512], F32)
        rstd = self.sb(st, "rstdo", [128, 512], F32)
        tmp = self.sb(st, "tmpo", [128, 512], F32)
        wv = wsrc.rearrange("(kc p) n -> p kc n", p=128)
        for kc in range(8):
            self.wload(wo[:, kc, :], wv[:, kc, :], kc % 2, ("wo", kc))
        for gi, (g0, n) in enumerate(GG):
            for dc in range(8):
                bank = ps[dc % 2]
                bk = "ps%d" % (dc % 2)
                for kc in range(8):
                    S.pe(lambda e, bank=bank, dc=dc, kc=kc, g0=g0, n=n: e.matmul(
                        bank[:, :n], lhsT=wo[:, kc, dc * 128:(dc + 1) * 128], rhs=src_act[:, kc, g0:g0 + n],
                        start=(kc == 0), stop=(kc == 7)), [("wo", kc), (tag, gi)], [bk])
                S.act(lambda e, bank=bank, dc=dc, n=n: e.activation(out=mix32[:, dc, :n], in_=bank[:, :n], func=AF.Copy),
                      [bk], [("mix32", dc)])
            self.rstd_from([(mix32[:, dc, :n], [("mix32", dc)]) for dc in range(8)], n, sq, rtmp, rstd, ps[2], "ps2")
            for dc in range(8):
                S.dve(lambda e, dc=dc, n=n: e.scalar_tensor_tensor(
                    out=tmp[:, :n], in0=mix32[:, dc, :n], scalar=self.gcol(l, jn, dc), in1=rstd[:, :n],
                    op0=ALU.mult, op1=ALU.mult), [("mix32", dc), "rstd"], ["tmpo"])
                S.dve(lambda e, dc=dc, g0=g0, n=n: e.tensor_tensor(
                    out=hT[:, dc, g0:g0 + n], in0=hT[:, dc, g0:g0 + n], in1=tmp[:, :n], op=ALU.add),
                    ["tmpo"], [("hT", dc, gi)])

    def hgrn2(self, s):
        S, ps, psb, hT = self.S, self.ps, self.psb, self.hT
        with ExitStack() as st0:
            og = self.sb(st0, "og", [128, 8, T], BF16)
            with ExitStack() as st:
                xn = self.sb(st, "xn", [128, 8, T], BF16)
                sq = self.sb(st, "sq", [128, 3, 512], BF16)
                rtmp = self.sb(st, "rtmp", [128, 512], F32)
                rstd = self.sb(st, "rstd", [128, 512], F32)
                for gi, (g0, n) in enumerate(GG):
                    self.rstd_from([(hT[:, c, g0:g0 + n], []) for c in range(8)], n, sq, rtmp, rstd, ps[2], "ps2")
                    for c in range(8):
                        S.dve(lambda e, c=c, g0=g0, n=n: e.scalar_tensor_tensor(
                            out=xn[:, c, g0:g0 + n], in0=hT[:, c, g0:g0 + n], scalar=self.gcol(0, 0, c), in1=rstd[:, :n],
                            op0=ALU.mult, op1=ALU.mult), ["rstd"], [("xn", gi)])
                wh = [self.sb(st, "wh%d" % i, [128, 8, 4, 128], BF16) for i in range(2)]
                A = self.sb(st, "A", [128, 512], F32)
                C = self.sb(st, "C", [128, 512], F32)
                Dn = self.sb(st, "Dn", [128, 512], F32)
                Bs = [self.sb(st, "B%d" % i, [128, 512], F32) for i in range(2)]
                SGs = [self.sb(st, "SG%d" % i, [128, 512], F32) for i in range(2)]
                Gcs = [self.sb(st, "Gc%d" % i, [128, 512], F32) for i in range(2)]
                qins = [self.sb(st, "qin%d" % i, [128, 512], BF16) for i in range(2)]
                kins = [self.sb(st, "kin%d" % i, [128, 512], BF16) for i in range(2)]
                kouts = [self.sb(st, "kout%d" % i, [128, 512], BF16) for i in range(2)]
                vtoks = [self.sb(st, "vtok%d" % i, [128, 4, 128], BF16) for i in range(2)]
                O32 = self.sb(st, "O32", [128, 512], F32)
                sqh = self.sb(st, "sqh", [128, 1, 512], BF16)
                ktok = self.sb(st, "ktok", [128, 4, 128], BF16)
                attT = self.sb(st, "attT", [128, 4, 128], BF16)
                S32 = self.sb(st, "S32", [128, 9, 128], F32)
                Sb = self.sb(st, "Sb", [128, 8, 128], BF16)
                win = self.a_w_in[0].rearrange("(kc p) n -> p kc n", p=128)

                def load_head(hd):
                    sl = hd % 2
                    for j in range(4):
                        self.wload(wh[sl][:, :, j, :], win[:, :, j * D + hd * 128: j * D + (hd + 1) * 128], sl, ("wh", sl, j))

                nheads = {"mix0a": 0, "mix0b": 1, "mix0c": 1}.get(self.stop, 8)
                iters = [(hd, gi) for hd in range(nheads) for gi in range(5)]

                def stage1(it):
                    hd, gi = iters[it]
                    g0, n = GG[gi]
                    z = it % 2
                    sl = hd % 2
                    w = wh[sl]
                    wk = [("wh", sl, j) for j in range(4)]
                    B, SG, qin, kin, kout, vtok, Gc = Bs[z], SGs[z], qins[z], kins[z], kouts[z], vtoks[z], Gcs[z]
                    kB, kSG, kq, kk_, ko, kv, kG = ("B", z), ("SG", z), ("qin", z), ("kin", z), ("kout", z), ("vtok", z), ("Gc", z)
                    tl_list = tiles_of_group(gi)
                    if gi == 0 and hd + 1 < nheads:
                        load_head(hd + 1)
                    yield

                    def proj(j, bi):
                        for kc in range(8):
                            S.pe(lambda e, kc=kc: e.matmul(ps[bi][:, :n], lhsT=w[:, kc, j, :], rhs=xn[:, kc, g0:g0 + n],
                                                           start=(kc == 0), stop=(kc == 7)), [wk[j], ("xn", gi)], ["ps%d" % bi])
                    yield
                    proj(1, 1)
                    yield
                    S.act(lambda e: e.activation(out=A[:, :n], in_=ps[1][:, :n], func=AF.Sigmoid), ["ps1"], ["A"])
                    yield
                    proj(0, 0)
                    yield
                    proj(3, 1)
                    yield
                    if False:
                        S.act(lambda e: e.activation(out=SG[:, :n], in_=ps[1][:, :n], func=AF.Sigmoid), ["ps1"], [kSG])
                        yield
                        S.dve(lambda e: e.tensor_copy(out=Gc[:, :n], in_=ps[1][:, :n]), ["ps1"], [kG])
                    else:
                        S.act(lambda e: e.activation(out=SG[:, :n], in_=ps[1][:, :n], func=AF.Silu), ["ps1"], [kSG])
                    yield
                    for li, ti in enumerate(tl_list):
                        t0, nt = TT[ti]
                        for kc in range(8):
                            S.pe(lambda e, li=li, kc=kc, t0=t0, nt=nt: e.matmul(
                                ps[2][:nt, li * 128:(li + 1) * 128], lhsT=xn[:, kc, t0:t0 + nt], rhs=w[:, kc, 2, :],
                                start=(kc == 0), stop=(kc == 7)), [wk[2], ("xn", gi)], ["ps2"])
                    yield
                    if True:
                        if gi == 0:
                            S.dve(lambda e: e.tensor_copy(out=vtok[:16, 0, :], in_=ps[2][:16, 0:128]), ["ps2"], [kv])
                        else:
                            S.dve(lambda e: e.tensor_copy(out=vtok[:, :, :], in_=ps[2][:, :].rearrange("p (a b) -> p a b", b=128)),
                                  ["ps2"], [kv])
                    else:
                        if gi == 0:
                            S.act(lambda e: e.activation(out=vtok[:16, 0, :], in_=ps[2][:16, 0:128], func=AF.Copy), ["ps2"], [kv])
                        else:
                            S.act(lambda e: e.activation(out=vtok[:, :, :], in_=ps[2][:, :].rearrange("p (a b) -> p a b", b=128),
                                                         func=AF.Copy), ["ps2"], [kv])
                    yield
                    S.act(lambda e: e.activation(out=B[:, :n], in_=A[:, :n], func=AF.Ln,
                                                 scale=self.lbc[:, 8 + hd:9 + hd], bias=self.lbc[:, hd:hd + 1]), ["A"], [kB])
                    yield
                    S.dve(lambda e: e.tensor_scalar(out=C[:, :n], in0=A[:, :n], scalar1=self.lbc[:, 16 + hd:17 + hd],
                                                    scalar2=self.lbc[:, 8 + hd:9 + hd], op0=ALU.mult, op1=ALU.add), ["A"], ["C"])
                    yield
                    S.dve(lambda e: e.tensor_tensor_scan(out=A[:, :n], data0=self.maskseg[:, :n], data1=B[:, :n],
                                                         initial=0.0, op0=ALU.mult, op1=ALU.add), [kB, "A"], ["A"])
                    yield
                    S.act(lambda e: e.activation(out=B[:, :n], in_=A[:, :n], func=AF.Exp), ["A"], [kB])
                    yield
                    S.act(lambda e: e.activation(out=Dn[:, :n], in_=A[:, :n], func=AF.Exp, scale=-1.0), ["A"], ["Dn"])
                    yield
                    S.dve(lambda e: e.tensor_tensor(out=qin[:, :n], in0=ps[0][:, :n], in1=B[:, :n], op=ALU.mult),
                          ["ps0", kB], [kq])
                    yield
                    S.dve(lambda e: e.tensor_tensor(out=C[:, :n], in0=C[:, :n], in1=Dn[:, :n], op=ALU.mult), ["C", "Dn"], ["C"])
                    yield
                    if True:
                        S.pool(lambda e: e.tensor_copy(out=kin[:, :n], in_=C[:, :n]), ["C"], [kk_])
                    else:
                        S.act(lambda e: e.activation(out=kin[:, :n], in_=C[:, :n], func=AF.Copy), ["C"], [kk_])
                    yield
                    if gi == 0:
                        S.dve(lambda e: e.tensor_scalar(out=kout[:, :16], in0=C[:, :16], scalar1=B[:, 15:16], scalar2=None,
                                                        op0=ALU.mult), ["C", kB], [ko])
                    else:
                        S.dve(lambda e: e.tensor_tensor(
                            out=kout[:, :].rearrange("p (c k) -> p c k", k=64),
                            in0=C[:, :].rearrange("p (c k) -> p c k", k=64),
                            in1=B[:, :].rearrange("p (c k) -> p c k", k=64)[:, :, 63:64].to_broadcast([128, 8, 64]),
                            op=ALU.mult), ["C", kB], [ko])
                    yield

                def stage2(it):
                    hd, gi = iters[it]
                    g0, n = GG[gi]
                    z = it % 2
                    B, SG, qin, kin, kout, vtok, Gc = Bs[z], SGs[z], qins[z], kins[z], kouts[z], vtoks[z], Gcs[z]
                    kB, kSG, kq, kk_, ko, kv, kG = ("B", z), ("SG", z), ("qin", z), ("kin", z), ("kout", z), ("vtok", z), ("Gc", z)
                    tl_list = tiles_of_group(gi)
                    nch = 1 if gi == 0 else 8
                    if gi == 0:
                        S.dve(lambda e: e.memset(S32[:, 0, :], 0.0), [], [("S32", 0)])
                    yield
                    for li, ti in enumerate(tl_list):
                        t0, nt = TT[ti]
                        S.pe(lambda e, li=li, nt=nt: e.transpose(psb[:nt, li * 128:(li + 1) * 128],
                                                                  kout[:, li * 128:li * 128 + nt], self.identb[:]), [ko], ["psb"])
                    yield
                    if gi == 0:
                        S.act(lambda e: e.activation(out=ktok[:16, 0, :], in_=psb[:16, 0:128], func=AF.Copy), ["psb"], ["ktok"])
                    else:
                        S.act(lambda e: e.activation(out=ktok[:, :, :], in_=psb[:, 0:512].rearrange("p (a b) -> p a b", b=128),
                                                     func=AF.Copy), ["psb"], ["ktok"])
                    yield
                    for li, ti in enumerate(tl_list):
                        t0, nt = TT[ti]
                        S.pe(lambda e, li=li, nt=nt: e.matmul(ps[5][:nt, li * 128:li * 128 + nt], lhsT=kin[:, li * 128:li * 128 + nt],
                                                               rhs=qin[:, li * 128:li * 128 + nt], start=True, stop=True),
                             [kk_, kq], ["ps5"])
                    yield
                    if gi == 0:
                        S.dve(lambda e: e.tensor_tensor(out=attT[:16, 0, :16], in0=ps[5][:16, 0:16], in1=self.mask2[:16, :16],
                                                        op=ALU.mult), ["ps5"], ["attT"])
                    else:
                        S.dve(lambda e: e.tensor_tensor(
                            out=attT[:, :, :], in0=ps[5][:, :].rearrange("p (a b) -> p a b", b=128),
                            in1=self.mask2[:, :].unsqueeze(1).to_broadcast([128, 4, 128]), op=ALU.mult), ["ps5"], ["attT"])
                    yield
                    for cl in range(nch):
                        li, r0 = cl // 2, (cl % 2) * 64
                        nr = 16 if gi == 0 else 64
                        bi = 3 + cl % 2
                        S.pe(lambda e, cl=cl, li=li, r0=r0, nr=nr, bi=bi: e.matmul(
                            ps[bi][:, (cl // 2) * 128:(cl // 2 + 1) * 128], lhsT=ktok[r0:r0 + nr, li, :],
                            rhs=vtok[r0:r0 + nr, li, :], start=True, stop=True), ["ktok", kv], ["ps%d" % bi])
                    yield
                    for cl in range(nch):
                        bi = 3 + cl % 2
                        dcol = B[:, 15:16] if gi == 0 else B[:, cl * 64 + 63:cl * 64 + 64]
                        S.dve(lambda e, cl=cl, bi=bi, dcol=dcol: e.scalar_tensor_tensor(
                            out=S32[:, cl + 1, :], in0=S32[:, cl, :], scalar=dcol,
                            in1=ps[bi][:, (cl // 2) * 128:(cl // 2 + 1) * 128], op0=ALU.mult, op1=ALU.add),
                            [("S32", cl), kB, "ps%d" % bi], [("S32", cl + 1)])
                    yield
                    S.act(lambda e: e.activation(out=Sb[:, 0:nch, :], in_=S32[:, 0:nch, :], func=AF.Copy),
                          [("S32", c) for c in range(nch)], ["Sb"])
                    yield
                    for li, ti in enumerate(tl_list):
                        t0, nt = TT[ti]
                        S.pe(lambda e, li=li, nt=nt: e.matmul(
                            ps[6][:, li * 128:li * 128 + nt], lhsT=vtok[:nt, li, :], rhs=attT[:nt, li, :nt],
                            start=True, stop=(gi == 0)), [kv, "attT"], ["ps6"])
                        if gi > 0:
                            for hh in range(2):
                                cl = 2 * li + hh
                                S.pe(lambda e, hh=hh, cl=cl: e.matmul(
                                    ps[6][:, cl * 64:(cl + 1) * 64], lhsT=Sb[:, cl, :], rhs=qin[:, cl * 64:(cl + 1) * 64],
                                    start=False, stop=(hh == 1)), ["Sb", kq], ["ps6"])
                    yield
                    S.dve(lambda e: e.tensor_copy(out=S32[:, 0, :], in_=S32[:, nch, :]), [("S32", nch), "Sb"], [("S32", 0)])
                    yield
                    self.rstd_from([(ps[6][:, :n], ["ps6"])], n, sqh, rtmp, rstd, ps[5], "ps5", dscale=1.0 / 128)
                    yield
                    S.dve(lambda e: e.scalar_tensor_tensor(
                        out=O32[:, :n], in0=ps[6][:, :n], scalar=self.colv[:, 80 + hd:81 + hd], in1=rstd[:, :n],
                        op0=ALU.mult, op1=ALU.mult), ["ps6", "rstd"], ["O32"])
                    yield
                    if False:
                        S.pool(lambda e: e.tensor_tensor(out=O32[:, :n], in0=O32[:, :n], in1=SG[:, :n], op=ALU.mult),
                               ["O32", kSG], ["O32"])
                        yield
                        S.pool(lambda e: e.tensor_tensor(out=og[:, hd, g0:g0 + n], in0=O32[:, :n], in1=Gc[:, :n], op=ALU.mult),
                               ["O32", kG], [("og", gi)])
                    else:
                        S.pool(lambda e: e.tensor_tensor(out=og[:, hd, g0:g0 + n], in0=O32[:, :n], in1=SG[:, :n], op=ALU.mult),
                               ["O32", kSG], [("og", gi)])

                def drain(g):
                    for _ in g:
                        pass

                def zipper(ga, gb, ra=1, rb=2):
                    alive_a, alive_b = True, True
                    while alive_a or alive_b:
                        for _ in range(ra):
                            if alive_a:
                                try:
                                    next(ga)
                                except StopIteration:
                                    alive_a = False
                        for _ in range(rb):
                            if alive_b:
                                try:
                                    next(gb)
                                except StopIteration:
                                    alive_b = False

                if nheads > 0:
                    load_head(0)
                    drain(stage1(0))
                for it in range(len(iters)):
                    if it + 1 < len(iters):
                        zipper(stage1(it + 1), stage2(it), 3, 2)
                    else:
                        drain(stage2(it))
            self.S.barrier()
            if self.stop in ("mix0a", "mix0b", "mix0c"):
                return
            with ExitStack() as st:
                self.out_proj_residual(st, self.a_w_out[0], og, 0, 1, "og")

    def ffn(self, s, l):
        S, ps, hT = self.S, self.ps, self.hT
        halves = [(0, 1032), (1032, 1032)]
        BLK = 344
        with ExitStack() as st0:
            halo = self.sb(st0, "halo", [128, 8, 2], BF16)
            S.dve(lambda e: e.memset(halo[:], 0.0), [], ["halo"])
            def half(hf, h0, nh):
                with ExitStack() as st1:
                    act = self.sb(st1, "act", [128, NJ, 1032], BF16)
                    blocks = [(o, BLK) for o in range(0, nh, BLK)]
                    with ExitStack() as st:
                        xn = self.sb(st, "xn2", [128, 8, 1034], BF16)
                        sq = self.sb(st, "sq2", [128, 8, 512], BF16)
                        rtmp = self.sb(st, "rtmp2", [128, 512], F32)
                        rstd = self.sb(st, "rstd2", [128, 512], F32)
                        S.dve(lambda e: e.tensor_copy(out=xn[:, :, 0:2], in_=halo[:]), ["halo"], [("xn2", -1)])
                        subs = list(blocks)
                        for si, (o, nn) in enumerate(subs):
                            g0 = h0 + o
                            self.rstd_from([(hT[:, c, g0:g0 + nn], []) for c in range(8)], nn, sq, rtmp, rstd, ps[2], "ps2")
                            for c in range(8):
                                S.dve(lambda e, c=c, g0=g0, nn=nn, o=o: e.scalar_tensor_tensor(
                                    out=xn[:, c, 2 + o:2 + o + nn], in0=hT[:, c, g0:g0 + nn], scalar=self.gcol(l, 2, c),
                                    in1=rstd[:, :nn], op0=ALU.mult, op1=ALU.mult), ["rstd"], [("xn2", si)])
                        xkeys = [("xn2", -1)] + [("xn2", si) for si in range(len(subs))]
                        S.dve(lambda e, nh=nh: e.tensor_copy(out=halo[:], in_=xn[:, :, nh:nh + 2]), xkeys, ["halo"])
                        wu = [self.sb(st, "wu%d" % i, [128, 8, 2, 128], BF16) for i in range(2)]
                        G32 = [self.sb(st, "G32_%d" % i, [128, 352], F32) for i in range(3)]
                        V32 = [self.sb(st, "V32_%d" % i, [128, 352], F32) for i in range(3)]
                        SGf = [self.sb(st, "SGf_%d" % i, [128, 352], F32) for i in range(3)]
                        wup = self.w_up[l].rearrange("(kc p) n -> p kc n", p=128)
                        units = [(j, bi_, o, nb) for j in range(NJ) for bi_, (o, nb) in enumerate(blocks)]

                        def load_pair(j):
                            mj = 128 if j < 21 else 64
                            sl = j % 2
                            for part in range(2):
                                self.wload(wu[sl][:, :, part, :mj], wup[:, :, part * DFF + j * 128: part * DFF + j * 128 + mj],
                                           sl, ("wu", sl, part))

                        load_pair(0)

                        def front(k):
                            j, bi_, o, nb = units[k]
                            mj = 128 if j < 21 else 64
                            sl = j % 2
                            w = wu[sl]
                            ub = k % 3
                            if bi_ == 0 and j + 1 < NJ:
                                load_pair(j + 1)
                            for part in range(2):
                                bi = part + 2 * ub
                                bank = ps[bi]
                                bk = "ps%d" % bi
                                for kc in range(8):
                                    S.pe(lambda e, bank=bank, part=part, kc=kc: e.matmul(
                                        bank[:mj, :nb + 2], lhsT=w[:, kc, part, :mj], rhs=xn[:, kc, o:o + nb + 2],
                                        start=(kc == 0), stop=(kc == 7)), [("wu", sl, part)] + xkeys, [bk])
                                dst = (G32 if part == 0 else V32)[ub]
                                dk = ("G32" if part == 0 else "V32", ub)
                                S.act(lambda e, bank=bank, dst=dst, part=part: e.activation(
                                    out=dst[:mj, :nb], in_=bank[:mj, 0:nb], func=AF.Identity, scale=self.ccol(l, 0, part, j)[:mj, :]),
                                    [bk], [dk])

                        def taps(k):
                            j, bi_, o, nb = units[k]
                            mj = 128 if j < 21 else 64
                            ub = k % 3
                            for tap in (1, 2):
                                for part in range(2):
                                    bi = part + 2 * ub
                                    bank = ps[bi]
                                    bk = "ps%d" % bi
                                    dst = (G32 if part == 0 else V32)[ub]
                                    dk = ("G32" if part == 0 else "V32", ub)
                                    S.dve(lambda e, bank=bank, dst=dst, part=part, tap=tap: e.scalar_tensor_tensor(
                                        out=dst[:mj, :nb], in0=bank[:mj, tap:tap + nb], scalar=self.ccol(l, tap, part, j)[:mj, :],
                                        in1=dst[:mj, :nb], op0=ALU.mult, op1=ALU.add), [bk, dk], [dk])

                        def back(k):
                            j, bi_, o, nb = units[k]
                            mj = 128 if j < 21 else 64
                            ub = k % 3
                            S.act(lambda e: e.activation(out=SGf[ub][:mj, :nb], in_=G32[ub][:mj, :nb], func=AF.Silu),
                                  [("G32", ub)], [("SGf", ub)])
                            S.dve(lambda e: e.tensor_tensor(out=act[:mj, j, o:o + nb], in0=SGf[ub][:mj, :nb], in1=V32[ub][:mj, :nb],
                                                            op=ALU.mult), [("SGf", ub), ("V32", ub)], [("act", j, bi_)])

                        for k in range(len(units)):
                            front(k)
                            if k > 0:
                                back(k - 1)
                            taps(k)
                        back(len(units) - 1)
                    S.barrier()
                    with ExitStack() as st:
                        wd = [self.sb(st, "wd%d" % i, [128, NJ, 128], BF16) for i in range(2)]
                        mix = self.sb(st, "mixf", [128, 8, 1032], F32)
                        sq3 = self.sb(st, "sq3", [128, 8, 512], BF16)
                        rtmp3 = self.sb(st, "rtmp3", [128, 512], F32)
                        rstd3 = self.sb(st, "rstd3", [128, 512], F32)
                        tmp3 = self.sb(st, "tmp3", [128, 512], F32)
                        def load_dc(dc):
                            sl = dc % 2
                            self.wload(wd[sl][:, 0:21, :],
                                       self.w_down[l, 0:2688, dc * 128:(dc + 1) * 128].rearrange("(j p) n -> p j n", p=128),
                                       sl, ("wd", sl, 0))
                            self.wload(wd[sl][0:64, 21, :], self.w_down[l, 2688:2752, dc * 128:(dc + 1) * 128], sl, ("wd", sl, 1))

                        load_dc(0)
                        for dc in range(8):
                            sl = dc % 2
                            w = wd[sl]
                            if dc + 1 < 8:
                                load_dc(dc + 1)
                            for bi_, (o, nb) in enumerate(blocks):
                                bq = (dc * len(blocks) + bi_) % 2
                                bank = ps[bq]
                                bk = "ps%d" % bq
                                for j in range(NJ):
                                    mj = 128 if j < 21 else 64
                                    S.pe(lambda e, bank=bank, j=j, mj=mj, o=o, nb=nb, w=w: e.matmul(
                                        bank[:, :nb], lhsT=w[:mj, j, :], rhs=act[:mj, j, o:o + nb],
                                        start=(j == 0), stop=(j == NJ - 1)),
                                        [("wd", sl, 0), ("wd", sl, 1), ("act", j, bi_)], [bk])
                                S.act(lambda e, bank=bank, dc=dc, o=o, nb=nb: e.activation(
                                    out=mix[:, dc, o:o + nb], in_=bank[:, :nb], func=AF.Copy), [bk], [("mix", dc, bi_)])
                        for bi_, (o, nb) in enumerate(blocks):
                            g0 = h0 + o
                            self.rstd_from([(mix[:, dc, o:o + nb], [("mix", dc, bi_)]) for dc in range(8)], nb, sq3, rtmp3, rstd3,
                                           ps[2], "ps2")
                            for dc in range(8):
                                S.dve(lambda e, dc=dc, o=o, nb=nb: e.scalar_tensor_tensor(
                                    out=tmp3[:, :nb], in0=mix[:, dc, o:o + nb], scalar=self.gcol(l, 3, dc), in1=rstd3[:, :nb],
                                    op0=ALU.mult, op1=ALU.mult), [("mix", dc, bi_), "rstd"], ["tmp3"])
                                S.dve(lambda e, dc=dc, g0=g0, nb=nb: e.tensor_tensor(
                                    out=hT[:, dc, g0:g0 + nb], in0=hT[:, dc, g0:g0 + nb], in1=tmp3[:, :nb], op=ALU.add),
                                    ["tmp3"], [("hT", dc, hf, bi_)])
                    S.barrier()

            for hf, (h0, nh) in enumerate(halves):
                half(hf, h0, nh)

    def fox(self, s):
        S, ps, psb, hT = self.S, self.ps, self.psb, self.hT
        scale = 1.0 / 8.0
        with ExitStack() as st0:
            O = self.sb(st0, "O", [128, 8, T], BF16)
            with ExitStack() as st:
                xr = self.sb(st, "xr", [128, 8, T], BF16)
                Ctok = self.sb(st, "Ctok", [128, 17, 16], F32)
                Cpb = self.sb(st, "Cpb", [16, T], BF16)
                with ExitStack() as stn:
                    sq = self.sb(stn, "sq4", [128, 8, 512], BF16)
                    rtmp = self.sb(stn, "rtmp4", [128, 512], F32)
                    rstd = self.sb(stn, "rstd4", [128, 512], F32)
                    for gi, (g0, n) in enumerate(GG):
                        self.rstd_from([(hT[:, c, g0:g0 + n], []) for c in range(8)], n, sq, rtmp, rstd, ps[2], "ps2")
                        for c in range(8):
                            S.dve(lambda e, c=c, g0=g0, n=n: e.tensor_tensor(
                                out=xr[:, c, g0:g0 + n], in0=hT[:, c, g0:g0 + n], in1=rstd[:, :n], op=ALU.mult),
                                ["rstd"], [("xr", gi)])
                    S.barrier()
                xrk = [("xr", gi) for gi in range(5)]
                kvw = self.kv_w.rearrange("(kc p) n -> p kc n", p=128)
                wqv = self.b_w_q[0].rearrange("(kc p) n -> p kc n", p=128)
                gkv = self.colv[:, 88:96]
                with ExitStack() as stf:
                    wfg = self.sb(stf, "wfg", [128, 8, 16], BF16)
                    Cp = self.sb(stf, "Cp", [16, T], F32)
                    sp = self.sb(stf, "sp", [16, 512], F32)
                    self.wload(wfg[:, :, :], kvw[:, :, 2048:2064], 0, "wfg")
                    S.dve(lambda e: e.tensor_tensor(out=wfg[:, :, :], in0=wfg[:, :, :],
                                                    in1=gkv.unsqueeze(2).to_broadcast([128, 8, 16]), op=ALU.mult),
                          ["wfg"], ["wfg"])
                    for gi, (g0, n) in enumerate(GG):
                        for kc in range(8):
                            S.pe(lambda e, kc=kc, g0=g0, n=n: e.matmul(ps[0][:16, :n], lhsT=wfg[:, kc, :], rhs=xr[:, kc, g0:g0 + n],
                                                                        start=(kc == 0), stop=(kc == 7)), ["wfg", ("xr", gi)], ["ps0"])
                        S.act(lambda e, n=n: e.activation(out=sp[:, :n], in_=ps[0][:16, :n], func=AF.Exp, scale=-1.0,
                                                          bias=self.nfgb[:, 0:1]), ["ps0", "nfgb2"], ["sp"])
                        S.act(lambda e, n=n: e.activation(out=sp[:, :n], in_=sp[:, :n], func=AF.Ln, scale=1.0,
                                                          bias=self.onec[0:16, 0:1]), ["sp"], ["sp"])
                        init = 0.0 if gi == 0 else Cp[:, g0 - 1:g0]
                        S.dve(lambda e, g0=g0, n=n, init=init: e.tensor_tensor_scan(
                            out=Cp[:, g0:g0 + n], data0=self.onesf[:16, :n], data1=sp[:, :n], initial=init,
                            op0=ALU.mult, op1=ALU.add), ["sp", ("Cp", gi - 1)], [("Cp", gi)])
                    cpk = [("Cp", gi) for gi in range(5)]
                    S.act(lambda e: e.activation(out=Cpb[:, :], in_=Cp[:, :], func=AF.Copy), cpk, ["Cpb"])
                    for ti, (t0, nt) in enumerate(TT):
                        S.pe(lambda e, t0=t0, nt=nt: e.transpose(ps[1][:nt, 0:16], Cp[:, t0:t0 + nt], self.i16[:]), cpk, ["ps1"])
                        S.dve(lambda e, ti=ti, nt=nt: e.tensor_copy(out=Ctok[:nt, ti, :], in_=ps[1][:nt, 0:16]), ["ps1"], ["Ctok"])
                    S.barrier()
                wp = self.sb(st, "wp", [128, 8, 3, 128], BF16)
                KTh = [[self.sb(st, "KT%d%d" % (z, i), [65, T], BF16) for i in range(2)] for z in range(2)]
                QTh = [[self.sb(st, "QT%d%d" % (z, i), [65, T], BF16) for i in range(2)] for z in range(2)]
                Vas = [self.sb(st, "Va%d" % z, [128, 17, 192], BF16) for z in range(2)]
                NPT = 3
                PT = [self.sb(st, "PT%d" % i, [128, 512], BF16) for i in range(NPT)]
                rinv = [self.sb(st, "rinv%d" % i, [128, 512], F32) for i in range(1)]
                for z in range(2):
                    S.pool(lambda e, z=z: e.memset(Vas[z][:, :, 64:128], 1.0), [], [("Vones", z)])
                    for hh in range(2):
                        S.pool(lambda e, z=z, hh=hh: e.memset(KTh[z][hh][64:65, :], 1.0), [], [("Kone", z, hh)])
                g0col = self.colv[:, 32:40]

                def load_wp(p):
                    self.wload(wp[:, :, 0, :], kvw[:, :, p * 128:(p + 1) * 128], 0, ("wp", 0))
                    self.wload(wp[:, :, 1, :], kvw[:, :, D + p * 128:D + (p + 1) * 128], 0, ("wp", 1))
                    self.wload(wp[:, :, 2, :], wqv[:, :, p * 128:(p + 1) * 128], 0, ("wp", 2))

                def proj_gen(p):
                    pz = p % 2
                    w = wp
                    for m in range(3):
                        gc = gkv if m < 2 else g0col
                        S.dve(lambda e, m=m, gc=gc: e.tensor_tensor(
                            out=w[:, :, m, :], in0=w[:, :, m, :], in1=gc.unsqueeze(2).to_broadcast([128, 8, 128]), op=ALU.mult),
                            [("wp", m)], [("wp", m)])
                    yield
                    for gi, (g0, n) in enumerate(GG):
                        for (m, dst, dk) in ((0, KTh[pz], "KT"), (2, QTh[pz], "QT")):
                            bank = ps[m // 2]
                            bk = "ps%d" % (m // 2)
                            for kc in range(8):
                                S.pe(lambda e, bank=bank, m=m, kc=kc, g0=g0, n=n: e.matmul(
                                    bank[:, :n], lhsT=w[:, kc, m, :], rhs=xr[:, kc, g0:g0 + n], start=(kc == 0), stop=(kc == 7)),
                                    [("wp", m), ("xr", gi)], [bk])
                            for hh in range(2):
                                S.dve(lambda e, bank=bank, dst=dst, hh=hh, g0=g0, n=n: e.tensor_copy(
                                    out=dst[hh][0:64, g0:g0 + n], in_=bank[hh * 64:(hh + 1) * 64, :n]), [bk], [(dk, pz, hh, gi)])
                            yield
                        for hh in range(2):
                            h = 2 * p + hh
                            S.pe(lambda e, h=h, g0=g0, n=n: e.matmul(ps[0][0:65, :n], lhsT=self.selq[:, h, :], rhs=Cpb[:, g0:g0 + n],
                                                                      start=True, stop=True), ["Cpb", "selq"], ["ps0"])
                            S.dve(lambda e, hh=hh, g0=g0, n=n: e.tensor_copy(out=QTh[pz][hh][64:65, g0:g0 + n], in_=ps[0][64:65, :n]),
                                  ["ps0"], [("QT", pz, hh, gi)])
                        yield
                    for kt, (t0, nt) in enumerate(TT):
                        vb = kt % 2
                        for kc in range(8):
                            S.pe(lambda e, kc=kc, t0=t0, nt=nt, vb=vb: e.matmul(
                                ps[vb][:nt, 0:128], lhsT=xr[:, kc, t0:t0 + nt], rhs=w[:, kc, 1, :], start=(kc == 0), stop=(kc == 7)),
                                [("wp", 1)] + xrk, ["ps%d" % vb])
                        S.dve(lambda e, kt=kt, nt=nt, vb=vb: e.tensor_copy(
                            out=Vas[pz][:nt, kt, :].rearrange("p (a b) -> p a b", b=64)[:, 0:3:2, :],
                            in_=ps[vb][:nt, 0:128].rearrange("p (a b) -> p a b", b=64)), ["ps%d" % vb], [("Va", pz, kt)])
                        yield
                    if p + 1 < 8:
                        load_wp(p + 1)
                    yield

                itc_box = [0]

                def att_gen(p):
                    pz = p % 2
                    Va = Vas[pz]
                    its = []
                    for hh in range(2):
                        for gi, (g0, n) in enumerate(GG):
                            tl = tiles_of_group(gi)
                            for kt in range(tl[-1] + 1):
                                its.append((hh, gi, kt))
                    LOOK = 2
                    SB = [5, 6, 2]
                    itc0 = itc_box[0]

                    def emit_qk(ix):
                        hh, gi, kt = its[ix]
                        g0, n = GG[gi]
                        tl = tiles_of_group(gi)
                        k0, nk = TT[kt]
                        c0 = (kt - tl[0]) * 128 if (kt in tl and gi > 0) else 0
                        nq = n - c0
                        sbi = SB[(itc0 + ix) % 3]
                        sb_ = ps[sbi]
                        KT, QT = KTh[pz][hh], QTh[pz][hh]
                        S.pe(lambda e: e.matmul(sb_[:nk, :nq], lhsT=KT[0:65, k0:k0 + nk], rhs=QT[0:65, g0 + c0:g0 + c0 + nq],
                                                start=True, stop=True),
                             [("KT", pz, hh, g_) for g_ in range(5)] + [("QT", pz, hh, gi), ("Kone", pz, hh)], ["ps%d" % sbi])

                    def emit_rest(ix):
                        hh, gi, kt = its[ix]
                        h = 2 * p + hh
                        g0, n = GG[gi]
                        tl = tiles_of_group(gi)
                        last = tl[-1]
                        k0, nk = TT[kt]
                        c0 = (kt - tl[0]) * 128 if (kt in tl and gi > 0) else 0
                        nq = n - c0
                        sbi = SB[(itc0 + ix) % 3]
                        sb_ = ps[sbi]
                        sbk = "ps%d" % sbi
                        pt = PT[(itc0 + ix) % NPT]
                        ptk = ("PT", (itc0 + ix) % NPT)
                        vlo = 0 if hh == 0 else 64
                        orow = hh * 64
                        lrow = 64 - orow
                        obi = 3 + (gi + hh) % 2
                        ob = ps[obi]
                        obk = "ps%d" % obi
                        S.act(lambda e: e.activation(out=pt[:nk, :nq], in_=sb_[:nk, :nq], func=AF.Exp, scale=scale,
                                                     bias=Ctok[:nk, kt, h:h + 1]), [sbk, "Ctok"], [ptk])
                        if kt in tl:
                            qn = min(128, nq)
                            S.pool(lambda e: e.tensor_tensor(out=pt[:nk, 0:qn], in0=pt[:nk, 0:qn], in1=self.triu[:nk, :qn],
                                                             op=ALU.mult), [ptk], [ptk])
                        S.pe(lambda e: e.matmul(ob[:, c0:c0 + nq], lhsT=Va[:nk, kt, vlo:vlo + 128], rhs=pt[:nk, 0:nq],
                                                start=(kt == 0), stop=(kt == last)), [ptk, ("Va", pz, kt), ("Vones", pz)], [obk])
                        if kt == last:
                            rv = rinv[0]
                            rk = ("rinv", 0)
                            S.dve(lambda e: e.reciprocal(out=rv[orow:orow + 64, :n], in_=ob[lrow:lrow + 64, :n]), [obk], [rk])
                            S.dve(lambda e: e.tensor_tensor(out=O[orow:orow + 64, p, g0:g0 + n], in0=ob[orow:orow + 64, :n],
                                                            in1=rv[orow:orow + 64, :n], op=ALU.mult), [obk, rk], [("O", gi)])

                    for ix in range(min(LOOK, len(its))):
                        emit_qk(ix)
                    for ix in range(len(its)):
                        if ix + LOOK < len(its):
                            emit_qk(ix + LOOK)
                        emit_rest(ix)
                        yield
                    itc_box[0] += len(its)

                def drain(g):
                    for _ in g:
                        pass

                def zipper(ga, gb, ra, rb):
                    alive_a, alive_b = True, True
                    while alive_a or alive_b:
                        for _ in range(ra):
                            if alive_a:
                                try:
                                    next(ga)
                                except StopIteration:
                                    alive_a = False
                        for _ in range(rb):
                            if alive_b:
                                try:
                                    next(gb)
                                except StopIteration:
                                    alive_b = False

                load_wp(0)
                drain(proj_gen(0))
                for p in range(8):
                    if p + 1 < 8:
                        zipper(att_gen(p), proj_gen(p + 1), 5, 2)
                    else:
                        drain(att_gen(p))
            self.S.barrier()
            with ExitStack() as st:
                self.out_proj_residual(st, self.b_w_out[0], O, 1, 1, "O")


_CACHE = {}


def _get_nc(nseq=2, stop=None):
    key = (nseq, stop)
    if key not in _CACHE:
        _CACHE[key] = Builder(nseq, stop).build()
    return _CACHE[key]


def kernel(**inputs):
    ncores = 8
    nc = _get_nc(2, None)
    shared = {k: np.ascontiguousarray(np.asarray(v, dtype=np.float32)) for k, v in inputs.items() if k != "x"}
    x = np.ascontiguousarray(np.asarray(inputs["x"], dtype=np.float32))
    in_maps = []
    for c in range(ncores):
        m = dict(shared)
        m["x"] = x[2 * c:2 * c + 2]
        in_maps.append(m)
    res = run_bass_kernel_spmd(nc, in_maps, core_ids=list(range(ncores)))
    return np.concatenate([np.asarray(r["out"]) for r in res.results], axis=0).astype(np.float32)
```

```python
import numpy as np
import concourse.bass as bass
import concourse.mybir as mybir
from concourse.bass_utils import run_bass_kernel_spmd
from contextlib import ExitStack

F32 = mybir.dt.float32
BF16 = mybir.dt.bfloat16
AF = mybir.ActivationFunctionType
ALU = mybir.AluOpType
AX = mybir.AxisListType

SAME_ENG_SYNC = True


class _Op:
    __slots__ = ("eng", "fn", "reads", "writes", "dma_sem", "ndma", "deps",
                 "needs_inc", "token", "waits", "idx", "is_bar")

    def __init__(self, eng, fn, reads, writes, dma_sem=None, ndma=0):
        self.eng = eng
        self.fn = fn
        self.reads = reads
        self.writes = writes
        self.dma_sem = dma_sem
        self.ndma = ndma
        self.deps = []
        self.needs_inc = False
        self.token = None
        self.waits = []
        self.is_bar = False


class Sched:
    CENG = ("pe", "act", "dve", "pool")
    ALLENG = ("pe", "act", "dve", "pool", "sp")

    def __init__(self, nc, stack):
        self.nc = nc
        self.stack = stack
        self.ops = []
        self.esem = {e: stack.enter_context(nc.semaphore("s_" + e)) for e in self.CENG}
        self.dma_cum = {}
        self.dma_sems = {}
        self.dma_exempt = set()
        self.last_w = {}
        self.readers = {}
        self.last_op = {e: None for e in self.ALLENG}
        self.dma_last = {}

    def dma_sem(self, name, exempt=False):
        if name not in self.dma_sems:
            self.dma_sems[name] = self.stack.enter_context(self.nc.semaphore("d_" + name))
            self.dma_cum[name] = 0
            if exempt:
                self.dma_exempt.add(name)
        return name

    def _add(self, op):
        op.idx = len(self.ops)
        deps = set()
        for k in op.reads:
            w = self.last_w.get(k)
            if w is not None:
                deps.add(w)
        for k in op.writes:
            w = self.last_w.get(k)
            if w is not None:
                deps.add(w)
            for r in self.readers.get(k, ()):
                deps.add(r)
        deps.discard(op)
        op.deps = sorted(deps, key=lambda o: o.idx)
        for k in op.reads:
            self.readers.setdefault(k, []).append(op)
        for k in op.writes:
            self.last_w[k] = op
            self.readers[k] = []
        self.ops.append(op)
        self.last_op[op.eng] = op
        if op.dma_sem is not None:
            self.dma_cum[op.dma_sem] += 16 * op.ndma
            op.token = (op.dma_sem, self.dma_cum[op.dma_sem])
            self.dma_last[op.dma_sem] = op
        return op

    def op(self, eng, fn, reads=(), writes=()):
        return self._add(_Op(eng, fn, tuple(reads), tuple(writes)))

    def pe(self, fn, reads=(), writes=()):
        return self.op("pe", fn, reads, writes)

    def act(self, fn, reads=(), writes=()):
        return self.op("act", fn, reads, writes)

    def dve(self, fn, reads=(), writes=()):
        return self.op("dve", fn, reads, writes)

    def pool(self, fn, reads=(), writes=()):
        return self.op("pool", fn, reads, writes)

    def dma(self, eng, fn, sem, reads=(), writes=(), n=1):
        return self._add(_Op(eng, fn, tuple(reads), tuple(writes), dma_sem=sem, ndma=n))

    def barrier(self):
        prev = dict(self.last_op)
        dl = {k: v for k, v in self.dma_last.items() if k not in self.dma_exempt}
        for e in self.ALLENG:
            b = _Op(e, None, (), ())
            b.is_bar = True
            b.idx = len(self.ops)
            b.deps = [o for ee, o in prev.items() if o is not None and (ee != e or (SAME_ENG_SYNC and e != 'pe'))] + list(dl.values())
            self.ops.append(b)
            self.last_op[e] = b

    def finalize(self):
        for op in self.ops:
            for d in op.deps:
                if d.dma_sem is not None or d.is_bar:
                    continue
                if d.eng == op.eng and (d.eng == "pe" or not SAME_ENG_SYNC):
                    continue
                d.needs_inc = True
        cnt = {e: 0 for e in self.CENG}
        for op in self.ops:
            if op.dma_sem is None and op.needs_inc:
                cnt[op.eng] += 1
                op.token = (op.eng, cnt[op.eng])
        known = {e: {} for e in self.ALLENG}
        for op in self.ops:
            kn = known[op.eng]
            need = {}
            for d in op.deps:
                if d.token is None:
                    continue
                if d.dma_sem is None and d.eng == op.eng and (d.eng == "pe" or not SAME_ENG_SYNC):
                    continue
                s, v = d.token
                if need.get(s, 0) < v:
                    need[s] = v
            for s, v in need.items():
                if kn.get(s, 0) >= v:
                    continue
                kn[s] = v
                op.waits.append((s, v))
        self.counts = cnt

    def _sem(self, s):
        return self.esem[s] if s in self.esem else self.dma_sems[s]

    def emit(self):
        self.finalize()
        by_eng = {e: [o for o in self.ops if o.eng == e] for e in self.ALLENG}
        with self.nc.Block() as block:
            def run(e):
                def body(eng):
                    for op in by_eng[e]:
                        for (s, v) in op.waits:
                            eng.wait_ge(self._sem(s), v)
                        if op.fn is None:
                            continue
                        if op.dma_sem is not None:
                            op.fn(eng, self.dma_sems[op.dma_sem])
                        else:
                            ins = op.fn(eng)
                            if op.needs_inc:
                                ins.then_inc(self.esem[e], 1)
                return body
            block.tensor(run("pe"))
            block.scalar(run("act"))
            block.vector(run("dve"))
            block.gpsimd(run("pool"))
            block.sync(run("sp"))


import os
CUT = int(os.environ.get('KCUT', '99'))
D = 1024
T = 2064
NMETA = 16
DFF = 2752
NJ = 22
EPS = 1e-6
TT = [(0, 16)] + [(16 + 128 * i, 128) for i in range(16)]
GG = [(0, 16)] + [(16 + 512 * j, 512) for j in range(4)]


def tiles_of_group(gi):
    return [0] if gi == 0 else list(range(1 + 4 * (gi - 1), 1 + 4 * gi))


class Builder:
    def __init__(self, nseq=2, stop=None):
        self.nseq = nseq
        self.stop = stop
        nc = self.nc = bass.Bass("TRN2", target_bir_lowering=False)
        dt = lambda name, shape: nc.dram_tensor(name, shape, F32, kind="ExternalInput").ap()
        self.x = dt("x", [nseq, 2048, D])
        self.meta = dt("meta_tokens", [NMETA, D])
        self.norm_gains = dt("norm_gains", [2, 4, D])
        self.a_w_in = dt("a_w_in", [1, D, 4 * D])
        self.a_lb = dt("a_lb_logits", [2, D])
        self.a_hn = dt("a_head_norm", [1, D])
        self.a_w_out = dt("a_w_out", [1, D, D])
        self.kv_norm = dt("kv_norm", [D])
        self.kv_w = dt("kv_w", [D, 2 * D + 16])
        self.fg_b = dt("fg_b", [16])
        self.b_w_q = dt("b_w_q", [1, D, D])
        self.b_w_out = dt("b_w_out", [1, D, D])
        self.w_up = dt("ffn_w_up", [2, D, 2 * DFF])
        self.conv = dt("ffn_conv", [2, 3, 2 * DFF])
        self.w_down = dt("ffn_w_down", [2, DFF, D])
        self.out = nc.dram_tensor("out", [nseq, 2048, D], F32, kind="ExternalOutput").ap()
        self.uid = 0

    def sb(self, st, name, shape, dtype):
        self.uid += 1
        return st.enter_context(self.nc.sbuf_tensor("%s_%d" % (name, self.uid), shape, dtype))

    def build(self):
        nc = self.nc
        with ExitStack() as st:
            S = self.S = Sched(nc, st)
            self.ps = [st.enter_context(nc.psum_tensor("ps%d" % i, [128, 512], F32)) for i in range(7)]
            self.psb = st.enter_context(nc.psum_tensor("psb", [128, 1024], BF16))
            self.hT = self.sb(st, "hT", [128, 8, T], F32)
            self.consts(st)
            S.barrier()
            for s in range(self.nseq):
                self.seq(s)
            S.barrier()
            S.emit()
        return nc

    def consts(self, st):
        S = self.S
        self.ident = self.sb(st, "ident", [128, 128], F32)
        self.identb = self.sb(st, "identb", [128, 128], BF16)
        self.onesb = self.sb(st, "onesb", [128, 128], BF16)
        self.triu = self.sb(st, "triu", [128, 128], BF16)
        self.mask2 = self.sb(st, "mask2", [128, 128], F32)
        self.maskseg = self.sb(st, "maskseg", [128, 512], F32)
        self.epsc = self.sb(st, "epsc", [128, 1], F32)
        self.colv = self.sb(st, "colv", [128, 96], F32)
        self.convT = self.sb(st, "convT", [128, 3, 128], F32)
        self.lbc = self.sb(st, "lbc", [128, 24], F32)
        self.nfgb = self.sb(st, "nfgb", [16, 1], F32)
        self.i16 = self.sb(st, "i16", [16, 16], F32)
        self.ones16 = self.sb(st, "ones16", [16, 128], F32)
        self.onesf = self.sb(st, "onesf", [16, 512], F32)
        self.onec = self.sb(st, "onec", [128, 1], F32)
        self.selq = self.sb(st, "selq", [16, 16, 65], BF16)
        P = lambda fn, r=(), w=(): S.pool(fn, r, w)
        P(lambda e: e.memset(self.ident[:], 1.0), w=["ident"])
        P(lambda e: e.affine_select(out=self.ident[:], in_=self.ident[:], pattern=[[-1, 128]],
                                    compare_op=ALU.is_equal, fill=0.0, base=0, channel_multiplier=1),
          r=["ident"], w=["ident"])
        S.dve(lambda e: e.tensor_copy(out=self.identb[:], in_=self.ident[:]), ["ident"], ["identb"])
        S.dve(lambda e: e.tensor_copy(out=self.i16[:], in_=self.ident[0:16, 0:16]), ["ident"], ["i16"])
        P(lambda e: e.memset(self.onesb[:], 1.0), w=["onesb"])
        P(lambda e: e.memset(self.ones16[:], 1.0), w=["ones16"])
        P(lambda e: e.memset(self.triu[:], 1.0), w=["triu"])
        P(lambda e: e.affine_select(out=self.triu[:], in_=self.triu[:], pattern=[[1, 128]],
                                    compare_op=ALU.is_ge, fill=0.0, base=0, channel_multiplier=-1),
          r=["triu"], w=["triu"])
        P(lambda e: e.memset(self.mask2[:], 1.0), w=["mask2"])
        P(lambda e: e.affine_select(out=self.mask2[:], in_=self.mask2[:], pattern=[[1, 128]],
                                    compare_op=ALU.is_ge, fill=0.0, base=0, channel_multiplier=-1),
          r=["mask2"], w=["mask2"])
        P(lambda e: e.memset(self.mask2[0:64, 64:128], 0.0), r=["mask2"], w=["mask2"])
        P(lambda e: e.memset(self.maskseg[:], 1.0), w=["maskseg"])
        P(lambda e: e.memset(self.maskseg[:].rearrange("p (c k) -> p c k", k=64)[:, :, 0:1], 0.0),
          r=["maskseg"], w=["maskseg"])
        P(lambda e: e.memset(self.epsc[:], EPS), w=["epsc"])
        P(lambda e: e.memset(self.onesf[:], 1.0), w=["onesf"])
        P(lambda e: e.memset(self.onec[:], 1.0), w=["onec"])
        P(lambda e: e.memset(self.selq[:], 0.0), w=["selq0"])
        S.dve(lambda e: e.tensor_scalar(out=self.selq[:, :, 64], in0=self.i16[:, :], scalar1=-8.0, scalar2=None, op0=ALU.mult),
              ["selq0", "i16"], ["selq"])
        rowsA = self.sb(st, "rowsA", [96, 128], F32)
        rowsC = self.sb(st, "rowsC", [128, 3, 128], F32)
        P(lambda e: e.memset(rowsC[:], 0.0), w=["rowsC"])
        cs = S.dma_sem("const")
        nd = [0]

        def ld(dst, src, rk):
            S.dma("sp", lambda e, s, dst=dst, src=src: e.dma_start(out=dst, in_=src).then_inc(s, 16), cs,
                  reads=[rk], writes=[("rowsd", nd[0])])
            nd[0] += 1
        ld(rowsA[0:64, :], self.norm_gains.rearrange("l j (c p) -> (l j c) p", p=128), "rowsA")
        ld(rowsA[64:80, :], self.a_lb.rearrange("l (c p) -> (l c) p", p=128), "rowsA")
        ld(rowsA[80:88, :], self.a_hn.rearrange("l (c p) -> (l c) p", p=128), "rowsA")
        ld(rowsA[88:96, :], self.kv_norm.rearrange("(c p) -> c p", p=128), "rowsA")
        for l in range(2):
            for tap in range(3):
                for part in range(2):
                    r0 = ((l * 3 + tap) * 2 + part) * 22
                    src = self.conv[l, tap, part * DFF: part * DFF + 2688].rearrange("(j k) -> j k", k=128)
                    done = 0
                    while done < 21:
                        ti, ri = divmod(r0 + done, 128)
                        cnt = min(21 - done, 128 - ri)
                        ld(rowsC[ri:ri + cnt, ti, :], src[done:done + cnt, :], "rowsC")
                        done += cnt
                    ti, ri = divmod(r0 + 21, 128)
                    ld(rowsC[ri:ri + 1, ti, 0:64],
                       self.conv[l, tap, part * DFF + 2688: part * DFF + 2752].rearrange("(a k) -> a k", a=1), "rowsC")
        ld(self.nfgb[:, :], self.fg_b.rearrange("(h a) -> h a", a=1), "nfgb")
        allrows = [("rowsd", i) for i in range(nd[0])]
        ps = self.ps
        S.pe(lambda e: e.transpose(ps[0][:, 0:96], rowsA[:, :], self.ident[0:96, 0:96]), allrows + ["ident"], ["ps0"])
        S.dve(lambda e: e.tensor_copy(out=self.colv[:], in_=ps[0][:, 0:96]), ["ps0"], ["colv"])
        for ti in range(3):
            S.pe(lambda e, ti=ti: e.transpose(ps[1][:, ti * 128:(ti + 1) * 128], rowsC[:, ti, :], self.ident[:]),
                 allrows + ["ident", "rowsC"], ["ps1"])
        S.dve(lambda e: e.tensor_copy(out=self.convT[:], in_=ps[1][:, 0:384].rearrange("p (a b) -> p a b", b=128)),
              ["ps1"], ["convT"])
        dl = self.sb(st, "dl", [128, 8], F32)
        S.dve(lambda e: e.tensor_tensor(out=dl[:], in0=self.colv[:, 64:72], in1=self.colv[:, 72:80], op=ALU.subtract),
              ["colv"], ["dl"])
        S.act(lambda e: e.activation(out=self.lbc[:, 0:8], in_=dl[:], func=AF.Sigmoid), ["dl"], ["lbc0"])
        S.act(lambda e: e.activation(out=self.lbc[:, 8:16], in_=dl[:], func=AF.Sigmoid, scale=-1.0), ["dl"], ["lbc1"])
        S.dve(lambda e: e.tensor_scalar(out=self.lbc[:, 16:24], in0=self.lbc[:, 8:16], scalar1=-1.0, scalar2=None,
                                        op0=ALU.mult), ["lbc1"], ["lbc2"])
        S.dve(lambda e: e.tensor_scalar(out=self.nfgb[:], in0=self.nfgb[:], scalar1=-1.0, scalar2=None, op0=ALU.mult),
              allrows, ["nfgb2"])

    def gcol(self, l, j, c):
        k = (l * 4 + j) * 8 + c
        return self.colv[:, k:k + 1]

    def ccol(self, l, tap, part, j):
        r = ((l * 3 + tap) * 2 + part) * 22 + j
        ti, ri = divmod(r, 128)
        return self.convT[:, ti, ri:ri + 1]

    def wload(self, dst, src, slot, key, reads=()):
        S = self.S
        sem = S.dma_sem("w_" + "_".join(str(k) for k in (key if isinstance(key, tuple) else (key,))), exempt=True)
        S.dma("pool", lambda e, s: e.dma_start(out=dst, in_=src).then_inc(s, 16), sem,
              reads=list(reads), writes=[key])

    def rstd_from(self, srcs, n, sq, rtmp, rstd, pst, pkey, dscale=1.0 / D):
        S = self.S
        nsrc = len(srcs)
        nsq = sq.shape[1]
        for c, (ap, rk) in enumerate(srcs):
            S.act(lambda e, ap=ap, c=c: e.activation(out=sq[:, c % nsq, :n], in_=ap, func=AF.Square), rk, [("sq", c % nsq)])
            S.pe(lambda e, c=c: e.matmul(pst[:, :n], lhsT=self.onesb[:], rhs=sq[:, c % nsq, :n], start=(c == 0),
                                         stop=(c == nsrc - 1)), [("sq", c % nsq)], [pkey])
        S.act(lambda e: e.activation(out=rtmp[:, :n], in_=pst[:, :n], func=AF.Ln, scale=dscale, bias=self.epsc[:, 0:1]),
              [pkey], ["rtmp"])
        S.act(lambda e: e.activation(out=rstd[:, :n], in_=rtmp[:, :n], func=AF.Exp, scale=-0.5), ["rtmp"], ["rstd"])

    def seq(self, s):
        S = self.S
        self.load_x(s)
        S.barrier()
        if self.stop != "load":
            self.hgrn2(s)
            S.barrier()
            if self.stop not in ("mix0", "mix0a", "mix0b", "mix0c"):
                self.ffn(s, 0)
                S.barrier()
                if self.stop != "ffn0":
                    self.fox(s)
                    S.barrier()
                    if self.stop != "mix1":
                        self.ffn(s, 1)
                        S.barrier()
        self.store(s)
        S.barrier()

    def load_x(self, s):
        S, ps, hT = self.S, self.ps, self.hT
        with ExitStack() as st:
            xin = [self.sb(st, "xin%d" % i, [128, D], F32) for i in range(2)]
            xs = [S.dma_sem("xin%d" % i) for i in range(2)]
            for ti, (t0, n) in enumerate(TT):
                sl = ti % 2
                src = self.meta if ti == 0 else self.x[s, t0 - 16:t0 - 16 + 128, :]
                S.dma("sp", lambda e, sm, sl=sl, src=src, n=n: e.dma_start(out=xin[sl][:n, :], in_=src).then_inc(sm, 16),
                      xs[sl], writes=[("xin", sl)])
                for half in range(2):
                    bank = ps[half + 2 * sl]
                    bk = "ps%d" % (half + 2 * sl)
                    for j in range(4):
                        c = half * 4 + j
                        S.pe(lambda e, bank=bank, j=j, c=c, n=n, sl=sl: e.transpose(
                            bank[:, j * 128:j * 128 + n], xin[sl][:n, c * 128:(c + 1) * 128], self.ident[:n, :n]),
                            [("xin", sl)], [bk])
                    fn = lambda e, bank=bank, half=half, t0=t0, n=n: e.tensor_copy(
                        out=hT[:, half * 4:(half + 1) * 4, t0:t0 + n],
                        in_=bank[:, :].rearrange("p (j k) -> p j k", k=128)[:, :, 0:n])
                    if half == 0:
                        S.dve(fn, [bk], [("hT", ti, half)])
                    else:
                        S.act(lambda e, bank=bank, half=half, t0=t0, n=n: e.activation(
                            out=hT[:, half * 4:(half + 1) * 4, t0:t0 + n],
                            in_=bank[:, :].rearrange("p (j k) -> p j k", k=128)[:, :, 0:n], func=AF.Copy),
                            [bk], [("hT", ti, half)])

    def store(self, s):
        S, ps, hT = self.S, self.ps, self.hT
        with ExitStack() as st:
            xo = [self.sb(st, "xo%d" % i, [128, D], F32) for i in range(4)]
            os_ = [S.dma_sem("xo%d" % i) for i in range(4)]
            for ti, (t0, n) in enumerate(TT):
                if ti == 0:
                    continue
                sl = ti % 4
                for half in range(2):
                    bank = ps[half + 2 * (sl % 2)]
                    bk = "ps%d" % (half + 2 * (sl % 2))
                    for j in range(4):
                        c = half * 4 + j
                        S.pe(lambda e, bank=bank, j=j, c=c, t0=t0: e.transpose(
                            bank[:, j * 128:(j + 1) * 128], hT[:, c, t0:t0 + 128], self.ident[:]), [], [bk])
                    if half == 0:
                        S.dve(lambda e, bank=bank, sl=sl: e.tensor_copy(out=xo[sl][:, 0:512], in_=bank[:, :]),
                              [bk], [("xo", sl)])
                    else:
                        S.act(lambda e, bank=bank, sl=sl: e.activation(out=xo[sl][:, 512:1024], in_=bank[:, :], func=AF.Copy),
                              [bk], [("xo", sl)])
                S.dma("sp", lambda e, sm, sl=sl, t0=t0: e.dma_start(out=self.out[s, t0 - 16:t0 - 16 + 128, :],
                                                                     in_=xo[sl][:, :]).then_inc(sm, 16),
                      os_[sl], reads=[("xo", sl)], writes=[("xo", sl)])

    def out_proj_residual(self, st, wsrc, src_act, l, jn, tag):
        S, ps, hT = self.S, self.ps, self.hT
        wo = self.sb(st, "wo", [128, 8, D], BF16)
        mix32 = self.sb(st, "mix32", [128, 8, 512], F32)
        sq = self.sb(st, "sqo", [128, 8, 512], BF16)
        rtmp = self.sb(st, "rtmpo", [128, 512], F32)
        rstd = self.sb(st, "rstdo", [128, 512], F32)
        tmp = self.sb(st, "tmpo", [128, 512], F32)
        wv = wsrc.rearrange("(kc p) n -> p kc n", p=128)
        for dc in range(8):
            self.wload(wo[:, :, dc * 128:(dc + 1) * 128], wv[:, :, dc * 128:(dc + 1) * 128], dc % 2, ("wo", dc))
        for gi, (g0, n) in enumerate(GG):
            for dc in range(8):
                bank = ps[dc % 2]
                bk = "ps%d" % (dc % 2)
                for kc in range(8):
                    S.pe(lambda e, bank=bank, dc=dc, kc=kc, g0=g0, n=n: e.matmul(
                        bank[:, :n], lhsT=wo[:, kc, dc * 128:(dc + 1) * 128], rhs=src_act[:, kc, g0:g0 + n],
                        start=(kc == 0), stop=(kc == 7)), [("wo", dc), (tag, gi)], [bk])
                S.act(lambda e, bank=bank, dc=dc, n=n: e.activation(out=mix32[:, dc, :n], in_=bank[:, :n], func=AF.Copy),
                      [bk], [("mix32", dc)])
            self.rstd_from([(mix32[:, dc, :n], [("mix32", dc)]) for dc in range(8)], n, sq, rtmp, rstd, ps[2], "ps2")
            for dc in range(8):
                S.dve(lambda e, dc=dc, n=n: e.scalar_tensor_tensor(
                    out=tmp[:, :n], in0=mix32[:, dc, :n], scalar=self.gcol(l, jn, dc), in1=rstd[:, :n],
                    op0=ALU.mult, op1=ALU.mult), [("mix32", dc), "rstd"], ["tmpo"])
                S.dve(lambda e, dc=dc, g0=g0, n=n: e.tensor_tensor(
                    out=hT[:, dc, g0:g0 + n], in0=hT[:, dc, g0:g0 + n], in1=tmp[:, :n], op=ALU.add),
                    ["tmpo"], [("hT", dc, gi)])

    def hgrn2(self, s):
        S, ps, psb, hT = self.S, self.ps, self.psb, self.hT
        with ExitStack() as st0:
            og = self.sb(st0, "og", [128, 8, T], BF16)
            with ExitStack() as st:
                xn = self.sb(st, "xn", [128, 8, T], BF16)
                sq = self.sb(st, "sq", [128, 3, 512], BF16)
                rtmp = self.sb(st, "rtmp", [128, 512], F32)
                rstd = self.sb(st, "rstd", [128, 512], F32)
                for gi, (g0, n) in enumerate(GG):
                    self.rstd_from([(hT[:, c, g0:g0 + n], []) for c in range(8)], n, sq, rtmp, rstd, ps[2], "ps2")
                    for c in range(8):
                        S.dve(lambda e, c=c, g0=g0, n=n: e.scalar_tensor_tensor(
                            out=xn[:, c, g0:g0 + n], in0=hT[:, c, g0:g0 + n], scalar=self.gcol(0, 0, c), in1=rstd[:, :n],
                            op0=ALU.mult, op1=ALU.mult), ["rstd"], [("xn", gi)])
                wh = [self.sb(st, "wh%d" % i, [128, 8, 4, 128], BF16) for i in range(2)]
                A = self.sb(st, "A", [128, 512], F32)
                C = self.sb(st, "C", [128, 512], F32)
                Dn = self.sb(st, "Dn", [128, 512], F32)
                Bs = [self.sb(st, "B%d" % i, [128, 512], F32) for i in range(2)]
                SGs = [self.sb(st, "SG%d" % i, [128, 512], F32) for i in range(2)]
                Gcs = [self.sb(st, "Gc%d" % i, [128, 512], F32) for i in range(2)]
                qins = [self.sb(st, "qin%d" % i, [128, 512], BF16) for i in range(2)]
                kins = [self.sb(st, "kin%d" % i, [128, 512], BF16) for i in range(2)]
                kouts = [self.sb(st, "kout%d" % i, [128, 512], BF16) for i in range(2)]
                vtoks = [self.sb(st, "vtok%d" % i, [128, 4, 128], BF16) for i in range(2)]
                O32 = self.sb(st, "O32", [128, 512], F32)
                sqh = self.sb(st, "sqh", [128, 1, 512], BF16)
                ktok = self.sb(st, "ktok", [128, 4, 128], BF16)
                attT = self.sb(st, "attT", [128, 4, 128], BF16)
                S32 = self.sb(st, "S32", [128, 9, 128], F32)
                Sb = self.sb(st, "Sb", [128, 8, 128], BF16)
                win = self.a_w_in[0].rearrange("(kc p) n -> p kc n", p=128)

                def load_head(hd):
                    sl = hd % 2
                    for j in range(4):
                        self.wload(wh[sl][:, :, j, :], win[:, :, j * D + hd * 128: j * D + (hd + 1) * 128], sl, ("wh", sl, j))

                nheads = {"mix0a": 0, "mix0b": 1, "mix0c": 1}.get(self.stop, 8)
                iters = [(hd, gi) for hd in range(nheads) for gi in range(5)]

                def stage1(it):
                    hd, gi = iters[it]
                    g0, n = GG[gi]
                    z = it % 2
                    sl = hd % 2
                    w = wh[sl]
                    wk = [("wh", sl, j) for j in range(4)]
                    B, SG, qin, kin, kout, vtok, Gc = Bs[z], SGs[z], qins[z], kins[z], kouts[z], vtoks[z], Gcs[z]
                    kB, kSG, kq, kk_, ko, kv, kG = ("B", z), ("SG", z), ("qin", z), ("kin", z), ("kout", z), ("vtok", z), ("Gc", z)
                    tl_list = tiles_of_group(gi)
                    if gi == 0 and hd + 1 < nheads:
                        load_head(hd + 1)
                    yield

                    def proj(j, bi):
                        for kc in range(8):
                            S.pe(lambda e, kc=kc: e.matmul(ps[bi][:, :n], lhsT=w[:, kc, j, :], rhs=xn[:, kc, g0:g0 + n],
                                                           start=(kc == 0), stop=(kc == 7)), [wk[j], ("xn", gi)], ["ps%d" % bi])
                    yield
                    proj(1, 1)
                    yield
                    S.act(lambda e: e.activation(out=A[:, :n], in_=ps[1][:, :n], func=AF.Sigmoid), ["ps1"], ["A"])
                    yield
                    proj(0, 0)
                    yield
                    proj(3, 1)
                    yield
                    if False:
                        S.act(lambda e: e.activation(out=SG[:, :n], in_=ps[1][:, :n], func=AF.Sigmoid), ["ps1"], [kSG])
                        yield
                        S.dve(lambda e: e.tensor_copy(out=Gc[:, :n], in_=ps[1][:, :n]), ["ps1"], [kG])
                    else:
                        S.act(lambda e: e.activation(out=SG[:, :n], in_=ps[1][:, :n], func=AF.Silu), ["ps1"], [kSG])
                    yield
                    for li, ti in enumerate(tl_list):
                        t0, nt = TT[ti]
                        for kc in range(8):
                            S.pe(lambda e, li=li, kc=kc, t0=t0, nt=nt: e.matmul(
                                ps[2][:nt, li * 128:(li + 1) * 128], lhsT=xn[:, kc, t0:t0 + nt], rhs=w[:, kc, 2, :],
                                start=(kc == 0), stop=(kc == 7)), [wk[2], ("xn", gi)], ["ps2"])
                    yield
                    if True:
                        if gi == 0:
                            S.dve(lambda e: e.tensor_copy(out=vtok[:16, 0, :], in_=ps[2][:16, 0:128]), ["ps2"], [kv])
                        else:
                            S.dve(lambda e: e.tensor_copy(out=vtok[:, :, :], in_=ps[2][:, :].rearrange("p (a b) -> p a b", b=128)),
                                  ["ps2"], [kv])
                    else:
                        if gi == 0:
                            S.act(lambda e: e.activation(out=vtok[:16, 0, :], in_=ps[2][:16, 0:128], func=AF.Copy), ["ps2"], [kv])
                        else:
                            S.act(lambda e: e.activation(out=vtok[:, :, :], in_=ps[2][:, :].rearrange("p (a b) -> p a b", b=128),
                                                         func=AF.Copy), ["ps2"], [kv])
                    yield
                    S.act(lambda e: e.activation(out=B[:, :n], in_=A[:, :n], func=AF.Ln,
                                                 scale=self.lbc[:, 8 + hd:9 + hd], bias=self.lbc[:, hd:hd + 1]), ["A"], [kB])
                    yield
                    S.dve(lambda e: e.tensor_scalar(out=C[:, :n], in0=A[:, :n], scalar1=self.lbc[:, 16 + hd:17 + hd],
                                                    scalar2=self.lbc[:, 8 + hd:9 + hd], op0=ALU.mult, op1=ALU.add), ["A"], ["C"])
                    yield
                    S.dve(lambda e: e.tensor_tensor_scan(out=A[:, :n], data0=self.maskseg[:, :n], data1=B[:, :n],
                                                         initial=0.0, op0=ALU.mult, op1=ALU.add), [kB, "A"], ["A"])
                    yield
                    S.act(lambda e: e.activation(out=B[:, :n], in_=A[:, :n], func=AF.Exp), ["A"], [kB])
                    yield
                    S.act(lambda e: e.activation(out=Dn[:, :n], in_=A[:, :n], func=AF.Exp, scale=-1.0), ["A"], ["Dn"])
                    yield
                    S.dve(lambda e: e.tensor_tensor(out=qin[:, :n], in0=ps[0][:, :n], in1=B[:, :n], op=ALU.mult),
                          ["ps0", kB], [kq])
                    yield
                    S.dve(lambda e: e.tensor_tensor(out=C[:, :n], in0=C[:, :n], in1=Dn[:, :n], op=ALU.mult), ["C", "Dn"], ["C"])
                    yield
                    if True:
                        S.pool(lambda e: e.tensor_copy(out=kin[:, :n], in_=C[:, :n]), ["C"], [kk_])
                    else:
                        S.act(lambda e: e.activation(out=kin[:, :n], in_=C[:, :n], func=AF.Copy), ["C"], [kk_])
                    yield
                    if gi == 0:
                        S.dve(lambda e: e.tensor_scalar(out=kout[:, :16], in0=C[:, :16], scalar1=B[:, 15:16], scalar2=None,
                                                        op0=ALU.mult), ["C", kB], [ko])
                    else:
                        S.dve(lambda e: e.tensor_tensor(
                            out=kout[:, :].rearrange("p (c k) -> p c k", k=64),
                            in0=C[:, :].rearrange("p (c k) -> p c k", k=64),
                            in1=B[:, :].rearrange("p (c k) -> p c k", k=64)[:, :, 63:64].to_broadcast([128, 8, 64]),
                            op=ALU.mult), ["C", kB], [ko])
                    yield

                def stage2(it):
                    hd, gi = iters[it]
                    g0, n = GG[gi]
                    z = it % 2
                    B, SG, qin, kin, kout, vtok, Gc = Bs[z], SGs[z], qins[z], kins[z], kouts[z], vtoks[z], Gcs[z]
                    kB, kSG, kq, kk_, ko, kv, kG = ("B", z), ("SG", z), ("qin", z), ("kin", z), ("kout", z), ("vtok", z), ("Gc", z)
                    tl_list = tiles_of_group(gi)
                    nch = 1 if gi == 0 else 8
                    if gi == 0:
                        S.dve(lambda e: e.memset(S32[:, 0, :], 0.0), [], [("S32", 0)])
                    yield
                    for li, ti in enumerate(tl_list):
                        t0, nt = TT[ti]
                        S.pe(lambda e, li=li, nt=nt: e.transpose(psb[:nt, li * 128:(li + 1) * 128],
                                                                  kout[:, li * 128:li * 128 + nt], self.identb[:]), [ko], ["psb"])
                    yield
                    if gi == 0:
                        S.act(lambda e: e.activation(out=ktok[:16, 0, :], in_=psb[:16, 0:128], func=AF.Copy), ["psb"], ["ktok"])
                    else:
                        S.act(lambda e: e.activation(out=ktok[:, :, :], in_=psb[:, 0:512].rearrange("p (a b) -> p a b", b=128),
                                                     func=AF.Copy), ["psb"], ["ktok"])
                    yield
                    for li, ti in enumerate(tl_list):
                        t0, nt = TT[ti]
                        S.pe(lambda e, li=li, nt=nt: e.matmul(ps[5][:nt, li * 128:li * 128 + nt], lhsT=kin[:, li * 128:li * 128 + nt],
                                                               rhs=qin[:, li * 128:li * 128 + nt], start=True, stop=True),
                             [kk_, kq], ["ps5"])
                    yield
                    if gi == 0:
                        S.dve(lambda e: e.tensor_tensor(out=attT[:16, 0, :16], in0=ps[5][:16, 0:16], in1=self.mask2[:16, :16],
                                                        op=ALU.mult), ["ps5"], ["attT"])
                    else:
                        S.dve(lambda e: e.tensor_tensor(
                            out=attT[:, :, :], in0=ps[5][:, :].rearrange("p (a b) -> p a b", b=128),
                            in1=self.mask2[:, :].unsqueeze(1).to_broadcast([128, 4, 128]), op=ALU.mult), ["ps5"], ["attT"])
                    yield
                    for cl in range(nch):
                        li, r0 = cl // 2, (cl % 2) * 64
                        nr = 16 if gi == 0 else 64
                        bi = 3 + cl % 2
                        S.pe(lambda e, cl=cl, li=li, r0=r0, nr=nr, bi=bi: e.matmul(
                            ps[bi][:, (cl // 2) * 128:(cl // 2 + 1) * 128], lhsT=ktok[r0:r0 + nr, li, :],
                            rhs=vtok[r0:r0 + nr, li, :], start=True, stop=True), ["ktok", kv], ["ps%d" % bi])
                    yield
                    for cl in range(nch):
                        bi = 3 + cl % 2
                        dcol = B[:, 15:16] if gi == 0 else B[:, cl * 64 + 63:cl * 64 + 64]
                        S.dve(lambda e, cl=cl, bi=bi, dcol=dcol: e.scalar_tensor_tensor(
                            out=S32[:, cl + 1, :], in0=S32[:, cl, :], scalar=dcol,
                            in1=ps[bi][:, (cl // 2) * 128:(cl // 2 + 1) * 128], op0=ALU.mult, op1=ALU.add),
                            [("S32", cl), kB, "ps%d" % bi], [("S32", cl + 1)])
                    yield
                    S.act(lambda e: e.activation(out=Sb[:, 0:nch, :], in_=S32[:, 0:nch, :], func=AF.Copy),
                          [("S32", c) for c in range(nch)], ["Sb"])
                    yield
                    for li, ti in enumerate(tl_list):
                        t0, nt = TT[ti]
                        S.pe(lambda e, li=li, nt=nt: e.matmul(
                            ps[6][:, li * 128:li * 128 + nt], lhsT=vtok[:nt, li, :], rhs=attT[:nt, li, :nt],
                            start=True, stop=(gi == 0)), [kv, "attT"], ["ps6"])
                        if gi > 0:
                            for hh in range(2):
                                cl = 2 * li + hh
                                S.pe(lambda e, hh=hh, cl=cl: e.matmul(
                                    ps[6][:, cl * 64:(cl + 1) * 64], lhsT=Sb[:, cl, :], rhs=qin[:, cl * 64:(cl + 1) * 64],
                                    start=False, stop=(hh == 1)), ["Sb", kq], ["ps6"])
                    yield
                    S.dve(lambda e: e.tensor_copy(out=S32[:, 0, :], in_=S32[:, nch, :]), [("S32", nch), "Sb"], [("S32", 0)])
                    yield
                    self.rstd_from([(ps[6][:, :n], ["ps6"])], n, sqh, rtmp, rstd, ps[5], "ps5", dscale=1.0 / 128)
                    yield
                    S.dve(lambda e: e.scalar_tensor_tensor(
                        out=O32[:, :n], in0=ps[6][:, :n], scalar=self.colv[:, 80 + hd:81 + hd], in1=rstd[:, :n],
                        op0=ALU.mult, op1=ALU.mult), ["ps6", "rstd"], ["O32"])
                    yield
                    if False:
                        S.pool(lambda e: e.tensor_tensor(out=O32[:, :n], in0=O32[:, :n], in1=SG[:, :n], op=ALU.mult),
                               ["O32", kSG], ["O32"])
                        yield
                        S.pool(lambda e: e.tensor_tensor(out=og[:, hd, g0:g0 + n], in0=O32[:, :n], in1=Gc[:, :n], op=ALU.mult),
                               ["O32", kG], [("og", gi)])
                    else:
                        S.pool(lambda e: e.tensor_tensor(out=og[:, hd, g0:g0 + n], in0=O32[:, :n], in1=SG[:, :n], op=ALU.mult),
                               ["O32", kSG], [("og", gi)])

                def drain(g):
                    for _ in g:
                        pass

                def zipper(ga, gb, ra=1, rb=2):
                    alive_a, alive_b = True, True
                    while alive_a or alive_b:
                        for _ in range(ra):
                            if alive_a:
                                try:
                                    next(ga)
                                except StopIteration:
                                    alive_a = False
                        for _ in range(rb):
                            if alive_b:
                                try:
                                    next(gb)
                                except StopIteration:
                                    alive_b = False

                if nheads > 0:
                    load_head(0)
                    drain(stage1(0))
                for it in range(len(iters)):
                    if it + 1 < len(iters):
                        zipper(stage1(it + 1), stage2(it), 3, 2)
                    else:
                        drain(stage2(it))
            self.S.barrier()
            if self.stop in ("mix0a", "mix0b", "mix0c"):
                return
            with ExitStack() as st:
                self.out_proj_residual(st, self.a_w_out[0], og, 0, 1, "og")

    def ffn(self, s, l):
        S, ps, hT = self.S, self.ps, self.hT
        halves = [(0, 1032), (1032, 1032)]
        BLK = 344
        with ExitStack() as st0:
            halo = self.sb(st0, "halo", [128, 8, 2], BF16)
            S.dve(lambda e: e.memset(halo[:], 0.0), [], ["halo"])
            def half(hf, h0, nh):
                with ExitStack() as st1:
                    act = self.sb(st1, "act", [128, NJ, 1032], BF16)
                    wd = [self.sb(st1, "wd%d" % i, [128, NJ, 128], BF16) for i in range(2)]
                    blocks = [(o, BLK) for o in range(0, nh, BLK)]

                    def load_dc(dc):
                        sl = dc % 2
                        self.wload(wd[sl][:, 0:21, :],
                                   self.w_down[l, 0:2688, dc * 128:(dc + 1) * 128].rearrange("(j p) n -> p j n", p=128),
                                   sl, ("wd", sl, 0))
                        self.wload(wd[sl][0:64, 21, :], self.w_down[l, 2688:2752, dc * 128:(dc + 1) * 128], sl, ("wd", sl, 1))
                    with ExitStack() as st:
                        xn = self.sb(st, "xn2", [128, 8, 1034], BF16)
                        sq = self.sb(st, "sq2", [128, 8, 512], BF16)
                        rtmp = self.sb(st, "rtmp2", [128, 512], F32)
                        rstd = self.sb(st, "rstd2", [128, 512], F32)
                        S.dve(lambda e: e.tensor_copy(out=xn[:, :, 0:2], in_=halo[:]), ["halo"], [("xn2", -1)])
                        subs = list(blocks)
                        for si, (o, nn) in enumerate(subs):
                            g0 = h0 + o
                            self.rstd_from([(hT[:, c, g0:g0 + nn], []) for c in range(8)], nn, sq, rtmp, rstd, ps[2], "ps2")
                            for c in range(8):
                                S.dve(lambda e, c=c, g0=g0, nn=nn, o=o: e.scalar_tensor_tensor(
                                    out=xn[:, c, 2 + o:2 + o + nn], in0=hT[:, c, g0:g0 + nn], scalar=self.gcol(l, 2, c),
                                    in1=rstd[:, :nn], op0=ALU.mult, op1=ALU.mult), ["rstd"], [("xn2", si)])
                        xkeys = [("xn2", -1)] + [("xn2", si) for si in range(len(subs))]
                        S.dve(lambda e, nh=nh: e.tensor_copy(out=halo[:], in_=xn[:, :, nh:nh + 2]), xkeys, ["halo"])
                        wu = [self.sb(st, "wu%d" % i, [128, 8, 2, 128], BF16) for i in range(2)]
                        G32 = [self.sb(st, "G32_%d" % i, [128, 352], F32) for i in range(3)]
                        V32 = [self.sb(st, "V32_%d" % i, [128, 352], F32) for i in range(3)]
                        SGf = [self.sb(st, "SGf_%d" % i, [128, 352], F32) for i in range(3)]
                        wup = self.w_up[l].rearrange("(kc p) n -> p kc n", p=128)
                        units = [(j, bi_, o, nb) for j in range(NJ) for bi_, (o, nb) in enumerate(blocks)]

                        def load_pair(j):
                            mj = 128 if j < 21 else 64
                            sl = j % 2
                            for part in range(2):
                                self.wload(wu[sl][:, :, part, :mj], wup[:, :, part * DFF + j * 128: part * DFF + j * 128 + mj],
                                           sl, ("wu", sl, part))

                        load_pair(0)

                        def front(k):
                            j, bi_, o, nb = units[k]
                            mj = 128 if j < 21 else 64
                            sl = j % 2
                            w = wu[sl]
                            ub = k % 3
                            if bi_ == 0 and j + 1 < NJ:
                                load_pair(j + 1)
                            if bi_ == 0 and j == NJ - 3:
                                load_dc(0)
                            if bi_ == 0 and j == NJ - 2:
                                load_dc(1)
                            for part in range(2):
                                bi = part + 2 * ub
                                bank = ps[bi]
                                bk = "ps%d" % bi
                                for kc in range(8):
                                    S.pe(lambda e, bank=bank, part=part, kc=kc: e.matmul(
                                        bank[:mj, :nb + 2], lhsT=w[:, kc, part, :mj], rhs=xn[:, kc, o:o + nb + 2],
                                        start=(kc == 0), stop=(kc == 7)), [("wu", sl, part)] + xkeys, [bk])
                                dst = (G32 if part == 0 else V32)[ub]
                                dk = ("G32" if part == 0 else "V32", ub)
                                S.act(lambda e, bank=bank, dst=dst, part=part: e.activation(
                                    out=dst[:mj, :nb], in_=bank[:mj, 0:nb], func=AF.Identity, scale=self.ccol(l, 0, part, j)[:mj, :]),
                                    [bk], [dk])

                        def taps(k):
                            j, bi_, o, nb = units[k]
                            mj = 128 if j < 21 else 64
                            ub = k % 3
                            for tap in (1, 2):
                                for part in range(2):
                                    bi = part + 2 * ub
                                    bank = ps[bi]
                                    bk = "ps%d" % bi
                                    dst = (G32 if part == 0 else V32)[ub]
                                    dk = ("G32" if part == 0 else "V32", ub)
                                    S.dve(lambda e, bank=bank, dst=dst, part=part, tap=tap: e.scalar_tensor_tensor(
                                        out=dst[:mj, :nb], in0=bank[:mj, tap:tap + nb], scalar=self.ccol(l, tap, part, j)[:mj, :],
                                        in1=dst[:mj, :nb], op0=ALU.mult, op1=ALU.add), [bk, dk], [dk])

                        def back(k):
                            j, bi_, o, nb = units[k]
                            mj = 128 if j < 21 else 64
                            ub = k % 3
                            S.act(lambda e: e.activation(out=SGf[ub][:mj, :nb], in_=G32[ub][:mj, :nb], func=AF.Silu),
                                  [("G32", ub)], [("SGf", ub)])
                            S.dve(lambda e: e.tensor_tensor(out=act[:mj, j, o:o + nb], in0=SGf[ub][:mj, :nb], in1=V32[ub][:mj, :nb],
                                                            op=ALU.mult), [("SGf", ub), ("V32", ub)], [("act", j, bi_)])

                        for k in range(len(units)):
                            front(k)
                            if k > 0:
                                back(k - 1)
                            taps(k)
                        back(len(units) - 1)
                    S.barrier()
                    with ExitStack() as st:
                        mix = self.sb(st, "mixf", [128, 8, 1032], F32)
                        sq3 = self.sb(st, "sq3", [128, 8, 512], BF16)
                        rtmp3 = self.sb(st, "rtmp3", [128, 512], F32)
                        rstd3 = self.sb(st, "rstd3", [128, 512], F32)
                        tmp3 = self.sb(st, "tmp3", [128, 512], F32)
                        for dc in range(8):
                            sl = dc % 2
                            w = wd[sl]
                            if 1 <= dc and dc + 1 < 8:
                                load_dc(dc + 1)
                            for bi_, (o, nb) in enumerate(blocks):
                                bq = (dc * len(blocks) + bi_) % 2
                                bank = ps[bq]
                                bk = "ps%d" % bq
                                for j in range(NJ):
                                    mj = 128 if j < 21 else 64
                                    S.pe(lambda e, bank=bank, j=j, mj=mj, o=o, nb=nb, w=w: e.matmul(
                                        bank[:, :nb], lhsT=w[:mj, j, :], rhs=act[:mj, j, o:o + nb],
                                        start=(j == 0), stop=(j == NJ - 1)),
                                        [("wd", sl, 0), ("wd", sl, 1), ("act", j, bi_)], [bk])
                                S.act(lambda e, bank=bank, dc=dc, o=o, nb=nb: e.activation(
                                    out=mix[:, dc, o:o + nb], in_=bank[:, :nb], func=AF.Copy), [bk], [("mix", dc, bi_)])
                        for bi_, (o, nb) in enumerate(blocks):
                            g0 = h0 + o
                            self.rstd_from([(mix[:, dc, o:o + nb], [("mix", dc, bi_)]) for dc in range(8)], nb, sq3, rtmp3, rstd3,
                                           ps[2], "ps2")
                            for dc in range(8):
                                S.dve(lambda e, dc=dc, o=o, nb=nb: e.scalar_tensor_tensor(
                                    out=tmp3[:, :nb], in0=mix[:, dc, o:o + nb], scalar=self.gcol(l, 3, dc), in1=rstd3[:, :nb],
                                    op0=ALU.mult, op1=ALU.mult), [("mix", dc, bi_), "rstd"], ["tmp3"])
                                S.dve(lambda e, dc=dc, g0=g0, nb=nb: e.tensor_tensor(
                                    out=hT[:, dc, g0:g0 + nb], in0=hT[:, dc, g0:g0 + nb], in1=tmp3[:, :nb], op=ALU.add),
                                    ["tmp3"], [("hT", dc, hf, bi_)])
                    S.barrier()

            for hf, (h0, nh) in enumerate(halves):
                half(hf, h0, nh)

    def fox(self, s):
        S, ps, psb, hT = self.S, self.ps, self.psb, self.hT
        scale = 1.0 / 8.0
        with ExitStack() as st0:
            O = self.sb(st0, "O", [128, 8, T], BF16)
            with ExitStack() as st:
                xr = self.sb(st, "xr", [128, 8, T], BF16)
                Ctok = self.sb(st, "Ctok", [128, 17, 16], F32)
                Cpb = self.sb(st, "Cpb", [16, T], BF16)
                with ExitStack() as stn:
                    sq = self.sb(stn, "sq4", [128, 8, 512], BF16)
                    rtmp = self.sb(stn, "rtmp4", [128, 512], F32)
                    rstd = self.sb(stn, "rstd4", [128, 512], F32)
                    for gi, (g0, n) in enumerate(GG):
                        self.rstd_from([(hT[:, c, g0:g0 + n], []) for c in range(8)], n, sq, rtmp, rstd, ps[2], "ps2")
                        for c in range(8):
                            S.dve(lambda e, c=c, g0=g0, n=n: e.tensor_tensor(
                                out=xr[:, c, g0:g0 + n], in0=hT[:, c, g0:g0 + n], in1=rstd[:, :n], op=ALU.mult),
                                ["rstd"], [("xr", gi)])
                    S.barrier()
                xrk = [("xr", gi) for gi in range(5)]
                kvw = self.kv_w.rearrange("(kc p) n -> p kc n", p=128)
                wqv = self.b_w_q[0].rearrange("(kc p) n -> p kc n", p=128)
                gkv = self.colv[:, 88:96]
                with ExitStack() as stf:
                    wfg = self.sb(stf, "wfg", [128, 8, 16], BF16)
                    Cp = self.sb(stf, "Cp", [16, T], F32)
                    sp = self.sb(stf, "sp", [16, 512], F32)
                    self.wload(wfg[:, :, :], kvw[:, :, 2048:2064], 0, "wfg")
                    S.dve(lambda e: e.tensor_tensor(out=wfg[:, :, :], in0=wfg[:, :, :],
                                                    in1=gkv.unsqueeze(2).to_broadcast([128, 8, 16]), op=ALU.mult),
                          ["wfg"], ["wfg"])
                    for gi, (g0, n) in enumerate(GG):
                        for kc in range(8):
                            S.pe(lambda e, kc=kc, g0=g0, n=n: e.matmul(ps[0][:16, :n], lhsT=wfg[:, kc, :], rhs=xr[:, kc, g0:g0 + n],
                                                                        start=(kc == 0), stop=(kc == 7)), ["wfg", ("xr", gi)], ["ps0"])
                        S.act(lambda e, n=n: e.activation(out=sp[:, :n], in_=ps[0][:16, :n], func=AF.Exp, scale=-1.0,
                                                          bias=self.nfgb[:, 0:1]), ["ps0", "nfgb2"], ["sp"])
                        S.act(lambda e, n=n: e.activation(out=sp[:, :n], in_=sp[:, :n], func=AF.Ln, scale=1.0,
                                                          bias=self.onec[0:16, 0:1]), ["sp"], ["sp"])
                        init = 0.0 if gi == 0 else Cp[:, g0 - 1:g0]
                        S.dve(lambda e, g0=g0, n=n, init=init: e.tensor_tensor_scan(
                            out=Cp[:, g0:g0 + n], data0=self.onesf[:16, :n], data1=sp[:, :n], initial=init,
                            op0=ALU.mult, op1=ALU.add), ["sp", ("Cp", gi - 1)], [("Cp", gi)])
                    cpk = [("Cp", gi) for gi in range(5)]
                    S.act(lambda e: e.activation(out=Cpb[:, :], in_=Cp[:, :], func=AF.Copy), cpk, ["Cpb"])
                    for ti, (t0, nt) in enumerate(TT):
                        S.pe(lambda e, t0=t0, nt=nt: e.transpose(ps[1][:nt, 0:16], Cp[:, t0:t0 + nt], self.i16[:]), cpk, ["ps1"])
                        S.dve(lambda e, ti=ti, nt=nt: e.tensor_copy(out=Ctok[:nt, ti, :], in_=ps[1][:nt, 0:16]), ["ps1"], ["Ctok"])
                    S.barrier()
                wp = self.sb(st, "wp", [128, 8, 3, 128], BF16)
                KTh = [[self.sb(st, "KT%d%d" % (z, i), [65, T], BF16) for i in range(2)] for z in range(2)]
                QTh = [[self.sb(st, "QT%d%d" % (z, i), [65, T], BF16) for i in range(2)] for z in range(2)]
                Vas = [self.sb(st, "Va%d" % z, [128, 17, 192], BF16) for z in range(2)]
                NPT = 3
                PT = [self.sb(st, "PT%d" % i, [128, 512], BF16) for i in range(NPT)]
                rinv = [self.sb(st, "rinv%d" % i, [128, 512], F32) for i in range(1)]
                for z in range(2):
                    S.pool(lambda e, z=z: e.memset(Vas[z][:, :, 64:128], 1.0), [], [("Vones", z)])
                    for hh in range(2):
                        S.pool(lambda e, z=z, hh=hh: e.memset(KTh[z][hh][64:65, :], 1.0), [], [("Kone", z, hh)])
                g0col = self.colv[:, 32:40]

                def load_wp(p):
                    self.wload(wp[:, :, 0, :], kvw[:, :, p * 128:(p + 1) * 128], 0, ("wp", 0))
                    self.wload(wp[:, :, 1, :], kvw[:, :, D + p * 128:D + (p + 1) * 128], 0, ("wp", 1))
                    self.wload(wp[:, :, 2, :], wqv[:, :, p * 128:(p + 1) * 128], 0, ("wp", 2))

                def proj_gen(p):
                    pz = p % 2
                    w = wp
                    for m in range(3):
                        gc = gkv if m < 2 else g0col
                        S.dve(lambda e, m=m, gc=gc: e.tensor_tensor(
                            out=w[:, :, m, :], in0=w[:, :, m, :], in1=gc.unsqueeze(2).to_broadcast([128, 8, 128]), op=ALU.mult),
                            [("wp", m)], [("wp", m)])
                    yield
                    for gi, (g0, n) in enumerate(GG):
                        for (m, dst, dk) in ((0, KTh[pz], "KT"), (2, QTh[pz], "QT")):
                            bank = ps[m // 2]
                            bk = "ps%d" % (m // 2)
                            for kc in range(8):
                                S.pe(lambda e, bank=bank, m=m, kc=kc, g0=g0, n=n: e.matmul(
                                    bank[:, :n], lhsT=w[:, kc, m, :], rhs=xr[:, kc, g0:g0 + n], start=(kc == 0), stop=(kc == 7)),
                                    [("wp", m), ("xr", gi)], [bk])
                            for hh in range(2):
                                S.dve(lambda e, bank=bank, dst=dst, hh=hh, g0=g0, n=n: e.tensor_copy(
                                    out=dst[hh][0:64, g0:g0 + n], in_=bank[hh * 64:(hh + 1) * 64, :n]), [bk], [(dk, pz, hh, gi)])
                            yield
                        for hh in range(2):
                            h = 2 * p + hh
                            S.pe(lambda e, h=h, g0=g0, n=n: e.matmul(ps[0][0:65, :n], lhsT=self.selq[:, h, :], rhs=Cpb[:, g0:g0 + n],
                                                                      start=True, stop=True), ["Cpb", "selq"], ["ps0"])
                            S.dve(lambda e, hh=hh, g0=g0, n=n: e.tensor_copy(out=QTh[pz][hh][64:65, g0:g0 + n], in_=ps[0][64:65, :n]),
                                  ["ps0"], [("QT", pz, hh, gi)])
                        yield
                    for kt, (t0, nt) in enumerate(TT):
                        vb = kt % 2
                        for kc in range(8):
                            S.pe(lambda e, kc=kc, t0=t0, nt=nt, vb=vb: e.matmul(
                                ps[vb][:nt, 0:128], lhsT=xr[:, kc, t0:t0 + nt], rhs=w[:, kc, 1, :], start=(kc == 0), stop=(kc == 7)),
                                [("wp", 1)] + xrk, ["ps%d" % vb])
                        S.dve(lambda e, kt=kt, nt=nt, vb=vb: e.tensor_copy(
                            out=Vas[pz][:nt, kt, :].rearrange("p (a b) -> p a b", b=64)[:, 0:3:2, :],
                            in_=ps[vb][:nt, 0:128].rearrange("p (a b) -> p a b", b=64)), ["ps%d" % vb], [("Va", pz, kt)])
                        yield
                    if p + 1 < 8:
                        load_wp(p + 1)
                    yield

                itc_box = [0]

                def att_gen(p):
                    pz = p % 2
                    Va = Vas[pz]
                    its = []
                    for hh in range(2):
                        for gi, (g0, n) in enumerate(GG):
                            tl = tiles_of_group(gi)
                            for kt in range(tl[-1] + 1):
                                its.append((hh, gi, kt))
                    LOOK = 2
                    SB = [5, 6, 2]
                    itc0 = itc_box[0]

                    def emit_qk(ix):
                        hh, gi, kt = its[ix]
                        g0, n = GG[gi]
                        tl = tiles_of_group(gi)
                        k0, nk = TT[kt]
                        c0 = (kt - tl[0]) * 128 if (kt in tl and gi > 0) else 0
                        nq = n - c0
                        sbi = SB[(itc0 + ix) % 3]
                        sb_ = ps[sbi]
                        KT, QT = KTh[pz][hh], QTh[pz][hh]
                        S.pe(lambda e: e.matmul(sb_[:nk, :nq], lhsT=KT[0:65, k0:k0 + nk], rhs=QT[0:65, g0 + c0:g0 + c0 + nq],
                                                start=True, stop=True),
                             [("KT", pz, hh, g_) for g_ in range(5)] + [("QT", pz, hh, gi), ("Kone", pz, hh)], ["ps%d" % sbi])

                    def emit_rest(ix):
                        hh, gi, kt = its[ix]
                        h = 2 * p + hh
                        g0, n = GG[gi]
                        tl = tiles_of_group(gi)
                        last = tl[-1]
                        k0, nk = TT[kt]
                        c0 = (kt - tl[0]) * 128 if (kt in tl and gi > 0) else 0
                        nq = n - c0
                        sbi = SB[(itc0 + ix) % 3]
                        sb_ = ps[sbi]
                        sbk = "ps%d" % sbi
                        pt = PT[(itc0 + ix) % NPT]
                        ptk = ("PT", (itc0 + ix) % NPT)
                        vlo = 0 if hh == 0 else 64
                        orow = hh * 64
                        lrow = 64 - orow
                        obi = 3 + (gi + hh) % 2
                        ob = ps[obi]
                        obk = "ps%d" % obi
                        S.act(lambda e: e.activation(out=pt[:nk, :nq], in_=sb_[:nk, :nq], func=AF.Exp, scale=scale,
                                                     bias=Ctok[:nk, kt, h:h + 1]), [sbk, "Ctok"], [ptk])
                        if kt in tl:
                            qn = min(128, nq)
                            S.pool(lambda e: e.tensor_tensor(out=pt[:nk, 0:qn], in0=pt[:nk, 0:qn], in1=self.triu[:nk, :qn],
                                                             op=ALU.mult), [ptk], [ptk])
                        S.pe(lambda e: e.matmul(ob[:, c0:c0 + nq], lhsT=Va[:nk, kt, vlo:vlo + 128], rhs=pt[:nk, 0:nq],
                                                start=(kt == 0), stop=(kt == last)), [ptk, ("Va", pz, kt), ("Vones", pz)], [obk])
                        if kt == last:
                            rv = rinv[0]
                            rk = ("rinv", 0)
                            S.dve(lambda e: e.reciprocal(out=rv[orow:orow + 64, :n], in_=ob[lrow:lrow + 64, :n]), [obk], [rk])
                            S.dve(lambda e: e.tensor_tensor(out=O[orow:orow + 64, p, g0:g0 + n], in0=ob[orow:orow + 64, :n],
                                                            in1=rv[orow:orow + 64, :n], op=ALU.mult), [obk, rk], [("O", gi)])

                    for ix in range(min(LOOK, len(its))):
                        emit_qk(ix)
                    for ix in range(len(its)):
                        if ix + LOOK < len(its):
                            emit_qk(ix + LOOK)
                        emit_rest(ix)
                        yield
                    itc_box[0] += len(its)

                def drain(g):
                    for _ in g:
                        pass

                def zipper(ga, gb, ra, rb):
                    alive_a, alive_b = True, True
                    while alive_a or alive_b:
                        for _ in range(ra):
                            if alive_a:
                                try:
                                    next(ga)
                                except StopIteration:
                                    alive_a = False
                        for _ in range(rb):
                            if alive_b:
                                try:
                                    next(gb)
                                except StopIteration:
                                    alive_b = False

                load_wp(0)
                drain(proj_gen(0))
                for p in range(8):
                    if p + 1 < 8:
                        zipper(att_gen(p), proj_gen(p + 1), 5, 2)
                    else:
                        drain(att_gen(p))
            self.S.barrier()
            with ExitStack() as st:
                self.out_proj_residual(st, self.b_w_out[0], O, 1, 1, "O")


_CACHE = {}


def _get_nc(nseq=2, stop=None):
    key = (nseq, stop)
    if key not in _CACHE:
        _CACHE[key] = Builder(nseq, stop).build()
    return _CACHE[key]


def kernel(**inputs):
    ncores = 8
    nc = _get_nc(2, None)
    shared = {k: np.ascontiguousarray(np.asarray(v, dtype=np.float32)) for k, v in inputs.items() if k != "x"}
    x = np.ascontiguousarray(np.asarray(inputs["x"], dtype=np.float32))
    in_maps = []
    for c in range(ncores):
        m = dict(shared)
        m["x"] = x[2 * c:2 * c + 2]
        in_maps.append(m)
    res = run_bass_kernel_spmd(nc, in_maps, core_ids=list(range(ncores)))
    return np.concatenate([np.asarray(r["out"]) for r in res.results], axis=0).astype(np.float32)
```
